# Optimizing a Trainium2 kernel written in Bass

```python
import math
import jax, jax.numpy as jnp
from jax import lax
import numpy as np

D_MODEL = 2048
BATCH = 4
SEQ = 8192
DEPTH = 2
DEC_BATCH = 8
DEC_SEQ = 2048
PAST_LEN = 128

N_MIXERS = 2
N_S5_LAYERS = (DEPTH + N_MIXERS - 1) // N_MIXERS
N_ML_LAYERS = DEPTH // N_MIXERS
N_DIR = 2
D_FF = 5632
CHUNK = 128
EPS = 1e-6
S5_WIDTH = D_MODEL
S5_GROUP = 16
S5_GROUPS = S5_WIDTH // S5_GROUP
S5_STATE = 64
ML_INNER = 2 * D_MODEL
ML_HEADS = 16
ML_HEAD_DIM = ML_INNER // ML_HEADS
ML_QKV_BLOCK = 4
ML_CONV = 5

kernel_name = "bidir_s5_mlstm_macaron_encoder"


def _rmsnorm(x, g):
    xf = x.astype(jnp.float32)
    y = xf * lax.rsqrt(jnp.mean(xf * xf, axis=-1, keepdims=True) + EPS)
    return (y * g.astype(jnp.float32)).astype(x.dtype)


def _swiglu(x, w_in, w_out):
    gate, up = jnp.split(x @ w_in, 2, axis=-1)
    return (jax.nn.silu(gate) * up) @ w_out


def _s5_combine(left, right):
    a_l, b_l = left
    a_r, b_r = right
    return a_l * a_r, a_r * b_l + b_r


def _s5_direction(u, lam_bar, b_bar, c):
    bsz, length = u.shape[0], u.shape[1]
    uc = jnp.moveaxis(u.reshape(bsz, length // CHUNK, CHUNK, S5_GROUPS, S5_GROUP), 1, 0)
    a = jnp.broadcast_to(lam_bar, (bsz, CHUNK, S5_GROUPS, S5_STATE))

    def step(h, u_blk):
        bu = jnp.einsum('btgc,gpc->btgp', u_blk.astype(jnp.complex64), b_bar)
        a_cum, hs = lax.associative_scan(_s5_combine, (a, bu), axis=1)
        hs = hs + a_cum * h[:, None]
        y = jnp.einsum('btgp,gcp->btgc', hs, c).real
        return hs[:, -1], y

    h0 = jnp.zeros((bsz, S5_GROUPS, S5_STATE), jnp.complex64)
    _, ys = lax.scan(step, h0, uc)
    return jnp.moveaxis(ys, 0, 1).reshape(bsz, length, S5_GROUPS, S5_GROUP)


def _s5_mixer(x, w_in, lam_re, lam_im, log_step, b_re, b_im, c_re, c_im, d, w_glu):
    f32 = jnp.float32
    bsz, length, _ = x.shape
    u = (x @ w_in).astype(f32)
    ug = u.reshape(bsz, length, S5_GROUPS, S5_GROUP)
    y = d.astype(f32) * u
    for direction in range(N_DIR):
        lam = lax.complex(jnp.minimum(lam_re[direction].astype(f32), -1e-4), lam_im[direction].astype(f32))
        delta = jnp.exp(log_step[direction].astype(f32))[:, None]
        lam_bar = jnp.exp(lam * delta)
        b = lax.complex(b_re[direction].astype(f32), b_im[direction].astype(f32))
        b_bar = ((lam_bar - 1.0) / lam)[..., None] * b
        c = lax.complex(c_re[direction].astype(f32), c_im[direction].astype(f32))
        if direction == 0:
            yd = _s5_direction(ug, lam_bar, b_bar, c)
        else:
            yd = _s5_direction(ug[:, ::-1], lam_bar, b_bar, c)[:, ::-1]
        y = y + yd.reshape(bsz, length, S5_WIDTH)
    y = jax.nn.gelu(y).astype(x.dtype)
    val, gate = jnp.split(y @ w_glu, 2, axis=-1)
    return val * jax.nn.sigmoid(gate)


def _mlstm_direction(q, k, v, log_i, log_f):
    bsz, nh, length, dh = q.shape
    nc = length // CHUNK

    def chunks(t):
        return jnp.moveaxis(t.reshape((bsz, nh, nc, CHUNK) + t.shape[3:]), 2, 0)

    mask = jnp.tril(jnp.ones((CHUNK, CHUNK), dtype=bool))

    def step(carry, blk):
        c_mat, n_vec, m = carry
        qb, kb, vb, li, lf = blk
        bcum = jnp.cumsum(lf, axis=-1)
        g = bcum[..., -1]
        a = bcum + m[..., None]
        dmat = jnp.where(mask, bcum[..., :, None] - bcum[..., None, :] + li[..., None, :], -jnp.inf)
        m_t = jnp.maximum(a, jnp.max(dmat, axis=-1))
        s = jnp.einsum('bhtd,bhsd->bhts', qb, kb) * jnp.exp(dmat - m_t[..., None])
        e = jnp.exp(a - m_t)
        num = e[..., None] * jnp.einsum('bhvk,bhtk->bhtv', c_mat, qb) + jnp.einsum('bhts,bhsv->bhtv', s, vb)
        den = e * jnp.einsum('bhk,bhtk->bht', n_vec, qb) + jnp.sum(s, axis=-1)
        h = num / jnp.maximum(jnp.abs(den), jnp.exp(-m_t))[..., None]
        r = g[..., None] - bcum + li
        m_new = jnp.maximum(g + m, jnp.max(r, axis=-1))
        decay = jnp.exp(g + m - m_new)
        wr = jnp.exp(r - m_new[..., None])
        c_new = decay[..., None, None] * c_mat + jnp.einsum('bhsv,bhsk->bhvk', vb * wr[..., None], kb)
        n_new = decay[..., None] * n_vec + jnp.einsum('bhs,bhsk->bhk', wr, kb)
        return (c_new, n_new, m_new), h

    init = (jnp.zeros((bsz, nh, dh, dh), jnp.float32), jnp.zeros((bsz, nh, dh), jnp.float32), jnp.zeros((bsz, nh), jnp.float32))
    _, hs = lax.scan(step, init, (chunks(q), chunks(k), chunks(v), chunks(log_i), chunks(log_f)))
    return jnp.moveaxis(hs, 0, 2).reshape(bsz, nh, length, dh)


def _mlstm_mixer(x, w_in, conv_w, conv_b, wq, wk, wv, w_gates, b_gates, norm_g, skip, w_out):
    f32 = jnp.float32
    bsz, length, _ = x.shape
    xm, z = jnp.split(x @ w_in, 2, axis=-1)
    xc = lax.conv_general_dilated(xm, conv_w[:, None, :], window_strides=(1,), padding=[(ML_CONV // 2, ML_CONV // 2)], dimension_numbers=('NWC', 'WIO', 'NWC'), feature_group_count=ML_INNER)
    xc = jax.nn.silu(xc + conv_b)

    def blockdiag(t, w):
        return jnp.einsum('blnc,ncd->blnd', t.reshape(bsz, length, ML_INNER // ML_QKV_BLOCK, ML_QKV_BLOCK), w).reshape(bsz, length, ML_INNER)

    q = blockdiag(xc, wq)
    k = blockdiag(xc, wk)
    v = blockdiag(xm, wv)
    gates = (q @ w_gates[0] + k @ w_gates[1] + v @ w_gates[2] + b_gates).astype(f32)
    gates = jnp.moveaxis(gates.reshape(bsz, length, N_DIR, 2, ML_HEADS), 1, -1)

    def heads(t):
        return t.reshape(bsz, length, ML_HEADS, ML_HEAD_DIM).transpose(0, 2, 1, 3).astype(f32)

    qh = heads(q)
    kh = heads(k) * (ML_HEAD_DIM ** -0.5)
    vh = heads(v)
    h_fw = _mlstm_direction(qh, kh, vh, gates[:, 0, 0], jax.nn.log_sigmoid(gates[:, 0, 1]))
    h_bw = _mlstm_direction(qh[:, :, ::-1], kh[:, :, ::-1], vh[:, :, ::-1], gates[:, 1, 0, :, ::-1], jax.nn.log_sigmoid(gates[:, 1, 1, :, ::-1]))[:, :, ::-1]
    h = h_fw + h_bw
    mu = jnp.mean(h, axis=-1, keepdims=True)
    var = jnp.mean(jnp.square(h - mu), axis=-1, keepdims=True)
    hn = ((h - mu) * lax.rsqrt(var + EPS)).transpose(0, 2, 1, 3).reshape(bsz, length, ML_INNER) * norm_g.astype(f32)
    out = jax.nn.sigmoid(z.astype(f32)) * (hn + skip.astype(f32) * xc.astype(f32))
    return out.astype(x.dtype) @ w_out


def _trunk(x, norm_g, final_g, ffn_w_in, ffn_w_out, s5_w_in, s5_lambda_re, s5_lambda_im, s5_log_step, s5_b_re, s5_b_im, s5_c_re, s5_c_im, s5_d, s5_w_glu, ml_w_in, ml_conv_w, ml_conv_b, ml_wq, ml_wk, ml_wv, ml_w_gates, ml_b_gates, ml_norm_g, ml_skip, ml_w_out):
    for layer in range(DEPTH):
        j = layer // N_MIXERS
        x = x + 0.5 * _swiglu(_rmsnorm(x, norm_g[layer, 0]), ffn_w_in[layer, 0], ffn_w_out[layer, 0])
        hn = _rmsnorm(x, norm_g[layer, 1])
        if layer % N_MIXERS == 0:
            x = x + _s5_mixer(hn, s5_w_in[j], s5_lambda_re[j], s5_lambda_im[j], s5_log_step[j], s5_b_re[j], s5_b_im[j], s5_c_re[j], s5_c_im[j], s5_d[j], s5_w_glu[j])
        else:
            x = x + _mlstm_mixer(hn, ml_w_in[j], ml_conv_w[j], ml_conv_b[j], ml_wq[j], ml_wk[j], ml_wv[j], ml_w_gates[j], ml_b_gates[j], ml_norm_g[j], ml_skip[j], ml_w_out[j])
        x = x + 0.5 * _swiglu(_rmsnorm(x, norm_g[layer, 2]), ffn_w_in[layer, 1], ffn_w_out[layer, 1])
    return _rmsnorm(x, final_g)


def setup_inputs(seed: int = 0) -> dict:
    key = jax.random.key(seed)
    ks = iter(jax.random.split(key, 40))
    f32 = jnp.float32

    def nrm(shape, scale):
        return scale * jax.random.normal(next(ks), shape, f32)

    x_prompt = nrm((BATCH, SEQ, D_MODEL), 1.0)
    x_sample = nrm((DEC_BATCH, DEC_SEQ, D_MODEL), 1.0)
    norm_g = 1.0 + nrm((DEPTH, 3, D_MODEL), 0.01)
    final_g = 1.0 + nrm((D_MODEL,), 0.01)
    ffn_w_in = nrm((DEPTH, 2, D_MODEL, 2 * D_FF), D_MODEL ** -0.5)
    ffn_w_out = nrm((DEPTH, 2, D_FF, D_MODEL), D_FF ** -0.5)
    s5_w_in = nrm((N_S5_LAYERS, D_MODEL, S5_WIDTH), D_MODEL ** -0.5)
    s5_lambda_re = -0.5 + nrm((N_S5_LAYERS, N_DIR, S5_GROUPS, S5_STATE), 0.01)
    s5_lambda_im = jnp.pi * jnp.arange(S5_STATE, dtype=f32) + nrm((N_S5_LAYERS, N_DIR, S5_GROUPS, S5_STATE), 0.01)
    s5_log_step = jax.random.uniform(next(ks), (N_S5_LAYERS, N_DIR, S5_GROUPS), f32, math.log(1e-3), math.log(1e-1))
    s5_b_re = nrm((N_S5_LAYERS, N_DIR, S5_GROUPS, S5_STATE, S5_GROUP), (2 * S5_GROUP) ** -0.5)
    s5_b_im = nrm((N_S5_LAYERS, N_DIR, S5_GROUPS, S5_STATE, S5_GROUP), (2 * S5_GROUP) ** -0.5)
    s5_c_re = nrm((N_S5_LAYERS, N_DIR, S5_GROUPS, S5_GROUP, S5_STATE), S5_STATE ** -0.5)
    s5_c_im = nrm((N_S5_LAYERS, N_DIR, S5_GROUPS, S5_GROUP, S5_STATE), S5_STATE ** -0.5)
    s5_d = nrm((N_S5_LAYERS, S5_WIDTH), 1.0)
    s5_w_glu = nrm((N_S5_LAYERS, S5_WIDTH, 2 * D_MODEL), S5_WIDTH ** -0.5)
    ml_w_in = nrm((N_ML_LAYERS, D_MODEL, 2 * ML_INNER), D_MODEL ** -0.5)
    ml_conv_w = nrm((N_ML_LAYERS, ML_CONV, ML_INNER), ML_CONV ** -0.5)
    ml_conv_b = nrm((N_ML_LAYERS, ML_INNER), 0.01)
    ml_wq = nrm((N_ML_LAYERS, ML_INNER // ML_QKV_BLOCK, ML_QKV_BLOCK, ML_QKV_BLOCK), ML_QKV_BLOCK ** -0.5)
    ml_wk = nrm((N_ML_LAYERS, ML_INNER // ML_QKV_BLOCK, ML_QKV_BLOCK, ML_QKV_BLOCK), ML_QKV_BLOCK ** -0.5)
    ml_wv = nrm((N_ML_LAYERS, ML_INNER // ML_QKV_BLOCK, ML_QKV_BLOCK, ML_QKV_BLOCK), ML_QKV_BLOCK ** -0.5)
    ml_w_gates = nrm((N_ML_LAYERS, 3, ML_INNER, N_DIR * 2 * ML_HEADS), (3 * ML_INNER) ** -0.5)
    i_bias = nrm((N_ML_LAYERS, N_DIR, 1, ML_HEADS), 0.1)
    f_bias = jnp.linspace(3.0, 6.0, ML_HEADS, dtype=f32) + nrm((N_ML_LAYERS, N_DIR, 1, ML_HEADS), 0.01)
    ml_b_gates = jnp.concatenate([i_bias, f_bias], axis=2).reshape(N_ML_LAYERS, N_DIR * 2 * ML_HEADS)
    ml_norm_g = 1.0 + nrm((N_ML_LAYERS, ML_INNER), 0.01)
    ml_skip = 1.0 + nrm((N_ML_LAYERS, ML_INNER), 0.01)
    ml_w_out = nrm((N_ML_LAYERS, ML_INNER, D_MODEL), ML_INNER ** -0.5)
    return {"x_prompt": x_prompt, "x_sample": x_sample, "norm_g": norm_g, "final_g": final_g, "ffn_w_in": ffn_w_in, "ffn_w_out": ffn_w_out, "s5_w_in": s5_w_in, "s5_lambda_re": s5_lambda_re, "s5_lambda_im": s5_lambda_im, "s5_log_step": s5_log_step, "s5_b_re": s5_b_re, "s5_b_im": s5_b_im, "s5_c_re": s5_c_re, "s5_c_im": s5_c_im, "s5_d": s5_d, "s5_w_glu": s5_w_glu, "ml_w_in": ml_w_in, "ml_conv_w": ml_conv_w, "ml_conv_b": ml_conv_b, "ml_wq": ml_wq, "ml_wk": ml_wk, "ml_wv": ml_wv, "ml_w_gates": ml_w_gates, "ml_b_gates": ml_b_gates, "ml_norm_g": ml_norm_g, "ml_skip": ml_skip, "ml_w_out": ml_w_out}


def reference(x_prompt, x_sample, norm_g, final_g, ffn_w_in, ffn_w_out, s5_w_in, s5_lambda_re, s5_lambda_im, s5_log_step, s5_b_re, s5_b_im, s5_c_re, s5_c_im, s5_d, s5_w_glu, ml_w_in, ml_conv_w, ml_conv_b, ml_wq, ml_wk, ml_wv, ml_w_gates, ml_b_gates, ml_norm_g, ml_skip, ml_w_out):
    y_prompt = _trunk(x_prompt, norm_g, final_g, ffn_w_in, ffn_w_out, s5_w_in, s5_lambda_re, s5_lambda_im, s5_log_step, s5_b_re, s5_b_im, s5_c_re, s5_c_im, s5_d, s5_w_glu, ml_w_in, ml_conv_w, ml_conv_b, ml_wq, ml_wk, ml_wv, ml_w_gates, ml_b_gates, ml_norm_g, ml_skip, ml_w_out)
    y_sample = _trunk(x_sample, norm_g, final_g, ffn_w_in, ffn_w_out, s5_w_in, s5_lambda_re, s5_lambda_im, s5_log_step, s5_b_re, s5_b_im, s5_c_re, s5_c_im, s5_d, s5_w_glu, ml_w_in, ml_conv_w, ml_conv_b, ml_wq, ml_wk, ml_wv, ml_w_gates, ml_b_gates, ml_norm_g, ml_skip, ml_w_out)
    return (y_prompt, y_sample)
```

```python
import numpy as np
import concourse.bass as bass
import concourse.mybir as mybir

F32 = mybir.dt.float32
BF16 = mybir.dt.bfloat16
I32 = mybir.dt.int32
ALU = mybir.AluOpType
AF = mybir.ActivationFunctionType


class Res:
    __slots__ = ("name", "lw", "rd")

    def __init__(self, name=""):
        self.name = name
        self.lw = None
        self.rd = {}


class Eng:
    def __init__(self, name, h, sem):
        self.name = name
        self.h = h
        self.sem = sem
        self.count = 0
        self.seen = {}
        self.prog = []


class Sched:
    def __init__(self, nc, stack):
        self.nc = nc
        self.stack = stack
        self.sems = {}
        self.engs = {}
        for name, h in (("pe", nc.tensor), ("dve", nc.vector), ("act", nc.scalar),
                        ("pool", nc.gpsimd), ("sp", nc.sync)):
            sem = stack.enter_context(nc.semaphore("sem_" + name))
            self.sems["e:" + name] = sem
            self.engs[name] = Eng(name, h, sem)
        self.dma_sem_val = {}
        self.n_inst = 0
        self.n_wait = 0

    def new_dma_sem(self, key):
        sem = self.stack.enter_context(self.nc.semaphore("dsem_" + key))
        self.sems["d:" + key] = sem
        self.dma_sem_val["d:" + key] = 0
        return "d:" + key

    def _wait(self, eng, deps):
        best = {}
        own = "e:" + eng.name
        for (k, v) in deps:
            if eng.name == "pe" and k == own:
                continue
            if best.get(k, 0) < v:
                best[k] = v
        for k, v in best.items():
            if eng.seen.get(k, 0) >= v:
                continue
            eng.prog.append(("w", self.sems[k], v))
            eng.seen[k] = v
            self.n_wait += 1

    def _deps(self, reads, writes):
        deps = []
        for r in reads:
            if r.lw is not None:
                deps.append(r.lw)
        for r in writes:
            if r.lw is not None:
                deps.append(r.lw)
            for k, v in r.rd.items():
                deps.append((k, v))
        return deps

    def op(self, engname, fn, reads=(), writes=()):
        eng = self.engs[engname]
        self._wait(eng, self._deps(reads, writes))
        eng.count += 1
        eng.prog.append(("i", fn, eng.sem, 1))
        ev = ("e:" + engname, eng.count)
        for r in reads:
            if r.rd.get(ev[0], 0) < ev[1]:
                r.rd[ev[0]] = ev[1]
        for r in writes:
            r.lw = ev
            r.rd = {}
        self.n_inst += 1

    def dma(self, qname, semkey, out, in_, reads=(), writes=()):
        eng = self.engs[qname]
        self._wait(eng, self._deps(reads, writes))
        eng.prog.append(("i", (lambda h, o=out, i=in_: h.dma_start(out=o, in_=i)), self.sems[semkey], 16))
        self.dma_sem_val[semkey] += 16
        v = self.dma_sem_val[semkey]
        ev = (semkey, v)
        for r in reads:
            if r.rd.get(ev[0], 0) < ev[1]:
                r.rd[ev[0]] = ev[1]
        for r in writes:
            r.lw = ev
            r.rd = {}
        self.n_inst += 1

    def wait_all(self, engname, resources):
        eng = self.engs[engname]
        deps = []
        for r in resources:
            if r.lw is not None:
                deps.append(r.lw)
        self._wait(eng, deps)

    def emit_all(self):
        nc = self.nc
        with nc.Block() as block:
            def run(eng):
                def body(h):
                    for it in eng.prog:
                        if it[0] == "w":
                            h.wait_ge(it[1], it[2])
                        else:
                            it[1](h).then_inc(it[2], it[3])
                return body
            block.tensor(run(self.engs["pe"]))
            block.vector(run(self.engs["dve"]))
            block.scalar(run(self.engs["act"]))
            block.gpsimd(run(self.engs["pool"]))
            block.sync(run(self.engs["sp"]))

import os
from contextlib import ExitStack
import ml_dtypes
from concourse.bass_utils import run_bass_kernel_spmd

NT = 512
D = 2048
KT = 16
DFF = 5632
JT = 44
DI = 4096
CT = 32
NH = 16
DH = 256
EPS = 1e-6
MUL = ALU.mult
ADD = ALU.add
SUB = ALU.subtract


class Buf:
    def __init__(self, t, name, nres=0):
        self.t = t
        self.r = Res(name)
        self.rs = [Res("%s_%d" % (name, i)) for i in range(nres)]


class Stream:
    def __init__(self, k, name, width, dt, nslots, queue="sp", hold=1):
        self.hold = hold
        self.k = k
        self.n = nslots
        self.queue = queue
        self.slots = [k.sb("%s_s%d" % (name, i), [128, width], dt) for i in range(nslots)]
        k.nst = getattr(k, "nst", 0) + 1
        self.sems = [k.S.new_dma_sem("%s%d_%d" % (name, k.nst, i)) for i in range(nslots)]
        self.items = []
        self.next_load = 0
        self.next_use = 0

    def push(self, aps):
        self.items.extend(aps)

    def get(self):
        S = self.k.S
        while self.next_load < min(len(self.items), self.next_use + self.n - self.hold + 1):
            i = self.next_load
            s = i % self.n
            ap = self.items[i]
            w = ap.shape[-1]
            S.dma(self.queue, self.sems[s], self.slots[s].t[:, 0:w], ap, writes=[self.slots[s].r])
            self.next_load += 1
        b = self.slots[self.next_use % self.n]
        self.next_use += 1
        return b


class K:
    def __init__(self, nc, st, S):
        self.nc = nc
        self.st = st
        self.S = S
        self.ph = st

    def sb(self, name, shape, dt, nres=0):
        self.nsb = getattr(self, "nsb", 0) + 1
        t = self.ph.enter_context(self.nc.sbuf_tensor("sb%d_%s" % (self.nsb, name), shape, dt))
        return Buf(t, name, nres)

    def MM(self, ps, lhsT, rhs, start, stop, R, W, **kw):
        self.S.op("pe", lambda h: h.matmul(ps, lhsT=lhsT, rhs=rhs, start=start, stop=stop, **kw), R, W)

    def TR(self, ps, in_, ident, R, W):
        self.S.op("pe", lambda h: h.transpose(ps, in_, ident), R, W)

    def ACTF(self, out, in_, func, R, W, bias=None, scale=None):
        kw = {}
        if bias is not None:
            kw["bias"] = bias
        if scale is not None:
            kw["scale"] = scale
        self.S.op("act", lambda h: h.activation(out=out, in_=in_, func=func, **kw), R, W)

    def TT(self, eng, out, in0, in1, op, R, W):
        self.S.op(eng, lambda h: h.tensor_tensor(out=out, in0=in0, in1=in1, op=op), R, W)

    def TS(self, eng, out, in0, s1, s2, op0, op1, R, W):
        if s2 is None:
            self.S.op(eng, lambda h: h.tensor_scalar(out=out, in0=in0, scalar1=s1, scalar2=None, op0=op0), R, W)
        else:
            self.S.op(eng, lambda h: h.tensor_scalar(out=out, in0=in0, scalar1=s1, scalar2=s2, op0=op0, op1=op1), R, W)

    def STT(self, out, in0, scalar, in1, op0, op1, R, W):
        self.S.op("dve", lambda h: h.scalar_tensor_tensor(out=out, in0=in0, scalar=scalar, in1=in1, op0=op0, op1=op1), R, W)

    def CP(self, eng, out, in_, R, W):
        if eng == "act":
            self.S.op("act", lambda h: h.activation(out=out, in_=in_, func=AF.Copy), R, W)
        else:
            self.S.op(eng, lambda h: h.tensor_copy(out=out, in_=in_), R, W)

    def MS(self, eng, ap, val, W):
        self.S.op(eng, lambda h: h.memset(ap, val), [], W)

    def RECIP(self, out, in_, R, W):
        self.S.op("dve", lambda h: h.reciprocal(out=out, in_=in_), R, W)

    def SCAN(self, out, d0, d1, init, op0, op1, R, W):
        self.S.op("dve", lambda h: h.tensor_tensor_scan(out=out, data0=d0, data1=d1, initial=init, op0=op0, op1=op1), R, W)


def barrier(S):
    evs = [("e:" + n, e.count) for n, e in S.engs.items() if e.count > 0]
    evs += [(kk, v) for kk, v in S.dma_sem_val.items() if v > 0]
    for n, e in S.engs.items():
        S._wait(e, [ev for ev in evs if ev[0] != "e:" + n])


def rms_stats(k, x):
    S = k.S
    ps = k.bank[7]
    for kt in range(KT):
        sq = k.sq[kt % 2]
        k.ACTF(sq.t[:], x.t[:, kt, :], AF.Square, [x.rs[kt]], [sq.r])
        k.MM(ps.t[:], k.ones32.t[:], sq.t[:], kt == 0, kt == KT - 1, [sq.r, k.ones32.r], [ps.r])
    k.ACTF(k.rstd.t[:], ps.t[:], AF.Sqrt, [ps.r, k.epsc.r], [k.rstd.r], bias=k.epsc.t[:, 0:1], scale=1.0 / D)
    k.RECIP(k.rstd.t[:], k.rstd.t[:], [k.rstd.r], [k.rstd.r])


def rms_apply(k, x, gi, out):
    for kt in range(KT):
        k.STT(out.t[:, kt, :], x.t[:, kt, :], k.gvec.t[:, gi, kt:kt + 1], k.rstd.t[:], MUL, MUL,
              [x.rs[kt], k.gvec.r, k.rstd.r], [out.rs[kt]])


def ffn(k, x, gi, wi):
    rms_stats(k, x)
    rms_apply(k, x, gi, k.hn)
    W = k.W
    W.push([k.ffn_win_b[wi * JT + j] for j in range(JT)])
    W.push([k.ffn_wout_b[wi * KT + m] for m in range(KT)])
    hn = k.hn
    for j in range(JT):
        wb = W.get()
        pg = k.bank[(j % 2) * 2]
        pu = k.bank[(j % 2) * 2 + 1]
        for g, ps in ((0, pg), (1, pu)):
            for kt in range(KT):
                c0 = (g * KT + kt) * 128
                k.MM(ps.t[:], wb.t[:, c0:c0 + 128], hn.t[:, kt, :], kt == 0, kt == KT - 1, [wb.r, hn.rs[kt]], [ps.r])
        sg = k.sg[j % 2]
        k.ACTF(sg.t[:], pg.t[:], AF.Silu, [pg.r], [sg.r])
        k.TT("dve", k.h.t[:, j, :], sg.t[:], pu.t[:], MUL, [sg.r, pu.r], [k.h.rs[j]])
    for m in range(KT):
        wo = W.get()
        ps = k.bank[4 + m % 2]
        for kt in range(JT):
            k.MM(ps.t[:], wo.t[:, kt * 128:(kt + 1) * 128], k.h.t[:, kt, :], kt == 0, kt == JT - 1, [wo.r, k.h.rs[kt]], [ps.r])
        k.STT(x.t[:, m, :], ps.t[:], 0.5, x.t[:, m, :], MUL, ADD, [ps.r, x.rs[m]], [x.rs[m]])


def proj(k, src, wlist, nkt, consume):
    W = k.W
    W.push(wlist)
    for m in range(len(wlist)):
        w = W.get()
        ps = k.bank[4 + m % 2]
        for kt in range(nkt):
            k.MM(ps.t[:], w.t[:, kt * 128:(kt + 1) * 128], src.t[:, kt, :], kt == 0, kt == nkt - 1, [w.r, src.rs[kt]], [ps.r])
        consume(m, ps)


def cast_weights(k, pairs):
    S = k.S
    CW = 4096
    cin = [k.sb("cin%d" % i, [128, CW], F32) for i in range(2)]
    cout = [k.sb("cout%d" % i, [128, CW], BF16) for i in range(3)]
    sin = [S.new_dma_sem("cin%d" % i) for i in range(2)]
    sout = [S.new_dma_sem("cout%d" % i) for i in range(3)]
    n = 0
    engs = ["dve", "act", "pool"]
    for (src, dst) in pairs:
        T, _, Fw = src.shape
        for t in range(T):
            for c0 in range(0, Fw, CW):
                w = min(CW, Fw - c0)
                a = cin[n % 2]
                b = cout[n % 3]
                S.dma("sp", sin[n % 2], a.t[:, 0:w], src[t, :, c0:c0 + w], writes=[a.r])
                k.CP(engs[n % 3], b.t[:, 0:w], a.t[:, 0:w], [a.r], [b.r])
                S.dma("sp", sout[n % 3], dst[t, :, c0:c0 + w], b.t[:, 0:w], reads=[b.r], writes=[k.wres])
                n += 1


PI = float(np.pi)


class VC:
    def __init__(self, k, name):
        self.k = k
        self.r = Res(name)

    def tt(self, o, a, b, op, eng="dve"):
        self.k.TT(eng, o, a, b, op, [self.r], [self.r])

    def ts(self, o, a, s1, op0, s2=None, op1=None):
        self.k.TS("dve", o, a, s1, s2, op0, op1, [self.r], [self.r])

    def stt(self, o, a, s, b, op0, op1):
        self.k.STT(o, a, s, b, op0, op1, [self.r], [self.r])

    def act(self, o, a, func, scale=None, bias=None):
        self.k.ACTF(o, a, func, [self.r], [self.r], bias=bias, scale=scale)

    def cp(self, o, a):
        self.k.CP("dve", o, a, [self.r], [self.r])

    def ms(self, o, v):
        self.k.MS("dve", o, v, [self.r])

    def recip(self, o, a):
        self.k.RECIP(o, a, [self.r], [self.r])

    def cmul(self, or_, oi, ar, ai, br, bi, t1, t2):
        self.tt(t1, ar, br, MUL)
        self.tt(t2, ai, bi, MUL)
        self.tt(or_, t1, t2, SUB)
        self.tt(t1, ar, bi, MUL)
        self.tt(t2, ai, br, MUL)
        self.tt(oi, t1, t2, ADD)


def s5_setup(k, dr):
    S = k.S
    v = VC(k, "s5setup")
    sem = S.new_dma_sem("s5set")

    def T(name, shape, dt=F32):
        return k.sb(name, shape, dt).t

    lre = T("lre", [128, 64]); lim = T("lim", [128, 64]); lst = T("lst", [128, 64])
    bre = T("bre", [128, 64, 16]); bim = T("bim", [128, 64, 16])
    dl = T("dl", [128, 64]); a = T("a_", [128, 64]); th = T("th", [128, 64])
    mag = T("mag", [128, 64]); imag = T("imag", [128, 64])
    sn = T("sn", [128, 64]); cs = T("cs", [128, 64])
    lr = T("lr", [128, 64]); li = T("li", [128, 64]); ir = T("ir", [128, 64]); ii = T("ii", [128, 64])
    w1 = T("w1", [128, 64]); w2 = T("w2", [128, 64]); w3 = T("w3", [128, 64]); w4 = T("w4", [128, 64])
    wi32 = T("wi32", [128, 64], I32)
    cr = T("cr", [128, 64]); ci = T("ci", [128, 64])
    pwr = T("pwr", [128, 64]); pwi = T("pwi", [128, 64])
    btp = [T("btp%d" % i, [128, 64, 32]) for i in range(2)]
    tabr = T("tabr", [128, 64, 128]); tabi = T("tabi", [128, 64, 128])
    tm1 = T("tm1", [128, 64, 64]); tm2 = T("tm2", [128, 64, 64])
    bt1 = tm1[:, :, 0:16]; bt2 = tm2[:, :, 0:16]
    ppb = T("ppb", [128, 64, 2, 128], BF16)
    ident = T("ident", [128, 128])
    io_i = T("io_i", [128, 128], I32)
    io_f = T("io_f", [128, 128])
    S.op("pool", lambda h: h.iota(io_i[:], [[1, 128]], 0, -1), [], [v.r])
    v.cp(io_f[:], io_i[:])
    v.ts(ident[:], io_f[:], 0.0, ALU.is_equal)

    def sinred(out, arg):
        v.ts(w1[:], arg, 1.0 / (2 * PI), MUL)
        v.cp(wi32[:], w1[:])
        v.cp(w2[:], wi32[:])
        v.stt(w1[:], w2[:], -2 * PI, arg, MUL, ADD)
        v.ts(w2[:], w1[:], PI, ALU.is_gt)
        v.stt(w1[:], w2[:], -2 * PI, w1[:], MUL, ADD)
        v.ts(w2[:], w1[:], -PI, ALU.is_lt)
        v.stt(w1[:], w2[:], 2 * PI, w1[:], MUL, ADD)
        v.act(out, w1[:], AF.Sin)

    for d in range(2):
        for (dst, key) in ((lre, "lamre"), (lim, "lamim"), (lst, "logstep")):
            S.dma("sp", sem, dst[:], dr[key][d], writes=[v.r])
        S.dma("sp", sem, bre[:].rearrange("p a b -> p (a b)"), dr["bre"][d], writes=[v.r])
        S.dma("sp", sem, bim[:].rearrange("p a b -> p (a b)"), dr["bim"][d], writes=[v.r])
        v.ts(lre[:], lre[:], -1e-4, ALU.min)
        v.act(dl[:], lst[:], AF.Exp)
        v.tt(a[:], lre[:], dl[:], MUL)
        v.tt(th[:], lim[:], dl[:], MUL)
        v.act(mag[:], a[:], AF.Exp)
        v.act(imag[:], a[:], AF.Exp, scale=-1.0)
        sinred(sn[:], th[:])
        v.ts(w3[:], th[:], PI / 2, ADD)
        sinred(cs[:], w3[:])
        v.tt(lr[:], mag[:], cs[:], MUL)
        v.tt(li[:], mag[:], sn[:], MUL)
        v.tt(ir[:], imag[:], cs[:], MUL)
        v.tt(ii[:], imag[:], sn[:], MUL)
        v.ts(ii[:], ii[:], -1.0, MUL)
        v.tt(w1[:], lre[:], lre[:], MUL)
        v.tt(w2[:], lim[:], lim[:], MUL)
        v.tt(w1[:], w1[:], w2[:], ADD)
        v.recip(w4[:], w1[:])
        v.ts(w3[:], lr[:], -1.0, ADD)
        v.tt(w1[:], w3[:], lre[:], MUL)
        v.tt(w2[:], li[:], lim[:], MUL)
        v.tt(w1[:], w1[:], w2[:], ADD)
        v.tt(cr[:], w1[:], w4[:], MUL)
        v.tt(w1[:], li[:], lre[:], MUL)
        v.tt(w2[:], w3[:], lim[:], MUL)
        v.tt(w1[:], w1[:], w2[:], SUB)
        v.tt(ci[:], w1[:], w4[:], MUL)
        v.ms(btp[0][:], 0.0)
        v.ms(btp[1][:], 0.0)
        for hf in range(2):
            p0, p1 = hf * 64, hf * 64 + 64
            crb = cr[p0:p1, :].unsqueeze(2).to_broadcast([64, 64, 16])
            cib = ci[p0:p1, :].unsqueeze(2).to_broadcast([64, 64, 16])
            v.tt(bt1[p0:p1], bre[p0:p1], crb, MUL)
            v.tt(bt2[p0:p1], bim[p0:p1], cib, MUL)
            v.tt(btp[0][p0:p1, :, hf * 16:hf * 16 + 16], bt1[p0:p1], bt2[p0:p1], SUB)
            v.tt(bt1[p0:p1], bim[p0:p1], crb, MUL)
            v.tt(bt2[p0:p1], bre[p0:p1], cib, MUL)
            v.tt(btp[1][p0:p1, :, hf * 16:hf * 16 + 16], bt1[p0:p1], bt2[p0:p1], ADD)
        for ri in range(2):
            for ct in range(KT):
                ps = k.bank[ct % 2]
                k.TR(ps.t[:, 0:128], btp[ri][:, 4 * ct:4 * ct + 4, :].rearrange("p a b -> p (a b)"), ident[:], [v.r], [ps.r])
                k.CP("act", k.BL.t[:, d, ri, ct, :], ps.t[:, 0:128], [ps.r], [k.BL.r])
        for tbl, (br_, bi_) in enumerate(((ir, ii), (lr, li))):
            rev = (d == 1)

            def sl(lo, hi):
                return slice(128 - hi, 128 - lo) if rev else slice(lo, hi)
            v.ms(tabr[:, :, sl(0, 1)], 1.0)
            v.ms(tabi[:, :, sl(0, 1)], 0.0)
            v.cp(pwr[:], br_[:])
            v.cp(pwi[:], bi_[:])
            n = 1
            while n < 128:
                pr_b = pwr[:].unsqueeze(2).to_broadcast([128, 64, n])
                pi_b = pwi[:].unsqueeze(2).to_broadcast([128, 64, n])
                src, dst = sl(0, n), sl(n, 2 * n)
                v.tt(tm1[:, :, 0:n], tabr[:, :, src], pr_b, MUL)
                v.tt(tm2[:, :, 0:n], tabi[:, :, src], pi_b, MUL)
                v.tt(tabr[:, :, dst], tm1[:, :, 0:n], tm2[:, :, 0:n], SUB)
                v.tt(tm1[:, :, 0:n], tabr[:, :, src], pi_b, MUL)
                v.tt(tm2[:, :, 0:n], tabi[:, :, src], pr_b, MUL)
                v.tt(tabi[:, :, dst], tm1[:, :, 0:n], tm2[:, :, 0:n], ADD)
                v.tt(w1[:], pwr[:], pwr[:], MUL)
                v.tt(w2[:], pwi[:], pwi[:], MUL)
                v.tt(w3[:], pwr[:], pwi[:], MUL)
                v.tt(pwr[:], w1[:], w2[:], SUB)
                v.ts(pwi[:], w3[:], 2.0, MUL)
                n *= 2
            for ct in range(KT):
                S.dma("sp", sem, dr["TAB"][d, ct][:, :, 2 * tbl, :], tabr[:, 4 * ct:4 * ct + 4, :], reads=[v.r])
                S.dma("sp", sem, dr["TAB"][d, ct][:, :, 2 * tbl + 1, :], tabi[:, 4 * ct:4 * ct + 4, :], reads=[v.r])
            if tbl == 1:
                v.cp(k.L128.t[:, d, 0, :], pwr[:])
                v.cp(k.L128.t[:, d, 1, :], pwi[:])
                v.cmul(k.L127.t[:, d, 0, :], k.L127.t[:, d, 1, :], pwr[:], pwi[:], ir[:], ii[:], w1[:], w2[:])
                lr_b = lr[:].unsqueeze(2).to_broadcast([128, 64, 64])
                li_b = li[:].unsqueeze(2).to_broadcast([128, 64, 64])
                for hj in range(2):
                    js = slice(hj * 64, hj * 64 + 64)
                    v.tt(tm1[:], tabr[:, :, js], lr_b, MUL)
                    v.tt(tm2[:], tabi[:, :, js], li_b, MUL)
                    v.tt(ppb[:, :, 0, js], tm1[:], tm2[:], SUB)
                    v.tt(tm1[:], tabr[:, :, js], li_b, MUL)
                    v.tt(tm2[:], tabi[:, :, js], lr_b, MUL)
                    v.tt(ppb[:, :, 1, js], tm1[:], tm2[:], ADD)
                for ct in range(KT):
                    S.dma("sp", sem, dr["PPL"][d, ct], ppb[:, 4 * ct:4 * ct + 4, :, :], reads=[v.r])


def pass_b(k, dr, TILES):
    S = k.S
    NCH = TILES * 4
    u32 = k.sb("u32", [128, KT, NT], F32, nres=KT)
    ub = k.sb("ub", [128, KT, NT], BF16, nres=KT)
    s5d = k.sb("s5d", [128, KT], F32)
    tabs = Stream(k, "TABS", 2048, F32, 3)
    nb = 2
    tq = [[k.sb("tq%d_%d" % (b, i), [128, NT], F32) for i in range(4)] for b in range(nb)]
    gq = [[k.sb("gq%d_%d" % (b, i), [128, NT], F32) for i in range(2)] for b in range(nb)]
    xq = [[k.sb("xq%d_%d" % (b, i), [128, NT], BF16) for i in range(4)] for b in range(nb)]
    gend = [k.sb("gend%d" % i, [128, 2, 64, 4, 2], F32) for i in range(2)]
    yst = [k.sb("yst%d" % i, [128, NT], F32) for i in range(2)]
    sem_u = S.new_dma_sem("pbu")
    sem_y = [S.new_dma_sem("pby%d" % i) for i in range(2)]
    sem_g = [S.new_dma_sem("pbg%d" % i) for i in range(2)]
    S.dma("sp", sem_u, s5d.t[:], dr["s5d"][:, :], writes=[s5d.r])
    CL = k.CL
    np_ = 0
    for i in range(TILES):
        t0 = i * NT
        S.dma("pool", sem_u, u32.t[:], dr["U"].rearrange("(c p) t -> p c t", p=128)[:, :, t0:t0 + NT], writes=u32.rs)
        for kt in range(KT):
            k.CP("act" if kt % 2 else "pool", ub.t[:, kt, :], u32.t[:, kt, :], [u32.rs[kt]], [ub.rs[kt]])
        ge = gend[i % 2]
        tabs.push([dr["TAB"][d, ct].rearrange("p a b c -> p (a b c)") for ct in range(KT) for d in range(2)])
        for ct in range(KT):
            yb = k.bank[4 + ct % 2]
            for d in range(2):
                tb = tabs.get()
                tb4 = tb.t[:].rearrange("p (q w j) -> p q w j", q=4, w=4)
                for q in range(4):
                    pr = 4 * ct + q
                    b = np_ % nb
                    np_ += 1
                    par, pai = k.bank[2 * b], k.bank[2 * b + 1]
                    rows = slice(32 * q, 32 * q + 32)
                    k.MM(par.t[:], k.BL.t[rows, d, 0, ct, :], ub.t[rows, ct, :], True, True, [k.BL.r, ub.rs[ct]], [par.r], tile_position=(32 * q, 0))
                    k.MM(pai.t[:], k.BL.t[rows, d, 1, ct, :], ub.t[rows, ct, :], True, True, [k.BL.r, ub.rs[ct]], [pai.r], tile_position=(32 * q, 0))
                    t1, t2, t3, t4 = tq[b]
                    gr, gi = gq[b]

                    def v3(t):
                        return t[:].rearrange("p (c j) -> p c j", c=4)

                    def tbb(w):
                        return tb4[:, q, w, :].unsqueeze(1).to_broadcast([128, 4, 128])
                    k.TT("dve", v3(t1.t), v3(par.t), tbb(0), MUL, [par.r, tb.r], [t1.r])
                    k.TT("dve", v3(t2.t), v3(pai.t), tbb(1), MUL, [pai.r, tb.r], [t2.r])
                    k.TT("dve", v3(t3.t), v3(par.t), tbb(1), MUL, [par.r, tb.r], [t3.r])
                    k.TT("dve", v3(t4.t), v3(pai.t), tbb(0), MUL, [pai.r, tb.r], [t4.r])
                    for c in range(4):
                        def cs_(t):
                            vv = v3(t.t)[:, c, :]
                            return vv[:, ::-1] if d == 1 else vv
                        k.SCAN(cs_(gr), cs_(t1), cs_(t2), 0.0, ADD, SUB, [t1.r, t2.r], [gr.r])
                        k.SCAN(cs_(gi), cs_(t3), cs_(t4), 0.0, ADD, ADD, [t3.r, t4.r], [gi.r])
                    e = 127 if d == 0 else 0
                    k.CP("pool", ge.t[:, d, pr, :, 0], v3(gr.t)[:, :, e], [gr.r], [ge.r])
                    k.CP("pool", ge.t[:, d, pr, :, 1], v3(gi.t)[:, :, e], [gi.r], [ge.r])
                    x1, x2, x3, x4 = xq[b]
                    k.TT("pool", v3(x1.t), v3(gr.t), tbb(2), MUL, [gr.r, tb.r], [x1.r])
                    k.TT("pool", v3(x2.t), v3(gi.t), tbb(3), MUL, [gi.r, tb.r], [x2.r])
                    k.TT("pool", v3(x3.t), v3(gr.t), tbb(3), MUL, [gr.r, tb.r], [x3.r])
                    k.TT("pool", v3(x4.t), v3(gi.t), tbb(2), MUL, [gi.r, tb.r], [x4.r])
                    yo = yb.t[rows, :]
                    first = (d == 0)
                    k.MM(yo, CL.t[:, d, pr, 0, :], x1.t[:], first, False, [CL.r, x1.r], [yb.r], skip_group_check=True, tile_position=(0, 32 * q))
                    k.MM(yo, CL.t[:, d, pr, 1, :], x2.t[:], False, False, [CL.r, x2.r], [yb.r], skip_group_check=True, tile_position=(0, 32 * q))
                    k.MM(yo, CL.t[:, d, pr, 2, :], x3.t[:], False, False, [CL.r, x3.r], [yb.r], skip_group_check=True, tile_position=(0, 32 * q))
                    k.MM(yo, CL.t[:, d, pr, 2, :], x4.t[:], False, d == 1, [CL.r, x4.r], [yb.r], skip_group_check=True, tile_position=(0, 32 * q))
            ys = yst[ct % 2]
            k.STT(ys.t[:], u32.t[:, ct, :], s5d.t[:, ct:ct + 1], yb.t[:], MUL, ADD, [u32.rs[ct], s5d.r, yb.r], [ys.r])
            S.dma("sp", sem_y[ct % 2], dr["YL"][ct * 128:(ct + 1) * 128, t0:t0 + NT], ys.t[:], reads=[ys.r])
        S.dma("sp", sem_g[i % 2], dr["GE"][i], ge.t[:].rearrange("p a b c e -> p (a b c e)"), reads=[ge.r])


def s5_chain(k, dr, TILES, SEGT):
    S = k.S
    NCH = TILES * 4
    v = VC(k, "chain")
    sem = S.new_dma_sem("chain")
    gE = k.sb("gE", [128, 64, NCH, 2], F32).t
    Sr = k.sb("Sr", [128, 64, NCH], F32).t
    Si = k.sb("Si", [128, 64, NCH], F32).t
    Hin = k.sb("Hin", [128, 64, NCH, 2], F32).t
    t1 = k.sb("ct1", [128, 64, NCH], F32).t
    t2 = k.sb("ct2", [128, 64, NCH], F32).t
    for d in range(2):
        for i in range(TILES):
            S.dma("sp", sem, gE[:, :, 4 * i:4 * i + 4, :],
                  dr["GE"][i].rearrange("p (a b c e) -> p a b c e", a=2, b=64, c=4)[:, d], reads=[], writes=[v.r])
        Lr = k.L127.t[:, d, 0, :].unsqueeze(2).to_broadcast([128, 64, NCH])
        Li = k.L127.t[:, d, 1, :].unsqueeze(2).to_broadcast([128, 64, NCH])
        v.tt(t1[:], gE[:, :, :, 0], Lr, MUL)
        v.tt(t2[:], gE[:, :, :, 1], Li, MUL)
        v.tt(Sr[:], t1[:], t2[:], SUB)
        v.tt(t1[:], gE[:, :, :, 0], Li, MUL)
        v.tt(t2[:], gE[:, :, :, 1], Lr, MUL)
        v.tt(Si[:], t1[:], t2[:], ADD)
        ar = k.L128.t[:, d, 0, :]
        ai = k.L128.t[:, d, 1, :]
        order = list(range(NCH)) if d == 0 else list(range(NCH - 1, -1, -1))
        a1 = t1[:, :, 0]
        a2 = t2[:, :, 0]
        for n, kk in enumerate(order):
            if n == 0:
                v.ms(Hin[:, :, kk, :], 0.0)
            if n == NCH - 1:
                break
            nxt = order[n + 1]
            pr_, pi_ = Hin[:, :, kk, 0], Hin[:, :, kk, 1]
            v.tt(a1, ar, pr_, MUL)
            v.tt(a2, ai, pi_, MUL)
            v.tt(a1, a1, a2, SUB)
            v.tt(Hin[:, :, nxt, 0], a1, Sr[:, :, kk], ADD)
            v.tt(a1, ar, pi_, MUL)
            v.tt(a2, ai, pr_, MUL)
            v.tt(a1, a1, a2, ADD)
            v.tt(Hin[:, :, nxt, 1], a1, Si[:, :, kk], ADD)
            bnd = (nxt % (4 * SEGT) == 0) if d == 0 else ((nxt + 1) % (4 * SEGT) == 0)
            if bnd:
                v.ts(Hin[:, :, nxt, :], Hin[:, :, nxt, :], k.keep.t[:, 0:1], MUL)
        for i in range(TILES):
            S.dma("sp", sem, dr["HIN"][i].rearrange("p (a b c e) -> p a b c e", a=2, b=64, c=4)[:, d],
                  Hin[:, :, 4 * i:4 * i + 4, :], reads=[v.r], writes=[v.r])


def s5_carry_tile(k, dr, i, yact):
    S = k.S
    t0 = i * NT
    c5 = k.c5
    hin = c5["hin"][i % 2]
    S.dma("pool", c5["sem_h"][i % 2], hin.t[:].rearrange("p a b c e -> p (a b c e)"), dr["HIN"][i], writes=[hin.r])
    c5["pp"].push([dr["PPL"][d, ct].rearrange("p a b c -> p (a b c)") for ct in range(KT) for d in range(2)])
    c5["cs"].push([dr["CRI"][ct] for ct in range(KT)])
    for ct in range(KT):
        cs = c5["cs"].get()
        c4v = cs.t[:].rearrange("p (d q w c) -> p d q w c", d=2, q=4, w=2)
        chb = c5["chb"][ct % 2]
        ta, tb_ = c5["ta"], c5["tb"]
        for d in range(2):
            eng = "dve" if d == 0 else "pool"
            Crb = c4v[:, d, :, 0, :].unsqueeze(2).to_broadcast([128, 4, 4, 32])
            Cib = c4v[:, d, :, 1, :].unsqueeze(2).to_broadcast([128, 4, 4, 32])
            Hr = hin.t[:, d, 4 * ct:4 * ct + 4, :, 0].unsqueeze(3).to_broadcast([128, 4, 4, 32])
            Hi = hin.t[:, d, 4 * ct:4 * ct + 4, :, 1].unsqueeze(3).to_broadcast([128, 4, 4, 32])
            A, B = ta[d], tb_[d]
            k.TT(eng, A.t[:], Crb, Hr, MUL, [cs.r, hin.r], [A.r])
            k.TT(eng, B.t[:], Cib, Hi, MUL, [cs.r, hin.r], [B.r])
            k.TT(eng, chb.t[:, d, :, :, 0, :], A.t[:], B.t[:], SUB, [A.r, B.r], [chb.r])
            k.TT(eng, A.t[:], Crb, Hi, MUL, [cs.r, hin.r], [A.r])
            k.TT(eng, B.t[:], Cib, Hr, MUL, [cs.r, hin.r], [B.r])
            k.TT(eng, A.t[:], A.t[:], B.t[:], ADD, [A.r, B.r], [A.r])
            k.TS(eng, chb.t[:, d, :, :, 1, :], A.t[:], -1.0, None, MUL, None, [A.r], [chb.r])
        yb = k.bank[6 + ct % 2]
        pps = [c5["pp"].get(), c5["pp"].get()]
        for q in range(4):
            n = 0
            for c4 in range(4):
                for d in range(2):
                    pv = pps[d].t[:].rearrange("p (q r j) -> p q r j", q=4, r=2)
                    for ri in range(2):
                        k.MM(yb.t[32 * q:32 * q + 32, c4 * 128:(c4 + 1) * 128], chb.t[:, d, q, c4, ri, :], pv[:, q, ri, :],
                             n == 0, n == 15, [chb.r, pps[d].r], [yb.r], skip_group_check=True, tile_position=(0, 32 * q))
                        n += 1
        yl = c5["yl"][ct % 2]
        S.dma("pool", c5["sem_yl"][ct % 2], yl.t[:], dr["YL"][ct * 128:(ct + 1) * 128, t0:t0 + NT], writes=[yl.r])
        k.TT("dve", yl.t[:], yl.t[:], yb.t[:], ADD, [yl.r, yb.r], [yl.r])
        if "YA" in dr:
            S.dma("pool", c5["sem_yl"][ct % 2], dr["YA"][ct * 128:(ct + 1) * 128, t0:t0 + NT], yl.t[:], reads=[yl.r])
        g2 = c5["g2"][ct % 2]
        k.TT("pool", g2.t[:], yl.t[:], yl.t[:], MUL, [yl.r], [g2.r])
        k.TS("pool", g2.t[:], g2.t[:], 0.044715, 1.0, MUL, ADD, [g2.r], [g2.r])
        k.TT("pool", g2.t[:], g2.t[:], yl.t[:], MUL, [g2.r, yl.r], [g2.r])
        k.ACTF(g2.t[:], g2.t[:], AF.Sigmoid, [g2.r], [g2.r], scale=1.5957691216057308)
        k.TT("dve", yact.t[:, ct, :], g2.t[:], yl.t[:], MUL, [g2.r, yl.r], [yact.rs[ct]])


def pass_d(k, dr, TILES, SEGT):
    S = k.S
    NTOK = TILES * NT
    ws = Stream(k, "WD", 1024, BF16, 3)
    xmh = [k.sb("xmh%d" % i, [128, CT, NT + 4], BF16) for i in range(2)]
    sem_x = [S.new_dma_sem("xmh%d" % i) for i in range(2)]
    xcb = [k.sb("xcb%d" % i, [128, NT], BF16) for i in range(2)]
    qst = [k.sb("qst%d" % i, [128, NT], BF16) for i in range(2)]
    kst = [k.sb("kst%d" % i, [128, NT], BF16) for i in range(2)]
    vst = [k.sb("vst%d" % i, [128, NT], BF16) for i in range(2)]
    kts = [k.sb("kts%d" % i, [128, 4, 128], BF16) for i in range(2)]
    vts = [k.sb("vts%d" % i, [128, 4, 128], BF16) for i in range(2)]
    gst = [k.sb("gst%d" % i, [128, 4, 64], F32) for i in range(2)]
    sems = {n: [S.new_dma_sem("pd%s%d" % (n, i)) for i in range(2)] for n in ("xc", "q", "k", "kt", "vt", "g", "w")}
    wgf = k.sb("wgf", [128, 3 * CT * 64], F32)
    wg = k.sb("wg", [128, 3, CT, 64], BF16)
    bgf = k.sb("bgf", [128, 64], F32)
    bgb = k.sb("bgb", [128, 64], BF16)
    onesb = k.sb("onesb", [128, 128], BF16)
    zerob = k.sb("zerob", [128, 256], BF16)
    k.MS("dve", zerob.t[:], 0.0, [zerob.r])
    k.MS("dve", bgf.t[:], 0.0, [bgf.r])
    S.dma("sp", sems["w"][0], wgf.t[:], dr["wg"][:, :], writes=[wgf.r])
    k.CP("dve", wg.t[:].rearrange("p a b c -> p (a b c)"), wgf.t[:], [wgf.r], [wg.r])
    S.dma("sp", sems["w"][1], bgf.t[0:1, :], dr["bgate"][:, :], writes=[bgf.r])
    k.CP("dve", bgb.t[:], bgf.t[:], [bgf.r], [bgb.r])
    k.MS("dve", onesb.t[:], 1.0, [onesb.r])
    XMv = dr["XM"].rearrange("(c p) t -> p c t", p=128)
    n = 0
    for i in range(TILES):
        t0 = i * NT
        xb = xmh[i % 2]
        lo = max(t0 - 2, 0)
        hi = min(t0 + NT + 2, NTOK)
        S.dma("pool", sem_x[i % 2], xb.t[:, :, lo - (t0 - 2):hi - (t0 - 2)], XMv[:, :, lo:hi], writes=[xb.r])
        if i == 0:
            k.MS("pool", xb.t[:, :, 0:2], 0.0, [xb.r])
        elif i % SEGT == 0:
            k.TS("pool", xb.t[:, :, 0:2], xb.t[:, :, 0:2], k.keep.t[:, 0:1], None, MUL, None, [xb.r, k.keep.r], [xb.r])
        if i == TILES - 1:
            k.MS("pool", xb.t[:, :, NT + 2:NT + 4], 0.0, [xb.r])
        elif (i + 1) % SEGT == 0:
            k.TS("pool", xb.t[:, :, NT + 2:NT + 4], xb.t[:, :, NT + 2:NT + 4], k.keep.t[:, 0:1], None, MUL, None, [xb.r, k.keep.r], [xb.r])
        ws.push([dr["mlD_b"][ct] for ct in range(CT)])
        gps = k.bank[7]
        k.MM(gps.t[:, 0:256], zerob.t[:, 0:128], zerob.t[:], True, False, [zerob.r], [gps.r], skip_group_check=True)
        for ct in range(CT):
            w = ws.get()
            b = n % 2
            n += 1
            pc = k.bank[b]
            for tau in range(5):
                k.MM(pc.t[:], w.t[:, tau * 128:(tau + 1) * 128], xb.t[:, ct, tau:tau + NT], tau == 0, tau == 4, [w.r, xb.r], [pc.r])
            xc = xcb[b]
            k.ACTF(xc.t[:], pc.t[:], AF.Silu, [pc.r, k.mlvec.r], [xc.r], bias=k.mlvec.t[:, ct:ct + 1])
            S.dma("sp", sems["xc"][b], dr["XC"][ct * 128:(ct + 1) * 128, t0:t0 + NT], xc.t[:], reads=[xc.r])
            pq, pk, pv = k.bank[2], k.bank[3], k.bank[4]
            k.MM(pq.t[:], w.t[:, 640:768], xc.t[:], True, True, [w.r, xc.r], [pq.r])
            k.MM(pk.t[:], w.t[:, 768:896], xc.t[:], True, True, [w.r, xc.r], [pk.r])
            k.MM(pv.t[:], w.t[:, 896:1024], xb.t[:, ct, 2:NT + 2], True, True, [w.r, xb.r], [pv.r])
            q_, k_, v_ = qst[b], kst[b], vst[b]
            k.CP("dve", q_.t[:], pq.t[:], [pq.r], [q_.r])
            k.CP("act", k_.t[:], pk.t[:], [pk.r], [k_.r])
            k.CP("dve", v_.t[:], pv.t[:], [pv.r], [v_.r])
            S.dma("sp", sems["q"][b], dr["QT"][ct * 128:(ct + 1) * 128, t0:t0 + NT], q_.t[:], reads=[q_.r])
            S.dma("sp", sems["k"][b], dr["KT"][ct * 128:(ct + 1) * 128, t0:t0 + NT], k_.t[:], reads=[k_.r])
            pkt, pvt = k.bank[5], k.bank[6]
            for c4 in range(4):
                k.MM(pkt.t[:, c4 * 128:(c4 + 1) * 128], xc.t[:, c4 * 128:(c4 + 1) * 128], w.t[:, 768:896], True, True, [w.r, xc.r], [pkt.r])
                k.MM(pvt.t[:, c4 * 128:(c4 + 1) * 128], xb.t[:, ct, 2 + c4 * 128:2 + (c4 + 1) * 128], w.t[:, 896:1024], True, True, [w.r, xb.r], [pvt.r])
            kt_, vt_ = kts[b], vts[b]
            k.CP("act", kt_.t[:].rearrange("p a b -> p (a b)"), pkt.t[:], [pkt.r], [kt_.r])
            k.CP("dve", vt_.t[:].rearrange("p a b -> p (a b)"), pvt.t[:], [pvt.r], [vt_.r])
            S.dma("sp", sems["kt"][b], dr["KTOK"][4 * i:4 * i + 4, :, ct * 128:(ct + 1) * 128].rearrange("c t h -> t c h"), kt_.t[:], reads=[kt_.r])
            S.dma("sp", sems["vt"][b], dr["VTOK"][4 * i:4 * i + 4, :, ct * 128:(ct + 1) * 128].rearrange("c t h -> t c h"), vt_.t[:], reads=[vt_.r])
            for c4 in range(4):
                cs_ = slice(c4 * 128, (c4 + 1) * 128)
                for j, src in enumerate((q_, k_, v_)):
                    first = False
                    k.MM(gps.t[:, c4 * 64:(c4 + 1) * 64], src.t[:, cs_], wg.t[:, j, ct, :], first, False, [src.r, wg.r], [gps.r], skip_group_check=True)
        for c4 in range(4):
            k.MM(gps.t[:, c4 * 64:(c4 + 1) * 64], onesb.t[:], bgb.t[:], False, c4 == 3, [onesb.r, bgb.r], [gps.r], skip_group_check=True)
        g_ = gst[i % 2]
        k.CP("dve", g_.t[:].rearrange("p a b -> p (a b)"), gps.t[:, 0:256], [gps.r], [g_.r])
        S.dma("sp", sems["g"][i % 2], dr["G"][4 * i:4 * i + 4].rearrange("c t g -> t c g"), g_.t[:], reads=[g_.r])


def ml_prep(k, dr, TILES, SEGT):
    S = k.S
    NCH = TILES * 4
    v = VC(k, "mlprep")
    sem = S.new_dma_sem("mlprep")
    gall = k.sb("gall", [128, NCH, 64], F32).t
    lf = k.sb("lfall", [128, NCH, 2, 16], F32).t
    bc = k.sb("bcum", [128, NCH, 2, 16], F32).t
    io_i = k.sb("mio_i", [128, 128], I32).t
    io_f = k.sb("mio_f", [128, 128], F32).t
    lnsc = k.sb("lnsc", [128, 1], F32).t
    onec = k.sb("onec", [128, 1], F32).t
    S.dma("sp", sem, gall[:], dr["G"].rearrange("c t g -> t c g"), writes=[v.r])
    S.op("pool", lambda h: h.iota(io_i[:], [[1, 128]], 0, -1), [], [v.r])
    v.cp(io_f[:], io_i[:])
    v.ts(k.TRI[0].t[:], io_f[:], 0.0, ALU.is_ge)
    v.ts(k.TRI[1].t[:], io_f[:], 0.0, ALU.is_le)
    v.ts(k.ident32.t[:], io_f[:], 0.0, ALU.is_equal)
    v.ms(lnsc[:], -0.5 * float(np.log(DH)))
    v.ms(onec[:], 1.0)
    g5 = gall[:].rearrange("p c (d w h) -> p c d w h", d=2, w=2)
    v.act(lf[:], g5[:, :, :, 1, :], AF.Exp, scale=-1.0)
    v.act(lf[:], lf[:], AF.Ln, bias=onec[:, 0:1])
    v.ts(lf[:], lf[:], -1.0, MUL)
    half = NCH // 2 if NCH >= 2 else 1
    for d in range(2):
        for (lhs, dst) in ((k.TRI[d], bc), (k.ones32, k.EG.t)):
            for c0 in range(0, NCH, 32):
                c1 = min(c0 + 32, NCH)
                ps = k.bank[(c0 // 32) % 2]
                k.MM(ps.t[:, 0:(c1 - c0) * 16].rearrange("p (c h) -> p c h", h=16), lhs.t[:], lf[:, c0:c1, d, :], True, True, [v.r], [ps.r])
                k.CP("dve", dst[:, c0:c1, d, :], ps.t[:, 0:(c1 - c0) * 16].rearrange("p (c h) -> p c h", h=16), [ps.r], [v.r])
    v.tt(k.ED.t[:], g5[:, :, :, 0, :], bc[:], SUB)
    v.act(k.ED.t[:], k.ED.t[:], AF.Exp, bias=lnsc[:, 0:1])
    v.act(k.EB.t[:], bc[:], AF.Exp, scale=-1.0)
    v.act(k.EG.t[:], k.EG.t[:], AF.Exp)
    for kk in range(NCH):
        if (kk + 1) % (4 * SEGT) == 0 and kk + 1 < NCH:
            v.ts(k.EG.t[:, kk, 0, :], k.EG.t[:, kk, 0, :], k.keep.t[:, 0:1], MUL)
        if kk % (4 * SEGT) == 0 and kk > 0:
            v.ts(k.EG.t[:, kk, 1, :], k.EG.t[:, kk, 1, :], k.keep.t[:, 0:1], MUL)
    k.mlprep_res = v.r


def pass_e(k, dr, TILES):
    S = k.S
    NCH = TILES * 4
    PR = k.mlprep_res
    C32 = k.sb("C32", [128, NH, 2, 257], F32, nres=NH)
    Cb = k.sb("Cb", [128, NH, 2, 257], BF16, nres=NH)
    qT = [k.sb("qT%d" % i, [128, CT, 128], BF16) for i in range(2)]
    kT = [k.sb("kT%d" % i, [128, CT, 128], BF16) for i in range(2)]
    ktk = [k.sb("ktk%d" % i, [128, DI], BF16) for i in range(2)]
    vau = [k.sb("vau%d" % i, [128, NH, 260], BF16) for i in range(2)]
    sem_l = [[S.new_dma_sem("pe%d_%d" % (j, i)) for i in range(2)] for j in range(4)]
    hbuf = k.sb("hbuf", [128, DI], F32, nres=NH)
    hbl = k.sb("hbl", [128, DI], F32)
    sem_hb = S.new_dma_sem("hbst")
    sem_hl = S.new_dma_sem("hbld")
    khat = [k.sb("khat%d" % i, [128, DH], BF16) for i in range(2)]
    smT = [k.sb("smT%d" % i, [128, 128], BF16) for i in range(2)]
    dn = [k.sb("dn%d" % i, [128, 1], F32) for i in range(2)]
    xck = k.sb("xck", [128, CT, 128], BF16)
    zk = k.sb("zk", [128, CT, 128], F32)
    oab = k.sb("oab", [128, CT, 128], BF16)
    skb = [k.sb("skb%d" % i, [128, 128], F32) for i in range(2)]
    o1b = [k.sb("o1b%d" % i, [128, 128], F32) for i in range(2)]
    bst = k.sb("bst", [128, NH, 6], F32)
    mv = k.sb("mv", [128, NH, 2], F32)
    rs = k.sb("rs", [128, NH], F32)
    sem_p = [S.new_dma_sem("pep%d" % i) for i in range(3)]
    for b in range(2):
        k.MS("pool", vau[b].t[:, :, 256:260], 1.0, [vau[b].r])
    QTv = dr["QT"].rearrange("(c p) t -> p c t", p=128)
    KTv = dr["KT"].rearrange("(c p) t -> p c t", p=128)
    XCv = dr["XC"].rearrange("(c p) t -> p c t", p=128)
    Zv = dr["Z"].rearrange("(c p) t -> p c t", p=128)
    OAv = dr["OA"].rearrange("(c p) t -> p c t", p=128)
    nn = 0
    for d in (1, 0):
        for h in range(NH):
            k.MS("dve", C32.t[:, h], 0.0, [C32.rs[h]])
            k.MS("pool", Cb.t[:, h], 0.0, [Cb.rs[h]])
        order = list(range(NCH)) if d == 0 else list(range(NCH - 1, -1, -1))
        for kk in order:
            b = nn % 2
            nn += 1
            ts_ = slice(kk * 128, (kk + 1) * 128)
            q_, k_, kt_, va = qT[b], kT[b], ktk[b], vau[b]
            S.dma("sp", sem_l[0][b], q_.t[:], QTv[:, :, ts_], writes=[q_.r])
            S.dma("sp", sem_l[1][b], k_.t[:], KTv[:, :, ts_], writes=[k_.r])
            S.dma("pool", sem_l[2][b], kt_.t[:], dr["KTOK"][kk], writes=[kt_.r])
            S.dma("pool", sem_l[3][b], va.t[:, :, 0:256], dr["VTOK"][kk].rearrange("t (h e) -> t h e", h=NH), writes=[va.r])
            k.MS("pool", va.t[:, :, 256:260], 1.0, [va.r])
            if d == 0:
                S.dma("pool", sem_hl, hbl.t[:], dr["HB"][kk], reads=[k.hbres], writes=[hbl.r])
            for h in range(NH):
                hb = h % 2
                pS = k.bank[hb]
                pX = k.bank[2 + hb]
                pC = [k.bank[4 + 2 * hb], k.bank[5 + 2 * hb]]
                for i2 in range(2):
                    k.MM(pS.t[:, 0:128], k_.t[:, 2 * h + i2, :], q_.t[:, 2 * h + i2, :], i2 == 0, i2 == 1, [k_.r, q_.r], [pS.r])
                sm = smT[hb]
                ed = k.ED.t[:, kk, d, h:h + 1]
                k.STT(sm.t[:], pS.t[:, 0:128], ed, k.TRI[d].t[:], MUL, MUL, [pS.r, PR], [sm.r])
                for i2 in range(2):
                    k.MM(pX.t[:, 0:257], q_.t[:, 2 * h + i2, :], Cb.t[:, h, i2, :], i2 == 0, False, [q_.r, Cb.rs[h]], [pX.r])
                k.MM(pX.t[:, 0:257], sm.t[:], va.t[:, h, 0:257], False, True, [sm.r, va.r], [pX.r])
                dn_ = dn[hb]
                k.TS("dve", dn_.t[:], pX.t[:, 256:257], k.EB.t[:, kk, d, h:h + 1], None, ALU.max, None, [pX.r, PR], [dn_.r])
                k.STT(dn_.t[:], pX.t[:, 256:257], -1.0, dn_.t[:], MUL, ALU.max, [pX.r, dn_.r], [dn_.r])
                k.RECIP(dn_.t[:], dn_.t[:], [dn_.r], [dn_.r])
                hs = slice(h * DH, (h + 1) * DH)
                if d == 1:
                    k.TS("dve", hbuf.t[:, hs], pX.t[:, 0:256], dn_.t[:, 0:1], None, MUL, None, [pX.r, dn_.r], [hbuf.rs[h]])
                else:
                    k.STT(hbuf.t[:, hs], pX.t[:, 0:256], dn_.t[:, 0:1], hbl.t[:, hs], MUL, ADD, [pX.r, dn_.r, hbl.r], [hbuf.rs[h]])
                kh = khat[hb]
                k.TS("pool", kh.t[:], kt_.t[:, hs], ed, None, MUL, None, [kt_.r, PR], [kh.r])
                eg = k.EG.t[:, kk, d, h:h + 1]
                k.TS("pool", C32.t[:, h], C32.t[:, h], eg, None, MUL, None, [C32.rs[h], PR], [C32.rs[h]])
                for i2 in range(2):
                    k.MM(pC[i2].t[:, 0:257], kh.t[:, i2 * 128:(i2 + 1) * 128], va.t[:, h, 0:257], True, True, [kh.r, va.r], [pC[i2].r])
                    k.STT(C32.t[:, h, i2, :], pC[i2].t[:, 0:257], eg, C32.t[:, h, i2, :], MUL, ADD, [pC[i2].r, PR, C32.rs[h]], [C32.rs[h]])
                k.CP("act", Cb.t[:, h], C32.t[:, h], [C32.rs[h]], [Cb.rs[h]])
            if d == 1:
                S.dma("sp", sem_hb, dr["HB"][kk], hbuf.t[:], reads=hbuf.rs, writes=[k.hbres])
                continue
            S.dma("pool", sem_p[0], xck.t[:], XCv[:, :, ts_], writes=[xck.r])
            S.dma("pool", sem_p[1], zk.t[:], Zv[:, :, ts_], writes=[zk.r])
            for h in range(NH):
                hs = slice(h * DH, (h + 1) * DH)
                S.op("dve", lambda hh, h=h, hs=hs: hh.bn_stats(out=bst.t[:, h, :], in_=hbuf.t[:, hs]), [hbuf.rs[h]], [bst.r])
                S.op("dve", lambda hh, h=h: hh.bn_aggr(out=mv.t[:, h, :], in_=bst.t[:, h, :]), [bst.r], [mv.r])
            k.ACTF(rs.t[:], mv.t[:, :, 1], AF.Sqrt, [mv.r, k.epsc.r], [rs.r], bias=k.epsc.t[:, 0:1])
            k.RECIP(rs.t[:], rs.t[:], [rs.r], [rs.r])
            for h in range(NH):
                hs = slice(h * DH, (h + 1) * DH)
                k.TS("dve", hbuf.t[:, hs], hbuf.t[:, hs], mv.t[:, h, 0:1], rs.t[:, h:h + 1], SUB, MUL, [hbuf.rs[h], mv.r, rs.r], [hbuf.rs[h]])
            k.ACTF(zk.t[:], zk.t[:], AF.Sigmoid, [zk.r], [zk.r])
            for ct in range(CT):
                pT = k.bank[(ct // 4) % 2]
                k.TR(pT.t[:, (ct % 4) * 128:(ct % 4 + 1) * 128], hbuf.t[:, ct * 128:(ct + 1) * 128], k.ident32.t[:], [hbuf.rs[ct // 2], PR], [pT.r])
                sk = skb[ct % 2]
                o1 = o1b[ct % 2]
                k.TS("pool", sk.t[:], xck.t[:, ct, :], k.mlvec.t[:, 64 + ct:65 + ct], None, MUL, None, [xck.r, k.mlvec.r], [sk.r])
                k.STT(o1.t[:], pT.t[:, (ct % 4) * 128:(ct % 4 + 1) * 128], k.mlvec.t[:, 32 + ct:33 + ct], sk.t[:], MUL, ADD, [pT.r, sk.r, k.mlvec.r], [o1.r])
                k.TT("pool", oab.t[:, ct, :], o1.t[:], zk.t[:, ct, :], MUL, [o1.r, zk.r], [oab.r])
            S.dma("sp", sem_p[2], OAv[:, :, ts_], oab.t[:], reads=[oab.r])


def build(TILES, SEGT, debug=(), stop_after="F"):
    NTOK = TILES * NT
    NCH = TILES * 4
    nc = bass.Bass("TRN2", target_bir_lowering=False)
    dbg = set(debug)

    def din(name, shape, dt=F32):
        return nc.dram_tensor(name, list(shape), dt, kind="ExternalInput").ap()

    def dscr(name, shape, dt):
        kind = "ExternalOutput" if name in dbg else "Internal"
        return nc.dram_tensor(name, list(shape), dt, kind=kind).ap()

    xT = din("xT", [D, NTOK])
    keep_d = din("keep", [128, 1])
    gvec_d = din("gvec", [128, 7 * KT])
    wsrc = {
        "ffn_win": din("ffn_win", [4 * JT, 128, 2 * KT * 128]),
        "ffn_wout": din("ffn_wout", [4 * KT, 128, JT * 128]),
        "s5_win": din("s5_win", [KT, 128, KT * 128]),
        "wglu": din("wglu", [2 * KT, 128, KT * 128]),
        "ml_win": din("ml_win", [2 * CT, 128, KT * 128]),
        "mlD": din("mlD", [CT, 128, 1024]),
        "ml_wout": din("ml_wout", [KT, 128, CT * 128]),
    }
    dr = {}
    for kk_ in ("lamre", "lamim", "logstep"):
        dr[kk_] = din(kk_, [2, 128, 64])
    dr["bre"] = din("bre", [2, 128, 64 * 16])
    dr["bim"] = din("bim", [2, 128, 64 * 16])
    dr["CRI"] = din("CRI", [KT, 128, 2 * 4 * 2 * 32])
    dr["s5d"] = din("s5d", [128, KT])
    dr["wg"] = din("wg", [128, 3 * CT * 64])
    dr["bgate"] = din("bgate", [1, 64])
    mlvec_d = din("mlvec", [128, 96])
    yT = nc.dram_tensor("yT", [D, NTOK], F32, kind="ExternalOutput").ap()

    wb = {n: dscr(n + "_b", a.shape, BF16) for n, a in wsrc.items()}
    X1 = dscr("X1", [D, NTOK], F32)
    dr["U"] = dscr("U", [D, NTOK], F32)
    dr["YL"] = dscr("YL", [D, NTOK], F32)
    dr["GE"] = dscr("GE", [TILES, 128, 2 * 64 * 4 * 2], F32)
    dr["HIN"] = dscr("HIN", [TILES, 128, 2 * 64 * 4 * 2], F32)
    dr["TAB"] = dscr("TAB", [2, KT, 128, 4, 4, 128], F32)
    dr["PPL"] = dscr("PPL", [2, KT, 128, 4, 2, 128], BF16)
    X2 = dscr("X2", [D, NTOK], F32)
    if "YA" in dbg:
        dr["YA"] = dscr("YA", [D, NTOK], F32)
    X4 = dscr("X4", [D, NTOK], F32)
    XM = dscr("XM", [DI, NTOK], BF16)
    Z = dscr("Z", [DI, NTOK], F32)
    U = dr["U"]
    dr["XM"] = XM
    dr["Z"] = Z
    dr["XC"] = dscr("XC", [DI, NTOK], BF16)
    dr["QT"] = dscr("QT", [DI, NTOK], BF16)
    dr["KT"] = dscr("KT", [DI, NTOK], BF16)
    dr["KTOK"] = dscr("KTOK", [NCH, 128, DI], BF16)
    dr["VTOK"] = dscr("VTOK", [NCH, 128, DI], BF16)
    dr["G"] = dscr("G", [NCH, 128, 64], F32)
    dr["HB"] = dscr("HB", [NCH, 128, DI], F32)
    dr["OA"] = dscr("OA", [DI, NTOK], BF16)
    X5 = dscr("X5", [D, NTOK], F32)

    def fm(ap):
        return ap.rearrange("(c p) t -> p c t", p=128)

    with ExitStack() as st:
        S = Sched(nc, st)
        k = K(nc, st, S)
        k.wres = Res("wres")
        k.bank = []
        for i in range(8):
            t = st.enter_context(nc.psum_tensor("bank%d" % i, [128, 512], F32))
            k.bank.append(Buf(t, "bank%d" % i))
        k.ffn_win_b = wb["ffn_win"]
        k.ffn_wout_b = wb["ffn_wout"]
        io_sem = [S.new_dma_sem("io%d" % i) for i in range(4)]
        st_sem = [S.new_dma_sem("st%d" % i) for i in range(6)]
        k.ones32 = k.sb("ones32", [128, 128], F32)
        k.epsc = k.sb("epsc", [128, 1], F32)
        k.gvec = k.sb("gvec", [128, 7, KT], F32)
        k.keep = k.sb("keepc", [128, 1], F32)
        k.MS("dve", k.ones32.t[:], 1.0, [k.ones32.r])
        k.MS("dve", k.epsc.t[:], EPS, [k.epsc.r])
        S.dma("sp", io_sem[0], k.gvec.t[:].rearrange("p a b -> p (a b)"), gvec_d[:, :], writes=[k.gvec.r])
        S.dma("sp", io_sem[0], k.keep.t[:], keep_d[:, :], writes=[k.keep.r])
        k.mlvec = k.sb("mlvec", [128, 96], F32)
        S.dma("sp", io_sem[0], k.mlvec.t[:], mlvec_d[:, :], writes=[k.mlvec.r])
        k.hbres = Res("hbres")
        dr["mlD_b"] = wb["mlD"]

        def phase():
            ph = ExitStack()
            k.ph = ph
            return ph

        def common_bufs():
            k.W = Stream(k, "W", JT * 128, BF16, 3)
            k.hn = k.sb("hn", [128, KT, NT], BF16, nres=KT)
            k.h = k.sb("h", [128, JT, NT], BF16, nres=JT)
            k.sq = [k.sb("sq%d" % i, [128, NT], F32) for i in range(2)]
            k.sg = [k.sb("sg%d" % i, [128, NT], F32) for i in range(2)]
            k.rstd = k.sb("rstd", [128, NT], F32)

        with phase():
            cast_weights(k, [(wsrc[n], wb[n]) for n in wsrc])
            barrier(S)

        with phase():
            common_bufs()
            x = k.sb("xa", [128, KT, NT], F32, nres=KT)
            ust = [k.sb("ust%d" % i, [128, NT], F32) for i in range(2)]
            for i in range(TILES):
                t0 = i * NT
                S.dma("pool", io_sem[0], x.t[:], fm(xT)[:, :, t0:t0 + NT], writes=x.rs)
                ffn(k, x, 0, 0)
                S.dma("pool", st_sem[0], fm(X1)[:, :, t0:t0 + NT], x.t[:], reads=x.rs)
                rms_stats(k, x)
                rms_apply(k, x, 1, k.hn)

                def cons(m, ps, t0=t0):
                    b = ust[m % 2]
                    k.CP("act", b.t[:], ps.t[:], [ps.r], [b.r])
                    S.dma("pool", st_sem[1 + m % 2], U[m * 128:(m + 1) * 128, t0:t0 + NT], b.t[:], reads=[b.r])
                proj(k, k.hn, [wb["s5_win"][m] for m in range(KT)], KT, cons)
            barrier(S)
        if stop_after == "A":
            S.emit_all()
            return nc, S

        s5scope = ExitStack()
        k.ph = s5scope
        k.BL = k.sb("BL", [128, 2, 2, KT, 128], BF16)
        k.CL = k.sb("CL", [128, 2, 64, 3, 32], BF16)
        k.L128 = k.sb("L128", [128, 2, 2, 64], F32)
        k.L127 = k.sb("L127", [128, 2, 2, 64], F32)
        with phase():
            s5_setup(k, dr)
            tmpc = k.sb("tmpc", [128, 2, 4, 2, 32], F32)
            for ct in range(KT):
                S.dma("sp", io_sem[1], tmpc.t[:].rearrange("p a b c e -> p (a b c e)"), dr["CRI"][ct], writes=[tmpc.r])
                k.CP("dve", k.CL.t[:, :, 4 * ct:4 * ct + 4, 0, :], tmpc.t[:, :, :, 0, :], [tmpc.r], [k.CL.r])
                k.TS("dve", k.CL.t[:, :, 4 * ct:4 * ct + 4, 1, :], tmpc.t[:, :, :, 0, :], -1.0, None, MUL, None, [tmpc.r], [k.CL.r])
                k.TS("dve", k.CL.t[:, :, 4 * ct:4 * ct + 4, 2, :], tmpc.t[:, :, :, 1, :], -1.0, None, MUL, None, [tmpc.r], [k.CL.r])
            barrier(S)
        with phase():
            pass_b(k, dr, TILES)
            barrier(S)
        with phase():
            s5_chain(k, dr, TILES, SEGT)
            barrier(S)
        s5scope.close()
        if stop_after == "B":
            S.emit_all()
            return nc, S

        with phase():
            common_bufs()
            x = k.sb("xc_", [128, KT, NT], F32, nres=KT)
            k.c5 = {
                "hin": [k.sb("hin%d" % i, [128, 2, 64, 4, 2], F32) for i in range(2)],
                "sem_h": [S.new_dma_sem("hin%d" % i) for i in range(2)],
                "pp": Stream(k, "PP", 4 * 2 * 128, BF16, 4, hold=2),
                "cs": Stream(k, "CS", 512, F32, 2),
                "chb": [k.sb("chb%d" % i, [128, 2, 4, 4, 2, 32], BF16) for i in range(2)],
                "ta": [k.sb("cta%d" % i, [128, 4, 4, 32], F32) for i in range(2)],
                "tb": [k.sb("ctb%d" % i, [128, 4, 4, 32], F32) for i in range(2)],
                "yl": [k.sb("yl%d" % i, [128, NT], F32) for i in range(2)],
                "g2": [k.sb("g2_%d" % i, [128, NT], F32) for i in range(2)],
                "sem_yl": [S.new_dma_sem("yl%d" % i) for i in range(2)],
            }
            gsb = [k.sb("gsb%d" % i, [128, NT], F32) for i in range(2)]
            zst = [k.sb("zst%d" % i, [128, NT], F32) for i in range(2)]
            mst = [k.sb("mst%d" % i, [128, NT], BF16) for i in range(2)]
            for i in range(TILES):
                t0 = i * NT
                S.dma("pool", io_sem[0], x.t[:], fm(X1)[:, :, t0:t0 + NT], writes=x.rs)
                s5_carry_tile(k, dr, i, k.hn)
                held = {}

                def cons_glu(idx, ps):
                    m = idx // 2
                    if idx % 2 == 0:
                        held["v"] = ps
                        return
                    pv = held["v"]
                    g = gsb[m % 2]
                    k.ACTF(g.t[:], ps.t[:], AF.Sigmoid, [ps.r], [g.r])
                    k.TT("dve", g.t[:], g.t[:], pv.t[:], MUL, [g.r, pv.r], [g.r])
                    k.TT("pool", x.t[:, m, :], x.t[:, m, :], g.t[:], ADD, [g.r, x.rs[m]], [x.rs[m]])
                wl = []
                for m in range(KT):
                    wl += [wb["wglu"][m], wb["wglu"][KT + m]]
                proj(k, k.hn, wl, KT, cons_glu)
                if "X2" in dbg:
                    S.dma("pool", st_sem[3], fm(X2)[:, :, t0:t0 + NT], x.t[:], reads=x.rs)
                ffn(k, x, 2, 1)
                ffn(k, x, 3, 2)
                S.dma("pool", st_sem[0], fm(X4)[:, :, t0:t0 + NT], x.t[:], reads=x.rs)
                rms_stats(k, x)
                rms_apply(k, x, 4, k.hn)

                def cons_ml(m, ps, t0=t0):
                    if m < CT:
                        b = mst[m % 2]
                        k.CP("act", b.t[:], ps.t[:], [ps.r], [b.r])
                        S.dma("pool", st_sem[1 + m % 2], XM[m * 128:(m + 1) * 128, t0:t0 + NT], b.t[:], reads=[b.r])
                    else:
                        b = zst[m % 2]
                        k.CP("act", b.t[:], ps.t[:], [ps.r], [b.r])
                        S.dma("pool", st_sem[4 + m % 2], Z[(m - CT) * 128:(m - CT + 1) * 128, t0:t0 + NT], b.t[:], reads=[b.r])
                proj(k, k.hn, [wb["ml_win"][m] for m in range(2 * CT)], KT, cons_ml)
            barrier(S)
        if stop_after == "C":
            S.emit_all()
            return nc, S

        with phase():
            pass_d(k, dr, TILES, SEGT)
            barrier(S)
        if stop_after == "D":
            S.emit_all()
            return nc, S
        mlscope = ExitStack()
        k.ph = mlscope
        k.ED = k.sb("ED", [128, NCH, 2, 16], F32)
        k.EB = k.sb("EB", [128, NCH, 2, 16], F32)
        k.EG = k.sb("EG", [128, NCH, 2, 16], F32)
        k.TRI = [k.sb("TRI%d" % i, [128, 128], F32) for i in range(2)]
        k.ident32 = k.sb("ident32", [128, 128], F32)
        with phase():
            ml_prep(k, dr, TILES, SEGT)
            barrier(S)
        with phase():
            pass_e(k, dr, TILES)
            barrier(S)
        mlscope.close()
        if stop_after == "E":
            S.emit_all()
            return nc, S
        with phase():
            common_bufs()
            x = k.sb("xf_", [128, KT, NT], F32, nres=KT)
            OAv = dr["OA"].rearrange("(c p) t -> p c t", p=128)
            for i in range(TILES):
                t0 = i * NT
                S.dma("pool", io_sem[0], x.t[:], fm(X4)[:, :, t0:t0 + NT], writes=x.rs)
                S.dma("pool", io_sem[1], k.h.t[:, 0:CT, :], OAv[:, :, t0:t0 + NT], writes=k.h.rs)

                def cons_o(m, ps):
                    k.TT("dve", x.t[:, m, :], x.t[:, m, :], ps.t[:], ADD, [ps.r, x.rs[m]], [x.rs[m]])
                proj(k, k.h, [wb["ml_wout"][m] for m in range(KT)], CT, cons_o)
                if "X5" in dbg:
                    S.dma("pool", st_sem[3], fm(X5)[:, :, t0:t0 + NT], x.t[:], reads=x.rs)
                ffn(k, x, 5, 3)
                rms_stats(k, x)
                rms_apply(k, x, 6, x)
                S.dma("pool", st_sem[0], fm(yT)[:, :, t0:t0 + NT], x.t[:], reads=x.rs)
            barrier(S)
        barrier(S)
        S.emit_all()
    return nc, S


def _tile_rows(w, nk):
    K_, M_ = w.shape
    m = M_ // 128
    return np.ascontiguousarray(w.reshape(nk, 128, m, 128).transpose(2, 1, 0, 3).reshape(m, 128, nk * 128))


def prep_shared(inp):
    f = np.float32
    out = {}
    g = np.concatenate([np.asarray(inp["norm_g"], f).reshape(6, D), np.asarray(inp["final_g"], f).reshape(1, D)], 0)
    out["gvec"] = np.ascontiguousarray(g.reshape(7, KT, 128).transpose(2, 0, 1).reshape(128, 7 * KT))
    win = np.asarray(inp["ffn_w_in"], f).reshape(4, D, 2, JT, 128)
    win = win.reshape(4, KT, 128, 2, JT, 128).transpose(0, 4, 2, 3, 1, 5)
    out["ffn_win"] = np.ascontiguousarray(win.reshape(4 * JT, 128, 2 * KT * 128))
    wout = np.asarray(inp["ffn_w_out"], f).reshape(4, DFF, D)
    out["ffn_wout"] = np.concatenate([_tile_rows(wout[i], JT) for i in range(4)], 0)
    out["s5_win"] = _tile_rows(np.asarray(inp["s5_w_in"], f)[0], KT)
    out["wglu"] = _tile_rows(np.asarray(inp["s5_w_glu"], f)[0], KT)
    out["ml_win"] = _tile_rows(np.asarray(inp["ml_w_in"], f)[0], KT)

    def gp(a):
        sh = a.shape
        a = a.reshape((2, 64, 2, 64) + sh[3:])
        perm = (0, 2, 3, 1) + tuple(range(4, a.ndim))
        a = a.transpose(perm)
        return np.ascontiguousarray(a.reshape((2, 128, 64) + sh[3:]))
    out["lamre"] = gp(np.asarray(inp["s5_lambda_re"], f)[0])
    out["lamim"] = gp(np.asarray(inp["s5_lambda_im"], f)[0])
    ls = np.asarray(inp["s5_log_step"], f)[0]
    out["logstep"] = gp(np.broadcast_to(ls[:, :, None], (2, 128, 64)).copy())
    out["bre"] = gp(np.asarray(inp["s5_b_re"], f)[0]).reshape(2, 128, 64 * 16)
    out["bim"] = gp(np.asarray(inp["s5_b_im"], f)[0]).reshape(2, 128, 64 * 16)
    cri = np.zeros((KT, 128, 2, 4, 2, 32), f)
    for ri, key in enumerate(("s5_c_re", "s5_c_im")):
        c = np.asarray(inp[key], f)[0]
        c = c.reshape(2, KT, 4, 2, 16, 64)
        for g2 in range(2):
            cri[:, g2 * 64:(g2 + 1) * 64, :, :, ri, g2 * 16:(g2 + 1) * 16] = c[:, :, :, g2].transpose(1, 4, 0, 2, 3)
    out["CRI"] = np.ascontiguousarray(cri.reshape(KT, 128, 512))
    out["s5d"] = np.ascontiguousarray(np.asarray(inp["s5_d"], f)[0].reshape(KT, 128).T)
    mlD = np.zeros((CT, 128, 8, 128), f)
    cw = np.asarray(inp["ml_conv_w"], f)[0].reshape(5, CT, 128)
    ar = np.arange(128)
    for tau in range(5):
        mlD[:, ar, tau, ar] = cw[tau]
    for j, key in enumerate(("ml_wq", "ml_wk", "ml_wv")):
        w = np.asarray(inp[key], f)[0].reshape(CT, 32, 4, 4)
        for n in range(32):
            mlD[:, 4 * n:4 * n + 4, 5 + j, 4 * n:4 * n + 4] = w[:, n]
    out["mlD"] = np.ascontiguousarray(mlD.reshape(CT, 128, 1024))
    out["ml_wout"] = _tile_rows(np.asarray(inp["ml_w_out"], f)[0], CT)
    wg = np.asarray(inp["ml_w_gates"], f)[0].reshape(3, CT, 128, 64)
    out["wg"] = np.ascontiguousarray(wg.transpose(2, 0, 1, 3).reshape(128, 3 * CT * 64))
    out["bgate"] = np.ascontiguousarray(np.asarray(inp["ml_b_gates"], f)[0].reshape(1, 64))
    vecs = [np.asarray(inp[kk], f)[0].reshape(CT, 128).T for kk in ("ml_conv_b", "ml_norm_g", "ml_skip")]
    out["mlvec"] = np.ascontiguousarray(np.concatenate(vecs, 1))
    return out


_CACHE = {}


def kernel(**inputs):
    f = np.float32
    TILES, SEGT = 16, 4
    NTOK = TILES * NT
    sh = prep_shared(inputs)
    xp = np.asarray(inputs["x_prompt"], f)
    xs = np.asarray(inputs["x_sample"], f)
    in_maps = []
    for c in range(8):
        m = dict(sh)
        if c < 4:
            m["xT"] = np.ascontiguousarray(xp[c].T)
            m["keep"] = np.ones((128, 1), f)
        else:
            xt = np.zeros((D, NTOK), f)
            for j in range(2):
                xt[:, j * 2048:(j + 1) * 2048] = xs[2 * (c - 4) + j].T
            m["xT"] = xt
            m["keep"] = np.zeros((128, 1), f)
        in_maps.append(m)
    if "nc" not in _CACHE:
        _CACHE["nc"] = build(TILES, SEGT)[0]
    res = run_bass_kernel_spmd(_CACHE["nc"], in_maps, core_ids=list(range(8)))
    yp = np.zeros((4, 8192, D), f)
    ys = np.zeros((8, 2048, D), f)
    for c in range(8):
        y = np.asarray(res.results[c]["yT"], f)
        if c < 4:
            yp[c] = y.T
        else:
            for j in range(2):
                ys[2 * (c - 4) + j] = y[:, j * 2048:(j + 1) * 2048].T
    return (yp, ys)
```

```python
import numpy as np
import concourse.bass as bass
import concourse.mybir as mybir

F32 = mybir.dt.float32
BF16 = mybir.dt.bfloat16
I32 = mybir.dt.int32
ALU = mybir.AluOpType
AF = mybir.ActivationFunctionType


class Res:
    __slots__ = ("name", "lw", "rd")

    def __init__(self, name=""):
        self.name = name
        self.lw = None
        self.rd = {}


class Eng:
    def __init__(self, name, h, sem):
        self.name = name
        self.h = h
        self.sem = sem
        self.count = 0
        self.seen = {}
        self.prog = []


class Sched:
    def __init__(self, nc, stack):
        self.nc = nc
        self.stack = stack
        self.sems = {}
        self.engs = {}
        for name, h in (("pe", nc.tensor), ("dve", nc.vector), ("act", nc.scalar),
                        ("pool", nc.gpsimd), ("sp", nc.sync)):
            sem = stack.enter_context(nc.semaphore("sem_" + name))
            self.sems["e:" + name] = sem
            self.engs[name] = Eng(name, h, sem)
        self.dma_sem_val = {}
        self.n_inst = 0
        self.n_wait = 0

    def new_dma_sem(self, key):
        sem = self.stack.enter_context(self.nc.semaphore("dsem_" + key))
        self.sems["d:" + key] = sem
        self.dma_sem_val["d:" + key] = 0
        return "d:" + key

    def _wait(self, eng, deps):
        best = {}
        own = "e:" + eng.name
        for (k, v) in deps:
            if eng.name == "pe" and k == own:
                continue
            if best.get(k, 0) < v:
                best[k] = v
        for k, v in best.items():
            if eng.seen.get(k, 0) >= v:
                continue
            eng.prog.append(("w", self.sems[k], v))
            eng.seen[k] = v
            self.n_wait += 1

    def _deps(self, reads, writes):
        deps = []
        for r in reads:
            if r.lw is not None:
                deps.append(r.lw)
        for r in writes:
            if r.lw is not None:
                deps.append(r.lw)
            for k, v in r.rd.items():
                deps.append((k, v))
        return deps

    def op(self, engname, fn, reads=(), writes=()):
        eng = self.engs[engname]
        self._wait(eng, self._deps(reads, writes))
        eng.count += 1
        eng.prog.append(("i", fn, eng.sem, 1))
        ev = ("e:" + engname, eng.count)
        for r in reads:
            if r.rd.get(ev[0], 0) < ev[1]:
                r.rd[ev[0]] = ev[1]
        for r in writes:
            r.lw = ev
            r.rd = {}
        self.n_inst += 1

    def dma(self, qname, semkey, out, in_, reads=(), writes=()):
        eng = self.engs[qname]
        self._wait(eng, self._deps(reads, writes))
        eng.prog.append(("i", (lambda h, o=out, i=in_: h.dma_start(out=o, in_=i)), self.sems[semkey], 16))
        self.dma_sem_val[semkey] += 16
        v = self.dma_sem_val[semkey]
        ev = (semkey, v)
        for r in reads:
            if r.rd.get(ev[0], 0) < ev[1]:
                r.rd[ev[0]] = ev[1]
        for r in writes:
            r.lw = ev
            r.rd = {}
        self.n_inst += 1

    def wait_all(self, engname, resources):
        eng = self.engs[engname]
        deps = []
        for r in resources:
            if r.lw is not None:
                deps.append(r.lw)
        self._wait(eng, deps)

    def emit_all(self):
        nc = self.nc
        with nc.Block() as block:
            def run(eng):
                def body(h):
                    for it in eng.prog:
                        if it[0] == "w":
                            h.wait_ge(it[1], it[2])
                        else:
                            it[1](h).then_inc(it[2], it[3])
                return body
            block.tensor(run(self.engs["pe"]))
            block.vector(run(self.engs["dve"]))
            block.scalar(run(self.engs["act"]))
            block.gpsimd(run(self.engs["pool"]))
            block.sync(run(self.engs["sp"]))

import os
from contextlib import ExitStack
import ml_dtypes
from concourse.bass_utils import run_bass_kernel_spmd

NT = 512
D = 2048
KT = 16
DFF = 5632
JT = 44
DI = 4096
CT = 32
NH = 16
DH = 256
EPS = 1e-6
MUL = ALU.mult
ADD = ALU.add
SUB = ALU.subtract


class Buf:
    def __init__(self, t, name, nres=0):
        self.t = t
        self.r = Res(name)
        self.rs = [Res("%s_%d" % (name, i)) for i in range(nres)]


class Stream:
    def __init__(self, k, name, width, dt, nslots, queue="sp", hold=1):
        self.hold = hold
        self.k = k
        self.n = nslots
        self.queue = queue
        self.slots = [k.sb("%s_s%d" % (name, i), [128, width], dt) for i in range(nslots)]
        k.nst = getattr(k, "nst", 0) + 1
        self.sems = [k.S.new_dma_sem("%s%d_%d" % (name, k.nst, i)) for i in range(nslots)]
        self.items = []
        self.next_load = 0
        self.next_use = 0

    def push(self, aps):
        self.items.extend(aps)

    def get(self):
        S = self.k.S
        while self.next_load < min(len(self.items), self.next_use + self.n - self.hold + 1):
            i = self.next_load
            s = i % self.n
            ap = self.items[i]
            w = ap.shape[-1]
            S.dma(self.queue, self.sems[s], self.slots[s].t[:, 0:w], ap, writes=[self.slots[s].r])
            self.next_load += 1
        b = self.slots[self.next_use % self.n]
        self.next_use += 1
        return b


class K:
    def __init__(self, nc, st, S):
        self.nc = nc
        self.st = st
        self.S = S
        self.ph = st

    def sb(self, name, shape, dt, nres=0):
        self.nsb = getattr(self, "nsb", 0) + 1
        t = self.ph.enter_context(self.nc.sbuf_tensor("sb%d_%s" % (self.nsb, name), shape, dt))
        return Buf(t, name, nres)

    def MM(self, ps, lhsT, rhs, start, stop, R, W, **kw):
        self.S.op("pe", lambda h: h.matmul(ps, lhsT=lhsT, rhs=rhs, start=start, stop=stop, **kw), R, W)

    def TR(self, ps, in_, ident, R, W):
        self.S.op("pe", lambda h: h.transpose(ps, in_, ident), R, W)

    def ACTF(self, out, in_, func, R, W, bias=None, scale=None):
        kw = {}
        if bias is not None:
            kw["bias"] = bias
        if scale is not None:
            kw["scale"] = scale
        self.S.op("act", lambda h: h.activation(out=out, in_=in_, func=func, **kw), R, W)

    def TT(self, eng, out, in0, in1, op, R, W):
        self.S.op(eng, lambda h: h.tensor_tensor(out=out, in0=in0, in1=in1, op=op), R, W)

    def TS(self, eng, out, in0, s1, s2, op0, op1, R, W):
        if s2 is None:
            self.S.op(eng, lambda h: h.tensor_scalar(out=out, in0=in0, scalar1=s1, scalar2=None, op0=op0), R, W)
        else:
            self.S.op(eng, lambda h: h.tensor_scalar(out=out, in0=in0, scalar1=s1, scalar2=s2, op0=op0, op1=op1), R, W)

    def STT(self, out, in0, scalar, in1, op0, op1, R, W):
        self.S.op("dve", lambda h: h.scalar_tensor_tensor(out=out, in0=in0, scalar=scalar, in1=in1, op0=op0, op1=op1), R, W)

    def CP(self, eng, out, in_, R, W):
        if eng == "act":
            self.S.op("act", lambda h: h.activation(out=out, in_=in_, func=AF.Copy), R, W)
        else:
            self.S.op(eng, lambda h: h.tensor_copy(out=out, in_=in_), R, W)

    def MS(self, eng, ap, val, W):
        self.S.op(eng, lambda h: h.memset(ap, val), [], W)

    def RECIP(self, out, in_, R, W):
        self.S.op("dve", lambda h: h.reciprocal(out=out, in_=in_), R, W)

    def SCAN(self, out, d0, d1, init, op0, op1, R, W):
        self.S.op("dve", lambda h: h.tensor_tensor_scan(out=out, data0=d0, data1=d1, initial=init, op0=op0, op1=op1), R, W)


def barrier(S):
    evs = [("e:" + n, e.count) for n, e in S.engs.items() if e.count > 0]
    evs += [(kk, v) for kk, v in S.dma_sem_val.items() if v > 0]
    for n, e in S.engs.items():
        S._wait(e, [ev for ev in evs if ev[0] != "e:" + n])


def rms_stats(k, x):
    S = k.S
    ps = k.bank[7]
    for kt in range(KT):
        sq = k.sq[kt % 2]
        k.ACTF(sq.t[:], x.t[:, kt, :], AF.Square, [x.rs[kt]], [sq.r])
        k.MM(ps.t[:], k.ones32.t[:], sq.t[:], kt == 0, kt == KT - 1, [sq.r, k.ones32.r], [ps.r])
    k.ACTF(k.rstd.t[:], ps.t[:], AF.Sqrt, [ps.r, k.epsc.r], [k.rstd.r], bias=k.epsc.t[:, 0:1], scale=1.0 / D)
    k.RECIP(k.rstd.t[:], k.rstd.t[:], [k.rstd.r], [k.rstd.r])


def rms_apply(k, x, gi, out):
    for kt in range(KT):
        k.STT(out.t[:, kt, :], x.t[:, kt, :], k.gvec.t[:, gi, kt:kt + 1], k.rstd.t[:], MUL, MUL,
              [x.rs[kt], k.gvec.r, k.rstd.r], [out.rs[kt]])


def ffn(k, x, gi, wi):
    rms_stats(k, x)
    rms_apply(k, x, gi, k.hn)
    W = k.W
    W.push([k.ffn_win_b[wi * JT + j] for j in range(JT)])
    W.push([k.ffn_wout_b[wi * KT + m] for m in range(KT)])
    hn = k.hn
    for j in range(JT):
        wb = W.get()
        pg = k.bank[(j % 2) * 2]
        pu = k.bank[(j % 2) * 2 + 1]
        for g, ps in ((0, pg), (1, pu)):
            for kt in range(KT):
                c0 = (g * KT + kt) * 128
                k.MM(ps.t[:], wb.t[:, c0:c0 + 128], hn.t[:, kt, :], kt == 0, kt == KT - 1, [wb.r, hn.rs[kt]], [ps.r])
        sg = k.sg[j % 2]
        k.ACTF(sg.t[:], pg.t[:], AF.Silu, [pg.r], [sg.r])
        k.TT("dve", k.h.t[:, j, :], sg.t[:], pu.t[:], MUL, [sg.r, pu.r], [k.h.rs[j]])
    for m in range(KT):
        wo = W.get()
        ps = k.bank[4 + m % 2]
        for kt in range(JT):
            k.MM(ps.t[:], wo.t[:, kt * 128:(kt + 1) * 128], k.h.t[:, kt, :], kt == 0, kt == JT - 1, [wo.r, k.h.rs[kt]], [ps.r])
        k.STT(x.t[:, m, :], ps.t[:], 0.5, x.t[:, m, :], MUL, ADD, [ps.r, x.rs[m]], [x.rs[m]])


def proj(k, src, wlist, nkt, consume):
    W = k.W
    W.push(wlist)
    for m in range(len(wlist)):
        w = W.get()
        ps = k.bank[4 + m % 2]
        for kt in range(nkt):
            k.MM(ps.t[:], w.t[:, kt * 128:(kt + 1) * 128], src.t[:, kt, :], kt == 0, kt == nkt - 1, [w.r, src.rs[kt]], [ps.r])
        consume(m, ps)


def cast_weights(k, pairs):
    S = k.S
    CW = 4096
    cin = [k.sb("cin%d" % i, [128, CW], F32) for i in range(2)]
    cout = [k.sb("cout%d" % i, [128, CW], BF16) for i in range(3)]
    sin = [S.new_dma_sem("cin%d" % i) for i in range(2)]
    sout = [S.new_dma_sem("cout%d" % i) for i in range(3)]
    n = 0
    engs = ["dve", "act", "dve"]
    for (src, dst) in pairs:
        T, _, Fw = src.shape
        for t in range(T):
            for c0 in range(0, Fw, CW):
                w = min(CW, Fw - c0)
                a = cin[n % 2]
                b = cout[n % 3]
                S.dma("sp", sin[n % 2], a.t[:, 0:w], src[t, :, c0:c0 + w], writes=[a.r])
                k.CP(engs[n % 3], b.t[:, 0:w], a.t[:, 0:w], [a.r], [b.r])
                S.dma("sp", sout[n % 3], dst[t, :, c0:c0 + w], b.t[:, 0:w], reads=[b.r], writes=[k.wres])
                n += 1


PI = float(np.pi)


class VC:
    def __init__(self, k, name):
        self.k = k
        self.r = Res(name)

    def tt(self, o, a, b, op, eng="dve"):
        self.k.TT(eng, o, a, b, op, [self.r], [self.r])

    def ts(self, o, a, s1, op0, s2=None, op1=None):
        self.k.TS("dve", o, a, s1, s2, op0, op1, [self.r], [self.r])

    def stt(self, o, a, s, b, op0, op1):
        self.k.STT(o, a, s, b, op0, op1, [self.r], [self.r])

    def act(self, o, a, func, scale=None, bias=None):
        self.k.ACTF(o, a, func, [self.r], [self.r], bias=bias, scale=scale)

    def cp(self, o, a):
        self.k.CP("dve", o, a, [self.r], [self.r])

    def ms(self, o, v):
        self.k.MS("dve", o, v, [self.r])

    def recip(self, o, a):
        self.k.RECIP(o, a, [self.r], [self.r])

    def cmul(self, or_, oi, ar, ai, br, bi, t1, t2):
        self.tt(t1, ar, br, MUL)
        self.tt(t2, ai, bi, MUL)
        self.tt(or_, t1, t2, SUB)
        self.tt(t1, ar, bi, MUL)
        self.tt(t2, ai, br, MUL)
        self.tt(oi, t1, t2, ADD)


def s5_setup(k, dr):
    S = k.S
    v = VC(k, "s5setup")
    sem = S.new_dma_sem("s5set")

    def T(name, shape, dt=F32):
        return k.sb(name, shape, dt).t

    lre = T("lre", [128, 64]); lim = T("lim", [128, 64]); lst = T("lst", [128, 64])
    bre = T("bre", [128, 64, 16]); bim = T("bim", [128, 64, 16])
    dl = T("dl", [128, 64]); a = T("a_", [128, 64]); th = T("th", [128, 64])
    mag = T("mag", [128, 64]); imag = T("imag", [128, 64])
    sn = T("sn", [128, 64]); cs = T("cs", [128, 64])
    lr = T("lr", [128, 64]); li = T("li", [128, 64]); ir = T("ir", [128, 64]); ii = T("ii", [128, 64])
    w1 = T("w1", [128, 64]); w2 = T("w2", [128, 64]); w3 = T("w3", [128, 64]); w4 = T("w4", [128, 64])
    wi32 = T("wi32", [128, 64], I32)
    cr = T("cr", [128, 64]); ci = T("ci", [128, 64])
    pwr = T("pwr", [128, 64]); pwi = T("pwi", [128, 64])
    btp = [T("btp%d" % i, [128, 64, 32]) for i in range(2)]
    tabr = T("tabr", [128, 64, 128]); tabi = T("tabi", [128, 64, 128])
    tm1 = T("tm1", [128, 64, 64]); tm2 = T("tm2", [128, 64, 64])
    bt1 = tm1[:, :, 0:16]; bt2 = tm2[:, :, 0:16]
    ppb = T("ppb", [128, 64, 2, 128], BF16)
    ident = T("ident", [128, 128])
    io_i = T("io_i", [128, 128], I32)
    io_f = T("io_f", [128, 128])
    S.op("pool", lambda h: h.iota(io_i[:], [[1, 128]], 0, -1), [], [v.r])
    v.cp(io_f[:], io_i[:])
    v.ts(ident[:], io_f[:], 0.0, ALU.is_equal)

    def sinred(out, arg):
        v.ts(w1[:], arg, 1.0 / (2 * PI), MUL)
        v.cp(wi32[:], w1[:])
        v.cp(w2[:], wi32[:])
        v.stt(w1[:], w2[:], -2 * PI, arg, MUL, ADD)
        v.ts(w2[:], w1[:], PI, ALU.is_gt)
        v.stt(w1[:], w2[:], -2 * PI, w1[:], MUL, ADD)
        v.ts(w2[:], w1[:], -PI, ALU.is_lt)
        v.stt(w1[:], w2[:], 2 * PI, w1[:], MUL, ADD)
        v.act(out, w1[:], AF.Sin)

    for d in range(2):
        for (dst, key) in ((lre, "lamre"), (lim, "lamim"), (lst, "logstep")):
            S.dma("sp", sem, dst[:], dr[key][d], writes=[v.r])
        S.dma("sp", sem, bre[:].rearrange("p a b -> p (a b)"), dr["bre"][d], writes=[v.r])
        S.dma("sp", sem, bim[:].rearrange("p a b -> p (a b)"), dr["bim"][d], writes=[v.r])
        v.ts(lre[:], lre[:], -1e-4, ALU.min)
        v.act(dl[:], lst[:], AF.Exp)
        v.tt(a[:], lre[:], dl[:], MUL)
        v.tt(th[:], lim[:], dl[:], MUL)
        v.act(mag[:], a[:], AF.Exp)
        v.act(imag[:], a[:], AF.Exp, scale=-1.0)
        sinred(sn[:], th[:])
        v.ts(w3[:], th[:], PI / 2, ADD)
        sinred(cs[:], w3[:])
        v.tt(lr[:], mag[:], cs[:], MUL)
        v.tt(li[:], mag[:], sn[:], MUL)
        v.tt(ir[:], imag[:], cs[:], MUL)
        v.tt(ii[:], imag[:], sn[:], MUL)
        v.ts(ii[:], ii[:], -1.0, MUL)
        v.tt(w1[:], lre[:], lre[:], MUL)
        v.tt(w2[:], lim[:], lim[:], MUL)
        v.tt(w1[:], w1[:], w2[:], ADD)
        v.recip(w4[:], w1[:])
        v.ts(w3[:], lr[:], -1.0, ADD)
        v.tt(w1[:], w3[:], lre[:], MUL)
        v.tt(w2[:], li[:], lim[:], MUL)
        v.tt(w1[:], w1[:], w2[:], ADD)
        v.tt(cr[:], w1[:], w4[:], MUL)
        v.tt(w1[:], li[:], lre[:], MUL)
        v.tt(w2[:], w3[:], lim[:], MUL)
        v.tt(w1[:], w1[:], w2[:], SUB)
        v.tt(ci[:], w1[:], w4[:], MUL)
        v.ms(btp[0][:], 0.0)
        v.ms(btp[1][:], 0.0)
        for hf in range(2):
            p0, p1 = hf * 64, hf * 64 + 64
            crb = cr[p0:p1, :].unsqueeze(2).to_broadcast([64, 64, 16])
            cib = ci[p0:p1, :].unsqueeze(2).to_broadcast([64, 64, 16])
            v.tt(bt1[p0:p1], bre[p0:p1], crb, MUL)
            v.tt(bt2[p0:p1], bim[p0:p1], cib, MUL)
            v.tt(btp[0][p0:p1, :, hf * 16:hf * 16 + 16], bt1[p0:p1], bt2[p0:p1], SUB)
            v.tt(bt1[p0:p1], bim[p0:p1], crb, MUL)
            v.tt(bt2[p0:p1], bre[p0:p1], cib, MUL)
            v.tt(btp[1][p0:p1, :, hf * 16:hf * 16 + 16], bt1[p0:p1], bt2[p0:p1], ADD)
        for ri in range(2):
            for ct in range(KT):
                ps = k.bank[ct % 2]
                k.TR(ps.t[:, 0:128], btp[ri][:, 4 * ct:4 * ct + 4, :].rearrange("p a b -> p (a b)"), ident[:], [v.r], [ps.r])
                k.CP("act", k.BL.t[:, d, ri, ct, :], ps.t[:, 0:128], [ps.r], [k.BL.r])
        for tbl, (br_, bi_) in enumerate(((ir, ii), (lr, li))):
            rev = (d == 1)

            def sl(lo, hi):
                return slice(128 - hi, 128 - lo) if rev else slice(lo, hi)
            v.ms(tabr[:, :, sl(0, 1)], 1.0)
            v.ms(tabi[:, :, sl(0, 1)], 0.0)
            v.cp(pwr[:], br_[:])
            v.cp(pwi[:], bi_[:])
            n = 1
            while n < 128:
                pr_b = pwr[:].unsqueeze(2).to_broadcast([128, 64, n])
                pi_b = pwi[:].unsqueeze(2).to_broadcast([128, 64, n])
                src, dst = sl(0, n), sl(n, 2 * n)
                v.tt(tm1[:, :, 0:n], tabr[:, :, src], pr_b, MUL)
                v.tt(tm2[:, :, 0:n], tabi[:, :, src], pi_b, MUL)
                v.tt(tabr[:, :, dst], tm1[:, :, 0:n], tm2[:, :, 0:n], SUB)
                v.tt(tm1[:, :, 0:n], tabr[:, :, src], pi_b, MUL)
                v.tt(tm2[:, :, 0:n], tabi[:, :, src], pr_b, MUL)
                v.tt(tabi[:, :, dst], tm1[:, :, 0:n], tm2[:, :, 0:n], ADD)
                v.tt(w1[:], pwr[:], pwr[:], MUL)
                v.tt(w2[:], pwi[:], pwi[:], MUL)
                v.tt(w3[:], pwr[:], pwi[:], MUL)
                v.tt(pwr[:], w1[:], w2[:], SUB)
                v.ts(pwi[:], w3[:], 2.0, MUL)
                n *= 2
            for ct in range(KT):
                S.dma("sp", sem, dr["TAB"][d, ct][:, :, 2 * tbl, :], tabr[:, 4 * ct:4 * ct + 4, :], reads=[v.r])
                S.dma("sp", sem, dr["TAB"][d, ct][:, :, 2 * tbl + 1, :], tabi[:, 4 * ct:4 * ct + 4, :], reads=[v.r])
            if tbl == 1:
                v.cp(k.L128.t[:, d, 0, :], pwr[:])
                v.cp(k.L128.t[:, d, 1, :], pwi[:])
                v.cmul(k.L127.t[:, d, 0, :], k.L127.t[:, d, 1, :], pwr[:], pwi[:], ir[:], ii[:], w1[:], w2[:])
                lr_b = lr[:].unsqueeze(2).to_broadcast([128, 64, 64])
                li_b = li[:].unsqueeze(2).to_broadcast([128, 64, 64])
                for hj in range(2):
                    js = slice(hj * 64, hj * 64 + 64)
                    v.tt(tm1[:], tabr[:, :, js], lr_b, MUL)
                    v.tt(tm2[:], tabi[:, :, js], li_b, MUL)
                    v.tt(ppb[:, :, 0, js], tm1[:], tm2[:], SUB)
                    v.tt(tm1[:], tabr[:, :, js], li_b, MUL)
                    v.tt(tm2[:], tabi[:, :, js], lr_b, MUL)
                    v.tt(ppb[:, :, 1, js], tm1[:], tm2[:], ADD)
                for ct in range(KT):
                    S.dma("sp", sem, dr["PPL"][d, ct], ppb[:, 4 * ct:4 * ct + 4, :, :], reads=[v.r])


def pass_b(k, dr, TILES):
    S = k.S
    NCH = TILES * 4
    u32 = [k.sb("u32_%d" % i, [128, KT, NT], F32, nres=KT) for i in range(1)]
    ub = k.sb("ub", [128, KT, NT], BF16, nres=KT)
    s5d = k.sb("s5d", [128, KT], F32)
    tabs = Stream(k, "TABS", 2048, F32, 4, hold=2)
    nb = 3
    tq = [[k.sb("tq%d_%d" % (b, i), [128, NT], F32) for i in range(4)] for b in range(nb)]
    gq = [[k.sb("gq%d_%d" % (b, i), [128, NT], F32, nres=4) for i in range(2)] for b in range(nb)]
    xq = [[k.sb("xq%d_%d" % (b, i), [128, NT], BF16) for i in range(4)] for b in range(nb)]
    gend = [k.sb("gend%d" % i, [128, 2, 64, 4, 2], F32) for i in range(2)]
    yst = [k.sb("yst%d" % i, [128, NT], F32) for i in range(2)]
    sem_u = S.new_dma_sem("pbu")
    sem_s = S.new_dma_sem("pbs")
    sem_y = [S.new_dma_sem("pby%d" % i) for i in range(2)]
    sem_g = [S.new_dma_sem("pbg%d" % i) for i in range(2)]
    S.dma("sp", sem_s, s5d.t[:], dr["s5d"][:, :], writes=[s5d.r])
    CL = k.CL

    def v3(t):
        return t[:].rearrange("p (c j) -> p c j", c=4)

    items = []
    for i in range(TILES):
        for ct in range(KT):
            for d in range(2):
                for q in range(4):
                    items.append(dict(i=i, ct=ct, d=d, q=q))
    NI = len(items)

    def tile_start(i):
        t0 = i * NT
        u = u32[0]
        S.dma("pool", sem_u, u.t[:], dr["U"].rearrange("(c p) t -> p c t", p=128)[:, :, t0:t0 + NT], writes=u.rs)
        for kt in range(KT):
            k.CP("act", ub.t[:, kt, :], u.t[:, kt, :], [u.rs[kt]], [ub.rs[kt]])
        tabs.push([dr["TAB"][d, ct].rearrange("p a b c -> p (a b c)") for ct in range(KT) for d in range(2)])

    cur_tb = {}

    def st0(n):
        it = items[n]
        i, ct, d, q = it["i"], it["ct"], it["d"], it["q"]
        if q == 0:
            cur_tb["tb"] = tabs.get()
        it["tb"] = cur_tb["tb"]
        b = n % nb
        par, pai = k.bank[2 * b], k.bank[2 * b + 1]
        rows = slice(32 * q, 32 * q + 32)
        k.MM(par.t[:], k.BL.t[rows, d, 0, ct, :], ub.t[rows, ct, :], True, True, [k.BL.r, ub.rs[ct]], [par.r], tile_position=(32 * q, 0))
        k.MM(pai.t[:], k.BL.t[rows, d, 1, ct, :], ub.t[rows, ct, :], True, True, [k.BL.r, ub.rs[ct]], [pai.r], tile_position=(32 * q, 0))

    def tbb(it, w):
        tb4 = it["tb"].t[:].rearrange("p (q w j) -> p q w j", q=4, w=4)
        return tb4[:, it["q"], w, :].unsqueeze(1).to_broadcast([128, 4, 128])

    def st1(n):
        it = items[n]
        b = n % nb
        tb = it["tb"]
        par, pai = k.bank[2 * b], k.bank[2 * b + 1]
        t1, t2, t3, t4 = tq[b]
        k.TT("dve", v3(t1.t), v3(par.t), tbb(it, 0), MUL, [par.r, tb.r], [t1.r])
        k.TT("dve", v3(t2.t), v3(pai.t), tbb(it, 1), MUL, [pai.r, tb.r], [t2.r])
        k.TT("dve", v3(t3.t), v3(par.t), tbb(it, 1), MUL, [par.r, tb.r], [t3.r])
        k.TT("dve", v3(t4.t), v3(pai.t), tbb(it, 0), MUL, [pai.r, tb.r], [t4.r])

    def st2(n):
        it = items[n]
        d = it["d"]
        b = n % nb
        t1, t2, t3, t4 = tq[b]
        gr, gi = gq[b]
        for c in range(4):
            def cs_(t):
                vv = v3(t.t)[:, c, :]
                return vv[:, ::-1] if d == 1 else vv
            k.SCAN(cs_(gr), cs_(t1), cs_(t2), 0.0, ADD, SUB, [t1.r, t2.r], [gr.rs[c]])
            k.SCAN(cs_(gi), cs_(t3), cs_(t4), 0.0, ADD, ADD, [t3.r, t4.r], [gi.rs[c]])

    def st3(n):
        it = items[n]
        i, ct, d, q = it["i"], it["ct"], it["d"], it["q"]
        b = n % nb
        tb = it["tb"]
        gr, gi = gq[b]
        ge = gend[i % 2]
        pr = 4 * ct + q
        e = 127 if d == 0 else 0
        k.CP("pool", ge.t[:, d, pr, :, 0], v3(gr.t)[:, :, e], gr.rs, [ge.r])
        k.CP("pool", ge.t[:, d, pr, :, 1], v3(gi.t)[:, :, e], gi.rs, [ge.r])
        x1, x2, x3, x4 = xq[b]
        k.TT("pool", v3(x1.t), v3(gr.t), tbb(it, 2), MUL, gr.rs + [tb.r], [x1.r])
        k.TT("pool", v3(x2.t), v3(gi.t), tbb(it, 3), MUL, gi.rs + [tb.r], [x2.r])
        k.TT("pool", v3(x3.t), v3(gr.t), tbb(it, 3), MUL, gr.rs + [tb.r], [x3.r])
        k.TT("pool", v3(x4.t), v3(gi.t), tbb(it, 2), MUL, gi.rs + [tb.r], [x4.r])

    def st4(n):
        it = items[n]
        i, ct, d, q = it["i"], it["ct"], it["d"], it["q"]
        b = n % nb
        pr = 4 * ct + q
        t0 = i * NT
        yb = k.bank[6 + ct % 2]
        rows = slice(32 * q, 32 * q + 32)
        x1, x2, x3, x4 = xq[b]
        yo = yb.t[rows, :]
        first = (d == 0)
        kw = dict(skip_group_check=True, tile_position=(0, 32 * q))
        k.MM(yo, CL.t[:, d, pr, 0, :], x1.t[:], first, False, [CL.r, x1.r], [yb.r], **kw)
        k.MM(yo, CL.t[:, d, pr, 1, :], x2.t[:], False, False, [CL.r, x2.r], [yb.r], **kw)
        k.MM(yo, CL.t[:, d, pr, 2, :], x3.t[:], False, False, [CL.r, x3.r], [yb.r], **kw)
        k.MM(yo, CL.t[:, d, pr, 2, :], x4.t[:], False, d == 1, [CL.r, x4.r], [yb.r], **kw)
        if d == 1 and q == 3:
            u = u32[0]
            ys = yst[ct % 2]
            k.STT(ys.t[:], u.t[:, ct, :], s5d.t[:, ct:ct + 1], yb.t[:], MUL, ADD, [u.rs[ct], s5d.r, yb.r], [ys.r])
            S.dma("sp", sem_y[ct % 2], dr["YL"][ct * 128:(ct + 1) * 128, t0:t0 + NT], ys.t[:], reads=[ys.r])
            if ct == KT - 1:
                ge = gend[i % 2]
                S.dma("sp", sem_g[i % 2], dr["GE"][i], ge.t[:].rearrange("p a b c e -> p (a b c e)"), reads=[ge.r])

    per = KT * 8
    for i in range(TILES):
        tile_start(i)
        lo, hi = i * per, (i + 1) * per
        for n in range(lo - 2, hi + 1):
            if lo <= n + 2 < hi:
                st0(n + 2)
            if lo <= n + 1 < hi:
                st1(n + 1)
            if lo <= n < hi:
                st2(n)
                st3(n)
            if lo <= n - 1 < hi:
                st4(n - 1)


def s5_chain(k, dr, TILES, SEGT):
    S = k.S
    NCH = TILES * 4
    v = VC(k, "chain")
    sem = S.new_dma_sem("chain")
    gE = k.sb("gE", [128, 64, NCH, 2], F32).t
    Sr = k.sb("Sr", [128, 64, NCH], F32).t
    Si = k.sb("Si", [128, 64, NCH], F32).t
    Hin = k.sb("Hin", [128, 64, NCH, 2], F32).t
    t1 = k.sb("ct1", [128, 64, NCH], F32).t
    t2 = k.sb("ct2", [128, 64, NCH], F32).t
    for d in range(2):
        for i in range(TILES):
            S.dma("sp", sem, gE[:, :, 4 * i:4 * i + 4, :],
                  dr["GE"][i].rearrange("p (a b c e) -> p a b c e", a=2, b=64, c=4)[:, d], reads=[], writes=[v.r])
        Lr = k.L127.t[:, d, 0, :].unsqueeze(2).to_broadcast([128, 64, NCH])
        Li = k.L127.t[:, d, 1, :].unsqueeze(2).to_broadcast([128, 64, NCH])
        v.tt(t1[:], gE[:, :, :, 0], Lr, MUL)
        v.tt(t2[:], gE[:, :, :, 1], Li, MUL)
        v.tt(Sr[:], t1[:], t2[:], SUB)
        v.tt(t1[:], gE[:, :, :, 0], Li, MUL)
        v.tt(t2[:], gE[:, :, :, 1], Lr, MUL)
        v.tt(Si[:], t1[:], t2[:], ADD)
        ar = k.L128.t[:, d, 0, :]
        ai = k.L128.t[:, d, 1, :]
        order = list(range(NCH)) if d == 0 else list(range(NCH - 1, -1, -1))
        a1 = t1[:, :, 0]
        a2 = t2[:, :, 0]
        for n, kk in enumerate(order):
            if n == 0:
                v.ms(Hin[:, :, kk, :], 0.0)
            if n == NCH - 1:
                break
            nxt = order[n + 1]
            pr_, pi_ = Hin[:, :, kk, 0], Hin[:, :, kk, 1]
            v.tt(a1, ar, pr_, MUL)
            v.tt(a2, ai, pi_, MUL)
            v.tt(a1, a1, a2, SUB)
            v.tt(Hin[:, :, nxt, 0], a1, Sr[:, :, kk], ADD)
            v.tt(a1, ar, pi_, MUL)
            v.tt(a2, ai, pr_, MUL)
            v.tt(a1, a1, a2, ADD)
            v.tt(Hin[:, :, nxt, 1], a1, Si[:, :, kk], ADD)
            bnd = (nxt % (4 * SEGT) == 0) if d == 0 else ((nxt + 1) % (4 * SEGT) == 0)
            if bnd:
                v.ts(Hin[:, :, nxt, :], Hin[:, :, nxt, :], k.keep.t[:, 0:1], MUL)
        for i in range(TILES):
            S.dma("sp", sem, dr["HIN"][i].rearrange("p (a b c e) -> p a b c e", a=2, b=64, c=4)[:, d],
                  Hin[:, :, 4 * i:4 * i + 4, :], reads=[v.r], writes=[v.r])


def s5_carry_tile(k, dr, i, yact):
    S = k.S
    t0 = i * NT
    c5 = k.c5
    hin = c5["hin"][i % 2]
    S.dma("pool", c5["sem_h"][i % 2], hin.t[:].rearrange("p a b c e -> p (a b c e)"), dr["HIN"][i], writes=[hin.r])
    c5["pp"].push([dr["PPL"][d, ct].rearrange("p a b c -> p (a b c)") for ct in range(KT) for d in range(2)])
    c5["cs"].push([dr["CRI"][ct] for ct in range(KT)])

    def build_ch(ct):
        cs = c5["cs"].get()
        c4v = cs.t[:].rearrange("p (d q w c) -> p d q w c", d=2, q=4, w=2)
        chb = c5["chb"][ct % 2]
        for d in range(2):
            Crb = c4v[:, d, :, 0, :].unsqueeze(2).to_broadcast([128, 4, 4, 32])
            Cib = c4v[:, d, :, 1, :].unsqueeze(2).to_broadcast([128, 4, 4, 32])
            Hr = hin.t[:, d, 4 * ct:4 * ct + 4, :, 0].unsqueeze(3).to_broadcast([128, 4, 4, 32])
            Hi = hin.t[:, d, 4 * ct:4 * ct + 4, :, 1].unsqueeze(3).to_broadcast([128, 4, 4, 32])
            A, B = c5["ta"][d], c5["tb"][d]
            k.TT("dve", A.t[:], Crb, Hr, MUL, [cs.r, hin.r], [A.r])
            k.TT("dve", B.t[:], Cib, Hi, MUL, [cs.r, hin.r], [B.r])
            k.TT("dve", chb.t[:, d, :, :, 0, :], A.t[:], B.t[:], SUB, [A.r, B.r], [chb.r])
            k.TT("dve", A.t[:], Crb, Hi, MUL, [cs.r, hin.r], [A.r])
            k.TT("dve", B.t[:], Cib, Hr, MUL, [cs.r, hin.r], [B.r])
            k.STT(chb.t[:, d, :, :, 1, :], A.t[:], -1.0, B.t[:], MUL, SUB, [A.r, B.r], [chb.r])
        return chb

    chn = build_ch(0)
    for ct in range(KT):
        chb = chn
        yb = k.bank[6 + ct % 2]
        pps = [c5["pp"].get(), c5["pp"].get()]
        yl = c5["yl"][ct % 2]
        S.dma("pool", c5["sem_yl"][ct % 2], yl.t[:], dr["YL"][ct * 128:(ct + 1) * 128, t0:t0 + NT], writes=[yl.r])
        for q in range(4):
            n = 0
            for c4 in range(4):
                for d in range(2):
                    pv = pps[d].t[:].rearrange("p (q r j) -> p q r j", q=4, r=2)
                    for ri in range(2):
                        k.MM(yb.t[32 * q:32 * q + 32, c4 * 128:(c4 + 1) * 128], chb.t[:, d, q, c4, ri, :], pv[:, q, ri, :],
                             n == 0, n == 15, [chb.r, pps[d].r], [yb.r], skip_group_check=True, tile_position=(0, 32 * q))
                        n += 1
        if ct + 1 < KT:
            chn = build_ch(ct + 1)
        k.TT("dve", yl.t[:], yl.t[:], yb.t[:], ADD, [yl.r, yb.r], [yl.r])
        if "YA" in dr:
            S.dma("pool", c5["sem_yl"][ct % 2], dr["YA"][ct * 128:(ct + 1) * 128, t0:t0 + NT], yl.t[:], reads=[yl.r])
        g2 = c5["g2"][ct % 2]
        k.ACTF(g2.t[:], yl.t[:], AF.Square, [yl.r], [g2.r])
        k.TS("dve", g2.t[:], g2.t[:], 0.044715, 1.0, MUL, ADD, [g2.r], [g2.r])
        k.TT("dve", g2.t[:], g2.t[:], yl.t[:], MUL, [g2.r, yl.r], [g2.r])
        k.ACTF(g2.t[:], g2.t[:], AF.Sigmoid, [g2.r], [g2.r], scale=1.5957691216057308)
        k.TT("dve", yact.t[:, ct, :], g2.t[:], yl.t[:], MUL, [g2.r, yl.r], [yact.rs[ct]])


def pass_d(k, dr, TILES, SEGT):
    S = k.S
    NTOK = TILES * NT
    ws = Stream(k, "WD", 1024, BF16, 3)
    xmh = [k.sb("xmh%d" % i, [128, CT, NT + 4], BF16) for i in range(2)]
    sem_x = [S.new_dma_sem("xmh%d" % i) for i in range(2)]
    xcb = [k.sb("xcb%d" % i, [128, NT], BF16) for i in range(2)]
    qst = [k.sb("qst%d" % i, [128, NT], BF16) for i in range(2)]
    kst = [k.sb("kst%d" % i, [128, NT], BF16) for i in range(2)]
    vst = [k.sb("vst%d" % i, [128, NT], BF16) for i in range(2)]
    kts = [k.sb("kts%d" % i, [128, 4, 128], BF16) for i in range(2)]
    vts = [k.sb("vts%d" % i, [128, 4, 128], BF16) for i in range(2)]
    gst = [k.sb("gst%d" % i, [128, 4, 64], F32) for i in range(2)]
    sems = {n: [S.new_dma_sem("pd%s%d" % (n, i)) for i in range(2)] for n in ("xc", "q", "k", "kt", "vt", "g", "w")}
    wgf = k.sb("wgf", [128, 3 * CT * 64], F32)
    wg = k.sb("wg", [128, 3, CT, 64], BF16)
    bgf = k.sb("bgf", [128, 64], F32)
    bgb = k.sb("bgb", [128, 64], BF16)
    onesb = k.sb("onesb", [128, 128], BF16)
    zerob = k.sb("zerob", [128, 256], BF16)
    k.MS("dve", zerob.t[:], 0.0, [zerob.r])
    k.MS("dve", bgf.t[:], 0.0, [bgf.r])
    S.dma("sp", sems["w"][0], wgf.t[:], dr["wg"][:, :], writes=[wgf.r])
    k.CP("dve", wg.t[:].rearrange("p a b c -> p (a b c)"), wgf.t[:], [wgf.r], [wg.r])
    S.dma("sp", sems["w"][1], bgf.t[0:1, :], dr["bgate"][:, :], writes=[bgf.r])
    k.CP("dve", bgb.t[:], bgf.t[:], [bgf.r], [bgb.r])
    k.MS("dve", onesb.t[:], 1.0, [onesb.r])
    XMv = dr["XM"].rearrange("(c p) t -> p c t", p=128)
    n = 0
    for i in range(TILES):
        t0 = i * NT
        xb = xmh[i % 2]
        lo = max(t0 - 2, 0)
        hi = min(t0 + NT + 2, NTOK)
        S.dma("pool", sem_x[i % 2], xb.t[:, :, lo - (t0 - 2):hi - (t0 - 2)], XMv[:, :, lo:hi], writes=[xb.r])
        if i == 0:
            k.MS("pool", xb.t[:, :, 0:2], 0.0, [xb.r])
        elif i % SEGT == 0:
            k.TS("pool", xb.t[:, :, 0:2], xb.t[:, :, 0:2], k.keep.t[:, 0:1], None, MUL, None, [xb.r, k.keep.r], [xb.r])
        if i == TILES - 1:
            k.MS("pool", xb.t[:, :, NT + 2:NT + 4], 0.0, [xb.r])
        elif (i + 1) % SEGT == 0:
            k.TS("pool", xb.t[:, :, NT + 2:NT + 4], xb.t[:, :, NT + 2:NT + 4], k.keep.t[:, 0:1], None, MUL, None, [xb.r, k.keep.r], [xb.r])
        ws.push([dr["mlD_b"][ct] for ct in range(CT)])
        gps = k.bank[7]
        k.MM(gps.t[:, 0:256], zerob.t[:, 0:128], zerob.t[:], True, False, [zerob.r], [gps.r], skip_group_check=True)
        for ct in range(CT):
            w = ws.get()
            b = n % 2
            n += 1
            pc = k.bank[b]
            for tau in range(5):
                k.MM(pc.t[:], w.t[:, tau * 128:(tau + 1) * 128], xb.t[:, ct, tau:tau + NT], tau == 0, tau == 4, [w.r, xb.r], [pc.r])
            xc = xcb[b]
            k.ACTF(xc.t[:], pc.t[:], AF.Silu, [pc.r, k.mlvec.r], [xc.r], bias=k.mlvec.t[:, ct:ct + 1])
            S.dma("sp", sems["xc"][b], dr["XC"][ct * 128:(ct + 1) * 128, t0:t0 + NT], xc.t[:], reads=[xc.r])
            pq, pk, pv = k.bank[2], k.bank[3], k.bank[4]
            k.MM(pq.t[:], w.t[:, 640:768], xc.t[:], True, True, [w.r, xc.r], [pq.r])
            k.MM(pk.t[:], w.t[:, 768:896], xc.t[:], True, True, [w.r, xc.r], [pk.r])
            k.MM(pv.t[:], w.t[:, 896:1024], xb.t[:, ct, 2:NT + 2], True, True, [w.r, xb.r], [pv.r])
            q_, k_, v_ = qst[b], kst[b], vst[b]
            k.CP("dve", q_.t[:], pq.t[:], [pq.r], [q_.r])
            k.CP("act", k_.t[:], pk.t[:], [pk.r], [k_.r])
            k.CP("dve", v_.t[:], pv.t[:], [pv.r], [v_.r])
            S.dma("sp", sems["q"][b], dr["QT"][ct * 128:(ct + 1) * 128, t0:t0 + NT], q_.t[:], reads=[q_.r])
            S.dma("sp", sems["k"][b], dr["KT"][ct * 128:(ct + 1) * 128, t0:t0 + NT], k_.t[:], reads=[k_.r])
            pkt, pvt = k.bank[5], k.bank[6]
            for c4 in range(4):
                k.MM(pkt.t[:, c4 * 128:(c4 + 1) * 128], xc.t[:, c4 * 128:(c4 + 1) * 128], w.t[:, 768:896], True, True, [w.r, xc.r], [pkt.r])
                k.MM(pvt.t[:, c4 * 128:(c4 + 1) * 128], xb.t[:, ct, 2 + c4 * 128:2 + (c4 + 1) * 128], w.t[:, 896:1024], True, True, [w.r, xb.r], [pvt.r])
            kt_, vt_ = kts[b], vts[b]
            k.CP("act", kt_.t[:].rearrange("p a b -> p (a b)"), pkt.t[:], [pkt.r], [kt_.r])
            k.CP("dve", vt_.t[:].rearrange("p a b -> p (a b)"), pvt.t[:], [pvt.r], [vt_.r])
            S.dma("sp", sems["kt"][b], dr["KTOK"][4 * i:4 * i + 4, :, ct * 128:(ct + 1) * 128].rearrange("c t h -> t c h"), kt_.t[:], reads=[kt_.r])
            S.dma("sp", sems["vt"][b], dr["VTOK"][4 * i:4 * i + 4, :, ct * 128:(ct + 1) * 128].rearrange("c t h -> t c h"), vt_.t[:], reads=[vt_.r])
            for c4 in range(4):
                cs_ = slice(c4 * 128, (c4 + 1) * 128)
                for j, src in enumerate((q_, k_, v_)):
                    first = False
                    k.MM(gps.t[:, c4 * 64:(c4 + 1) * 64], src.t[:, cs_], wg.t[:, j, ct, :], first, False, [src.r, wg.r], [gps.r], skip_group_check=True)
        for c4 in range(4):
            k.MM(gps.t[:, c4 * 64:(c4 + 1) * 64], onesb.t[:], bgb.t[:], False, c4 == 3, [onesb.r, bgb.r], [gps.r], skip_group_check=True)
        g_ = gst[i % 2]
        k.CP("dve", g_.t[:].rearrange("p a b -> p (a b)"), gps.t[:, 0:256], [gps.r], [g_.r])
        S.dma("sp", sems["g"][i % 2], dr["G"][4 * i:4 * i + 4].rearrange("c t g -> t c g"), g_.t[:], reads=[g_.r])


def ml_prep(k, dr, TILES, SEGT):
    S = k.S
    NCH = TILES * 4
    v = VC(k, "mlprep")
    sem = S.new_dma_sem("mlprep")
    gall = k.sb("gall", [128, NCH, 64], F32).t
    lf = k.sb("lfall", [128, NCH, 2, 16], F32).t
    bc = k.sb("bcum", [128, NCH, 2, 16], F32).t
    io_i = k.sb("mio_i", [128, 128], I32).t
    io_f = k.sb("mio_f", [128, 128], F32).t
    lnsc = k.sb("lnsc", [128, 1], F32).t
    onec = k.sb("onec", [128, 1], F32).t
    S.dma("sp", sem, gall[:], dr["G"].rearrange("c t g -> t c g"), writes=[v.r])
    S.op("pool", lambda h: h.iota(io_i[:], [[1, 128]], 0, -1), [], [v.r])
    v.cp(io_f[:], io_i[:])
    v.ts(k.TRI[0].t[:], io_f[:], 0.0, ALU.is_ge)
    v.ts(k.TRI[1].t[:], io_f[:], 0.0, ALU.is_le)
    v.ts(k.ident32.t[:], io_f[:], 0.0, ALU.is_equal)
    v.ms(lnsc[:], -0.5 * float(np.log(DH)))
    v.ms(onec[:], 1.0)
    g5 = gall[:].rearrange("p c (d w h) -> p c d w h", d=2, w=2)
    v.act(lf[:], g5[:, :, :, 1, :], AF.Exp, scale=-1.0)
    v.act(lf[:], lf[:], AF.Ln, bias=onec[:, 0:1])
    v.ts(lf[:], lf[:], -1.0, MUL)
    half = NCH // 2 if NCH >= 2 else 1
    for d in range(2):
        for (lhs, dst) in ((k.TRI[d], bc), (k.ones32, k.EG.t)):
            for c0 in range(0, NCH, 32):
                c1 = min(c0 + 32, NCH)
                ps = k.bank[(c0 // 32) % 2]
                k.MM(ps.t[:, 0:(c1 - c0) * 16].rearrange("p (c h) -> p c h", h=16), lhs.t[:], lf[:, c0:c1, d, :], True, True, [v.r], [ps.r])
                k.CP("dve", dst[:, c0:c1, d, :], ps.t[:, 0:(c1 - c0) * 16].rearrange("p (c h) -> p c h", h=16), [ps.r], [v.r])
    v.tt(k.ED.t[:], g5[:, :, :, 0, :], bc[:], SUB)
    v.act(k.ED.t[:], k.ED.t[:], AF.Exp, bias=lnsc[:, 0:1])
    v.act(k.EB.t[:], bc[:], AF.Exp, scale=-1.0)
    v.act(k.EG.t[:], k.EG.t[:], AF.Exp)
    for kk in range(NCH):
        if (kk + 1) % (4 * SEGT) == 0 and kk + 1 < NCH:
            v.ts(k.EG.t[:, kk, 0, :], k.EG.t[:, kk, 0, :], k.keep.t[:, 0:1], MUL)
        if kk % (4 * SEGT) == 0 and kk > 0:
            v.ts(k.EG.t[:, kk, 1, :], k.EG.t[:, kk, 1, :], k.keep.t[:, 0:1], MUL)
    k.mlprep_res = v.r


def pass_e(k, dr, TILES):
    S = k.S
    NCH = TILES * 4
    PR = k.mlprep_res
    C32 = k.sb("C32", [128, NH, 2, 257], F32, nres=NH)
    Cb = k.sb("Cb", [128, NH, 2, 257], BF16, nres=NH)
    qT = [k.sb("qT%d" % i, [128, CT, 128], BF16) for i in range(2)]
    kT = [k.sb("kT%d" % i, [128, CT, 128], BF16) for i in range(2)]
    ktk = [k.sb("ktk%d" % i, [128, DI], BF16) for i in range(2)]
    vau = [k.sb("vau%d" % i, [128, NH, 260], BF16) for i in range(2)]
    sem_l = [[S.new_dma_sem("pe%d_%d" % (j, i)) for i in range(2)] for j in range(4)]
    hx = k.sb("hx", [128, NH, 257], F32, nres=NH)
    hbuf = k.sb("hbuf", [128, DI], F32, nres=NH)
    sem_hb = S.new_dma_sem("hbst")
    sem_hl = S.new_dma_sem("hbld")
    khat = k.sb("khat", [128, NH, DH], BF16, nres=NH)
    smT = k.sb("smT", [128, NH, 128], BF16, nres=NH)
    dn = k.sb("dn", [128, NH], F32)
    dn2 = k.sb("dn2", [128, NH], F32)
    xck = [k.sb("xck%d" % i, [128, CT // 2, 128], BF16) for i in range(1)]
    zk = [k.sb("zk%d" % i, [128, CT // 2, 128], F32) for i in range(1)]
    oab = [k.sb("oab%d" % i, [128, CT // 2, 128], BF16, nres=CT // 2) for i in range(1)]
    skb = [k.sb("skb%d" % i, [128, 128], F32) for i in range(2)]
    o1b = [k.sb("o1b%d" % i, [128, 128], F32) for i in range(2)]
    bst = k.sb("bst", [128, NH, 6], F32, nres=NH)
    mv = k.sb("mv", [128, NH, 2], F32)
    rs = k.sb("rs", [128, NH], F32)
    sem_p = [S.new_dma_sem("pep%d" % i) for i in range(4)]
    pq = [[Res("pq%d_%d" % (b, j)) for j in range(4)] for b in range(2)]
    for b in range(2):
        k.MS("pool", vau[b].t[:, :, 256:260], 1.0, [vau[b].r])
    QTv = dr["QT"].rearrange("(c p) t -> p c t", p=128)
    KTv = dr["KT"].rearrange("(c p) t -> p c t", p=128)
    XCv = dr["XC"].rearrange("(c p) t -> p c t", p=128)
    Zv = dr["Z"].rearrange("(c p) t -> p c t", p=128)
    OAv = dr["OA"].rearrange("(c p) t -> p c t", p=128)
    hx3 = hx.t
    nn = 0
    for d in (1, 0):
        for h in range(NH):
            k.MS("dve", C32.t[:, h], 0.0, [C32.rs[h]])
            k.MS("pool", Cb.t[:, h], 0.0, [Cb.rs[h]])
        order = list(range(NCH)) if d == 0 else list(range(NCH - 1, -1, -1))
        for kk in order:
            b = nn % 2
            nn += 1
            ts_ = slice(kk * 128, (kk + 1) * 128)
            q_, k_, kt_, va = qT[b], kT[b], ktk[b], vau[b]
            S.dma("sp", sem_l[0][b], q_.t[:], QTv[:, :, ts_], writes=[q_.r])
            S.dma("sp", sem_l[1][b], k_.t[:], KTv[:, :, ts_], writes=[k_.r])
            S.dma("pool", sem_l[2][b], kt_.t[:], dr["KTOK"][kk], writes=[kt_.r])
            S.dma("pool", sem_l[3][b], va.t[:, :, 0:256], dr["VTOK"][kk].rearrange("t (h e) -> t h e", h=NH), writes=[va.r])
            k.MS("pool", va.t[:, :, 256:260], 1.0, [va.r])
            if d == 0:
                S.dma("pool", sem_hl, hbuf.t[:], dr["HB"][kk], reads=[k.hbres], writes=hbuf.rs)
            for g4 in range(NH // 4):
                pS = k.bank[g4 % 2]
                for h in range(4 * g4, 4 * g4 + 4):
                    cols = slice((h % 4) * 128, (h % 4 + 1) * 128)
                    for i2 in range(2):
                        k.MM(pS.t[:, cols], k_.t[:, 2 * h + i2, :], q_.t[:, 2 * h + i2, :], i2 == 0, i2 == 1, [k_.r, q_.r], [pS.r], skip_group_check=True)
                for h in range(4 * g4, 4 * g4 + 4):
                    cols = slice((h % 4) * 128, (h % 4 + 1) * 128)
                    ed = k.ED.t[:, kk, d, h:h + 1]
                    k.STT(smT.t[:, h, :], pS.t[:, cols], ed, k.TRI[d].t[:], MUL, MUL, [pS.r, PR], [smT.rs[h]])
                    hs = slice(h * DH, (h + 1) * DH)
                    k.TS("dve", khat.t[:, h, :], kt_.t[:, hs], ed, None, MUL, None, [kt_.r, PR], [khat.rs[h]])
                    eg = k.EG.t[:, kk, d, h:h + 1]
                    k.ACTF(C32.t[:, h], C32.t[:, h], AF.Identity, [C32.rs[h], PR], [C32.rs[h]], scale=eg)
            for h in range(NH):
                pX = k.bank[2 + h % 2]
                for i2 in range(2):
                    k.MM(pX.t[:, 0:257], q_.t[:, 2 * h + i2, :], Cb.t[:, h, i2, :], i2 == 0, False, [q_.r, Cb.rs[h]], [pX.r])
                k.MM(pX.t[:, 0:257], smT.t[:, h, :], va.t[:, h, 0:257], False, True, [smT.rs[h], va.r], [pX.r])
                k.CP("act", hx3[:, h, :], pX.t[:, 0:257], [pX.r], [hx.rs[h]])
            for h in range(NH):
                pC = [k.bank[4 + 2 * (h % 2)], k.bank[5 + 2 * (h % 2)]]
                eg = k.EG.t[:, kk, d, h:h + 1]
                for i2 in range(2):
                    k.MM(pC[i2].t[:, 0:257], khat.t[:, h, i2 * 128:(i2 + 1) * 128], va.t[:, h, 0:257], True, True, [khat.rs[h], va.r], [pC[i2].r])
                    k.STT(C32.t[:, h, i2, :], pC[i2].t[:, 0:257], eg, C32.t[:, h, i2, :], MUL, ADD, [pC[i2].r, PR, C32.rs[h]], [C32.rs[h]])
                k.CP("act", Cb.t[:, h], C32.t[:, h], [C32.rs[h]], [Cb.rs[h]])
            ycol = hx3[:, :, 256]
            k.TT("dve", dn.t[:], ycol, k.EB.t[:, kk, d, :], ALU.max, hx.rs + [PR], [dn.r])
            k.STT(dn2.t[:], ycol, -1.0, dn.t[:], MUL, ALU.max, hx.rs + [dn.r], [dn2.r])
            k.RECIP(dn2.t[:], dn2.t[:], [dn2.r], [dn2.r])
            hb3 = hbuf.t[:].rearrange("p (h e) -> p h e", h=NH)
            rb = dn2.t[:].unsqueeze(2).to_broadcast([128, NH, DH])
            if d == 1:
                k.TT("dve", hb3, hx3[:, :, 0:256], rb, MUL, hx.rs + [dn2.r], hbuf.rs)
                S.dma("sp", sem_hb, dr["HB"][kk], hbuf.t[:], reads=hbuf.rs, writes=[k.hbres])
                continue
            k.TT("dve", hx3[:, :, 0:256], hx3[:, :, 0:256], rb, MUL, hx.rs + [dn2.r], hx.rs)
            k.TT("pool", hb3, hb3, hx3[:, :, 0:256], ADD, hbuf.rs + hx.rs, hbuf.rs)
            for h in range(NH):
                hs = slice(h * DH, (h + 1) * DH)
                S.op("dve", lambda hh, h=h, hs=hs: hh.bn_stats(out=bst.t[:, h, :], in_=hbuf.t[:, hs]), [hbuf.rs[h]], [bst.rs[h]])
            for h in range(NH):
                S.op("dve", lambda hh, h=h: hh.bn_aggr(out=mv.t[:, h, :], in_=bst.t[:, h, :]), [bst.rs[h]], [mv.r])
            k.ACTF(rs.t[:], mv.t[:, :, 1], AF.Sqrt, [mv.r, k.epsc.r], [rs.r], bias=k.epsc.t[:, 0:1])
            k.RECIP(rs.t[:], rs.t[:], [rs.r], [rs.r])
            for h in range(NH):
                hs = slice(h * DH, (h + 1) * DH)
                k.TS("dve", hbuf.t[:, hs], hbuf.t[:, hs], mv.t[:, h, 0:1], rs.t[:, h:h + 1], SUB, MUL, [hbuf.rs[h], mv.r, rs.r], [hbuf.rs[h]])
            for hf in range(2):
                c0 = hf * (CT // 2)
                xc_, z_, oa_ = xck[0], zk[0], oab[0]
                S.dma("pool", sem_p[hf], xc_.t[:], XCv[:, c0:c0 + CT // 2, ts_], writes=[xc_.r])
                S.dma("pool", sem_p[2], z_.t[:], Zv[:, c0:c0 + CT // 2, ts_], writes=[z_.r])
                k.ACTF(z_.t[:], z_.t[:], AF.Sigmoid, [z_.r], [z_.r])
                for g4 in range(CT // 8):
                    pT = k.bank[g4 % 2]
                    for cl in range(4 * g4, 4 * g4 + 4):
                        ct = c0 + cl
                        cols = slice((cl % 4) * 128, (cl % 4 + 1) * 128)
                        k.TR(pT.t[:, cols], hbuf.t[:, ct * 128:(ct + 1) * 128], k.ident32.t[:], [hbuf.rs[ct // 2], PR], [pT.r])
                    for cl in range(4 * g4, 4 * g4 + 4):
                        ct = c0 + cl
                        cols = slice((cl % 4) * 128, (cl % 4 + 1) * 128)
                        sk = skb[ct % 2]
                        o1 = o1b[ct % 2]
                        k.ACTF(sk.t[:], xc_.t[:, cl, :], AF.Identity, [xc_.r, k.mlvec.r], [sk.r], scale=k.mlvec.t[:, 64 + ct:65 + ct])
                        k.STT(o1.t[:], pT.t[:, cols], k.mlvec.t[:, 32 + ct:33 + ct], sk.t[:], MUL, ADD, [pT.r, sk.r, k.mlvec.r], [o1.r])
                        k.TT("pool", oa_.t[:, cl, :], o1.t[:], z_.t[:, cl, :], MUL, [o1.r, z_.r], [oa_.rs[cl]])
                S.dma("sp", sem_p[3], OAv[:, c0:c0 + CT // 2, ts_], oa_.t[:], reads=oa_.rs)


def build(TILES, SEGT, debug=(), stop_after="F"):
    NTOK = TILES * NT
    NCH = TILES * 4
    nc = bass.Bass("TRN2", target_bir_lowering=False)
    dbg = set(debug)

    def din(name, shape, dt=F32):
        return nc.dram_tensor(name, list(shape), dt, kind="ExternalInput").ap()

    def dscr(name, shape, dt):
        kind = "ExternalOutput" if name in dbg else "Internal"
        return nc.dram_tensor(name, list(shape), dt, kind=kind).ap()

    xT = din("xT", [D, NTOK])
    keep_d = din("keep", [128, 1])
    gvec_d = din("gvec", [128, 7 * KT])
    wsrc = {
        "ffn_win": din("ffn_win", [4 * JT, 128, 2 * KT * 128]),
        "ffn_wout": din("ffn_wout", [4 * KT, 128, JT * 128]),
        "s5_win": din("s5_win", [KT, 128, KT * 128]),
        "wglu": din("wglu", [2 * KT, 128, KT * 128]),
        "ml_win": din("ml_win", [2 * CT, 128, KT * 128]),
        "mlD": din("mlD", [CT, 128, 1024]),
        "ml_wout": din("ml_wout", [KT, 128, CT * 128]),
    }
    dr = {}
    for kk_ in ("lamre", "lamim", "logstep"):
        dr[kk_] = din(kk_, [2, 128, 64])
    dr["bre"] = din("bre", [2, 128, 64 * 16])
    dr["bim"] = din("bim", [2, 128, 64 * 16])
    dr["CRI"] = din("CRI", [KT, 128, 2 * 4 * 2 * 32])
    dr["s5d"] = din("s5d", [128, KT])
    dr["wg"] = din("wg", [128, 3 * CT * 64])
    dr["bgate"] = din("bgate", [1, 64])
    mlvec_d = din("mlvec", [128, 96])
    yT = nc.dram_tensor("yT", [D, NTOK], F32, kind="ExternalOutput").ap()

    wb = {n: dscr(n + "_b", a.shape, BF16) for n, a in wsrc.items()}
    X1 = dscr("X1", [D, NTOK], F32)
    dr["U"] = dscr("U", [D, NTOK], F32)
    dr["YL"] = dscr("YL", [D, NTOK], F32)
    dr["GE"] = dscr("GE", [TILES, 128, 2 * 64 * 4 * 2], F32)
    dr["HIN"] = dscr("HIN", [TILES, 128, 2 * 64 * 4 * 2], F32)
    dr["TAB"] = dscr("TAB", [2, KT, 128, 4, 4, 128], F32)
    dr["PPL"] = dscr("PPL", [2, KT, 128, 4, 2, 128], BF16)
    X2 = dscr("X2", [D, NTOK], F32)
    if "YA" in dbg:
        dr["YA"] = dscr("YA", [D, NTOK], F32)
    X4 = dscr("X4", [D, NTOK], F32)
    XM = dscr("XM", [DI, NTOK], BF16)
    Z = dscr("Z", [DI, NTOK], F32)
    U = dr["U"]
    dr["XM"] = XM
    dr["Z"] = Z
    dr["XC"] = dscr("XC", [DI, NTOK], BF16)
    dr["QT"] = dscr("QT", [DI, NTOK], BF16)
    dr["KT"] = dscr("KT", [DI, NTOK], BF16)
    dr["KTOK"] = dscr("KTOK", [NCH, 128, DI], BF16)
    dr["VTOK"] = dscr("VTOK", [NCH, 128, DI], BF16)
    dr["G"] = dscr("G", [NCH, 128, 64], F32)
    dr["HB"] = dscr("HB", [NCH, 128, DI], F32)
    dr["OA"] = dscr("OA", [DI, NTOK], BF16)
    X5 = dscr("X5", [D, NTOK], F32)

    def fm(ap):
        return ap.rearrange("(c p) t -> p c t", p=128)

    with ExitStack() as st:
        S = Sched(nc, st)
        k = K(nc, st, S)
        k.wres = Res("wres")
        k.bank = []
        for i in range(8):
            t = st.enter_context(nc.psum_tensor("bank%d" % i, [128, 512], F32))
            k.bank.append(Buf(t, "bank%d" % i))
        k.ffn_win_b = wb["ffn_win"]
        k.ffn_wout_b = wb["ffn_wout"]
        io_sem = [S.new_dma_sem("io%d" % i) for i in range(4)]
        st_sem = [S.new_dma_sem("st%d" % i) for i in range(6)]
        k.ones32 = k.sb("ones32", [128, 128], F32)
        k.epsc = k.sb("epsc", [128, 1], F32)
        k.gvec = k.sb("gvec", [128, 7, KT], F32)
        k.keep = k.sb("keepc", [128, 1], F32)
        k.MS("dve", k.ones32.t[:], 1.0, [k.ones32.r])
        k.MS("dve", k.epsc.t[:], EPS, [k.epsc.r])
        S.dma("sp", io_sem[0], k.gvec.t[:].rearrange("p a b -> p (a b)"), gvec_d[:, :], writes=[k.gvec.r])
        S.dma("sp", io_sem[0], k.keep.t[:], keep_d[:, :], writes=[k.keep.r])
        k.mlvec = k.sb("mlvec", [128, 96], F32)
        S.dma("sp", io_sem[0], k.mlvec.t[:], mlvec_d[:, :], writes=[k.mlvec.r])
        k.hbres = Res("hbres")
        dr["mlD_b"] = wb["mlD"]

        def phase():
            ph = ExitStack()
            k.ph = ph
            return ph

        def common_bufs():
            k.W = Stream(k, "W", JT * 128, BF16, 3)
            k.hn = k.sb("hn", [128, KT, NT], BF16, nres=KT)
            k.h = k.sb("h", [128, JT, NT], BF16, nres=JT)
            k.sq = [k.sb("sq%d" % i, [128, NT], F32) for i in range(2)]
            k.sg = [k.sb("sg%d" % i, [128, NT], F32) for i in range(2)]
            k.rstd = k.sb("rstd", [128, NT], F32)

        with phase():
            cast_weights(k, [(wsrc[n], wb[n]) for n in wsrc])
            barrier(S)

        with phase():
            common_bufs()
            x = k.sb("xa", [128, KT, NT], F32, nres=KT)
            ust = [k.sb("ust%d" % i, [128, NT], F32) for i in range(2)]
            for i in range(TILES):
                t0 = i * NT
                S.dma("pool", io_sem[0], x.t[:], fm(xT)[:, :, t0:t0 + NT], writes=x.rs)
                ffn(k, x, 0, 0)
                S.dma("pool", st_sem[0], fm(X1)[:, :, t0:t0 + NT], x.t[:], reads=x.rs)
                rms_stats(k, x)
                rms_apply(k, x, 1, k.hn)

                def cons(m, ps, t0=t0):
                    b = ust[m % 2]
                    k.CP("act", b.t[:], ps.t[:], [ps.r], [b.r])
                    S.dma("pool", st_sem[1 + m % 2], U[m * 128:(m + 1) * 128, t0:t0 + NT], b.t[:], reads=[b.r])
                proj(k, k.hn, [wb["s5_win"][m] for m in range(KT)], KT, cons)
            barrier(S)
        if stop_after == "A":
            S.emit_all()
            return nc, S

        s5scope = ExitStack()
        k.ph = s5scope
        k.BL = k.sb("BL", [128, 2, 2, KT, 128], BF16)
        k.CL = k.sb("CL", [128, 2, 64, 3, 32], BF16)
        k.L128 = k.sb("L128", [128, 2, 2, 64], F32)
        k.L127 = k.sb("L127", [128, 2, 2, 64], F32)
        with phase():
            s5_setup(k, dr)
            tmpc = k.sb("tmpc", [128, 2, 4, 2, 32], F32)
            for ct in range(KT):
                S.dma("sp", io_sem[1], tmpc.t[:].rearrange("p a b c e -> p (a b c e)"), dr["CRI"][ct], writes=[tmpc.r])
                k.CP("dve", k.CL.t[:, :, 4 * ct:4 * ct + 4, 0, :], tmpc.t[:, :, :, 0, :], [tmpc.r], [k.CL.r])
                k.TS("dve", k.CL.t[:, :, 4 * ct:4 * ct + 4, 1, :], tmpc.t[:, :, :, 0, :], -1.0, None, MUL, None, [tmpc.r], [k.CL.r])
                k.TS("dve", k.CL.t[:, :, 4 * ct:4 * ct + 4, 2, :], tmpc.t[:, :, :, 1, :], -1.0, None, MUL, None, [tmpc.r], [k.CL.r])
            barrier(S)
        with phase():
            pass_b(k, dr, TILES)
            barrier(S)
        with phase():
            s5_chain(k, dr, TILES, SEGT)
            barrier(S)
        s5scope.close()
        if stop_after == "B":
            S.emit_all()
            return nc, S

        with phase():
            common_bufs()
            x = k.sb("xc_", [128, KT, NT], F32, nres=KT)
            k.c5 = {
                "hin": [k.sb("hin%d" % i, [128, 2, 64, 4, 2], F32) for i in range(2)],
                "sem_h": [S.new_dma_sem("hin%d" % i) for i in range(2)],
                "pp": Stream(k, "PP", 4 * 2 * 128, BF16, 4, hold=2),
                "cs": Stream(k, "CS", 512, F32, 3, hold=2),
                "chb": [k.sb("chb%d" % i, [128, 2, 4, 4, 2, 32], BF16) for i in range(2)],
                "ta": [k.sb("cta%d" % i, [128, 4, 4, 32], F32) for i in range(2)],
                "tb": [k.sb("ctb%d" % i, [128, 4, 4, 32], F32) for i in range(2)],
                "yl": [k.sb("yl%d" % i, [128, NT], F32) for i in range(2)],
                "g2": [k.sb("g2_%d" % i, [128, NT], F32) for i in range(2)],
                "sem_yl": [S.new_dma_sem("yl%d" % i) for i in range(2)],
            }
            gsb = [k.sb("gsb%d" % i, [128, NT], F32) for i in range(2)]
            zst = [k.sb("zst%d" % i, [128, NT], F32) for i in range(2)]
            mst = [k.sb("mst%d" % i, [128, NT], BF16) for i in range(2)]
            for i in range(TILES):
                t0 = i * NT
                S.dma("pool", io_sem[0], x.t[:], fm(X1)[:, :, t0:t0 + NT], writes=x.rs)
                s5_carry_tile(k, dr, i, k.hn)
                held = {}

                def cons_glu(idx, ps):
                    m = idx // 2
                    if idx % 2 == 0:
                        held["v"] = ps
                        return
                    pv = held["v"]
                    g = gsb[m % 2]
                    k.ACTF(g.t[:], ps.t[:], AF.Sigmoid, [ps.r], [g.r])
                    k.TT("dve", g.t[:], g.t[:], pv.t[:], MUL, [g.r, pv.r], [g.r])
                    k.TT("dve", x.t[:, m, :], x.t[:, m, :], g.t[:], ADD, [g.r, x.rs[m]], [x.rs[m]])
                wl = []
                for m in range(KT):
                    wl += [wb["wglu"][m], wb["wglu"][KT + m]]
                proj(k, k.hn, wl, KT, cons_glu)
                if "X2" in dbg:
                    S.dma("pool", st_sem[3], fm(X2)[:, :, t0:t0 + NT], x.t[:], reads=x.rs)
                ffn(k, x, 2, 1)
                ffn(k, x, 3, 2)
                S.dma("pool", st_sem[0], fm(X4)[:, :, t0:t0 + NT], x.t[:], reads=x.rs)
                rms_stats(k, x)
                rms_apply(k, x, 4, k.hn)

                def cons_ml(m, ps, t0=t0):
                    if m < CT:
                        b = mst[m % 2]
                        k.CP("act", b.t[:], ps.t[:], [ps.r], [b.r])
                        S.dma("pool", st_sem[1 + m % 2], XM[m * 128:(m + 1) * 128, t0:t0 + NT], b.t[:], reads=[b.r])
                    else:
                        b = zst[m % 2]
                        k.CP("act", b.t[:], ps.t[:], [ps.r], [b.r])
                        S.dma("pool", st_sem[4 + m % 2], Z[(m - CT) * 128:(m - CT + 1) * 128, t0:t0 + NT], b.t[:], reads=[b.r])
                proj(k, k.hn, [wb["ml_win"][m] for m in range(2 * CT)], KT, cons_ml)
            barrier(S)
        if stop_after == "C":
            S.emit_all()
            return nc, S

        with phase():
            pass_d(k, dr, TILES, SEGT)
            barrier(S)
        if stop_after == "D":
            S.emit_all()
            return nc, S
        mlscope = ExitStack()
        k.ph = mlscope
        k.ED = k.sb("ED", [128, NCH, 2, 16], F32)
        k.EB = k.sb("EB", [128, NCH, 2, 16], F32)
        k.EG = k.sb("EG", [128, NCH, 2, 16], F32)
        k.TRI = [k.sb("TRI%d" % i, [128, 128], F32) for i in range(2)]
        k.ident32 = k.sb("ident32", [128, 128], F32)
        with phase():
            ml_prep(k, dr, TILES, SEGT)
            barrier(S)
        with phase():
            pass_e(k, dr, TILES)
            barrier(S)
        mlscope.close()
        if stop_after == "E":
            S.emit_all()
            return nc, S
        with phase():
            common_bufs()
            x = k.sb("xf_", [128, KT, NT], F32, nres=KT)
            OAv = dr["OA"].rearrange("(c p) t -> p c t", p=128)
            for i in range(TILES):
                t0 = i * NT
                S.dma("pool", io_sem[0], x.t[:], fm(X4)[:, :, t0:t0 + NT], writes=x.rs)
                S.dma("pool", io_sem[1], k.h.t[:, 0:CT, :], OAv[:, :, t0:t0 + NT], writes=k.h.rs)

                def cons_o(m, ps):
                    k.TT("dve", x.t[:, m, :], x.t[:, m, :], ps.t[:], ADD, [ps.r, x.rs[m]], [x.rs[m]])
                proj(k, k.h, [wb["ml_wout"][m] for m in range(KT)], CT, cons_o)
                if "X5" in dbg:
                    S.dma("pool", st_sem[3], fm(X5)[:, :, t0:t0 + NT], x.t[:], reads=x.rs)
                ffn(k, x, 5, 3)
                rms_stats(k, x)
                rms_apply(k, x, 6, x)
                S.dma("pool", st_sem[0], fm(yT)[:, :, t0:t0 + NT], x.t[:], reads=x.rs)
            barrier(S)
        barrier(S)
        S.emit_all()
    return nc, S


def _tile_rows(w, nk):
    K_, M_ = w.shape
    m = M_ // 128
    return np.ascontiguousarray(w.reshape(nk, 128, m, 128).transpose(2, 1, 0, 3).reshape(m, 128, nk * 128))


def prep_shared(inp):
    f = np.float32
    out = {}
    g = np.concatenate([np.asarray(inp["norm_g"], f).reshape(6, D), np.asarray(inp["final_g"], f).reshape(1, D)], 0)
    out["gvec"] = np.ascontiguousarray(g.reshape(7, KT, 128).transpose(2, 0, 1).reshape(128, 7 * KT))
    win = np.asarray(inp["ffn_w_in"], f).reshape(4, D, 2, JT, 128)
    win = win.reshape(4, KT, 128, 2, JT, 128).transpose(0, 4, 2, 3, 1, 5)
    out["ffn_win"] = np.ascontiguousarray(win.reshape(4 * JT, 128, 2 * KT * 128))
    wout = np.asarray(inp["ffn_w_out"], f).reshape(4, DFF, D)
    out["ffn_wout"] = np.concatenate([_tile_rows(wout[i], JT) for i in range(4)], 0)
    out["s5_win"] = _tile_rows(np.asarray(inp["s5_w_in"], f)[0], KT)
    out["wglu"] = _tile_rows(np.asarray(inp["s5_w_glu"], f)[0], KT)
    out["ml_win"] = _tile_rows(np.asarray(inp["ml_w_in"], f)[0], KT)

    def gp(a):
        sh = a.shape
        a = a.reshape((2, 64, 2, 64) + sh[3:])
        perm = (0, 2, 3, 1) + tuple(range(4, a.ndim))
        a = a.transpose(perm)
        return np.ascontiguousarray(a.reshape((2, 128, 64) + sh[3:]))
    out["lamre"] = gp(np.asarray(inp["s5_lambda_re"], f)[0])
    out["lamim"] = gp(np.asarray(inp["s5_lambda_im"], f)[0])
    ls = np.asarray(inp["s5_log_step"], f)[0]
    out["logstep"] = gp(np.broadcast_to(ls[:, :, None], (2, 128, 64)).copy())
    out["bre"] = gp(np.asarray(inp["s5_b_re"], f)[0]).reshape(2, 128, 64 * 16)
    out["bim"] = gp(np.asarray(inp["s5_b_im"], f)[0]).reshape(2, 128, 64 * 16)
    cri = np.zeros((KT, 128, 2, 4, 2, 32), f)
    for ri, key in enumerate(("s5_c_re", "s5_c_im")):
        c = np.asarray(inp[key], f)[0]
        c = c.reshape(2, KT, 4, 2, 16, 64)
        for g2 in range(2):
            cri[:, g2 * 64:(g2 + 1) * 64, :, :, ri, g2 * 16:(g2 + 1) * 16] = c[:, :, :, g2].transpose(1, 4, 0, 2, 3)
    out["CRI"] = np.ascontiguousarray(cri.reshape(KT, 128, 512))
    out["s5d"] = np.ascontiguousarray(np.asarray(inp["s5_d"], f)[0].reshape(KT, 128).T)
    mlD = np.zeros((CT, 128, 8, 128), f)
    cw = np.asarray(inp["ml_conv_w"], f)[0].reshape(5, CT, 128)
    ar = np.arange(128)
    for tau in range(5):
        mlD[:, ar, tau, ar] = cw[tau]
    for j, key in enumerate(("ml_wq", "ml_wk", "ml_wv")):
        w = np.asarray(inp[key], f)[0].reshape(CT, 32, 4, 4)
        for n in range(32):
            mlD[:, 4 * n:4 * n + 4, 5 + j, 4 * n:4 * n + 4] = w[:, n]
    out["mlD"] = np.ascontiguousarray(mlD.reshape(CT, 128, 1024))
    out["ml_wout"] = _tile_rows(np.asarray(inp["ml_w_out"], f)[0], CT)
    wg = np.asarray(inp["ml_w_gates"], f)[0].reshape(3, CT, 128, 64)
    out["wg"] = np.ascontiguousarray(wg.transpose(2, 0, 1, 3).reshape(128, 3 * CT * 64))
    out["bgate"] = np.ascontiguousarray(np.asarray(inp["ml_b_gates"], f)[0].reshape(1, 64))
    vecs = [np.asarray(inp[kk], f)[0].reshape(CT, 128).T for kk in ("ml_conv_b", "ml_norm_g", "ml_skip")]
    out["mlvec"] = np.ascontiguousarray(np.concatenate(vecs, 1))
    return out


_CACHE = {}


def kernel(**inputs):
    f = np.float32
    TILES, SEGT = 16, 4
    NTOK = TILES * NT
    sh = prep_shared(inputs)
    xp = np.asarray(inputs["x_prompt"], f)
    xs = np.asarray(inputs["x_sample"], f)
    in_maps = []
    for c in range(8):
        m = dict(sh)
        if c < 4:
            m["xT"] = np.ascontiguousarray(xp[c].T)
            m["keep"] = np.ones((128, 1), f)
        else:
            xt = np.zeros((D, NTOK), f)
            for j in range(2):
                xt[:, j * 2048:(j + 1) * 2048] = xs[2 * (c - 4) + j].T
            m["xT"] = xt
            m["keep"] = np.zeros((128, 1), f)
        in_maps.append(m)
    if "nc" not in _CACHE:
        _CACHE["nc"] = build(TILES, SEGT)[0]
    res = run_bass_kernel_spmd(_CACHE["nc"], in_maps, core_ids=list(range(8)))
    yp = np.zeros((4, 8192, D), f)
    ys = np.zeros((8, 2048, D), f)
    for c in range(8):
        y = np.asarray(res.results[c]["yT"], f)
        if c < 4:
            yp[c] = y.T
        else:
            for j in range(2):
                ys[2 * (c - 4) + j] = y[:, j * 2048:(j + 1) * 2048].T
    return (yp, ys)
```

```python
import numpy as np
import concourse.bass as bass
import concourse.mybir as mybir

F32 = mybir.dt.float32
BF16 = mybir.dt.bfloat16
I32 = mybir.dt.int32
ALU = mybir.AluOpType
AF = mybir.ActivationFunctionType


class Res:
    __slots__ = ("name", "lw", "rd")

    def __init__(self, name=""):
        self.name = name
        self.lw = None
        self.rd = {}


class Eng:
    def __init__(self, name, h, sem):
        self.name = name
        self.h = h
        self.sem = sem
        self.count = 0
        self.seen = {}
        self.prog = []


class Sched:
    def __init__(self, nc, stack):
        self.nc = nc
        self.stack = stack
        self.sems = {}
        self.engs = {}
        for name, h in (("pe", nc.tensor), ("dve", nc.vector), ("act", nc.scalar),
                        ("pool", nc.gpsimd), ("sp", nc.sync)):
            sem = stack.enter_context(nc.semaphore("sem_" + name))
            self.sems["e:" + name] = sem
            self.engs[name] = Eng(name, h, sem)
        self.dma_sem_val = {}
        self.n_inst = 0
        self.n_wait = 0

    def new_dma_sem(self, key):
        sem = self.stack.enter_context(self.nc.semaphore("dsem_" + key))
        self.sems["d:" + key] = sem
        self.dma_sem_val["d:" + key] = 0
        return "d:" + key

    def _wait(self, eng, deps):
        best = {}
        own = "e:" + eng.name
        for (k, v) in deps:
            if eng.name == "pe" and k == own:
                continue
            if best.get(k, 0) < v:
                best[k] = v
        for k, v in best.items():
            if eng.seen.get(k, 0) >= v:
                continue
            eng.prog.append(("w", self.sems[k], v))
            eng.seen[k] = v
            self.n_wait += 1

    def _deps(self, reads, writes):
        deps = []
        for r in reads:
            if r.lw is not None:
                deps.append(r.lw)
        for r in writes:
            if r.lw is not None:
                deps.append(r.lw)
            for k, v in r.rd.items():
                deps.append((k, v))
        return deps

    def op(self, engname, fn, reads=(), writes=()):
        eng = self.engs[engname]
        self._wait(eng, self._deps(reads, writes))
        eng.count += 1
        eng.prog.append(("i", fn, eng.sem, 1))
        ev = ("e:" + engname, eng.count)
        for r in reads:
            if r.rd.get(ev[0], 0) < ev[1]:
                r.rd[ev[0]] = ev[1]
        for r in writes:
            r.lw = ev
            r.rd = {}
        self.n_inst += 1

    def dma(self, qname, semkey, out, in_, reads=(), writes=()):
        eng = self.engs[qname]
        self._wait(eng, self._deps(reads, writes))
        eng.prog.append(("i", (lambda h, o=out, i=in_: h.dma_start(out=o, in_=i)), self.sems[semkey], 16))
        self.dma_sem_val[semkey] += 16
        v = self.dma_sem_val[semkey]
        ev = (semkey, v)
        for r in reads:
            if r.rd.get(ev[0], 0) < ev[1]:
                r.rd[ev[0]] = ev[1]
        for r in writes:
            r.lw = ev
            r.rd = {}
        self.n_inst += 1

    def wait_all(self, engname, resources):
        eng = self.engs[engname]
        deps = []
        for r in resources:
            if r.lw is not None:
                deps.append(r.lw)
        self._wait(eng, deps)

    def emit_all(self):
        nc = self.nc
        with nc.Block() as block:
            def run(eng):
                def body(h):
                    for it in eng.prog:
                        if it[0] == "w":
                            h.wait_ge(it[1], it[2])
                        else:
                            it[1](h).then_inc(it[2], it[3])
                return body
            block.tensor(run(self.engs["pe"]))
            block.vector(run(self.engs["dve"]))
            block.scalar(run(self.engs["act"]))
            block.gpsimd(run(self.engs["pool"]))
            block.sync(run(self.engs["sp"]))

import os
from contextlib import ExitStack
import ml_dtypes
from concourse.bass_utils import run_bass_kernel_spmd

NT = 512
D = 2048
KT = 16
DFF = 5632
JT = 44
DI = 4096
CT = 32
NH = 16
DH = 256
EPS = 1e-6
MUL = ALU.mult
ADD = ALU.add
SUB = ALU.subtract


class Buf:
    def __init__(self, t, name, nres=0):
        self.t = t
        self.r = Res(name)
        self.rs = [Res("%s_%d" % (name, i)) for i in range(nres)]


class Stream:
    def __init__(self, k, name, width, dt, nslots, queue="sp", hold=1):
        self.hold = hold
        self.k = k
        self.n = nslots
        self.queue = queue
        self.slots = [k.sb("%s_s%d" % (name, i), [128, width], dt) for i in range(nslots)]
        k.nst = getattr(k, "nst", 0) + 1
        self.sems = [k.S.new_dma_sem("%s%d_%d" % (name, k.nst, i)) for i in range(nslots)]
        self.items = []
        self.next_load = 0
        self.next_use = 0

    def push(self, aps):
        self.items.extend(aps)

    def get(self):
        S = self.k.S
        while self.next_load < min(len(self.items), self.next_use + self.n - self.hold + 1):
            i = self.next_load
            s = i % self.n
            ap = self.items[i]
            w = ap.shape[-1]
            S.dma(self.queue, self.sems[s], self.slots[s].t[:, 0:w], ap, writes=[self.slots[s].r])
            self.next_load += 1
        b = self.slots[self.next_use % self.n]
        self.next_use += 1
        return b


class K:
    def __init__(self, nc, st, S):
        self.nc = nc
        self.st = st
        self.S = S
        self.ph = st

    def sb(self, name, shape, dt, nres=0):
        self.nsb = getattr(self, "nsb", 0) + 1
        t = self.ph.enter_context(self.nc.sbuf_tensor("sb%d_%s" % (self.nsb, name), shape, dt))
        return Buf(t, name, nres)

    def MM(self, ps, lhsT, rhs, start, stop, R, W, **kw):
        self.S.op("pe", lambda h: h.matmul(ps, lhsT=lhsT, rhs=rhs, start=start, stop=stop, **kw), R, W)

    def TR(self, ps, in_, ident, R, W):
        self.S.op("pe", lambda h: h.transpose(ps, in_, ident), R, W)

    def ACTF(self, out, in_, func, R, W, bias=None, scale=None):
        kw = {}
        if bias is not None:
            kw["bias"] = bias
        if scale is not None:
            kw["scale"] = scale
        self.S.op("act", lambda h: h.activation(out=out, in_=in_, func=func, **kw), R, W)

    def TT(self, eng, out, in0, in1, op, R, W):
        self.S.op(eng, lambda h: h.tensor_tensor(out=out, in0=in0, in1=in1, op=op), R, W)

    def TS(self, eng, out, in0, s1, s2, op0, op1, R, W):
        if s2 is None:
            self.S.op(eng, lambda h: h.tensor_scalar(out=out, in0=in0, scalar1=s1, scalar2=None, op0=op0), R, W)
        else:
            self.S.op(eng, lambda h: h.tensor_scalar(out=out, in0=in0, scalar1=s1, scalar2=s2, op0=op0, op1=op1), R, W)

    def STT(self, out, in0, scalar, in1, op0, op1, R, W):
        self.S.op("dve", lambda h: h.scalar_tensor_tensor(out=out, in0=in0, scalar=scalar, in1=in1, op0=op0, op1=op1), R, W)

    def CP(self, eng, out, in_, R, W):
        if eng == "act":
            self.S.op("act", lambda h: h.activation(out=out, in_=in_, func=AF.Copy), R, W)
        else:
            self.S.op(eng, lambda h: h.tensor_copy(out=out, in_=in_), R, W)

    def MS(self, eng, ap, val, W):
        self.S.op(eng, lambda h: h.memset(ap, val), [], W)

    def RECIP(self, out, in_, R, W):
        self.S.op("dve", lambda h: h.reciprocal(out=out, in_=in_), R, W)

    def SCAN(self, out, d0, d1, init, op0, op1, R, W):
        self.S.op("dve", lambda h: h.tensor_tensor_scan(out=out, data0=d0, data1=d1, initial=init, op0=op0, op1=op1), R, W)


def barrier(S):
    evs = [("e:" + n, e.count) for n, e in S.engs.items() if e.count > 0]
    evs += [(kk, v) for kk, v in S.dma_sem_val.items() if v > 0]
    for n, e in S.engs.items():
        S._wait(e, [ev for ev in evs if ev[0] != "e:" + n])


def rms_stats(k, x):
    S = k.S
    ps = k.bank[7]
    for kt in range(KT):
        sq = k.sq[kt % 2]
        k.ACTF(sq.t[:], x.t[:, kt, :], AF.Square, [x.rs[kt]], [sq.r])
        k.MM(ps.t[:], k.ones32.t[:], sq.t[:], kt == 0, kt == KT - 1, [sq.r, k.ones32.r], [ps.r])
    k.ACTF(k.rstd.t[:], ps.t[:], AF.Sqrt, [ps.r, k.epsc.r], [k.rstd.r], bias=k.epsc.t[:, 0:1], scale=1.0 / D)
    k.RECIP(k.rstd.t[:], k.rstd.t[:], [k.rstd.r], [k.rstd.r])


def rms_apply(k, x, gi, out):
    for kt in range(KT):
        k.STT(out.t[:, kt, :], x.t[:, kt, :], k.gvec.t[:, gi, kt:kt + 1], k.rstd.t[:], MUL, MUL,
              [x.rs[kt], k.gvec.r, k.rstd.r], [out.rs[kt]])


def ffn(k, x, gi, wi):
    rms_stats(k, x)
    rms_apply(k, x, gi, k.hn)
    W = k.W
    W.push([k.ffn_win_b[wi * JT + j] for j in range(JT)])
    W.push([k.ffn_wout_b[wi * KT + m] for m in range(KT)])
    hn = k.hn
    for j in range(JT):
        wb = W.get()
        pg = k.bank[(j % 2) * 2]
        pu = k.bank[(j % 2) * 2 + 1]
        for g, ps in ((0, pg), (1, pu)):
            for kt in range(KT):
                c0 = (g * KT + kt) * 128
                k.MM(ps.t[:], wb.t[:, c0:c0 + 128], hn.t[:, kt, :], kt == 0, kt == KT - 1, [wb.r, hn.rs[kt]], [ps.r])
        if getattr(k, "bg", None) is not None:
            k.bg()
        sg = k.sg[j % 2]
        k.ACTF(sg.t[:], pg.t[:], AF.Silu, [pg.r], [sg.r])
        k.TT("dve", k.h.t[:, j, :], sg.t[:], pu.t[:], MUL, [sg.r, pu.r], [k.h.rs[j]])
    for m in range(KT):
        wo = W.get()
        ps = k.bank[4 + m % 2]
        for kt in range(JT):
            k.MM(ps.t[:], wo.t[:, kt * 128:(kt + 1) * 128], k.h.t[:, kt, :], kt == 0, kt == JT - 1, [wo.r, k.h.rs[kt]], [ps.r])
        k.STT(x.t[:, m, :], ps.t[:], 0.5, x.t[:, m, :], MUL, ADD, [ps.r, x.rs[m]], [x.rs[m]])


def proj(k, src, wlist, nkt, consume):
    W = k.W
    W.push(wlist)
    for m in range(len(wlist)):
        w = W.get()
        ps = k.bank[4 + m % 2]
        for kt in range(nkt):
            k.MM(ps.t[:], w.t[:, kt * 128:(kt + 1) * 128], src.t[:, kt, :], kt == 0, kt == nkt - 1, [w.r, src.rs[kt]], [ps.r])
        consume(m, ps)


def cast_weights(k, pairs, queue="sp", engs=("dve", "act", "dve")):
    S = k.S
    CW = 4096
    cin = [k.sb("cin%d" % i, [128, CW], F32) for i in range(2)]
    cout = [k.sb("cout%d" % i, [128, CW], BF16) for i in range(3)]
    k.ncast = getattr(k, "ncast", 0) + 1
    sin = [S.new_dma_sem("cin%d_%d" % (k.ncast, i)) for i in range(2)]
    sout = [S.new_dma_sem("cout%d_%d" % (k.ncast, i)) for i in range(3)]
    n = 0
    for (src, dst) in pairs:
        T, _, Fw = src.shape
        for t in range(T):
            for c0 in range(0, Fw, CW):
                w = min(CW, Fw - c0)
                a = cin[n % 2]
                b = cout[n % 3]
                S.dma(queue, sin[n % 2], a.t[:, 0:w], src[t, :, c0:c0 + w], writes=[a.r])
                k.CP(engs[n % len(engs)], b.t[:, 0:w], a.t[:, 0:w], [a.r], [b.r])
                S.dma(queue, sout[n % 3], dst[t, :, c0:c0 + w], b.t[:, 0:w], reads=[b.r], writes=[k.wres])
                n += 1
                yield n


PI = float(np.pi)


class VC:
    def __init__(self, k, name):
        self.k = k
        self.r = Res(name)

    def tt(self, o, a, b, op, eng="dve"):
        self.k.TT(eng, o, a, b, op, [self.r], [self.r])

    def ts(self, o, a, s1, op0, s2=None, op1=None):
        self.k.TS("dve", o, a, s1, s2, op0, op1, [self.r], [self.r])

    def stt(self, o, a, s, b, op0, op1):
        self.k.STT(o, a, s, b, op0, op1, [self.r], [self.r])

    def act(self, o, a, func, scale=None, bias=None):
        self.k.ACTF(o, a, func, [self.r], [self.r], bias=bias, scale=scale)

    def cp(self, o, a):
        self.k.CP("dve", o, a, [self.r], [self.r])

    def ms(self, o, v):
        self.k.MS("dve", o, v, [self.r])

    def recip(self, o, a):
        self.k.RECIP(o, a, [self.r], [self.r])

    def cmul(self, or_, oi, ar, ai, br, bi, t1, t2):
        self.tt(t1, ar, br, MUL)
        self.tt(t2, ai, bi, MUL)
        self.tt(or_, t1, t2, SUB)
        self.tt(t1, ar, bi, MUL)
        self.tt(t2, ai, br, MUL)
        self.tt(oi, t1, t2, ADD)


def s5_setup(k, dr):
    S = k.S
    v = VC(k, "s5setup")
    sem = S.new_dma_sem("s5set")

    def T(name, shape, dt=F32):
        return k.sb(name, shape, dt).t

    lre = T("lre", [128, 64]); lim = T("lim", [128, 64]); lst = T("lst", [128, 64])
    bre = T("bre", [128, 64, 16]); bim = T("bim", [128, 64, 16])
    dl = T("dl", [128, 64]); a = T("a_", [128, 64]); th = T("th", [128, 64])
    mag = T("mag", [128, 64]); imag = T("imag", [128, 64])
    sn = T("sn", [128, 64]); cs = T("cs", [128, 64])
    lr = T("lr", [128, 64]); li = T("li", [128, 64]); ir = T("ir", [128, 64]); ii = T("ii", [128, 64])
    w1 = T("w1", [128, 64]); w2 = T("w2", [128, 64]); w3 = T("w3", [128, 64]); w4 = T("w4", [128, 64])
    wi32 = T("wi32", [128, 64], I32)
    cr = T("cr", [128, 64]); ci = T("ci", [128, 64])
    pwr = T("pwr", [128, 64]); pwi = T("pwi", [128, 64])
    btp = [T("btp%d" % i, [128, 64, 32]) for i in range(2)]
    tabr = T("tabr", [128, 64, 128]); tabi = T("tabi", [128, 64, 128])
    tm1 = T("tm1", [128, 64, 64]); tm2 = T("tm2", [128, 64, 64])
    bt1 = tm1[:, :, 0:16]; bt2 = tm2[:, :, 0:16]
    ppb = T("ppb", [128, 64, 2, 128], BF16)
    ident = T("ident", [128, 128])
    io_i = T("io_i", [128, 128], I32)
    io_f = T("io_f", [128, 128])
    S.op("pool", lambda h: h.iota(io_i[:], [[1, 128]], 0, -1), [], [v.r])
    v.cp(io_f[:], io_i[:])
    v.ts(ident[:], io_f[:], 0.0, ALU.is_equal)

    def sinred(out, arg):
        v.ts(w1[:], arg, 1.0 / (2 * PI), MUL)
        v.cp(wi32[:], w1[:])
        v.cp(w2[:], wi32[:])
        v.stt(w1[:], w2[:], -2 * PI, arg, MUL, ADD)
        v.ts(w2[:], w1[:], PI, ALU.is_gt)
        v.stt(w1[:], w2[:], -2 * PI, w1[:], MUL, ADD)
        v.ts(w2[:], w1[:], -PI, ALU.is_lt)
        v.stt(w1[:], w2[:], 2 * PI, w1[:], MUL, ADD)
        v.act(out, w1[:], AF.Sin)

    for d in range(2):
        for (dst, key) in ((lre, "lamre"), (lim, "lamim"), (lst, "logstep")):
            S.dma("sp", sem, dst[:], dr[key][d], writes=[v.r])
        S.dma("sp", sem, bre[:].rearrange("p a b -> p (a b)"), dr["bre"][d], writes=[v.r])
        S.dma("sp", sem, bim[:].rearrange("p a b -> p (a b)"), dr["bim"][d], writes=[v.r])
        v.ts(lre[:], lre[:], -1e-4, ALU.min)
        v.act(dl[:], lst[:], AF.Exp)
        v.tt(a[:], lre[:], dl[:], MUL)
        v.tt(th[:], lim[:], dl[:], MUL)
        v.act(mag[:], a[:], AF.Exp)
        v.act(imag[:], a[:], AF.Exp, scale=-1.0)
        sinred(sn[:], th[:])
        v.ts(w3[:], th[:], PI / 2, ADD)
        sinred(cs[:], w3[:])
        v.tt(lr[:], mag[:], cs[:], MUL)
        v.tt(li[:], mag[:], sn[:], MUL)
        v.tt(ir[:], imag[:], cs[:], MUL)
        v.tt(ii[:], imag[:], sn[:], MUL)
        v.ts(ii[:], ii[:], -1.0, MUL)
        v.tt(w1[:], lre[:], lre[:], MUL)
        v.tt(w2[:], lim[:], lim[:], MUL)
        v.tt(w1[:], w1[:], w2[:], ADD)
        v.recip(w4[:], w1[:])
        v.ts(w3[:], lr[:], -1.0, ADD)
        v.tt(w1[:], w3[:], lre[:], MUL)
        v.tt(w2[:], li[:], lim[:], MUL)
        v.tt(w1[:], w1[:], w2[:], ADD)
        v.tt(cr[:], w1[:], w4[:], MUL)
        v.tt(w1[:], li[:], lre[:], MUL)
        v.tt(w2[:], w3[:], lim[:], MUL)
        v.tt(w1[:], w1[:], w2[:], SUB)
        v.tt(ci[:], w1[:], w4[:], MUL)
        v.ms(btp[0][:], 0.0)
        v.ms(btp[1][:], 0.0)
        for hf in range(2):
            p0, p1 = hf * 64, hf * 64 + 64
            crb = cr[p0:p1, :].unsqueeze(2).to_broadcast([64, 64, 16])
            cib = ci[p0:p1, :].unsqueeze(2).to_broadcast([64, 64, 16])
            v.tt(bt1[p0:p1], bre[p0:p1], crb, MUL)
            v.tt(bt2[p0:p1], bim[p0:p1], cib, MUL)
            v.tt(btp[0][p0:p1, :, hf * 16:hf * 16 + 16], bt1[p0:p1], bt2[p0:p1], SUB)
            v.tt(bt1[p0:p1], bim[p0:p1], crb, MUL)
            v.tt(bt2[p0:p1], bre[p0:p1], cib, MUL)
            v.tt(btp[1][p0:p1, :, hf * 16:hf * 16 + 16], bt1[p0:p1], bt2[p0:p1], ADD)
        for ri in range(2):
            for ct in range(KT):
                ps = k.bank[ct % 2]
                k.TR(ps.t[:, 0:128], btp[ri][:, 4 * ct:4 * ct + 4, :].rearrange("p a b -> p (a b)"), ident[:], [v.r], [ps.r])
                k.CP("act", k.BL.t[:, d, ri, ct, :], ps.t[:, 0:128], [ps.r], [k.BL.r])
        for tbl, (br_, bi_) in enumerate(((ir, ii), (lr, li))):
            rev = (d == 1)

            def sl(lo, hi):
                return slice(128 - hi, 128 - lo) if rev else slice(lo, hi)
            v.ms(tabr[:, :, sl(0, 1)], 1.0)
            v.ms(tabi[:, :, sl(0, 1)], 0.0)
            v.cp(pwr[:], br_[:])
            v.cp(pwi[:], bi_[:])
            n = 1
            while n < 128:
                pr_b = pwr[:].unsqueeze(2).to_broadcast([128, 64, n])
                pi_b = pwi[:].unsqueeze(2).to_broadcast([128, 64, n])
                src, dst = sl(0, n), sl(n, 2 * n)
                v.tt(tm1[:, :, 0:n], tabr[:, :, src], pr_b, MUL)
                v.tt(tm2[:, :, 0:n], tabi[:, :, src], pi_b, MUL)
                v.tt(tabr[:, :, dst], tm1[:, :, 0:n], tm2[:, :, 0:n], SUB)
                v.tt(tm1[:, :, 0:n], tabr[:, :, src], pi_b, MUL)
                v.tt(tm2[:, :, 0:n], tabi[:, :, src], pr_b, MUL)
                v.tt(tabi[:, :, dst], tm1[:, :, 0:n], tm2[:, :, 0:n], ADD)
                v.tt(w1[:], pwr[:], pwr[:], MUL)
                v.tt(w2[:], pwi[:], pwi[:], MUL)
                v.tt(w3[:], pwr[:], pwi[:], MUL)
                v.tt(pwr[:], w1[:], w2[:], SUB)
                v.ts(pwi[:], w3[:], 2.0, MUL)
                n *= 2
            for ct in range(KT):
                S.dma("sp", sem, dr["TAB"][d, ct][:, :, 2 * tbl, :], tabr[:, 4 * ct:4 * ct + 4, :], reads=[v.r])
                S.dma("sp", sem, dr["TAB"][d, ct][:, :, 2 * tbl + 1, :], tabi[:, 4 * ct:4 * ct + 4, :], reads=[v.r])
            if tbl == 1:
                v.cp(k.L128.t[:, d, 0, :], pwr[:])
                v.cp(k.L128.t[:, d, 1, :], pwi[:])
                v.cmul(k.L127.t[:, d, 0, :], k.L127.t[:, d, 1, :], pwr[:], pwi[:], ir[:], ii[:], w1[:], w2[:])
                lr_b = lr[:].unsqueeze(2).to_broadcast([128, 64, 64])
                li_b = li[:].unsqueeze(2).to_broadcast([128, 64, 64])
                for hj in range(2):
                    js = slice(hj * 64, hj * 64 + 64)
                    v.tt(tm1[:], tabr[:, :, js], lr_b, MUL)
                    v.tt(tm2[:], tabi[:, :, js], li_b, MUL)
                    v.tt(ppb[:, :, 0, js], tm1[:], tm2[:], SUB)
                    v.tt(tm1[:], tabr[:, :, js], li_b, MUL)
                    v.tt(tm2[:], tabi[:, :, js], lr_b, MUL)
                    v.tt(ppb[:, :, 1, js], tm1[:], tm2[:], ADD)
                for ct in range(KT):
                    S.dma("sp", sem, dr["PPL"][d, ct], ppb[:, 4 * ct:4 * ct + 4, :, :], reads=[v.r])


def pass_b(k, dr, TILES):
    S = k.S
    NCH = TILES * 4
    u32 = [k.sb("u32_%d" % i, [128, KT, NT], F32, nres=KT) for i in range(1)]
    ub = k.sb("ub", [128, KT, NT], BF16, nres=KT)
    s5d = k.sb("s5d", [128, KT], F32)
    tabs = Stream(k, "TABS", 2048, F32, 4, hold=2)
    nb = 3
    tq = [[k.sb("tq%d_%d" % (b, i), [128, NT], F32) for i in range(4)] for b in range(nb)]
    gq = [[k.sb("gq%d_%d" % (b, i), [128, NT], F32, nres=4) for i in range(2)] for b in range(nb)]
    xq = [[k.sb("xq%d_%d" % (b, i), [128, NT], BF16) for i in range(4)] for b in range(nb)]
    gend = [k.sb("gend%d" % i, [128, 2, 64, 4, 2], F32) for i in range(2)]
    yst = [k.sb("yst%d" % i, [128, NT], F32) for i in range(2)]
    sem_u = S.new_dma_sem("pbu")
    sem_s = S.new_dma_sem("pbs")
    sem_y = [S.new_dma_sem("pby%d" % i) for i in range(2)]
    sem_g = [S.new_dma_sem("pbg%d" % i) for i in range(2)]
    S.dma("sp", sem_s, s5d.t[:], dr["s5d"][:, :], writes=[s5d.r])
    CL = k.CL

    def v3(t):
        return t[:].rearrange("p (c j) -> p c j", c=4)

    items = []
    for i in range(TILES):
        for ct in range(KT):
            for d in range(2):
                for q in range(4):
                    items.append(dict(i=i, ct=ct, d=d, q=q))
    NI = len(items)

    def tile_start(i):
        t0 = i * NT
        u = u32[0]
        S.dma("pool", sem_u, u.t[:], dr["U"].rearrange("(c p) t -> p c t", p=128)[:, :, t0:t0 + NT], writes=u.rs)
        for kt in range(KT):
            k.CP("act", ub.t[:, kt, :], u.t[:, kt, :], [u.rs[kt]], [ub.rs[kt]])
        tabs.push([dr["TAB"][d, ct].rearrange("p a b c -> p (a b c)") for ct in range(KT) for d in range(2)])

    cur_tb = {}

    def st0(n):
        it = items[n]
        i, ct, d, q = it["i"], it["ct"], it["d"], it["q"]
        if q == 0:
            cur_tb["tb"] = tabs.get()
        it["tb"] = cur_tb["tb"]
        b = n % nb
        par, pai = k.bank[2 * b], k.bank[2 * b + 1]
        rows = slice(32 * q, 32 * q + 32)
        k.MM(par.t[:], k.BL.t[rows, d, 0, ct, :], ub.t[rows, ct, :], True, True, [k.BL.r, ub.rs[ct]], [par.r], tile_position=(32 * q, 0))
        k.MM(pai.t[:], k.BL.t[rows, d, 1, ct, :], ub.t[rows, ct, :], True, True, [k.BL.r, ub.rs[ct]], [pai.r], tile_position=(32 * q, 0))

    def tbb(it, w):
        tb4 = it["tb"].t[:].rearrange("p (q w j) -> p q w j", q=4, w=4)
        return tb4[:, it["q"], w, :].unsqueeze(1).to_broadcast([128, 4, 128])

    def st1(n):
        it = items[n]
        b = n % nb
        tb = it["tb"]
        par, pai = k.bank[2 * b], k.bank[2 * b + 1]
        t1, t2, t3, t4 = tq[b]
        k.TT("dve", v3(t1.t), v3(par.t), tbb(it, 0), MUL, [par.r, tb.r], [t1.r])
        k.TT("dve", v3(t2.t), v3(pai.t), tbb(it, 1), MUL, [pai.r, tb.r], [t2.r])
        k.TT("dve", v3(t3.t), v3(par.t), tbb(it, 1), MUL, [par.r, tb.r], [t3.r])
        k.TT("dve", v3(t4.t), v3(pai.t), tbb(it, 0), MUL, [pai.r, tb.r], [t4.r])

    def st2(n):
        it = items[n]
        d = it["d"]
        b = n % nb
        t1, t2, t3, t4 = tq[b]
        gr, gi = gq[b]
        for c in range(4):
            def cs_(t):
                vv = v3(t.t)[:, c, :]
                return vv[:, ::-1] if d == 1 else vv
            k.SCAN(cs_(gr), cs_(t1), cs_(t2), 0.0, ADD, SUB, [t1.r, t2.r], [gr.rs[c]])
            k.SCAN(cs_(gi), cs_(t3), cs_(t4), 0.0, ADD, ADD, [t3.r, t4.r], [gi.rs[c]])

    def st3(n):
        it = items[n]
        i, ct, d, q = it["i"], it["ct"], it["d"], it["q"]
        b = n % nb
        tb = it["tb"]
        gr, gi = gq[b]
        ge = gend[i % 2]
        pr = 4 * ct + q
        e = 127 if d == 0 else 0
        k.CP("pool", ge.t[:, d, pr, :, 0], v3(gr.t)[:, :, e], gr.rs, [ge.r])
        k.CP("pool", ge.t[:, d, pr, :, 1], v3(gi.t)[:, :, e], gi.rs, [ge.r])
        x1, x2, x3, x4 = xq[b]
        k.TT("pool", v3(x1.t), v3(gr.t), tbb(it, 2), MUL, gr.rs + [tb.r], [x1.r])
        k.TT("pool", v3(x2.t), v3(gi.t), tbb(it, 3), MUL, gi.rs + [tb.r], [x2.r])
        k.TT("pool", v3(x3.t), v3(gr.t), tbb(it, 3), MUL, gr.rs + [tb.r], [x3.r])
        k.TT("pool", v3(x4.t), v3(gi.t), tbb(it, 2), MUL, gi.rs + [tb.r], [x4.r])

    def st4(n):
        it = items[n]
        i, ct, d, q = it["i"], it["ct"], it["d"], it["q"]
        b = n % nb
        pr = 4 * ct + q
        t0 = i * NT
        yb = k.bank[6 + ct % 2]
        rows = slice(32 * q, 32 * q + 32)
        x1, x2, x3, x4 = xq[b]
        yo = yb.t[rows, :]
        first = (d == 0)
        kw = dict(skip_group_check=True, tile_position=(0, 32 * q))
        k.MM(yo, CL.t[:, d, pr, 0, :], x1.t[:], first, False, [CL.r, x1.r], [yb.r], **kw)
        k.MM(yo, CL.t[:, d, pr, 1, :], x2.t[:], False, False, [CL.r, x2.r], [yb.r], **kw)
        k.MM(yo, CL.t[:, d, pr, 2, :], x3.t[:], False, False, [CL.r, x3.r], [yb.r], **kw)
        k.MM(yo, CL.t[:, d, pr, 2, :], x4.t[:], False, d == 1, [CL.r, x4.r], [yb.r], **kw)
        if d == 1 and q == 3:
            u = u32[0]
            ys = yst[ct % 2]
            k.STT(ys.t[:], u.t[:, ct, :], s5d.t[:, ct:ct + 1], yb.t[:], MUL, ADD, [u.rs[ct], s5d.r, yb.r], [ys.r])
            S.dma("sp", sem_y[ct % 2], dr["YL"][ct * 128:(ct + 1) * 128, t0:t0 + NT], ys.t[:], reads=[ys.r])
            if ct == KT - 1:
                ge = gend[i % 2]
                S.dma("sp", sem_g[i % 2], dr["GE"][i], ge.t[:].rearrange("p a b c e -> p (a b c e)"), reads=[ge.r])

    per = KT * 8
    for i in range(TILES):
        tile_start(i)
        lo, hi = i * per, (i + 1) * per
        for n in range(lo - 2, hi + 1):
            if lo <= n + 2 < hi:
                st0(n + 2)
            if lo <= n + 1 < hi:
                st1(n + 1)
            if lo <= n < hi:
                st2(n)
                st3(n)
            if lo <= n - 1 < hi:
                st4(n - 1)


def s5_chain(k, dr, TILES, SEGT):
    S = k.S
    NCH = TILES * 4
    v = VC(k, "chain")
    sem = S.new_dma_sem("chain")
    gE = k.sb("gE", [128, 64, NCH, 2], F32).t
    Sr = k.sb("Sr", [128, 64, NCH], F32).t
    Si = k.sb("Si", [128, 64, NCH], F32).t
    Hin = k.sb("Hin", [128, 64, NCH, 2], F32).t
    t1 = k.sb("ct1", [128, 64, NCH], F32).t
    t2 = k.sb("ct2", [128, 64, NCH], F32).t
    for d in range(2):
        for i in range(TILES):
            S.dma("sp", sem, gE[:, :, 4 * i:4 * i + 4, :],
                  dr["GE"][i].rearrange("p (a b c e) -> p a b c e", a=2, b=64, c=4)[:, d], reads=[], writes=[v.r])
        Lr = k.L127.t[:, d, 0, :].unsqueeze(2).to_broadcast([128, 64, NCH])
        Li = k.L127.t[:, d, 1, :].unsqueeze(2).to_broadcast([128, 64, NCH])
        v.tt(t1[:], gE[:, :, :, 0], Lr, MUL)
        v.tt(t2[:], gE[:, :, :, 1], Li, MUL)
        v.tt(Sr[:], t1[:], t2[:], SUB)
        v.tt(t1[:], gE[:, :, :, 0], Li, MUL)
        v.tt(t2[:], gE[:, :, :, 1], Lr, MUL)
        v.tt(Si[:], t1[:], t2[:], ADD)
        ar = k.L128.t[:, d, 0, :]
        ai = k.L128.t[:, d, 1, :]
        order = list(range(NCH)) if d == 0 else list(range(NCH - 1, -1, -1))
        a1 = t1[:, :, 0]
        a2 = t2[:, :, 0]
        for n, kk in enumerate(order):
            if n == 0:
                v.ms(Hin[:, :, kk, :], 0.0)
            if n == NCH - 1:
                break
            nxt = order[n + 1]
            pr_, pi_ = Hin[:, :, kk, 0], Hin[:, :, kk, 1]
            v.tt(a1, ar, pr_, MUL)
            v.tt(a2, ai, pi_, MUL)
            v.tt(a1, a1, a2, SUB)
            v.tt(Hin[:, :, nxt, 0], a1, Sr[:, :, kk], ADD)
            v.tt(a1, ar, pi_, MUL)
            v.tt(a2, ai, pr_, MUL)
            v.tt(a1, a1, a2, ADD)
            v.tt(Hin[:, :, nxt, 1], a1, Si[:, :, kk], ADD)
            bnd = (nxt % (4 * SEGT) == 0) if d == 0 else ((nxt + 1) % (4 * SEGT) == 0)
            if bnd:
                v.ts(Hin[:, :, nxt, :], Hin[:, :, nxt, :], k.keep.t[:, 0:1], MUL)
        for i in range(TILES):
            S.dma("sp", sem, dr["HIN"][i].rearrange("p (a b c e) -> p a b c e", a=2, b=64, c=4)[:, d],
                  Hin[:, :, 4 * i:4 * i + 4, :], reads=[v.r], writes=[v.r])


def s5_carry_tile(k, dr, i, yact):
    S = k.S
    t0 = i * NT
    c5 = k.c5
    hin = c5["hin"][i % 2]
    S.dma("pool", c5["sem_h"][i % 2], hin.t[:].rearrange("p a b c e -> p (a b c e)"), dr["HIN"][i], writes=[hin.r])
    c5["pp"].push([dr["PPL"][d, ct].rearrange("p a b c -> p (a b c)") for ct in range(KT) for d in range(2)])
    c5["cs"].push([dr["CRI"][ct] for ct in range(KT)])

    def build_ch(ct):
        cs = c5["cs"].get()
        c4v = cs.t[:].rearrange("p (d q w c) -> p d q w c", d=2, q=4, w=2)
        chb = c5["chb"][ct % 2]
        for d in range(2):
            Crb = c4v[:, d, :, 0, :].unsqueeze(2).to_broadcast([128, 4, 4, 32])
            Cib = c4v[:, d, :, 1, :].unsqueeze(2).to_broadcast([128, 4, 4, 32])
            Hr = hin.t[:, d, 4 * ct:4 * ct + 4, :, 0].unsqueeze(3).to_broadcast([128, 4, 4, 32])
            Hi = hin.t[:, d, 4 * ct:4 * ct + 4, :, 1].unsqueeze(3).to_broadcast([128, 4, 4, 32])
            A, B = c5["ta"][d], c5["tb"][d]
            k.TT("dve", A.t[:], Crb, Hr, MUL, [cs.r, hin.r], [A.r])
            k.TT("dve", B.t[:], Cib, Hi, MUL, [cs.r, hin.r], [B.r])
            k.TT("dve", chb.t[:, d, :, :, 0, :], A.t[:], B.t[:], SUB, [A.r, B.r], [chb.r])
            k.TT("dve", A.t[:], Crb, Hi, MUL, [cs.r, hin.r], [A.r])
            k.TT("dve", B.t[:], Cib, Hr, MUL, [cs.r, hin.r], [B.r])
            k.STT(chb.t[:, d, :, :, 1, :], A.t[:], -1.0, B.t[:], MUL, SUB, [A.r, B.r], [chb.r])
        return chb

    chn = build_ch(0)
    for ct in range(KT):
        chb = chn
        yb = k.bank[6 + ct % 2]
        pps = [c5["pp"].get(), c5["pp"].get()]
        yl = c5["yl"][ct % 2]
        S.dma("pool", c5["sem_yl"][ct % 2], yl.t[:], dr["YL"][ct * 128:(ct + 1) * 128, t0:t0 + NT], writes=[yl.r])
        for q in range(4):
            n = 0
            for c4 in range(4):
                for d in range(2):
                    pv = pps[d].t[:].rearrange("p (q r j) -> p q r j", q=4, r=2)
                    for ri in range(2):
                        k.MM(yb.t[32 * q:32 * q + 32, c4 * 128:(c4 + 1) * 128], chb.t[:, d, q, c4, ri, :], pv[:, q, ri, :],
                             n == 0, n == 15, [chb.r, pps[d].r], [yb.r], skip_group_check=True, tile_position=(0, 32 * q))
                        n += 1
        if ct + 1 < KT:
            chn = build_ch(ct + 1)
        k.TT("dve", yl.t[:], yl.t[:], yb.t[:], ADD, [yl.r, yb.r], [yl.r])
        if "YA" in dr:
            S.dma("pool", c5["sem_yl"][ct % 2], dr["YA"][ct * 128:(ct + 1) * 128, t0:t0 + NT], yl.t[:], reads=[yl.r])
        g2 = c5["g2"][ct % 2]
        k.ACTF(g2.t[:], yl.t[:], AF.Square, [yl.r], [g2.r])
        k.TS("dve", g2.t[:], g2.t[:], 0.044715, 1.0, MUL, ADD, [g2.r], [g2.r])
        k.TT("dve", g2.t[:], g2.t[:], yl.t[:], MUL, [g2.r, yl.r], [g2.r])
        k.ACTF(g2.t[:], g2.t[:], AF.Sigmoid, [g2.r], [g2.r], scale=1.5957691216057308)
        k.TT("dve", yact.t[:, ct, :], g2.t[:], yl.t[:], MUL, [g2.r, yl.r], [yact.rs[ct]])


def pass_d(k, dr, TILES, SEGT):
    S = k.S
    NTOK = TILES * NT
    ws = Stream(k, "WD", 1024, BF16, 3)
    xmh = [k.sb("xmh%d" % i, [128, CT, NT + 4], BF16) for i in range(2)]
    sem_x = [S.new_dma_sem("xmh%d" % i) for i in range(2)]
    xcb = [k.sb("xcb%d" % i, [128, NT], BF16) for i in range(2)]
    qst = [k.sb("qst%d" % i, [128, NT], BF16) for i in range(2)]
    kst = [k.sb("kst%d" % i, [128, NT], BF16) for i in range(2)]
    vst = [k.sb("vst%d" % i, [128, NT], BF16) for i in range(2)]
    kts = [k.sb("kts%d" % i, [128, 4, 128], BF16) for i in range(2)]
    vts = [k.sb("vts%d" % i, [128, 4, 128], BF16) for i in range(2)]
    gst = [k.sb("gst%d" % i, [128, 4, 64], F32) for i in range(2)]
    sems = {n: [S.new_dma_sem("pd%s%d" % (n, i)) for i in range(2)] for n in ("xc", "q", "k", "kt", "vt", "g", "w")}
    wgf = k.sb("wgf", [128, 3 * CT * 64], F32)
    wg = k.sb("wg", [128, 3, CT, 64], BF16)
    bgf = k.sb("bgf", [128, 64], F32)
    bgb = k.sb("bgb", [128, 64], BF16)
    onesb = k.sb("onesb", [128, 128], BF16)
    zerob = k.sb("zerob", [128, 256], BF16)
    k.MS("dve", zerob.t[:], 0.0, [zerob.r])
    k.MS("dve", bgf.t[:], 0.0, [bgf.r])
    S.dma("sp", sems["w"][0], wgf.t[:], dr["wg"][:, :], writes=[wgf.r])
    k.CP("dve", wg.t[:].rearrange("p a b c -> p (a b c)"), wgf.t[:], [wgf.r], [wg.r])
    S.dma("sp", sems["w"][1], bgf.t[0:1, :], dr["bgate"][:, :], writes=[bgf.r])
    k.CP("dve", bgb.t[:], bgf.t[:], [bgf.r], [bgb.r])
    k.MS("dve", onesb.t[:], 1.0, [onesb.r])
    XMv = dr["XM"].rearrange("(c p) t -> p c t", p=128)
    n = 0
    for i in range(TILES):
        t0 = i * NT
        xb = xmh[i % 2]
        lo = max(t0 - 2, 0)
        hi = min(t0 + NT + 2, NTOK)
        S.dma("pool", sem_x[i % 2], xb.t[:, :, lo - (t0 - 2):hi - (t0 - 2)], XMv[:, :, lo:hi], writes=[xb.r])
        if i == 0:
            k.MS("pool", xb.t[:, :, 0:2], 0.0, [xb.r])
        elif i % SEGT == 0:
            k.TS("pool", xb.t[:, :, 0:2], xb.t[:, :, 0:2], k.keep.t[:, 0:1], None, MUL, None, [xb.r, k.keep.r], [xb.r])
        if i == TILES - 1:
            k.MS("pool", xb.t[:, :, NT + 2:NT + 4], 0.0, [xb.r])
        elif (i + 1) % SEGT == 0:
            k.TS("pool", xb.t[:, :, NT + 2:NT + 4], xb.t[:, :, NT + 2:NT + 4], k.keep.t[:, 0:1], None, MUL, None, [xb.r, k.keep.r], [xb.r])
        ws.push([dr["mlD_b"][ct] for ct in range(CT)])
        gps = k.bank[7]
        k.MM(gps.t[:, 0:256], zerob.t[:, 0:128], zerob.t[:], True, False, [zerob.r], [gps.r], skip_group_check=True)
        for ct in range(CT):
            w = ws.get()
            b = n % 2
            n += 1
            pc = k.bank[b]
            for tau in range(5):
                k.MM(pc.t[:], w.t[:, tau * 128:(tau + 1) * 128], xb.t[:, ct, tau:tau + NT], tau == 0, tau == 4, [w.r, xb.r], [pc.r])
            xc = xcb[b]
            k.ACTF(xc.t[:], pc.t[:], AF.Silu, [pc.r, k.mlvec.r], [xc.r], bias=k.mlvec.t[:, ct:ct + 1])
            S.dma("sp", sems["xc"][b], dr["XC"][ct * 128:(ct + 1) * 128, t0:t0 + NT], xc.t[:], reads=[xc.r])
            pq, pk, pv = k.bank[2], k.bank[3], k.bank[4]
            k.MM(pq.t[:], w.t[:, 640:768], xc.t[:], True, True, [w.r, xc.r], [pq.r])
            k.MM(pk.t[:], w.t[:, 768:896], xc.t[:], True, True, [w.r, xc.r], [pk.r])
            k.MM(pv.t[:], w.t[:, 896:1024], xb.t[:, ct, 2:NT + 2], True, True, [w.r, xb.r], [pv.r])
            q_, k_, v_ = qst[b], kst[b], vst[b]
            k.CP("dve", q_.t[:], pq.t[:], [pq.r], [q_.r])
            k.CP("act", k_.t[:], pk.t[:], [pk.r], [k_.r])
            k.CP("dve", v_.t[:], pv.t[:], [pv.r], [v_.r])
            S.dma("sp", sems["q"][b], dr["QT"][ct * 128:(ct + 1) * 128, t0:t0 + NT], q_.t[:], reads=[q_.r])
            S.dma("sp", sems["k"][b], dr["KT"][ct * 128:(ct + 1) * 128, t0:t0 + NT], k_.t[:], reads=[k_.r])
            pkt, pvt = k.bank[5], k.bank[6]
            for c4 in range(4):
                k.MM(pkt.t[:, c4 * 128:(c4 + 1) * 128], xc.t[:, c4 * 128:(c4 + 1) * 128], w.t[:, 768:896], True, True, [w.r, xc.r], [pkt.r])
                k.MM(pvt.t[:, c4 * 128:(c4 + 1) * 128], xb.t[:, ct, 2 + c4 * 128:2 + (c4 + 1) * 128], w.t[:, 896:1024], True, True, [w.r, xb.r], [pvt.r])
            kt_, vt_ = kts[b], vts[b]
            k.CP("act", kt_.t[:].rearrange("p a b -> p (a b)"), pkt.t[:], [pkt.r], [kt_.r])
            k.CP("dve", vt_.t[:].rearrange("p a b -> p (a b)"), pvt.t[:], [pvt.r], [vt_.r])
            S.dma("sp", sems["kt"][b], dr["KTOK"][4 * i:4 * i + 4, :, ct * 128:(ct + 1) * 128].rearrange("c t h -> t c h"), kt_.t[:], reads=[kt_.r])
            S.dma("sp", sems["vt"][b], dr["VTOK"][4 * i:4 * i + 4, :, ct * 128:(ct + 1) * 128].rearrange("c t h -> t c h"), vt_.t[:], reads=[vt_.r])
            for c4 in range(4):
                cs_ = slice(c4 * 128, (c4 + 1) * 128)
                for j, src in enumerate((q_, k_, v_)):
                    first = False
                    k.MM(gps.t[:, c4 * 64:(c4 + 1) * 64], src.t[:, cs_], wg.t[:, j, ct, :], first, False, [src.r, wg.r], [gps.r], skip_group_check=True)
        for c4 in range(4):
            k.MM(gps.t[:, c4 * 64:(c4 + 1) * 64], onesb.t[:], bgb.t[:], False, c4 == 3, [onesb.r, bgb.r], [gps.r], skip_group_check=True)
        g_ = gst[i % 2]
        k.CP("dve", g_.t[:].rearrange("p a b -> p (a b)"), gps.t[:, 0:256], [gps.r], [g_.r])
        S.dma("sp", sems["g"][i % 2], dr["G"][4 * i:4 * i + 4].rearrange("c t g -> t c g"), g_.t[:], reads=[g_.r])


def ml_prep(k, dr, TILES, SEGT):
    S = k.S
    NCH = TILES * 4
    v = VC(k, "mlprep")
    sem = S.new_dma_sem("mlprep")
    gall = k.sb("gall", [128, NCH, 64], F32).t
    lf = k.sb("lfall", [128, NCH, 2, 16], F32).t
    bc = k.sb("bcum", [128, NCH, 2, 16], F32).t
    io_i = k.sb("mio_i", [128, 128], I32).t
    io_f = k.sb("mio_f", [128, 128], F32).t
    lnsc = k.sb("lnsc", [128, 1], F32).t
    onec = k.sb("onec", [128, 1], F32).t
    S.dma("sp", sem, gall[:], dr["G"].rearrange("c t g -> t c g"), writes=[v.r])
    S.op("pool", lambda h: h.iota(io_i[:], [[1, 128]], 0, -1), [], [v.r])
    v.cp(io_f[:], io_i[:])
    v.ts(k.TRI[0].t[:], io_f[:], 0.0, ALU.is_ge)
    v.ts(k.TRI[1].t[:], io_f[:], 0.0, ALU.is_le)
    v.ts(k.ident32.t[:], io_f[:], 0.0, ALU.is_equal)
    v.ms(lnsc[:], -0.5 * float(np.log(DH)))
    v.ms(onec[:], 1.0)
    g5 = gall[:].rearrange("p c (d w h) -> p c d w h", d=2, w=2)
    v.act(lf[:], g5[:, :, :, 1, :], AF.Exp, scale=-1.0)
    v.act(lf[:], lf[:], AF.Ln, bias=onec[:, 0:1])
    v.ts(lf[:], lf[:], -1.0, MUL)
    half = NCH // 2 if NCH >= 2 else 1
    for d in range(2):
        for (lhs, dst) in ((k.TRI[d], bc), (k.ones32, k.EG.t)):
            for c0 in range(0, NCH, 32):
                c1 = min(c0 + 32, NCH)
                ps = k.bank[(c0 // 32) % 2]
                k.MM(ps.t[:, 0:(c1 - c0) * 16].rearrange("p (c h) -> p c h", h=16), lhs.t[:], lf[:, c0:c1, d, :], True, True, [v.r], [ps.r])
                k.CP("dve", dst[:, c0:c1, d, :], ps.t[:, 0:(c1 - c0) * 16].rearrange("p (c h) -> p c h", h=16), [ps.r], [v.r])
    v.tt(k.ED.t[:], g5[:, :, :, 0, :], bc[:], SUB)
    v.act(k.ED.t[:], k.ED.t[:], AF.Exp, bias=lnsc[:, 0:1])
    v.act(k.EB.t[:], bc[:], AF.Exp, scale=-1.0)
    v.act(k.EG.t[:], k.EG.t[:], AF.Exp)
    for kk in range(NCH):
        if (kk + 1) % (4 * SEGT) == 0 and kk + 1 < NCH:
            v.ts(k.EG.t[:, kk, 0, :], k.EG.t[:, kk, 0, :], k.keep.t[:, 0:1], MUL)
        if kk % (4 * SEGT) == 0 and kk > 0:
            v.ts(k.EG.t[:, kk, 1, :], k.EG.t[:, kk, 1, :], k.keep.t[:, 0:1], MUL)
    k.mlprep_res = v.r


def pass_e(k, dr, TILES):
    S = k.S
    NCH = TILES * 4
    PR = k.mlprep_res
    C32 = k.sb("C32", [128, NH, 2, 257], F32, nres=NH)
    Cb = k.sb("Cb", [128, NH, 2, 257], BF16, nres=NH)
    qT = [k.sb("qT%d" % i, [128, CT, 128], BF16) for i in range(2)]
    kT = [k.sb("kT%d" % i, [128, CT, 128], BF16) for i in range(2)]
    ktk = [k.sb("ktk%d" % i, [128, DI], BF16) for i in range(2)]
    vau = [k.sb("vau%d" % i, [128, NH, 260], BF16) for i in range(2)]
    sem_l = [[S.new_dma_sem("pe%d_%d" % (j, i)) for i in range(2)] for j in range(4)]
    hx = k.sb("hx", [128, NH, 257], F32, nres=NH)
    hbuf = k.sb("hbuf", [128, DI], F32, nres=NH)
    sem_hb = S.new_dma_sem("hbst")
    sem_hl = S.new_dma_sem("hbld")
    khat = k.sb("khat", [128, NH, DH], BF16, nres=NH)
    smT = k.sb("smT", [128, NH, 128], BF16, nres=NH)
    dn = k.sb("dn", [128, NH], F32)
    dn2 = k.sb("dn2", [128, NH], F32)
    xck = [k.sb("xck%d" % i, [128, CT // 2, 128], BF16) for i in range(1)]
    zk = [k.sb("zk%d" % i, [128, CT // 2, 128], F32) for i in range(1)]
    oab = [k.sb("oab%d" % i, [128, CT // 2, 128], BF16, nres=CT // 2) for i in range(1)]
    skb = [k.sb("skb%d" % i, [128, 128], F32) for i in range(2)]
    o1b = [k.sb("o1b%d" % i, [128, 128], F32) for i in range(2)]
    bst = k.sb("bst", [128, NH, 6], F32, nres=NH)
    mv = k.sb("mv", [128, NH, 2], F32)
    rs = k.sb("rs", [128, NH], F32)
    sem_p = [S.new_dma_sem("pep%d" % i) for i in range(4)]
    pq = [[Res("pq%d_%d" % (b, j)) for j in range(4)] for b in range(2)]
    for b in range(2):
        k.MS("pool", vau[b].t[:, :, 256:260], 1.0, [vau[b].r])
    QTv = dr["QT"].rearrange("(c p) t -> p c t", p=128)
    KTv = dr["KT"].rearrange("(c p) t -> p c t", p=128)
    XCv = dr["XC"].rearrange("(c p) t -> p c t", p=128)
    Zv = dr["Z"].rearrange("(c p) t -> p c t", p=128)
    OAv = dr["OA"].rearrange("(c p) t -> p c t", p=128)
    hx3 = hx.t
    nn = 0
    for d in (1, 0):
        for h in range(NH):
            k.MS("dve", C32.t[:, h], 0.0, [C32.rs[h]])
            k.MS("pool", Cb.t[:, h], 0.0, [Cb.rs[h]])
        order = list(range(NCH)) if d == 0 else list(range(NCH - 1, -1, -1))
        for kk in order:
            b = nn % 2
            nn += 1
            ts_ = slice(kk * 128, (kk + 1) * 128)
            q_, k_, kt_, va = qT[b], kT[b], ktk[b], vau[b]
            S.dma("sp", sem_l[0][b], q_.t[:], QTv[:, :, ts_], writes=[q_.r])
            S.dma("sp", sem_l[1][b], k_.t[:], KTv[:, :, ts_], writes=[k_.r])
            S.dma("pool", sem_l[2][b], kt_.t[:], dr["KTOK"][kk], writes=[kt_.r])
            S.dma("pool", sem_l[3][b], va.t[:, :, 0:256], dr["VTOK"][kk].rearrange("t (h e) -> t h e", h=NH), writes=[va.r])
            k.MS("pool", va.t[:, :, 256:260], 1.0, [va.r])
            if d == 0:
                S.dma("pool", sem_hl, hbuf.t[:], dr["HB"][kk], reads=[k.hbres], writes=hbuf.rs)
            for g4 in range(NH // 4):
                pS = k.bank[g4 % 2]
                for h in range(4 * g4, 4 * g4 + 4):
                    cols = slice((h % 4) * 128, (h % 4 + 1) * 128)
                    for i2 in range(2):
                        k.MM(pS.t[:, cols], k_.t[:, 2 * h + i2, :], q_.t[:, 2 * h + i2, :], i2 == 0, i2 == 1, [k_.r, q_.r], [pS.r], skip_group_check=True)
                for h in range(4 * g4, 4 * g4 + 4):
                    cols = slice((h % 4) * 128, (h % 4 + 1) * 128)
                    ed = k.ED.t[:, kk, d, h:h + 1]
                    k.STT(smT.t[:, h, :], pS.t[:, cols], ed, k.TRI[d].t[:], MUL, MUL, [pS.r, PR], [smT.rs[h]])
                    hs = slice(h * DH, (h + 1) * DH)
                    k.TS("dve", khat.t[:, h, :], kt_.t[:, hs], ed, None, MUL, None, [kt_.r, PR], [khat.rs[h]])
                    eg = k.EG.t[:, kk, d, h:h + 1]
                    k.ACTF(C32.t[:, h], C32.t[:, h], AF.Identity, [C32.rs[h], PR], [C32.rs[h]], scale=eg)
            for h in range(NH):
                pX = k.bank[2 + h % 2]
                for i2 in range(2):
                    k.MM(pX.t[:, 0:257], q_.t[:, 2 * h + i2, :], Cb.t[:, h, i2, :], i2 == 0, False, [q_.r, Cb.rs[h]], [pX.r])
                k.MM(pX.t[:, 0:257], smT.t[:, h, :], va.t[:, h, 0:257], False, True, [smT.rs[h], va.r], [pX.r])
                k.CP("act", hx3[:, h, :], pX.t[:, 0:257], [pX.r], [hx.rs[h]])
            for h in range(NH):
                pC = [k.bank[4 + 2 * (h % 2)], k.bank[5 + 2 * (h % 2)]]
                eg = k.EG.t[:, kk, d, h:h + 1]
                for i2 in range(2):
                    k.MM(pC[i2].t[:, 0:257], khat.t[:, h, i2 * 128:(i2 + 1) * 128], va.t[:, h, 0:257], True, True, [khat.rs[h], va.r], [pC[i2].r])
                    k.STT(C32.t[:, h, i2, :], pC[i2].t[:, 0:257], eg, C32.t[:, h, i2, :], MUL, ADD, [pC[i2].r, PR, C32.rs[h]], [C32.rs[h]])
                k.CP("act", Cb.t[:, h], C32.t[:, h], [C32.rs[h]], [Cb.rs[h]])
            ycol = hx3[:, :, 256]
            k.TT("dve", dn.t[:], ycol, k.EB.t[:, kk, d, :], ALU.max, hx.rs + [PR], [dn.r])
            k.STT(dn2.t[:], ycol, -1.0, dn.t[:], MUL, ALU.max, hx.rs + [dn.r], [dn2.r])
            k.RECIP(dn2.t[:], dn2.t[:], [dn2.r], [dn2.r])
            hb3 = hbuf.t[:].rearrange("p (h e) -> p h e", h=NH)
            rb = dn2.t[:].unsqueeze(2).to_broadcast([128, NH, DH])
            if d == 1:
                k.TT("dve", hb3, hx3[:, :, 0:256], rb, MUL, hx.rs + [dn2.r], hbuf.rs)
                S.dma("sp", sem_hb, dr["HB"][kk], hbuf.t[:], reads=hbuf.rs, writes=[k.hbres])
                continue
            k.TT("dve", hx3[:, :, 0:256], hx3[:, :, 0:256], rb, MUL, hx.rs + [dn2.r], hx.rs)
            k.TT("pool", hb3, hb3, hx3[:, :, 0:256], ADD, hbuf.rs + hx.rs, hbuf.rs)
            for h in range(NH):
                hs = slice(h * DH, (h + 1) * DH)
                S.op("dve", lambda hh, h=h, hs=hs: hh.bn_stats(out=bst.t[:, h, :], in_=hbuf.t[:, hs]), [hbuf.rs[h]], [bst.rs[h]])
            for h in range(NH):
                S.op("dve", lambda hh, h=h: hh.bn_aggr(out=mv.t[:, h, :], in_=bst.t[:, h, :]), [bst.rs[h]], [mv.r])
            k.ACTF(rs.t[:], mv.t[:, :, 1], AF.Sqrt, [mv.r, k.epsc.r], [rs.r], bias=k.epsc.t[:, 0:1])
            k.RECIP(rs.t[:], rs.t[:], [rs.r], [rs.r])
            for h in range(NH):
                hs = slice(h * DH, (h + 1) * DH)
                k.TS("dve", hbuf.t[:, hs], hbuf.t[:, hs], mv.t[:, h, 0:1], rs.t[:, h:h + 1], SUB, MUL, [hbuf.rs[h], mv.r, rs.r], [hbuf.rs[h]])
            for hf in range(2):
                c0 = hf * (CT // 2)
                xc_, z_, oa_ = xck[0], zk[0], oab[0]
                S.dma("pool", sem_p[hf], xc_.t[:], XCv[:, c0:c0 + CT // 2, ts_], writes=[xc_.r])
                S.dma("pool", sem_p[2], z_.t[:], Zv[:, c0:c0 + CT // 2, ts_], writes=[z_.r])
                k.ACTF(z_.t[:], z_.t[:], AF.Sigmoid, [z_.r], [z_.r])
                for g4 in range(CT // 8):
                    pT = k.bank[g4 % 2]
                    for cl in range(4 * g4, 4 * g4 + 4):
                        ct = c0 + cl
                        cols = slice((cl % 4) * 128, (cl % 4 + 1) * 128)
                        k.TR(pT.t[:, cols], hbuf.t[:, ct * 128:(ct + 1) * 128], k.ident32.t[:], [hbuf.rs[ct // 2], PR], [pT.r])
                    for cl in range(4 * g4, 4 * g4 + 4):
                        ct = c0 + cl
                        cols = slice((cl % 4) * 128, (cl % 4 + 1) * 128)
                        sk = skb[ct % 2]
                        o1 = o1b[ct % 2]
                        k.ACTF(sk.t[:], xc_.t[:, cl, :], AF.Identity, [xc_.r, k.mlvec.r], [sk.r], scale=k.mlvec.t[:, 64 + ct:65 + ct])
                        k.STT(o1.t[:], pT.t[:, cols], k.mlvec.t[:, 32 + ct:33 + ct], sk.t[:], MUL, ADD, [pT.r, sk.r, k.mlvec.r], [o1.r])
                        k.TT("pool", oa_.t[:, cl, :], o1.t[:], z_.t[:, cl, :], MUL, [o1.r, z_.r], [oa_.rs[cl]])
                S.dma("sp", sem_p[3], OAv[:, c0:c0 + CT // 2, ts_], oa_.t[:], reads=oa_.rs)


def build(TILES, SEGT, debug=(), stop_after="F"):
    NTOK = TILES * NT
    NCH = TILES * 4
    nc = bass.Bass("TRN2", target_bir_lowering=False)
    dbg = set(debug)

    def din(name, shape, dt=F32):
        return nc.dram_tensor(name, list(shape), dt, kind="ExternalInput").ap()

    def dscr(name, shape, dt):
        kind = "ExternalOutput" if name in dbg else "Internal"
        return nc.dram_tensor(name, list(shape), dt, kind=kind).ap()

    xT = din("xT", [D, NTOK])
    keep_d = din("keep", [128, 1])
    gvec_d = din("gvec", [128, 7 * KT])
    wsrc = {
        "ffn_win": din("ffn_win", [4 * JT, 128, 2 * KT * 128]),
        "ffn_wout": din("ffn_wout", [4 * KT, 128, JT * 128]),
        "s5_win": din("s5_win", [KT, 128, KT * 128]),
        "wglu": din("wglu", [2 * KT, 128, KT * 128]),
        "ml_win": din("ml_win", [2 * CT, 128, KT * 128]),
        "mlD": din("mlD", [CT, 128, 1024]),
        "ml_wout": din("ml_wout", [KT, 128, CT * 128]),
    }
    dr = {}
    for kk_ in ("lamre", "lamim", "logstep"):
        dr[kk_] = din(kk_, [2, 128, 64])
    dr["bre"] = din("bre", [2, 128, 64 * 16])
    dr["bim"] = din("bim", [2, 128, 64 * 16])
    dr["CRI"] = din("CRI", [KT, 128, 2 * 4 * 2 * 32])
    dr["s5d"] = din("s5d", [128, KT])
    dr["wg"] = din("wg", [128, 3 * CT * 64])
    dr["bgate"] = din("bgate", [1, 64])
    mlvec_d = din("mlvec", [128, 96])
    yT = nc.dram_tensor("yT", [D, NTOK], F32, kind="ExternalOutput").ap()

    wb = {n: dscr(n + "_b", a.shape, BF16) for n, a in wsrc.items()}
    X1 = dscr("X1", [D, NTOK], F32)
    dr["U"] = dscr("U", [D, NTOK], F32)
    dr["YL"] = dscr("YL", [D, NTOK], F32)
    dr["GE"] = dscr("GE", [TILES, 128, 2 * 64 * 4 * 2], F32)
    dr["HIN"] = dscr("HIN", [TILES, 128, 2 * 64 * 4 * 2], F32)
    dr["TAB"] = dscr("TAB", [2, KT, 128, 4, 4, 128], F32)
    dr["PPL"] = dscr("PPL", [2, KT, 128, 4, 2, 128], BF16)
    X2 = dscr("X2", [D, NTOK], F32)
    if "YA" in dbg:
        dr["YA"] = dscr("YA", [D, NTOK], F32)
    X4 = dscr("X4", [D, NTOK], F32)
    XM = dscr("XM", [DI, NTOK], BF16)
    Z = dscr("Z", [DI, NTOK], F32)
    U = dr["U"]
    dr["XM"] = XM
    dr["Z"] = Z
    dr["XC"] = dscr("XC", [DI, NTOK], BF16)
    dr["QT"] = dscr("QT", [DI, NTOK], BF16)
    dr["KT"] = dscr("KT", [DI, NTOK], BF16)
    dr["KTOK"] = dscr("KTOK", [NCH, 128, DI], BF16)
    dr["VTOK"] = dscr("VTOK", [NCH, 128, DI], BF16)
    dr["G"] = dscr("G", [NCH, 128, 64], F32)
    dr["HB"] = dscr("HB", [NCH, 128, DI], F32)
    dr["OA"] = dscr("OA", [DI, NTOK], BF16)
    X5 = dscr("X5", [D, NTOK], F32)

    def fm(ap):
        return ap.rearrange("(c p) t -> p c t", p=128)

    with ExitStack() as st:
        S = Sched(nc, st)
        k = K(nc, st, S)
        k.wres = Res("wres")
        k.bank = []
        for i in range(8):
            t = st.enter_context(nc.psum_tensor("bank%d" % i, [128, 512], F32))
            k.bank.append(Buf(t, "bank%d" % i))
        k.ffn_win_b = wb["ffn_win"]
        k.ffn_wout_b = wb["ffn_wout"]
        io_sem = [S.new_dma_sem("io%d" % i) for i in range(4)]
        st_sem = [S.new_dma_sem("st%d" % i) for i in range(6)]
        k.ones32 = k.sb("ones32", [128, 128], F32)
        k.epsc = k.sb("epsc", [128, 1], F32)
        k.gvec = k.sb("gvec", [128, 7, KT], F32)
        k.keep = k.sb("keepc", [128, 1], F32)
        k.MS("dve", k.ones32.t[:], 1.0, [k.ones32.r])
        k.MS("dve", k.epsc.t[:], EPS, [k.epsc.r])
        S.dma("sp", io_sem[0], k.gvec.t[:].rearrange("p a b -> p (a b)"), gvec_d[:, :], writes=[k.gvec.r])
        S.dma("sp", io_sem[0], k.keep.t[:], keep_d[:, :], writes=[k.keep.r])
        k.mlvec = k.sb("mlvec", [128, 96], F32)
        S.dma("sp", io_sem[0], k.mlvec.t[:], mlvec_d[:, :], writes=[k.mlvec.r])
        k.hbres = Res("hbres")
        dr["mlD_b"] = wb["mlD"]

        def phase():
            ph = ExitStack()
            k.ph = ph
            return ph

        def common_bufs():
            k.W = Stream(k, "W", JT * 128, BF16, 3)
            k.hn = k.sb("hn", [128, KT, NT], BF16, nres=KT)
            k.h = k.sb("h", [128, JT, NT], BF16, nres=JT)
            k.sq = [k.sb("sq%d" % i, [128, NT], F32) for i in range(2)]
            k.sg = [k.sb("sg%d" % i, [128, NT], F32) for i in range(2)]
            k.rstd = k.sb("rstd", [128, NT], F32)

        with phase():
            for _ in cast_weights(k, [(wsrc["ffn_win"][0:JT], wb["ffn_win"][0:JT]), (wsrc["ffn_wout"][0:KT], wb["ffn_wout"][0:KT]),
                                      (wsrc["s5_win"], wb["s5_win"])]):
                pass
            barrier(S)

        with phase():
            common_bufs()
            x = k.sb("xa", [128, KT, NT], F32, nres=KT)
            ust = [k.sb("ust%d" % i, [128, NT], F32) for i in range(2)]
            rest = [(wsrc["ffn_win"][JT:4 * JT], wb["ffn_win"][JT:4 * JT]), (wsrc["ffn_wout"][KT:4 * KT], wb["ffn_wout"][KT:4 * KT])]
            rest += [(wsrc[n], wb[n]) for n in ("wglu", "ml_win", "mlD", "ml_wout")]
            cgen = cast_weights(k, rest, queue="pool")
            nrest = sum(a.shape[0] * ((a.shape[2] + 4095) // 4096) for a, _ in rest)
            per_tile = -(-nrest // TILES)
            state = {"left": 0}

            def bg():
                if state["left"] > 0:
                    state["left"] -= 1
                    next(cgen, None)
            k.bg = bg
            for i in range(TILES):
                t0 = i * NT
                state["left"] = per_tile
                S.dma("pool", io_sem[0], x.t[:], fm(xT)[:, :, t0:t0 + NT], writes=x.rs)
                ffn(k, x, 0, 0)
                while state["left"] > 0:
                    bg()
                S.dma("pool", st_sem[0], fm(X1)[:, :, t0:t0 + NT], x.t[:], reads=x.rs)
                rms_stats(k, x)
                rms_apply(k, x, 1, k.hn)

                def cons(m, ps, t0=t0):
                    b = ust[m % 2]
                    k.CP("act", b.t[:], ps.t[:], [ps.r], [b.r])
                    S.dma("pool", st_sem[1 + m % 2], U[m * 128:(m + 1) * 128, t0:t0 + NT], b.t[:], reads=[b.r])
                proj(k, k.hn, [wb["s5_win"][m] for m in range(KT)], KT, cons)
            k.bg = None
            for _ in cgen:
                pass
            barrier(S)
        if stop_after == "A":
            S.emit_all()
            return nc, S

        s5scope = ExitStack()
        k.ph = s5scope
        k.BL = k.sb("BL", [128, 2, 2, KT, 128], BF16)
        k.CL = k.sb("CL", [128, 2, 64, 3, 32], BF16)
        k.L128 = k.sb("L128", [128, 2, 2, 64], F32)
        k.L127 = k.sb("L127", [128, 2, 2, 64], F32)
        with phase():
            s5_setup(k, dr)
            tmpc = k.sb("tmpc", [128, 2, 4, 2, 32], F32)
            for ct in range(KT):
                S.dma("sp", io_sem[1], tmpc.t[:].rearrange("p a b c e -> p (a b c e)"), dr["CRI"][ct], writes=[tmpc.r])
                k.CP("dve", k.CL.t[:, :, 4 * ct:4 * ct + 4, 0, :], tmpc.t[:, :, :, 0, :], [tmpc.r], [k.CL.r])
                k.TS("dve", k.CL.t[:, :, 4 * ct:4 * ct + 4, 1, :], tmpc.t[:, :, :, 0, :], -1.0, None, MUL, None, [tmpc.r], [k.CL.r])
                k.TS("dve", k.CL.t[:, :, 4 * ct:4 * ct + 4, 2, :], tmpc.t[:, :, :, 1, :], -1.0, None, MUL, None, [tmpc.r], [k.CL.r])
            barrier(S)
        with phase():
            pass_b(k, dr, TILES)
            barrier(S)
        with phase():
            s5_chain(k, dr, TILES, SEGT)
            barrier(S)
        s5scope.close()
        if stop_after == "B":
            S.emit_all()
            return nc, S

        with phase():
            common_bufs()
            x = k.sb("xc_", [128, KT, NT], F32, nres=KT)
            k.c5 = {
                "hin": [k.sb("hin%d" % i, [128, 2, 64, 4, 2], F32) for i in range(2)],
                "sem_h": [S.new_dma_sem("hin%d" % i) for i in range(2)],
                "pp": Stream(k, "PP", 4 * 2 * 128, BF16, 4, hold=2),
                "cs": Stream(k, "CS", 512, F32, 3, hold=2),
                "chb": [k.sb("chb%d" % i, [128, 2, 4, 4, 2, 32], BF16) for i in range(2)],
                "ta": [k.sb("cta%d" % i, [128, 4, 4, 32], F32) for i in range(2)],
                "tb": [k.sb("ctb%d" % i, [128, 4, 4, 32], F32) for i in range(2)],
                "yl": [k.sb("yl%d" % i, [128, NT], F32) for i in range(2)],
                "g2": [k.sb("g2_%d" % i, [128, NT], F32) for i in range(2)],
                "sem_yl": [S.new_dma_sem("yl%d" % i) for i in range(2)],
            }
            gsb = [k.sb("gsb%d" % i, [128, NT], F32) for i in range(2)]
            zst = [k.sb("zst%d" % i, [128, NT], F32) for i in range(2)]
            mst = [k.sb("mst%d" % i, [128, NT], BF16) for i in range(2)]
            for i in range(TILES):
                t0 = i * NT
                S.dma("pool", io_sem[0], x.t[:], fm(X1)[:, :, t0:t0 + NT], writes=x.rs)
                s5_carry_tile(k, dr, i, k.hn)
                held = {}

                def cons_glu(idx, ps):
                    m = idx // 2
                    if idx % 2 == 0:
                        held["v"] = ps
                        return
                    pv = held["v"]
                    g = gsb[m % 2]
                    k.ACTF(g.t[:], ps.t[:], AF.Sigmoid, [ps.r], [g.r])
                    k.TT("dve", g.t[:], g.t[:], pv.t[:], MUL, [g.r, pv.r], [g.r])
                    k.TT("dve", x.t[:, m, :], x.t[:, m, :], g.t[:], ADD, [g.r, x.rs[m]], [x.rs[m]])
                wl = []
                for m in range(KT):
                    wl += [wb["wglu"][m], wb["wglu"][KT + m]]
                proj(k, k.hn, wl, KT, cons_glu)
                if "X2" in dbg:
                    S.dma("pool", st_sem[3], fm(X2)[:, :, t0:t0 + NT], x.t[:], reads=x.rs)
                ffn(k, x, 2, 1)
                ffn(k, x, 3, 2)
                S.dma("pool", st_sem[0], fm(X4)[:, :, t0:t0 + NT], x.t[:], reads=x.rs)
                rms_stats(k, x)
                rms_apply(k, x, 4, k.hn)

                def cons_ml(m, ps, t0=t0):
                    if m < CT:
                        b = mst[m % 2]
                        k.CP("act", b.t[:], ps.t[:], [ps.r], [b.r])
                        S.dma("pool", st_sem[1 + m % 2], XM[m * 128:(m + 1) * 128, t0:t0 + NT], b.t[:], reads=[b.r])
                    else:
                        b = zst[m % 2]
                        k.CP("act", b.t[:], ps.t[:], [ps.r], [b.r])
                        S.dma("pool", st_sem[4 + m % 2], Z[(m - CT) * 128:(m - CT + 1) * 128, t0:t0 + NT], b.t[:], reads=[b.r])
                proj(k, k.hn, [wb["ml_win"][m] for m in range(2 * CT)], KT, cons_ml)
            barrier(S)
        if stop_after == "C":
            S.emit_all()
            return nc, S

        with phase():
            pass_d(k, dr, TILES, SEGT)
            barrier(S)
        if stop_after == "D":
            S.emit_all()
            return nc, S
        mlscope = ExitStack()
        k.ph = mlscope
        k.ED = k.sb("ED", [128, NCH, 2, 16], F32)
        k.EB = k.sb("EB", [128, NCH, 2, 16], F32)
        k.EG = k.sb("EG", [128, NCH, 2, 16], F32)
        k.TRI = [k.sb("TRI%d" % i, [128, 128], F32) for i in range(2)]
        k.ident32 = k.sb("ident32", [128, 128], F32)
        with phase():
            ml_prep(k, dr, TILES, SEGT)
            barrier(S)
        with phase():
            pass_e(k, dr, TILES)
            barrier(S)
        mlscope.close()
        if stop_after == "E":
            S.emit_all()
            return nc, S
        with phase():
            common_bufs()
            x = k.sb("xf_", [128, KT, NT], F32, nres=KT)
            OAv = dr["OA"].rearrange("(c p) t -> p c t", p=128)
            for i in range(TILES):
                t0 = i * NT
                S.dma("pool", io_sem[0], x.t[:], fm(X4)[:, :, t0:t0 + NT], writes=x.rs)
                S.dma("pool", io_sem[1], k.h.t[:, 0:CT, :], OAv[:, :, t0:t0 + NT], writes=k.h.rs)

                def cons_o(m, ps):
                    k.TT("dve", x.t[:, m, :], x.t[:, m, :], ps.t[:], ADD, [ps.r, x.rs[m]], [x.rs[m]])
                proj(k, k.h, [wb["ml_wout"][m] for m in range(KT)], CT, cons_o)
                if "X5" in dbg:
                    S.dma("pool", st_sem[3], fm(X5)[:, :, t0:t0 + NT], x.t[:], reads=x.rs)
                ffn(k, x, 5, 3)
                rms_stats(k, x)
                rms_apply(k, x, 6, x)
                S.dma("pool", st_sem[0], fm(yT)[:, :, t0:t0 + NT], x.t[:], reads=x.rs)
            barrier(S)
        barrier(S)
        S.emit_all()
    return nc, S


def _tile_rows(w, nk):
    K_, M_ = w.shape
    m = M_ // 128
    return np.ascontiguousarray(w.reshape(nk, 128, m, 128).transpose(2, 1, 0, 3).reshape(m, 128, nk * 128))


def prep_shared(inp):
    f = np.float32
    out = {}
    g = np.concatenate([np.asarray(inp["norm_g"], f).reshape(6, D), np.asarray(inp["final_g"], f).reshape(1, D)], 0)
    out["gvec"] = np.ascontiguousarray(g.reshape(7, KT, 128).transpose(2, 0, 1).reshape(128, 7 * KT))
    win = np.asarray(inp["ffn_w_in"], f).reshape(4, D, 2, JT, 128)
    win = win.reshape(4, KT, 128, 2, JT, 128).transpose(0, 4, 2, 3, 1, 5)
    out["ffn_win"] = np.ascontiguousarray(win.reshape(4 * JT, 128, 2 * KT * 128))
    wout = np.asarray(inp["ffn_w_out"], f).reshape(4, DFF, D)
    out["ffn_wout"] = np.concatenate([_tile_rows(wout[i], JT) for i in range(4)], 0)
    out["s5_win"] = _tile_rows(np.asarray(inp["s5_w_in"], f)[0], KT)
    out["wglu"] = _tile_rows(np.asarray(inp["s5_w_glu"], f)[0], KT)
    out["ml_win"] = _tile_rows(np.asarray(inp["ml_w_in"], f)[0], KT)

    def gp(a):
        sh = a.shape
        a = a.reshape((2, 64, 2, 64) + sh[3:])
        perm = (0, 2, 3, 1) + tuple(range(4, a.ndim))
        a = a.transpose(perm)
        return np.ascontiguousarray(a.reshape((2, 128, 64) + sh[3:]))
    out["lamre"] = gp(np.asarray(inp["s5_lambda_re"], f)[0])
    out["lamim"] = gp(np.asarray(inp["s5_lambda_im"], f)[0])
    ls = np.asarray(inp["s5_log_step"], f)[0]
    out["logstep"] = gp(np.broadcast_to(ls[:, :, None], (2, 128, 64)).copy())
    out["bre"] = gp(np.asarray(inp["s5_b_re"], f)[0]).reshape(2, 128, 64 * 16)
    out["bim"] = gp(np.asarray(inp["s5_b_im"], f)[0]).reshape(2, 128, 64 * 16)
    cri = np.zeros((KT, 128, 2, 4, 2, 32), f)
    for ri, key in enumerate(("s5_c_re", "s5_c_im")):
        c = np.asarray(inp[key], f)[0]
        c = c.reshape(2, KT, 4, 2, 16, 64)
        for g2 in range(2):
            cri[:, g2 * 64:(g2 + 1) * 64, :, :, ri, g2 * 16:(g2 + 1) * 16] = c[:, :, :, g2].transpose(1, 4, 0, 2, 3)
    out["CRI"] = np.ascontiguousarray(cri.reshape(KT, 128, 512))
    out["s5d"] = np.ascontiguousarray(np.asarray(inp["s5_d"], f)[0].reshape(KT, 128).T)
    mlD = np.zeros((CT, 128, 8, 128), f)
    cw = np.asarray(inp["ml_conv_w"], f)[0].reshape(5, CT, 128)
    ar = np.arange(128)
    for tau in range(5):
        mlD[:, ar, tau, ar] = cw[tau]
    for j, key in enumerate(("ml_wq", "ml_wk", "ml_wv")):
        w = np.asarray(inp[key], f)[0].reshape(CT, 32, 4, 4)
        for n in range(32):
            mlD[:, 4 * n:4 * n + 4, 5 + j, 4 * n:4 * n + 4] = w[:, n]
    out["mlD"] = np.ascontiguousarray(mlD.reshape(CT, 128, 1024))
    out["ml_wout"] = _tile_rows(np.asarray(inp["ml_w_out"], f)[0], CT)
    wg = np.asarray(inp["ml_w_gates"], f)[0].reshape(3, CT, 128, 64)
    out["wg"] = np.ascontiguousarray(wg.transpose(2, 0, 1, 3).reshape(128, 3 * CT * 64))
    out["bgate"] = np.ascontiguousarray(np.asarray(inp["ml_b_gates"], f)[0].reshape(1, 64))
    vecs = [np.asarray(inp[kk], f)[0].reshape(CT, 128).T for kk in ("ml_conv_b", "ml_norm_g", "ml_skip")]
    out["mlvec"] = np.ascontiguousarray(np.concatenate(vecs, 1))
    return out


_CACHE = {}


def kernel(**inputs):
    f = np.float32
    TILES, SEGT = 16, 4
    NTOK = TILES * NT
    sh = prep_shared(inputs)
    xp = np.asarray(inputs["x_prompt"], f)
    xs = np.asarray(inputs["x_sample"], f)
    in_maps = []
    for c in range(8):
        m = dict(sh)
        if c < 4:
            m["xT"] = np.ascontiguousarray(xp[c].T)
            m["keep"] = np.ones((128, 1), f)
        else:
            xt = np.zeros((D, NTOK), f)
            for j in range(2):
                xt[:, j * 2048:(j + 1) * 2048] = xs[2 * (c - 4) + j].T
            m["xT"] = xt
            m["keep"] = np.zeros((128, 1), f)
        in_maps.append(m)
    if "nc" not in _CACHE:
        _CACHE["nc"] = build(TILES, SEGT)[0]
    res = run_bass_kernel_spmd(_CACHE["nc"], in_maps, core_ids=list(range(8)))
    yp = np.zeros((4, 8192, D), f)
    ys = np.zeros((8, 2048, D), f)
    for c in range(8):
        y = np.asarray(res.results[c]["yT"], f)
        if c < 4:
            yp[c] = y.T
        else:
            for j in range(2):
                ys[2 * (c - 4) + j] = y[:, j * 2048:(j + 1) * 2048].T
    return (yp, ys)
```

```python
import numpy as np
import concourse.bass as bass
import concourse.mybir as mybir

F32 = mybir.dt.float32
BF16 = mybir.dt.bfloat16
I32 = mybir.dt.int32
ALU = mybir.AluOpType
AF = mybir.ActivationFunctionType


class Res:
    __slots__ = ("name", "lw", "rd")

    def __init__(self, name=""):
        self.name = name
        self.lw = None
        self.rd = {}


class Eng:
    def __init__(self, name, h, sem):
        self.name = name
        self.h = h
        self.sem = sem
        self.count = 0
        self.seen = {}
        self.prog = []


class Sched:
    def __init__(self, nc, stack):
        self.nc = nc
        self.stack = stack
        self.sems = {}
        self.engs = {}
        for name, h in (("pe", nc.tensor), ("dve", nc.vector), ("act", nc.scalar),
                        ("pool", nc.gpsimd), ("sp", nc.sync)):
            sem = stack.enter_context(nc.semaphore("sem_" + name))
            self.sems["e:" + name] = sem
            self.engs[name] = Eng(name, h, sem)
        self.dma_sem_val = {}
        self.n_inst = 0
        self.n_wait = 0

    def new_dma_sem(self, key):
        sem = self.stack.enter_context(self.nc.semaphore("dsem_" + key))
        self.sems["d:" + key] = sem
        self.dma_sem_val["d:" + key] = 0
        return "d:" + key

    def _wait(self, eng, deps):
        best = {}
        own = "e:" + eng.name
        for (k, v) in deps:
            if eng.name == "pe" and k == own:
                continue
            if best.get(k, 0) < v:
                best[k] = v
        for k, v in best.items():
            if eng.seen.get(k, 0) >= v:
                continue
            eng.prog.append(("w", self.sems[k], v))
            eng.seen[k] = v
            self.n_wait += 1

    def _deps(self, reads, writes):
        deps = []
        for r in reads:
            if r.lw is not None:
                deps.append(r.lw)
        for r in writes:
            if r.lw is not None:
                deps.append(r.lw)
            for k, v in r.rd.items():
                deps.append((k, v))
        return deps

    def op(self, engname, fn, reads=(), writes=()):
        eng = self.engs[engname]
        self._wait(eng, self._deps(reads, writes))
        eng.count += 1
        eng.prog.append(("i", fn, eng.sem, 1))
        ev = ("e:" + engname, eng.count)
        for r in reads:
            if r.rd.get(ev[0], 0) < ev[1]:
                r.rd[ev[0]] = ev[1]
        for r in writes:
            r.lw = ev
            r.rd = {}
        self.n_inst += 1

    def dma(self, qname, semkey, out, in_, reads=(), writes=()):
        eng = self.engs[qname]
        self._wait(eng, self._deps(reads, writes))
        eng.prog.append(("i", (lambda h, o=out, i=in_: h.dma_start(out=o, in_=i)), self.sems[semkey], 16))
        self.dma_sem_val[semkey] += 16
        v = self.dma_sem_val[semkey]
        ev = (semkey, v)
        for r in reads:
            if r.rd.get(ev[0], 0) < ev[1]:
                r.rd[ev[0]] = ev[1]
        for r in writes:
            r.lw = ev
            r.rd = {}
        self.n_inst += 1

    def wait_all(self, engname, resources):
        eng = self.engs[engname]
        deps = []
        for r in resources:
            if r.lw is not None:
                deps.append(r.lw)
        self._wait(eng, deps)

    def emit_all(self):
        nc = self.nc
        with nc.Block() as block:
            def run(eng):
                def body(h):
                    for it in eng.prog:
                        if it[0] == "w":
                            h.wait_ge(it[1], it[2])
                        else:
                            it[1](h).then_inc(it[2], it[3])
                return body
            block.tensor(run(self.engs["pe"]))
            block.vector(run(self.engs["dve"]))
            block.scalar(run(self.engs["act"]))
            block.gpsimd(run(self.engs["pool"]))
            block.sync(run(self.engs["sp"]))

import os
from contextlib import ExitStack
import ml_dtypes
from concourse.bass_utils import run_bass_kernel_spmd

NT = 512
D = 2048
KT = 16
DFF = 5632
JT = 44
DI = 4096
CT = 32
NH = 16
DH = 256
EPS = 1e-6
MUL = ALU.mult
ADD = ALU.add
SUB = ALU.subtract


class Buf:
    def __init__(self, t, name, nres=0):
        self.t = t
        self.r = Res(name)
        self.rs = [Res("%s_%d" % (name, i)) for i in range(nres)]


class Stream:
    def __init__(self, k, name, width, dt, nslots, queue="sp", hold=1):
        self.hold = hold
        self.k = k
        self.n = nslots
        self.queue = queue
        self.slots = [k.sb("%s_s%d" % (name, i), [128, width], dt) for i in range(nslots)]
        k.nst = getattr(k, "nst", 0) + 1
        self.sems = [k.S.new_dma_sem("%s%d_%d" % (name, k.nst, i)) for i in range(nslots)]
        self.items = []
        self.next_load = 0
        self.next_use = 0

    def push(self, aps):
        self.items.extend(aps)

    def get(self):
        S = self.k.S
        while self.next_load < min(len(self.items), self.next_use + self.n - self.hold + 1):
            i = self.next_load
            s = i % self.n
            ap = self.items[i]
            w = ap.shape[-1]
            S.dma(self.queue, self.sems[s], self.slots[s].t[:, 0:w], ap, writes=[self.slots[s].r])
            self.next_load += 1
        b = self.slots[self.next_use % self.n]
        self.next_use += 1
        return b


class K:
    def __init__(self, nc, st, S):
        self.nc = nc
        self.st = st
        self.S = S
        self.ph = st

    def sb(self, name, shape, dt, nres=0):
        self.nsb = getattr(self, "nsb", 0) + 1
        t = self.ph.enter_context(self.nc.sbuf_tensor("sb%d_%s" % (self.nsb, name), shape, dt))
        return Buf(t, name, nres)

    def MM(self, ps, lhsT, rhs, start, stop, R, W, **kw):
        self.S.op("pe", lambda h: h.matmul(ps, lhsT=lhsT, rhs=rhs, start=start, stop=stop, **kw), R, W)

    def TR(self, ps, in_, ident, R, W):
        self.S.op("pe", lambda h: h.transpose(ps, in_, ident), R, W)

    def ACTF(self, out, in_, func, R, W, bias=None, scale=None):
        kw = {}
        if bias is not None:
            kw["bias"] = bias
        if scale is not None:
            kw["scale"] = scale
        self.S.op("act", lambda h: h.activation(out=out, in_=in_, func=func, **kw), R, W)

    def TT(self, eng, out, in0, in1, op, R, W):
        self.S.op(eng, lambda h: h.tensor_tensor(out=out, in0=in0, in1=in1, op=op), R, W)

    def TS(self, eng, out, in0, s1, s2, op0, op1, R, W):
        if s2 is None:
            self.S.op(eng, lambda h: h.tensor_scalar(out=out, in0=in0, scalar1=s1, scalar2=None, op0=op0), R, W)
        else:
            self.S.op(eng, lambda h: h.tensor_scalar(out=out, in0=in0, scalar1=s1, scalar2=s2, op0=op0, op1=op1), R, W)

    def STT(self, out, in0, scalar, in1, op0, op1, R, W):
        self.S.op("dve", lambda h: h.scalar_tensor_tensor(out=out, in0=in0, scalar=scalar, in1=in1, op0=op0, op1=op1), R, W)

    def CP(self, eng, out, in_, R, W):
        if eng == "act":
            self.S.op("act", lambda h: h.activation(out=out, in_=in_, func=AF.Copy), R, W)
        else:
            self.S.op(eng, lambda h: h.tensor_copy(out=out, in_=in_), R, W)

    def MS(self, eng, ap, val, W):
        self.S.op(eng, lambda h: h.memset(ap, val), [], W)

    def RECIP(self, out, in_, R, W):
        self.S.op("dve", lambda h: h.reciprocal(out=out, in_=in_), R, W)

    def SCAN(self, out, d0, d1, init, op0, op1, R, W):
        self.S.op("dve", lambda h: h.tensor_tensor_scan(out=out, data0=d0, data1=d1, initial=init, op0=op0, op1=op1), R, W)


def barrier(S):
    evs = [("e:" + n, e.count) for n, e in S.engs.items() if e.count > 0]
    evs += [(kk, v) for kk, v in S.dma_sem_val.items() if v > 0]
    for n, e in S.engs.items():
        S._wait(e, [ev for ev in evs if ev[0] != "e:" + n])


def rms_stats(k, x):
    S = k.S
    ps = k.bank[7]
    for kt in range(KT):
        sq = k.sq[kt % 2]
        k.ACTF(sq.t[:], x.t[:, kt, :], AF.Square, [x.rs[kt]], [sq.r])
        k.MM(ps.t[:], k.ones32.t[:], sq.t[:], kt == 0, kt == KT - 1, [sq.r, k.ones32.r], [ps.r])
    k.ACTF(k.rstd.t[:], ps.t[:], AF.Sqrt, [ps.r, k.epsc.r], [k.rstd.r], bias=k.epsc.t[:, 0:1], scale=1.0 / D)
    k.RECIP(k.rstd.t[:], k.rstd.t[:], [k.rstd.r], [k.rstd.r])


def rms_apply(k, x, gi, out):
    for kt in range(KT):
        k.STT(out.t[:, kt, :], x.t[:, kt, :], k.gvec.t[:, gi, kt:kt + 1], k.rstd.t[:], MUL, MUL,
              [x.rs[kt], k.gvec.r, k.rstd.r], [out.rs[kt]])


def ffn(k, x, gi, wi):
    rms_stats(k, x)
    rms_apply(k, x, gi, k.hn)
    W = k.W
    W.push([k.ffn_win_b[wi * JT + j] for j in range(JT)])
    W.push([k.ffn_wout_b[wi * KT + m] for m in range(KT)])
    hn = k.hn
    for j in range(JT):
        wb = W.get()
        pg = k.bank[(j % 2) * 2]
        pu = k.bank[(j % 2) * 2 + 1]
        for g, ps in ((0, pg), (1, pu)):
            for kt in range(KT):
                c0 = (g * KT + kt) * 128
                k.MM(ps.t[:], wb.t[:, c0:c0 + 128], hn.t[:, kt, :], kt == 0, kt == KT - 1, [wb.r, hn.rs[kt]], [ps.r])
        if getattr(k, "bg", None) is not None:
            k.bg()
        sg = k.sg[j % 2]
        k.ACTF(sg.t[:], pg.t[:], AF.Silu, [pg.r], [sg.r])
        k.TT("dve", k.h.t[:, j, :], sg.t[:], pu.t[:], MUL, [sg.r, pu.r], [k.h.rs[j]])
    for m in range(KT):
        wo = W.get()
        ps = k.bank[4 + m % 2]
        for kt in range(JT):
            k.MM(ps.t[:], wo.t[:, kt * 128:(kt + 1) * 128], k.h.t[:, kt, :], kt == 0, kt == JT - 1, [wo.r, k.h.rs[kt]], [ps.r])
        k.STT(x.t[:, m, :], ps.t[:], 0.5, x.t[:, m, :], MUL, ADD, [ps.r, x.rs[m]], [x.rs[m]])


def proj(k, src, wlist, nkt, consume):
    W = k.W
    W.push(wlist)
    for m in range(len(wlist)):
        w = W.get()
        ps = k.bank[4 + m % 2]
        for kt in range(nkt):
            k.MM(ps.t[:], w.t[:, kt * 128:(kt + 1) * 128], src.t[:, kt, :], kt == 0, kt == nkt - 1, [w.r, src.rs[kt]], [ps.r])
        consume(m, ps)


def cast_weights(k, pairs, queue="sp", engs=("dve", "act", "dve")):
    S = k.S
    CW = 4096
    cin = [k.sb("cin%d" % i, [128, CW], F32) for i in range(2)]
    cout = [k.sb("cout%d" % i, [128, CW], BF16) for i in range(3)]
    k.ncast = getattr(k, "ncast", 0) + 1
    sin = [S.new_dma_sem("cin%d_%d" % (k.ncast, i)) for i in range(2)]
    sout = [S.new_dma_sem("cout%d_%d" % (k.ncast, i)) for i in range(3)]
    n = 0
    for (src, dst) in pairs:
        T, _, Fw = src.shape
        for t in range(T):
            for c0 in range(0, Fw, CW):
                w = min(CW, Fw - c0)
                a = cin[n % 2]
                b = cout[n % 3]
                S.dma(queue, sin[n % 2], a.t[:, 0:w], src[t, :, c0:c0 + w], writes=[a.r])
                k.CP(engs[n % len(engs)], b.t[:, 0:w], a.t[:, 0:w], [a.r], [b.r])
                S.dma(queue, sout[n % 3], dst[t, :, c0:c0 + w], b.t[:, 0:w], reads=[b.r], writes=[k.wres])
                n += 1
                yield n


PI = float(np.pi)


class VC:
    def __init__(self, k, name):
        self.k = k
        self.r = Res(name)

    def tt(self, o, a, b, op, eng="dve"):
        self.k.TT(eng, o, a, b, op, [self.r], [self.r])

    def ts(self, o, a, s1, op0, s2=None, op1=None):
        self.k.TS("dve", o, a, s1, s2, op0, op1, [self.r], [self.r])

    def stt(self, o, a, s, b, op0, op1):
        self.k.STT(o, a, s, b, op0, op1, [self.r], [self.r])

    def act(self, o, a, func, scale=None, bias=None):
        self.k.ACTF(o, a, func, [self.r], [self.r], bias=bias, scale=scale)

    def cp(self, o, a):
        self.k.CP("dve", o, a, [self.r], [self.r])

    def ms(self, o, v):
        self.k.MS("dve", o, v, [self.r])

    def recip(self, o, a):
        self.k.RECIP(o, a, [self.r], [self.r])

    def cmul(self, or_, oi, ar, ai, br, bi, t1, t2):
        self.tt(t1, ar, br, MUL)
        self.tt(t2, ai, bi, MUL)
        self.tt(or_, t1, t2, SUB)
        self.tt(t1, ar, bi, MUL)
        self.tt(t2, ai, br, MUL)
        self.tt(oi, t1, t2, ADD)


def s5_setup(k, dr):
    S = k.S
    v = VC(k, "s5setup")
    sem = S.new_dma_sem("s5set")

    def T(name, shape, dt=F32):
        return k.sb(name, shape, dt).t

    lre = T("lre", [128, 64]); lim = T("lim", [128, 64]); lst = T("lst", [128, 64])
    bre = T("bre", [128, 64, 16]); bim = T("bim", [128, 64, 16])
    dl = T("dl", [128, 64]); a = T("a_", [128, 64]); th = T("th", [128, 64])
    mag = T("mag", [128, 64]); imag = T("imag", [128, 64])
    sn = T("sn", [128, 64]); cs = T("cs", [128, 64])
    lr = T("lr", [128, 64]); li = T("li", [128, 64]); ir = T("ir", [128, 64]); ii = T("ii", [128, 64])
    w1 = T("w1", [128, 64]); w2 = T("w2", [128, 64]); w3 = T("w3", [128, 64]); w4 = T("w4", [128, 64])
    wi32 = T("wi32", [128, 64], I32)
    cr = T("cr", [128, 64]); ci = T("ci", [128, 64])
    pwr = T("pwr", [128, 64]); pwi = T("pwi", [128, 64])
    btp = [T("btp%d" % i, [128, 64, 32]) for i in range(2)]
    tabr = T("tabr", [128, 64, 128]); tabi = T("tabi", [128, 64, 128])
    tm1 = T("tm1", [128, 64, 64]); tm2 = T("tm2", [128, 64, 64])
    bt1 = tm1[:, :, 0:16]; bt2 = tm2[:, :, 0:16]
    ppb = T("ppb", [128, 64, 2, 128], BF16)
    ident = T("ident", [128, 128])
    io_i = T("io_i", [128, 128], I32)
    io_f = T("io_f", [128, 128])
    S.op("pool", lambda h: h.iota(io_i[:], [[1, 128]], 0, -1), [], [v.r])
    v.cp(io_f[:], io_i[:])
    v.ts(ident[:], io_f[:], 0.0, ALU.is_equal)

    def sinred(out, arg):
        v.ts(w1[:], arg, 1.0 / (2 * PI), MUL)
        v.cp(wi32[:], w1[:])
        v.cp(w2[:], wi32[:])
        v.stt(w1[:], w2[:], -2 * PI, arg, MUL, ADD)
        v.ts(w2[:], w1[:], PI, ALU.is_gt)
        v.stt(w1[:], w2[:], -2 * PI, w1[:], MUL, ADD)
        v.ts(w2[:], w1[:], -PI, ALU.is_lt)
        v.stt(w1[:], w2[:], 2 * PI, w1[:], MUL, ADD)
        v.act(out, w1[:], AF.Sin)

    for d in range(2):
        for (dst, key) in ((lre, "lamre"), (lim, "lamim"), (lst, "logstep")):
            S.dma("sp", sem, dst[:], dr[key][d], writes=[v.r])
        S.dma("sp", sem, bre[:].rearrange("p a b -> p (a b)"), dr["bre"][d], writes=[v.r])
        S.dma("sp", sem, bim[:].rearrange("p a b -> p (a b)"), dr["bim"][d], writes=[v.r])
        v.ts(lre[:], lre[:], -1e-4, ALU.min)
        v.act(dl[:], lst[:], AF.Exp)
        v.tt(a[:], lre[:], dl[:], MUL)
        v.tt(th[:], lim[:], dl[:], MUL)
        v.act(mag[:], a[:], AF.Exp)
        v.act(imag[:], a[:], AF.Exp, scale=-1.0)
        sinred(sn[:], th[:])
        v.ts(w3[:], th[:], PI / 2, ADD)
        sinred(cs[:], w3[:])
        v.tt(lr[:], mag[:], cs[:], MUL)
        v.tt(li[:], mag[:], sn[:], MUL)
        v.tt(ir[:], imag[:], cs[:], MUL)
        v.tt(ii[:], imag[:], sn[:], MUL)
        v.ts(ii[:], ii[:], -1.0, MUL)
        v.tt(w1[:], lre[:], lre[:], MUL)
        v.tt(w2[:], lim[:], lim[:], MUL)
        v.tt(w1[:], w1[:], w2[:], ADD)
        v.recip(w4[:], w1[:])
        v.ts(w3[:], lr[:], -1.0, ADD)
        v.tt(w1[:], w3[:], lre[:], MUL)
        v.tt(w2[:], li[:], lim[:], MUL)
        v.tt(w1[:], w1[:], w2[:], ADD)
        v.tt(cr[:], w1[:], w4[:], MUL)
        v.tt(w1[:], li[:], lre[:], MUL)
        v.tt(w2[:], w3[:], lim[:], MUL)
        v.tt(w1[:], w1[:], w2[:], SUB)
        v.tt(ci[:], w1[:], w4[:], MUL)
        v.ms(btp[0][:], 0.0)
        v.ms(btp[1][:], 0.0)
        for hf in range(2):
            p0, p1 = hf * 64, hf * 64 + 64
            crb = cr[p0:p1, :].unsqueeze(2).to_broadcast([64, 64, 16])
            cib = ci[p0:p1, :].unsqueeze(2).to_broadcast([64, 64, 16])
            v.tt(bt1[p0:p1], bre[p0:p1], crb, MUL)
            v.tt(bt2[p0:p1], bim[p0:p1], cib, MUL)
            v.tt(btp[0][p0:p1, :, hf * 16:hf * 16 + 16], bt1[p0:p1], bt2[p0:p1], SUB)
            v.tt(bt1[p0:p1], bim[p0:p1], crb, MUL)
            v.tt(bt2[p0:p1], bre[p0:p1], cib, MUL)
            v.tt(btp[1][p0:p1, :, hf * 16:hf * 16 + 16], bt1[p0:p1], bt2[p0:p1], ADD)
        for ri in range(2):
            for ct in range(KT):
                ps = k.bank[ct % 2]
                k.TR(ps.t[:, 0:128], btp[ri][:, 4 * ct:4 * ct + 4, :].rearrange("p a b -> p (a b)"), ident[:], [v.r], [ps.r])
                k.CP("act", k.BL.t[:, d, ri, ct, :], ps.t[:, 0:128], [ps.r], [k.BL.r])
        for tbl, (br_, bi_) in enumerate(((ir, ii), (lr, li))):
            rev = (d == 1)

            def sl(lo, hi):
                return slice(128 - hi, 128 - lo) if rev else slice(lo, hi)
            v.ms(tabr[:, :, sl(0, 1)], 1.0)
            v.ms(tabi[:, :, sl(0, 1)], 0.0)
            v.cp(pwr[:], br_[:])
            v.cp(pwi[:], bi_[:])
            n = 1
            while n < 128:
                pr_b = pwr[:].unsqueeze(2).to_broadcast([128, 64, n])
                pi_b = pwi[:].unsqueeze(2).to_broadcast([128, 64, n])
                src, dst = sl(0, n), sl(n, 2 * n)
                v.tt(tm1[:, :, 0:n], tabr[:, :, src], pr_b, MUL)
                v.tt(tm2[:, :, 0:n], tabi[:, :, src], pi_b, MUL)
                v.tt(tabr[:, :, dst], tm1[:, :, 0:n], tm2[:, :, 0:n], SUB)
                v.tt(tm1[:, :, 0:n], tabr[:, :, src], pi_b, MUL)
                v.tt(tm2[:, :, 0:n], tabi[:, :, src], pr_b, MUL)
                v.tt(tabi[:, :, dst], tm1[:, :, 0:n], tm2[:, :, 0:n], ADD)
                v.tt(w1[:], pwr[:], pwr[:], MUL)
                v.tt(w2[:], pwi[:], pwi[:], MUL)
                v.tt(w3[:], pwr[:], pwi[:], MUL)
                v.tt(pwr[:], w1[:], w2[:], SUB)
                v.ts(pwi[:], w3[:], 2.0, MUL)
                n *= 2
            for ct in range(KT):
                S.dma("sp", sem, dr["TAB"][d, ct][:, :, 2 * tbl, :], tabr[:, 4 * ct:4 * ct + 4, :], reads=[v.r])
                S.dma("sp", sem, dr["TAB"][d, ct][:, :, 2 * tbl + 1, :], tabi[:, 4 * ct:4 * ct + 4, :], reads=[v.r])
            if tbl == 1:
                v.cp(k.L128.t[:, d, 0, :], pwr[:])
                v.cp(k.L128.t[:, d, 1, :], pwi[:])
                v.cmul(k.L127.t[:, d, 0, :], k.L127.t[:, d, 1, :], pwr[:], pwi[:], ir[:], ii[:], w1[:], w2[:])
                lr_b = lr[:].unsqueeze(2).to_broadcast([128, 64, 64])
                li_b = li[:].unsqueeze(2).to_broadcast([128, 64, 64])
                for hj in range(2):
                    js = slice(hj * 64, hj * 64 + 64)
                    v.tt(tm1[:], tabr[:, :, js], lr_b, MUL)
                    v.tt(tm2[:], tabi[:, :, js], li_b, MUL)
                    v.tt(ppb[:, :, 0, js], tm1[:], tm2[:], SUB)
                    v.tt(tm1[:], tabr[:, :, js], li_b, MUL)
                    v.tt(tm2[:], tabi[:, :, js], lr_b, MUL)
                    v.tt(ppb[:, :, 1, js], tm1[:], tm2[:], ADD)
                for ct in range(KT):
                    S.dma("sp", sem, dr["PPL"][d, ct], ppb[:, 4 * ct:4 * ct + 4, :, :], reads=[v.r])


def pass_b(k, dr, TILES):
    S = k.S
    NCH = TILES * 4
    u32 = [k.sb("u32_%d" % i, [128, KT, NT], F32, nres=KT) for i in range(1)]
    ub = k.sb("ub", [128, KT, NT], BF16, nres=KT)
    s5d = k.sb("s5d", [128, KT], F32)
    tabs = Stream(k, "TABS", 2048, F32, 4, hold=2)
    nb = 3
    tq = [[k.sb("tq%d_%d" % (b, i), [128, NT], F32) for i in range(4)] for b in range(nb)]
    gq = [[k.sb("gq%d_%d" % (b, i), [128, NT], F32, nres=4) for i in range(2)] for b in range(nb)]
    xq = [[k.sb("xq%d_%d" % (b, i), [128, NT], BF16) for i in range(4)] for b in range(nb)]
    gend = [k.sb("gend%d" % i, [128, 2, 64, 4, 2], F32) for i in range(2)]
    yst = [k.sb("yst%d" % i, [128, NT], F32) for i in range(2)]
    sem_u = S.new_dma_sem("pbu")
    sem_s = S.new_dma_sem("pbs")
    sem_y = [S.new_dma_sem("pby%d" % i) for i in range(2)]
    sem_g = [S.new_dma_sem("pbg%d" % i) for i in range(2)]
    S.dma("sp", sem_s, s5d.t[:], dr["s5d"][:, :], writes=[s5d.r])
    CL = k.CL

    def v3(t):
        return t[:].rearrange("p (c j) -> p c j", c=4)

    items = []
    for i in range(TILES):
        for ct in range(KT):
            for d in range(2):
                for q in range(4):
                    items.append(dict(i=i, ct=ct, d=d, q=q))
    NI = len(items)

    def tile_start(i):
        t0 = i * NT
        u = u32[0]
        S.dma("pool", sem_u, u.t[:], dr["U"].rearrange("(c p) t -> p c t", p=128)[:, :, t0:t0 + NT], writes=u.rs)
        for kt in range(KT):
            k.CP("act", ub.t[:, kt, :], u.t[:, kt, :], [u.rs[kt]], [ub.rs[kt]])
        tabs.push([dr["TAB"][d, ct].rearrange("p a b c -> p (a b c)") for ct in range(KT) for d in range(2)])

    cur_tb = {}

    def st0(n):
        it = items[n]
        i, ct, d, q = it["i"], it["ct"], it["d"], it["q"]
        if q == 0:
            cur_tb["tb"] = tabs.get()
        it["tb"] = cur_tb["tb"]
        b = n % nb
        par, pai = k.bank[2 * b], k.bank[2 * b + 1]
        rows = slice(32 * q, 32 * q + 32)
        k.MM(par.t[:], k.BL.t[rows, d, 0, ct, :], ub.t[rows, ct, :], True, True, [k.BL.r, ub.rs[ct]], [par.r], tile_position=(32 * q, 0))
        k.MM(pai.t[:], k.BL.t[rows, d, 1, ct, :], ub.t[rows, ct, :], True, True, [k.BL.r, ub.rs[ct]], [pai.r], tile_position=(32 * q, 0))

    def tbb(it, w):
        tb4 = it["tb"].t[:].rearrange("p (q w j) -> p q w j", q=4, w=4)
        return tb4[:, it["q"], w, :].unsqueeze(1).to_broadcast([128, 4, 128])

    def st1(n):
        it = items[n]
        b = n % nb
        tb = it["tb"]
        par, pai = k.bank[2 * b], k.bank[2 * b + 1]
        t1, t2, t3, t4 = tq[b]
        k.TT("dve", v3(t1.t), v3(par.t), tbb(it, 0), MUL, [par.r, tb.r], [t1.r])
        k.TT("dve", v3(t2.t), v3(pai.t), tbb(it, 1), MUL, [pai.r, tb.r], [t2.r])
        k.TT("dve", v3(t3.t), v3(par.t), tbb(it, 1), MUL, [par.r, tb.r], [t3.r])
        k.TT("dve", v3(t4.t), v3(pai.t), tbb(it, 0), MUL, [pai.r, tb.r], [t4.r])

    def st2(n):
        it = items[n]
        d = it["d"]
        b = n % nb
        t1, t2, t3, t4 = tq[b]
        gr, gi = gq[b]
        for c in range(4):
            def cs_(t):
                vv = v3(t.t)[:, c, :]
                return vv[:, ::-1] if d == 1 else vv
            k.SCAN(cs_(gr), cs_(t1), cs_(t2), 0.0, ADD, SUB, [t1.r, t2.r], [gr.rs[c]])
            k.SCAN(cs_(gi), cs_(t3), cs_(t4), 0.0, ADD, ADD, [t3.r, t4.r], [gi.rs[c]])

    def st3(n):
        it = items[n]
        i, ct, d, q = it["i"], it["ct"], it["d"], it["q"]
        b = n % nb
        tb = it["tb"]
        gr, gi = gq[b]
        ge = gend[i % 2]
        pr = 4 * ct + q
        e = 127 if d == 0 else 0
        k.CP("pool", ge.t[:, d, pr, :, 0], v3(gr.t)[:, :, e], gr.rs, [ge.r])
        k.CP("pool", ge.t[:, d, pr, :, 1], v3(gi.t)[:, :, e], gi.rs, [ge.r])
        x1, x2, x3, x4 = xq[b]
        k.TT("pool", v3(x1.t), v3(gr.t), tbb(it, 2), MUL, gr.rs + [tb.r], [x1.r])
        k.TT("pool", v3(x2.t), v3(gi.t), tbb(it, 3), MUL, gi.rs + [tb.r], [x2.r])
        k.TT("pool", v3(x3.t), v3(gr.t), tbb(it, 3), MUL, gr.rs + [tb.r], [x3.r])
        k.TT("pool", v3(x4.t), v3(gi.t), tbb(it, 2), MUL, gi.rs + [tb.r], [x4.r])

    def st4(n):
        it = items[n]
        i, ct, d, q = it["i"], it["ct"], it["d"], it["q"]
        b = n % nb
        pr = 4 * ct + q
        t0 = i * NT
        yb = k.bank[6 + ct % 2]
        rows = slice(32 * q, 32 * q + 32)
        x1, x2, x3, x4 = xq[b]
        yo = yb.t[rows, :]
        first = (d == 0)
        kw = dict(skip_group_check=True, tile_position=(0, 32 * q))
        k.MM(yo, CL.t[:, d, pr, 0, :], x1.t[:], first, False, [CL.r, x1.r], [yb.r], **kw)
        k.MM(yo, CL.t[:, d, pr, 1, :], x2.t[:], False, False, [CL.r, x2.r], [yb.r], **kw)
        k.MM(yo, CL.t[:, d, pr, 2, :], x3.t[:], False, False, [CL.r, x3.r], [yb.r], **kw)
        k.MM(yo, CL.t[:, d, pr, 2, :], x4.t[:], False, d == 1, [CL.r, x4.r], [yb.r], **kw)
        if d == 1 and q == 3:
            u = u32[0]
            ys = yst[ct % 2]
            k.STT(ys.t[:], u.t[:, ct, :], s5d.t[:, ct:ct + 1], yb.t[:], MUL, ADD, [u.rs[ct], s5d.r, yb.r], [ys.r])
            S.dma("sp", sem_y[ct % 2], dr["YL"][ct * 128:(ct + 1) * 128, t0:t0 + NT], ys.t[:], reads=[ys.r])
            if ct == KT - 1:
                ge = gend[i % 2]
                S.dma("sp", sem_g[i % 2], dr["GE"][i], ge.t[:].rearrange("p a b c e -> p (a b c e)"), reads=[ge.r])

    per = KT * 8
    for i in range(TILES):
        tile_start(i)
        lo, hi = i * per, (i + 1) * per
        for n in range(lo - 2, hi + 1):
            if lo <= n + 2 < hi:
                st0(n + 2)
            if lo <= n + 1 < hi:
                st1(n + 1)
            if lo <= n < hi:
                st2(n)
                st3(n)
            if lo <= n - 1 < hi:
                st4(n - 1)


def s5_chain(k, dr, TILES, SEGT):
    S = k.S
    NCH = TILES * 4
    v = VC(k, "chain")
    sem = S.new_dma_sem("chain")
    gE = k.sb("gE", [128, 64, NCH, 2], F32).t
    Sr = k.sb("Sr", [128, 64, NCH], F32).t
    Si = k.sb("Si", [128, 64, NCH], F32).t
    Hin = k.sb("Hin", [128, 64, NCH, 2], F32).t
    t1 = k.sb("ct1", [128, 64, NCH], F32).t
    t2 = k.sb("ct2", [128, 64, NCH], F32).t
    for d in range(2):
        for i in range(TILES):
            S.dma("sp", sem, gE[:, :, 4 * i:4 * i + 4, :],
                  dr["GE"][i].rearrange("p (a b c e) -> p a b c e", a=2, b=64, c=4)[:, d], reads=[], writes=[v.r])
        Lr = k.L127.t[:, d, 0, :].unsqueeze(2).to_broadcast([128, 64, NCH])
        Li = k.L127.t[:, d, 1, :].unsqueeze(2).to_broadcast([128, 64, NCH])
        v.tt(t1[:], gE[:, :, :, 0], Lr, MUL)
        v.tt(t2[:], gE[:, :, :, 1], Li, MUL)
        v.tt(Sr[:], t1[:], t2[:], SUB)
        v.tt(t1[:], gE[:, :, :, 0], Li, MUL)
        v.tt(t2[:], gE[:, :, :, 1], Lr, MUL)
        v.tt(Si[:], t1[:], t2[:], ADD)
        ar = k.L128.t[:, d, 0, :]
        ai = k.L128.t[:, d, 1, :]
        order = list(range(NCH)) if d == 0 else list(range(NCH - 1, -1, -1))
        a1 = t1[:, :, 0]
        a2 = t2[:, :, 0]
        for n, kk in enumerate(order):
            if n == 0:
                v.ms(Hin[:, :, kk, :], 0.0)
            if n == NCH - 1:
                break
            nxt = order[n + 1]
            pr_, pi_ = Hin[:, :, kk, 0], Hin[:, :, kk, 1]
            v.tt(a1, ar, pr_, MUL)
            v.tt(a2, ai, pi_, MUL)
            v.tt(a1, a1, a2, SUB)
            v.tt(Hin[:, :, nxt, 0], a1, Sr[:, :, kk], ADD)
            v.tt(a1, ar, pi_, MUL)
            v.tt(a2, ai, pr_, MUL)
            v.tt(a1, a1, a2, ADD)
            v.tt(Hin[:, :, nxt, 1], a1, Si[:, :, kk], ADD)
            bnd = (nxt % (4 * SEGT) == 0) if d == 0 else ((nxt + 1) % (4 * SEGT) == 0)
            if bnd:
                v.ts(Hin[:, :, nxt, :], Hin[:, :, nxt, :], k.keep.t[:, 0:1], MUL)
        for i in range(TILES):
            S.dma("sp", sem, dr["HIN"][i].rearrange("p (a b c e) -> p a b c e", a=2, b=64, c=4)[:, d],
                  Hin[:, :, 4 * i:4 * i + 4, :], reads=[v.r], writes=[v.r])


def s5_carry_tile(k, dr, i, yact):
    S = k.S
    t0 = i * NT
    c5 = k.c5
    hin = c5["hin"][i % 2]
    S.dma("pool", c5["sem_h"][i % 2], hin.t[:].rearrange("p a b c e -> p (a b c e)"), dr["HIN"][i], writes=[hin.r])
    c5["pp"].push([dr["PPL"][d, ct].rearrange("p a b c -> p (a b c)") for ct in range(KT) for d in range(2)])
    c5["cs"].push([dr["CRI"][ct] for ct in range(KT)])

    def build_ch(ct):
        cs = c5["cs"].get()
        c4v = cs.t[:].rearrange("p (d q w c) -> p d q w c", d=2, q=4, w=2)
        chb = c5["chb"][ct % 2]
        for d in range(2):
            Crb = c4v[:, d, :, 0, :].unsqueeze(2).to_broadcast([128, 4, 4, 32])
            Cib = c4v[:, d, :, 1, :].unsqueeze(2).to_broadcast([128, 4, 4, 32])
            Hr = hin.t[:, d, 4 * ct:4 * ct + 4, :, 0].unsqueeze(3).to_broadcast([128, 4, 4, 32])
            Hi = hin.t[:, d, 4 * ct:4 * ct + 4, :, 1].unsqueeze(3).to_broadcast([128, 4, 4, 32])
            A, B = c5["ta"][d], c5["tb"][d]
            k.TT("dve", A.t[:], Crb, Hr, MUL, [cs.r, hin.r], [A.r])
            k.TT("dve", B.t[:], Cib, Hi, MUL, [cs.r, hin.r], [B.r])
            k.TT("dve", chb.t[:, d, :, :, 0, :], A.t[:], B.t[:], SUB, [A.r, B.r], [chb.r])
            k.TT("dve", A.t[:], Crb, Hi, MUL, [cs.r, hin.r], [A.r])
            k.TT("dve", B.t[:], Cib, Hr, MUL, [cs.r, hin.r], [B.r])
            k.STT(chb.t[:, d, :, :, 1, :], A.t[:], -1.0, B.t[:], MUL, SUB, [A.r, B.r], [chb.r])
        return chb

    chn = build_ch(0)
    for ct in range(KT):
        chb = chn
        yb = k.bank[6 + ct % 2]
        pps = [c5["pp"].get(), c5["pp"].get()]
        yl = c5["yl"][ct % 2]
        S.dma("pool", c5["sem_yl"][ct % 2], yl.t[:], dr["YL"][ct * 128:(ct + 1) * 128, t0:t0 + NT], writes=[yl.r])
        for q in range(4):
            n = 0
            for c4 in range(4):
                for d in range(2):
                    pv = pps[d].t[:].rearrange("p (q r j) -> p q r j", q=4, r=2)
                    for ri in range(2):
                        k.MM(yb.t[32 * q:32 * q + 32, c4 * 128:(c4 + 1) * 128], chb.t[:, d, q, c4, ri, :], pv[:, q, ri, :],
                             n == 0, n == 15, [chb.r, pps[d].r], [yb.r], skip_group_check=True, tile_position=(0, 32 * q))
                        n += 1
        if ct + 1 < KT:
            chn = build_ch(ct + 1)
        k.TT("dve", yl.t[:], yl.t[:], yb.t[:], ADD, [yl.r, yb.r], [yl.r])
        if "YA" in dr:
            S.dma("pool", c5["sem_yl"][ct % 2], dr["YA"][ct * 128:(ct + 1) * 128, t0:t0 + NT], yl.t[:], reads=[yl.r])
        g2 = c5["g2"][ct % 2]
        k.ACTF(g2.t[:], yl.t[:], AF.Square, [yl.r], [g2.r])
        k.TS("dve", g2.t[:], g2.t[:], 0.044715, 1.0, MUL, ADD, [g2.r], [g2.r])
        k.TT("dve", g2.t[:], g2.t[:], yl.t[:], MUL, [g2.r, yl.r], [g2.r])
        k.ACTF(g2.t[:], g2.t[:], AF.Sigmoid, [g2.r], [g2.r], scale=1.5957691216057308)
        k.TT("dve", yact.t[:, ct, :], g2.t[:], yl.t[:], MUL, [g2.r, yl.r], [yact.rs[ct]])


def pass_d(k, dr, TILES, SEGT):
    S = k.S
    NTOK = TILES * NT
    ws = Stream(k, "WD", 1024, BF16, 3)
    xmh = [k.sb("xmh%d" % i, [128, CT, NT + 4], BF16) for i in range(2)]
    sem_x = [S.new_dma_sem("xmh%d" % i) for i in range(2)]
    xcb = [k.sb("xcb%d" % i, [128, NT], BF16) for i in range(2)]
    qst = [k.sb("qst%d" % i, [128, NT], BF16) for i in range(2)]
    kst = [k.sb("kst%d" % i, [128, NT], BF16) for i in range(2)]
    vst = [k.sb("vst%d" % i, [128, NT], BF16) for i in range(2)]
    kts = [k.sb("kts%d" % i, [128, 4, 128], BF16) for i in range(2)]
    vts = [k.sb("vts%d" % i, [128, 4, 128], BF16) for i in range(2)]
    gst = [k.sb("gst%d" % i, [128, 4, 64], F32) for i in range(2)]
    sems = {n: [S.new_dma_sem("pd%s%d" % (n, i)) for i in range(2)] for n in ("xc", "q", "k", "kt", "vt", "g", "w")}
    wgf = k.sb("wgf", [128, 3 * CT * 64], F32)
    wg = k.sb("wg", [128, 3, CT, 64], BF16)
    bgf = k.sb("bgf", [128, 64], F32)
    bgb = k.sb("bgb", [128, 64], BF16)
    onesb = k.sb("onesb", [128, 128], BF16)
    zerob = k.sb("zerob", [128, 256], BF16)
    k.MS("dve", zerob.t[:], 0.0, [zerob.r])
    k.MS("dve", bgf.t[:], 0.0, [bgf.r])
    S.dma("sp", sems["w"][0], wgf.t[:], dr["wg"][:, :], writes=[wgf.r])
    k.CP("dve", wg.t[:].rearrange("p a b c -> p (a b c)"), wgf.t[:], [wgf.r], [wg.r])
    S.dma("sp", sems["w"][1], bgf.t[0:1, :], dr["bgate"][:, :], writes=[bgf.r])
    k.CP("dve", bgb.t[:], bgf.t[:], [bgf.r], [bgb.r])
    k.MS("dve", onesb.t[:], 1.0, [onesb.r])
    XMv = dr["XM"].rearrange("(c p) t -> p c t", p=128)
    n = 0
    for i in range(TILES):
        t0 = i * NT
        xb = xmh[i % 2]
        lo = max(t0 - 2, 0)
        hi = min(t0 + NT + 2, NTOK)
        S.dma("pool", sem_x[i % 2], xb.t[:, :, lo - (t0 - 2):hi - (t0 - 2)], XMv[:, :, lo:hi], writes=[xb.r])
        if i == 0:
            k.MS("pool", xb.t[:, :, 0:2], 0.0, [xb.r])
        elif i % SEGT == 0:
            k.TS("pool", xb.t[:, :, 0:2], xb.t[:, :, 0:2], k.keep.t[:, 0:1], None, MUL, None, [xb.r, k.keep.r], [xb.r])
        if i == TILES - 1:
            k.MS("pool", xb.t[:, :, NT + 2:NT + 4], 0.0, [xb.r])
        elif (i + 1) % SEGT == 0:
            k.TS("pool", xb.t[:, :, NT + 2:NT + 4], xb.t[:, :, NT + 2:NT + 4], k.keep.t[:, 0:1], None, MUL, None, [xb.r, k.keep.r], [xb.r])
        ws.push([dr["mlD_b"][ct] for ct in range(CT)])
        gps = k.bank[7]
        k.MM(gps.t[:, 0:256], zerob.t[:, 0:128], zerob.t[:], True, False, [zerob.r], [gps.r], skip_group_check=True)
        for ct in range(CT):
            w = ws.get()
            b = n % 2
            n += 1
            pc = k.bank[b]
            for tau in range(5):
                k.MM(pc.t[:], w.t[:, tau * 128:(tau + 1) * 128], xb.t[:, ct, tau:tau + NT], tau == 0, tau == 4, [w.r, xb.r], [pc.r])
            xc = xcb[b]
            k.ACTF(xc.t[:], pc.t[:], AF.Silu, [pc.r, k.mlvec.r], [xc.r], bias=k.mlvec.t[:, ct:ct + 1])
            S.dma("sp", sems["xc"][b], dr["XC"][ct * 128:(ct + 1) * 128, t0:t0 + NT], xc.t[:], reads=[xc.r])
            pq, pk, pv = k.bank[2], k.bank[3], k.bank[4]
            k.MM(pq.t[:], w.t[:, 640:768], xc.t[:], True, True, [w.r, xc.r], [pq.r])
            k.MM(pk.t[:], w.t[:, 768:896], xc.t[:], True, True, [w.r, xc.r], [pk.r])
            k.MM(pv.t[:], w.t[:, 896:1024], xb.t[:, ct, 2:NT + 2], True, True, [w.r, xb.r], [pv.r])
            q_, k_, v_ = qst[b], kst[b], vst[b]
            k.CP("dve", q_.t[:], pq.t[:], [pq.r], [q_.r])
            k.CP("act", k_.t[:], pk.t[:], [pk.r], [k_.r])
            k.CP("dve", v_.t[:], pv.t[:], [pv.r], [v_.r])
            S.dma("sp", sems["q"][b], dr["QT"][ct * 128:(ct + 1) * 128, t0:t0 + NT], q_.t[:], reads=[q_.r])
            S.dma("sp", sems["k"][b], dr["KT"][ct * 128:(ct + 1) * 128, t0:t0 + NT], k_.t[:], reads=[k_.r])
            pkt, pvt = k.bank[5], k.bank[6]
            for c4 in range(4):
                k.MM(pkt.t[:, c4 * 128:(c4 + 1) * 128], xc.t[:, c4 * 128:(c4 + 1) * 128], w.t[:, 768:896], True, True, [w.r, xc.r], [pkt.r])
                k.MM(pvt.t[:, c4 * 128:(c4 + 1) * 128], xb.t[:, ct, 2 + c4 * 128:2 + (c4 + 1) * 128], w.t[:, 896:1024], True, True, [w.r, xb.r], [pvt.r])
            kt_, vt_ = kts[b], vts[b]
            k.CP("act", kt_.t[:].rearrange("p a b -> p (a b)"), pkt.t[:], [pkt.r], [kt_.r])
            k.CP("dve", vt_.t[:].rearrange("p a b -> p (a b)"), pvt.t[:], [pvt.r], [vt_.r])
            S.dma("sp", sems["kt"][b], dr["KTOK"][4 * i:4 * i + 4, :, ct * 128:(ct + 1) * 128].rearrange("c t h -> t c h"), kt_.t[:], reads=[kt_.r])
            S.dma("sp", sems["vt"][b], dr["VTOK"][4 * i:4 * i + 4, :, ct * 128:(ct + 1) * 128].rearrange("c t h -> t c h"), vt_.t[:], reads=[vt_.r])
            for c4 in range(4):
                cs_ = slice(c4 * 128, (c4 + 1) * 128)
                for j, src in enumerate((q_, k_, v_)):
                    first = False
                    k.MM(gps.t[:, c4 * 64:(c4 + 1) * 64], src.t[:, cs_], wg.t[:, j, ct, :], first, False, [src.r, wg.r], [gps.r], skip_group_check=True)
        for c4 in range(4):
            k.MM(gps.t[:, c4 * 64:(c4 + 1) * 64], onesb.t[:], bgb.t[:], False, c4 == 3, [onesb.r, bgb.r], [gps.r], skip_group_check=True)
        g_ = gst[i % 2]
        k.CP("dve", g_.t[:].rearrange("p a b -> p (a b)"), gps.t[:, 0:256], [gps.r], [g_.r])
        S.dma("sp", sems["g"][i % 2], dr["G"][4 * i:4 * i + 4].rearrange("c t g -> t c g"), g_.t[:], reads=[g_.r])


def ml_prep(k, dr, TILES, SEGT):
    S = k.S
    NCH = TILES * 4
    v = VC(k, "mlprep")
    sem = S.new_dma_sem("mlprep")
    gall = k.sb("gall", [128, NCH, 64], F32).t
    lf = k.sb("lfall", [128, NCH, 2, 16], F32).t
    bc = k.sb("bcum", [128, NCH, 2, 16], F32).t
    io_i = k.sb("mio_i", [128, 128], I32).t
    io_f = k.sb("mio_f", [128, 128], F32).t
    lnsc = k.sb("lnsc", [128, 1], F32).t
    onec = k.sb("onec", [128, 1], F32).t
    S.dma("sp", sem, gall[:], dr["G"].rearrange("c t g -> t c g"), writes=[v.r])
    S.op("pool", lambda h: h.iota(io_i[:], [[1, 128]], 0, -1), [], [v.r])
    v.cp(io_f[:], io_i[:])
    v.ts(k.TRI[0].t[:], io_f[:], 0.0, ALU.is_ge)
    v.ts(k.TRI[1].t[:], io_f[:], 0.0, ALU.is_le)
    v.ts(k.ident32.t[:], io_f[:], 0.0, ALU.is_equal)
    v.ms(lnsc[:], -0.5 * float(np.log(DH)))
    v.ms(onec[:], 1.0)
    g5 = gall[:].rearrange("p c (d w h) -> p c d w h", d=2, w=2)
    v.act(lf[:], g5[:, :, :, 1, :], AF.Exp, scale=-1.0)
    v.act(lf[:], lf[:], AF.Ln, bias=onec[:, 0:1])
    v.ts(lf[:], lf[:], -1.0, MUL)
    half = NCH // 2 if NCH >= 2 else 1
    for d in range(2):
        for (lhs, dst) in ((k.TRI[d], bc), (k.ones32, k.EG.t)):
            for c0 in range(0, NCH, 32):
                c1 = min(c0 + 32, NCH)
                ps = k.bank[(c0 // 32) % 2]
                k.MM(ps.t[:, 0:(c1 - c0) * 16].rearrange("p (c h) -> p c h", h=16), lhs.t[:], lf[:, c0:c1, d, :], True, True, [v.r], [ps.r])
                k.CP("dve", dst[:, c0:c1, d, :], ps.t[:, 0:(c1 - c0) * 16].rearrange("p (c h) -> p c h", h=16), [ps.r], [v.r])
    v.tt(k.ED.t[:], g5[:, :, :, 0, :], bc[:], SUB)
    v.act(k.ED.t[:], k.ED.t[:], AF.Exp, bias=lnsc[:, 0:1])
    v.act(k.EB.t[:], bc[:], AF.Exp, scale=-1.0)
    v.act(k.EG.t[:], k.EG.t[:], AF.Exp)
    for kk in range(NCH):
        if (kk + 1) % (4 * SEGT) == 0 and kk + 1 < NCH:
            v.ts(k.EG.t[:, kk, 0, :], k.EG.t[:, kk, 0, :], k.keep.t[:, 0:1], MUL)
        if kk % (4 * SEGT) == 0 and kk > 0:
            v.ts(k.EG.t[:, kk, 1, :], k.EG.t[:, kk, 1, :], k.keep.t[:, 0:1], MUL)
    k.mlprep_res = v.r


def pass_e(k, dr, TILES):
    S = k.S
    NCH = TILES * 4
    PR = k.mlprep_res
    C32 = k.sb("C32", [128, NH, 2, 257], F32, nres=NH)
    Cb = k.sb("Cb", [128, NH, 2, 257], BF16, nres=NH)
    qT = [k.sb("qT%d" % i, [128, CT, 128], BF16) for i in range(2)]
    kT = [k.sb("kT%d" % i, [128, CT, 128], BF16) for i in range(2)]
    ktk = [k.sb("ktk%d" % i, [128, DI], BF16) for i in range(2)]
    vau = [k.sb("vau%d" % i, [128, NH, 260], BF16) for i in range(2)]
    sem_l = [[S.new_dma_sem("pe%d_%d" % (j, i)) for i in range(2)] for j in range(4)]
    hx = k.sb("hx", [128, NH, 257], F32, nres=NH)
    hbuf = k.sb("hbuf", [128, DI], F32, nres=NH)
    sem_hb = S.new_dma_sem("hbst")
    sem_hl = S.new_dma_sem("hbld")
    khat = k.sb("khat", [128, NH, DH], BF16, nres=NH)
    smT = k.sb("smT", [128, NH, 128], BF16, nres=NH)
    dn = k.sb("dn", [128, NH], F32)
    dn2 = k.sb("dn2", [128, NH], F32)
    xck = [k.sb("xck%d" % i, [128, CT // 2, 128], BF16) for i in range(1)]
    zk = [k.sb("zk%d" % i, [128, CT // 2, 128], F32) for i in range(1)]
    oab = [k.sb("oab%d" % i, [128, CT // 2, 128], BF16, nres=CT // 2) for i in range(1)]
    skb = [k.sb("skb%d" % i, [128, 128], F32) for i in range(2)]
    o1b = [k.sb("o1b%d" % i, [128, 128], F32) for i in range(2)]
    bst = k.sb("bst", [128, NH, 6], F32, nres=NH)
    mv = k.sb("mv", [128, NH, 2], F32)
    rs = k.sb("rs", [128, NH], F32)
    sem_p = [S.new_dma_sem("pep%d" % i) for i in range(4)]
    pq = [[Res("pq%d_%d" % (b, j)) for j in range(4)] for b in range(2)]
    for b in range(2):
        k.MS("pool", vau[b].t[:, :, 256:260], 1.0, [vau[b].r])
    QTv = dr["QT"].rearrange("(c p) t -> p c t", p=128)
    KTv = dr["KT"].rearrange("(c p) t -> p c t", p=128)
    XCv = dr["XC"].rearrange("(c p) t -> p c t", p=128)
    Zv = dr["Z"].rearrange("(c p) t -> p c t", p=128)
    OAv = dr["OA"].rearrange("(c p) t -> p c t", p=128)
    hx3 = hx.t
    seq = [(d, kk) for d in (1, 0) for kk in (list(range(NCH)) if d == 0 else list(range(NCH - 1, -1, -1)))]

    def emit_loads(n):
        d, kk = seq[n]
        b = n % 2
        ts_ = slice(kk * 128, (kk + 1) * 128)
        q_, k_, kt_, va = qT[b], kT[b], ktk[b], vau[b]
        S.dma("sp", sem_l[0][b], q_.t[:], QTv[:, :, ts_], writes=[q_.r])
        S.dma("sp", sem_l[1][b], k_.t[:], KTv[:, :, ts_], writes=[k_.r])
        S.dma("pool", sem_l[2][b], kt_.t[:], dr["KTOK"][kk], writes=[kt_.r])
        S.dma("pool", sem_l[3][b], va.t[:, :, 0:256], dr["VTOK"][kk].rearrange("t (h e) -> t h e", h=NH), writes=[va.r])
        k.MS("pool", va.t[:, :, 256:260], 1.0, [va.r])

    def stage1(n):
        d, kk = seq[n]
        b = n % 2
        q_, k_, kt_ = qT[b], kT[b], ktk[b]
        for g4 in range(NH // 4):
            pS = k.bank[g4 % 2]
            for h in range(4 * g4, 4 * g4 + 4):
                cols = slice((h % 4) * 128, (h % 4 + 1) * 128)
                for i2 in range(2):
                    k.MM(pS.t[:, cols], k_.t[:, 2 * h + i2, :], q_.t[:, 2 * h + i2, :], i2 == 0, i2 == 1, [k_.r, q_.r], [pS.r], skip_group_check=True)
            for h in range(4 * g4, 4 * g4 + 4):
                cols = slice((h % 4) * 128, (h % 4 + 1) * 128)
                ed = k.ED.t[:, kk, d, h:h + 1]
                k.STT(smT.t[:, h, :], pS.t[:, cols], ed, k.TRI[d].t[:], MUL, MUL, [pS.r, PR], [smT.rs[h]])
                hs = slice(h * DH, (h + 1) * DH)
                k.TS("dve", khat.t[:, h, :], kt_.t[:, hs], ed, None, MUL, None, [kt_.r, PR], [khat.rs[h]])

    emit_loads(0)
    for n in range(len(seq)):
        d, kk = seq[n]
        first_of_dir = (n == 0 or seq[n - 1][0] != d)
        if first_of_dir:
            for h in range(NH):
                k.MS("dve", C32.t[:, h], 0.0, [C32.rs[h]])
                k.MS("pool", Cb.t[:, h], 0.0, [Cb.rs[h]])
        if True:
            b = n % 2
            ts_ = slice(kk * 128, (kk + 1) * 128)
            q_, k_, kt_, va = qT[b], kT[b], ktk[b], vau[b]
            if n + 1 < len(seq):
                emit_loads(n + 1)
            if d == 0:
                S.dma("pool", sem_hl, hbuf.t[:], dr["HB"][kk], reads=[k.hbres], writes=hbuf.rs)
            if n == 0:
                stage1(0)
            for h in range(NH):
                eg = k.EG.t[:, kk, d, h:h + 1]
                k.ACTF(C32.t[:, h], C32.t[:, h], AF.Identity, [C32.rs[h], PR], [C32.rs[h]], scale=eg)
            for h in range(NH):
                pX = k.bank[2 + h % 2]
                for i2 in range(2):
                    k.MM(pX.t[:, 0:257], q_.t[:, 2 * h + i2, :], Cb.t[:, h, i2, :], i2 == 0, False, [q_.r, Cb.rs[h]], [pX.r])
                k.MM(pX.t[:, 0:257], smT.t[:, h, :], va.t[:, h, 0:257], False, True, [smT.rs[h], va.r], [pX.r])
                k.CP("act", hx3[:, h, :], pX.t[:, 0:257], [pX.r], [hx.rs[h]])
            for h in range(NH):
                pC = [k.bank[4 + 2 * (h % 2)], k.bank[5 + 2 * (h % 2)]]
                eg = k.EG.t[:, kk, d, h:h + 1]
                for i2 in range(2):
                    k.MM(pC[i2].t[:, 0:257], khat.t[:, h, i2 * 128:(i2 + 1) * 128], va.t[:, h, 0:257], True, True, [khat.rs[h], va.r], [pC[i2].r])
                    k.STT(C32.t[:, h, i2, :], pC[i2].t[:, 0:257], eg, C32.t[:, h, i2, :], MUL, ADD, [pC[i2].r, PR, C32.rs[h]], [C32.rs[h]])
                k.CP("act", Cb.t[:, h], C32.t[:, h], [C32.rs[h]], [Cb.rs[h]])
            if n + 1 < len(seq):
                stage1(n + 1)
            ycol = hx3[:, :, 256]
            k.TT("dve", dn.t[:], ycol, k.EB.t[:, kk, d, :], ALU.max, hx.rs + [PR], [dn.r])
            k.STT(dn2.t[:], ycol, -1.0, dn.t[:], MUL, ALU.max, hx.rs + [dn.r], [dn2.r])
            k.RECIP(dn2.t[:], dn2.t[:], [dn2.r], [dn2.r])
            hb3 = hbuf.t[:].rearrange("p (h e) -> p h e", h=NH)
            rb = dn2.t[:].unsqueeze(2).to_broadcast([128, NH, DH])
            if d == 1:
                k.TT("dve", hb3, hx3[:, :, 0:256], rb, MUL, hx.rs + [dn2.r], hbuf.rs)
                S.dma("sp", sem_hb, dr["HB"][kk], hbuf.t[:], reads=hbuf.rs, writes=[k.hbres])
                continue
            k.TT("dve", hx3[:, :, 0:256], hx3[:, :, 0:256], rb, MUL, hx.rs + [dn2.r], hx.rs)
            k.TT("pool", hb3, hb3, hx3[:, :, 0:256], ADD, hbuf.rs + hx.rs, hbuf.rs)
            for h in range(NH):
                hs = slice(h * DH, (h + 1) * DH)
                S.op("dve", lambda hh, h=h, hs=hs: hh.bn_stats(out=bst.t[:, h, :], in_=hbuf.t[:, hs]), [hbuf.rs[h]], [bst.rs[h]])
            for h in range(NH):
                S.op("dve", lambda hh, h=h: hh.bn_aggr(out=mv.t[:, h, :], in_=bst.t[:, h, :]), [bst.rs[h]], [mv.r])
            k.ACTF(rs.t[:], mv.t[:, :, 1], AF.Sqrt, [mv.r, k.epsc.r], [rs.r], bias=k.epsc.t[:, 0:1])
            k.RECIP(rs.t[:], rs.t[:], [rs.r], [rs.r])
            for h in range(NH):
                hs = slice(h * DH, (h + 1) * DH)
                k.TS("dve", hbuf.t[:, hs], hbuf.t[:, hs], mv.t[:, h, 0:1], rs.t[:, h:h + 1], SUB, MUL, [hbuf.rs[h], mv.r, rs.r], [hbuf.rs[h]])
            for hf in range(2):
                c0 = hf * (CT // 2)
                xc_, z_, oa_ = xck[0], zk[0], oab[0]
                S.dma("pool", sem_p[hf], xc_.t[:], XCv[:, c0:c0 + CT // 2, ts_], writes=[xc_.r])
                S.dma("pool", sem_p[2], z_.t[:], Zv[:, c0:c0 + CT // 2, ts_], writes=[z_.r])
                k.ACTF(z_.t[:], z_.t[:], AF.Sigmoid, [z_.r], [z_.r])
                for g4 in range(CT // 8):
                    pT = k.bank[g4 % 2]
                    for cl in range(4 * g4, 4 * g4 + 4):
                        ct = c0 + cl
                        cols = slice((cl % 4) * 128, (cl % 4 + 1) * 128)
                        k.TR(pT.t[:, cols], hbuf.t[:, ct * 128:(ct + 1) * 128], k.ident32.t[:], [hbuf.rs[ct // 2], PR], [pT.r])
                    for cl in range(4 * g4, 4 * g4 + 4):
                        ct = c0 + cl
                        cols = slice((cl % 4) * 128, (cl % 4 + 1) * 128)
                        sk = skb[ct % 2]
                        o1 = o1b[ct % 2]
                        k.ACTF(sk.t[:], xc_.t[:, cl, :], AF.Identity, [xc_.r, k.mlvec.r], [sk.r], scale=k.mlvec.t[:, 64 + ct:65 + ct])
                        k.STT(o1.t[:], pT.t[:, cols], k.mlvec.t[:, 32 + ct:33 + ct], sk.t[:], MUL, ADD, [pT.r, sk.r, k.mlvec.r], [o1.r])
                        k.TT("pool", oa_.t[:, cl, :], o1.t[:], z_.t[:, cl, :], MUL, [o1.r, z_.r], [oa_.rs[cl]])
                S.dma("sp", sem_p[3], OAv[:, c0:c0 + CT // 2, ts_], oa_.t[:], reads=oa_.rs)


def build(TILES, SEGT, debug=(), stop_after="F"):
    NTOK = TILES * NT
    NCH = TILES * 4
    nc = bass.Bass("TRN2", target_bir_lowering=False)
    dbg = set(debug)

    def din(name, shape, dt=F32):
        return nc.dram_tensor(name, list(shape), dt, kind="ExternalInput").ap()

    def dscr(name, shape, dt):
        kind = "ExternalOutput" if name in dbg else "Internal"
        return nc.dram_tensor(name, list(shape), dt, kind=kind).ap()

    xT = din("xT", [D, NTOK])
    keep_d = din("keep", [128, 1])
    gvec_d = din("gvec", [128, 7 * KT])
    wsrc = {
        "ffn_win": din("ffn_win", [4 * JT, 128, 2 * KT * 128]),
        "ffn_wout": din("ffn_wout", [4 * KT, 128, JT * 128]),
        "s5_win": din("s5_win", [KT, 128, KT * 128]),
        "wglu": din("wglu", [2 * KT, 128, KT * 128]),
        "ml_win": din("ml_win", [2 * CT, 128, KT * 128]),
        "mlD": din("mlD", [CT, 128, 1024]),
        "ml_wout": din("ml_wout", [KT, 128, CT * 128]),
    }
    dr = {}
    for kk_ in ("lamre", "lamim", "logstep"):
        dr[kk_] = din(kk_, [2, 128, 64])
    dr["bre"] = din("bre", [2, 128, 64 * 16])
    dr["bim"] = din("bim", [2, 128, 64 * 16])
    dr["CRI"] = din("CRI", [KT, 128, 2 * 4 * 2 * 32])
    dr["s5d"] = din("s5d", [128, KT])
    dr["wg"] = din("wg", [128, 3 * CT * 64])
    dr["bgate"] = din("bgate", [1, 64])
    mlvec_d = din("mlvec", [128, 96])
    yT = nc.dram_tensor("yT", [D, NTOK], F32, kind="ExternalOutput").ap()

    wb = {n: dscr(n + "_b", a.shape, BF16) for n, a in wsrc.items()}
    X1 = dscr("X1", [D, NTOK], F32)
    dr["U"] = dscr("U", [D, NTOK], F32)
    dr["YL"] = dscr("YL", [D, NTOK], F32)
    dr["GE"] = dscr("GE", [TILES, 128, 2 * 64 * 4 * 2], F32)
    dr["HIN"] = dscr("HIN", [TILES, 128, 2 * 64 * 4 * 2], F32)
    dr["TAB"] = dscr("TAB", [2, KT, 128, 4, 4, 128], F32)
    dr["PPL"] = dscr("PPL", [2, KT, 128, 4, 2, 128], BF16)
    X2 = dscr("X2", [D, NTOK], F32)
    if "YA" in dbg:
        dr["YA"] = dscr("YA", [D, NTOK], F32)
    X4 = dscr("X4", [D, NTOK], F32)
    XM = dscr("XM", [DI, NTOK], BF16)
    Z = dscr("Z", [DI, NTOK], F32)
    U = dr["U"]
    dr["XM"] = XM
    dr["Z"] = Z
    dr["XC"] = dscr("XC", [DI, NTOK], BF16)
    dr["QT"] = dscr("QT", [DI, NTOK], BF16)
    dr["KT"] = dscr("KT", [DI, NTOK], BF16)
    dr["KTOK"] = dscr("KTOK", [NCH, 128, DI], BF16)
    dr["VTOK"] = dscr("VTOK", [NCH, 128, DI], BF16)
    dr["G"] = dscr("G", [NCH, 128, 64], F32)
    dr["HB"] = dscr("HB", [NCH, 128, DI], F32)
    dr["OA"] = dscr("OA", [DI, NTOK], BF16)
    X5 = dscr("X5", [D, NTOK], F32)

    def fm(ap):
        return ap.rearrange("(c p) t -> p c t", p=128)

    with ExitStack() as st:
        S = Sched(nc, st)
        k = K(nc, st, S)
        k.wres = Res("wres")
        k.bank = []
        for i in range(8):
            t = st.enter_context(nc.psum_tensor("bank%d" % i, [128, 512], F32))
            k.bank.append(Buf(t, "bank%d" % i))
        k.ffn_win_b = wb["ffn_win"]
        k.ffn_wout_b = wb["ffn_wout"]
        io_sem = [S.new_dma_sem("io%d" % i) for i in range(4)]
        st_sem = [S.new_dma_sem("st%d" % i) for i in range(6)]
        k.ones32 = k.sb("ones32", [128, 128], F32)
        k.epsc = k.sb("epsc", [128, 1], F32)
        k.gvec = k.sb("gvec", [128, 7, KT], F32)
        k.keep = k.sb("keepc", [128, 1], F32)
        k.MS("dve", k.ones32.t[:], 1.0, [k.ones32.r])
        k.MS("dve", k.epsc.t[:], EPS, [k.epsc.r])
        S.dma("sp", io_sem[0], k.gvec.t[:].rearrange("p a b -> p (a b)"), gvec_d[:, :], writes=[k.gvec.r])
        S.dma("sp", io_sem[0], k.keep.t[:], keep_d[:, :], writes=[k.keep.r])
        k.mlvec = k.sb("mlvec", [128, 96], F32)
        S.dma("sp", io_sem[0], k.mlvec.t[:], mlvec_d[:, :], writes=[k.mlvec.r])
        k.hbres = Res("hbres")
        dr["mlD_b"] = wb["mlD"]

        def phase():
            ph = ExitStack()
            k.ph = ph
            return ph

        def common_bufs():
            k.W = Stream(k, "W", JT * 128, BF16, 3)
            k.hn = k.sb("hn", [128, KT, NT], BF16, nres=KT)
            k.h = k.sb("h", [128, JT, NT], BF16, nres=JT)
            k.sq = [k.sb("sq%d" % i, [128, NT], F32) for i in range(2)]
            k.sg = [k.sb("sg%d" % i, [128, NT], F32) for i in range(2)]
            k.rstd = k.sb("rstd", [128, NT], F32)

        with phase():
            for _ in cast_weights(k, [(wsrc["ffn_win"][0:JT], wb["ffn_win"][0:JT]), (wsrc["ffn_wout"][0:KT], wb["ffn_wout"][0:KT]),
                                      (wsrc["s5_win"], wb["s5_win"])]):
                pass
            barrier(S)

        with phase():
            common_bufs()
            x = k.sb("xa", [128, KT, NT], F32, nres=KT)
            ust = [k.sb("ust%d" % i, [128, NT], F32) for i in range(2)]
            rest = [(wsrc["ffn_win"][JT:4 * JT], wb["ffn_win"][JT:4 * JT]), (wsrc["ffn_wout"][KT:4 * KT], wb["ffn_wout"][KT:4 * KT])]
            rest += [(wsrc[n], wb[n]) for n in ("wglu", "ml_win", "mlD", "ml_wout")]
            cgen = cast_weights(k, rest, queue="pool")
            nrest = sum(a.shape[0] * ((a.shape[2] + 4095) // 4096) for a, _ in rest)
            per_tile = -(-nrest // TILES)
            state = {"left": 0}

            def bg():
                if state["left"] > 0:
                    state["left"] -= 1
                    next(cgen, None)
            k.bg = bg
            for i in range(TILES):
                t0 = i * NT
                state["left"] = per_tile
                S.dma("pool", io_sem[0], x.t[:], fm(xT)[:, :, t0:t0 + NT], writes=x.rs)
                ffn(k, x, 0, 0)
                while state["left"] > 0:
                    bg()
                S.dma("pool", st_sem[0], fm(X1)[:, :, t0:t0 + NT], x.t[:], reads=x.rs)
                rms_stats(k, x)
                rms_apply(k, x, 1, k.hn)

                def cons(m, ps, t0=t0):
                    b = ust[m % 2]
                    k.CP("act", b.t[:], ps.t[:], [ps.r], [b.r])
                    S.dma("pool", st_sem[1 + m % 2], U[m * 128:(m + 1) * 128, t0:t0 + NT], b.t[:], reads=[b.r])
                proj(k, k.hn, [wb["s5_win"][m] for m in range(KT)], KT, cons)
            k.bg = None
            for _ in cgen:
                pass
            barrier(S)
        if stop_after == "A":
            S.emit_all()
            return nc, S

        s5scope = ExitStack()
        k.ph = s5scope
        k.BL = k.sb("BL", [128, 2, 2, KT, 128], BF16)
        k.CL = k.sb("CL", [128, 2, 64, 3, 32], BF16)
        k.L128 = k.sb("L128", [128, 2, 2, 64], F32)
        k.L127 = k.sb("L127", [128, 2, 2, 64], F32)
        with phase():
            s5_setup(k, dr)
            tmpc = k.sb("tmpc", [128, 2, 4, 2, 32], F32)
            for ct in range(KT):
                S.dma("sp", io_sem[1], tmpc.t[:].rearrange("p a b c e -> p (a b c e)"), dr["CRI"][ct], writes=[tmpc.r])
                k.CP("dve", k.CL.t[:, :, 4 * ct:4 * ct + 4, 0, :], tmpc.t[:, :, :, 0, :], [tmpc.r], [k.CL.r])
                k.TS("dve", k.CL.t[:, :, 4 * ct:4 * ct + 4, 1, :], tmpc.t[:, :, :, 0, :], -1.0, None, MUL, None, [tmpc.r], [k.CL.r])
                k.TS("dve", k.CL.t[:, :, 4 * ct:4 * ct + 4, 2, :], tmpc.t[:, :, :, 1, :], -1.0, None, MUL, None, [tmpc.r], [k.CL.r])
            barrier(S)
        with phase():
            pass_b(k, dr, TILES)
            barrier(S)
        with phase():
            s5_chain(k, dr, TILES, SEGT)
            barrier(S)
        s5scope.close()
        if stop_after == "B":
            S.emit_all()
            return nc, S

        with phase():
            common_bufs()
            x = k.sb("xc_", [128, KT, NT], F32, nres=KT)
            k.c5 = {
                "hin": [k.sb("hin%d" % i, [128, 2, 64, 4, 2], F32) for i in range(2)],
                "sem_h": [S.new_dma_sem("hin%d" % i) for i in range(2)],
                "pp": Stream(k, "PP", 4 * 2 * 128, BF16, 4, hold=2),
                "cs": Stream(k, "CS", 512, F32, 3, hold=2),
                "chb": [k.sb("chb%d" % i, [128, 2, 4, 4, 2, 32], BF16) for i in range(2)],
                "ta": [k.sb("cta%d" % i, [128, 4, 4, 32], F32) for i in range(2)],
                "tb": [k.sb("ctb%d" % i, [128, 4, 4, 32], F32) for i in range(2)],
                "yl": [k.sb("yl%d" % i, [128, NT], F32) for i in range(2)],
                "g2": [k.sb("g2_%d" % i, [128, NT], F32) for i in range(2)],
                "sem_yl": [S.new_dma_sem("yl%d" % i) for i in range(2)],
            }
            gsb = [k.sb("gsb%d" % i, [128, NT], F32) for i in range(2)]
            zst = [k.sb("zst%d" % i, [128, NT], F32) for i in range(2)]
            mst = [k.sb("mst%d" % i, [128, NT], BF16) for i in range(2)]
            for i in range(TILES):
                t0 = i * NT
                S.dma("pool", io_sem[0], x.t[:], fm(X1)[:, :, t0:t0 + NT], writes=x.rs)
                s5_carry_tile(k, dr, i, k.hn)
                held = {}

                def cons_glu(idx, ps):
                    m = idx // 2
                    if idx % 2 == 0:
                        held["v"] = ps
                        return
                    pv = held["v"]
                    g = gsb[m % 2]
                    k.ACTF(g.t[:], ps.t[:], AF.Sigmoid, [ps.r], [g.r])
                    k.TT("dve", g.t[:], g.t[:], pv.t[:], MUL, [g.r, pv.r], [g.r])
                    k.TT("dve", x.t[:, m, :], x.t[:, m, :], g.t[:], ADD, [g.r, x.rs[m]], [x.rs[m]])
                wl = []
                for m in range(KT):
                    wl += [wb["wglu"][m], wb["wglu"][KT + m]]
                proj(k, k.hn, wl, KT, cons_glu)
                if "X2" in dbg:
                    S.dma("pool", st_sem[3], fm(X2)[:, :, t0:t0 + NT], x.t[:], reads=x.rs)
                ffn(k, x, 2, 1)
                ffn(k, x, 3, 2)
                S.dma("pool", st_sem[0], fm(X4)[:, :, t0:t0 + NT], x.t[:], reads=x.rs)
                rms_stats(k, x)
                rms_apply(k, x, 4, k.hn)

                def cons_ml(m, ps, t0=t0):
                    if m < CT:
                        b = mst[m % 2]
                        k.CP("act", b.t[:], ps.t[:], [ps.r], [b.r])
                        S.dma("pool", st_sem[1 + m % 2], XM[m * 128:(m + 1) * 128, t0:t0 + NT], b.t[:], reads=[b.r])
                    else:
                        b = zst[m % 2]
                        k.CP("act", b.t[:], ps.t[:], [ps.r], [b.r])
                        S.dma("pool", st_sem[4 + m % 2], Z[(m - CT) * 128:(m - CT + 1) * 128, t0:t0 + NT], b.t[:], reads=[b.r])
                proj(k, k.hn, [wb["ml_win"][m] for m in range(2 * CT)], KT, cons_ml)
            barrier(S)
        if stop_after == "C":
            S.emit_all()
            return nc, S

        with phase():
            pass_d(k, dr, TILES, SEGT)
            barrier(S)
        if stop_after == "D":
            S.emit_all()
            return nc, S
        mlscope = ExitStack()
        k.ph = mlscope
        k.ED = k.sb("ED", [128, NCH, 2, 16], F32)
        k.EB = k.sb("EB", [128, NCH, 2, 16], F32)
        k.EG = k.sb("EG", [128, NCH, 2, 16], F32)
        k.TRI = [k.sb("TRI%d" % i, [128, 128], F32) for i in range(2)]
        k.ident32 = k.sb("ident32", [128, 128], F32)
        with phase():
            ml_prep(k, dr, TILES, SEGT)
            barrier(S)
        with phase():
            pass_e(k, dr, TILES)
            barrier(S)
        mlscope.close()
        if stop_after == "E":
            S.emit_all()
            return nc, S
        with phase():
            common_bufs()
            x = k.sb("xf_", [128, KT, NT], F32, nres=KT)
            OAv = dr["OA"].rearrange("(c p) t -> p c t", p=128)
            for i in range(TILES):
                t0 = i * NT
                S.dma("pool", io_sem[0], x.t[:], fm(X4)[:, :, t0:t0 + NT], writes=x.rs)
                S.dma("pool", io_sem[1], k.h.t[:, 0:CT, :], OAv[:, :, t0:t0 + NT], writes=k.h.rs)

                def cons_o(m, ps):
                    k.TT("dve", x.t[:, m, :], x.t[:, m, :], ps.t[:], ADD, [ps.r, x.rs[m]], [x.rs[m]])
                proj(k, k.h, [wb["ml_wout"][m] for m in range(KT)], CT, cons_o)
                if "X5" in dbg:
                    S.dma("pool", st_sem[3], fm(X5)[:, :, t0:t0 + NT], x.t[:], reads=x.rs)
                ffn(k, x, 5, 3)
                rms_stats(k, x)
                rms_apply(k, x, 6, x)
                S.dma("pool", st_sem[0], fm(yT)[:, :, t0:t0 + NT], x.t[:], reads=x.rs)
            barrier(S)
        barrier(S)
        S.emit_all()
    return nc, S


def _tile_rows(w, nk):
    K_, M_ = w.shape
    m = M_ // 128
    return np.ascontiguousarray(w.reshape(nk, 128, m, 128).transpose(2, 1, 0, 3).reshape(m, 128, nk * 128))


def prep_shared(inp):
    f = np.float32
    out = {}
    g = np.concatenate([np.asarray(inp["norm_g"], f).reshape(6, D), np.asarray(inp["final_g"], f).reshape(1, D)], 0)
    out["gvec"] = np.ascontiguousarray(g.reshape(7, KT, 128).transpose(2, 0, 1).reshape(128, 7 * KT))
    win = np.asarray(inp["ffn_w_in"], f).reshape(4, D, 2, JT, 128)
    win = win.reshape(4, KT, 128, 2, JT, 128).transpose(0, 4, 2, 3, 1, 5)
    out["ffn_win"] = np.ascontiguousarray(win.reshape(4 * JT, 128, 2 * KT * 128))
    wout = np.asarray(inp["ffn_w_out"], f).reshape(4, DFF, D)
    out["ffn_wout"] = np.concatenate([_tile_rows(wout[i], JT) for i in range(4)], 0)
    out["s5_win"] = _tile_rows(np.asarray(inp["s5_w_in"], f)[0], KT)
    out["wglu"] = _tile_rows(np.asarray(inp["s5_w_glu"], f)[0], KT)
    out["ml_win"] = _tile_rows(np.asarray(inp["ml_w_in"], f)[0], KT)

    def gp(a):
        sh = a.shape
        a = a.reshape((2, 64, 2, 64) + sh[3:])
        perm = (0, 2, 3, 1) + tuple(range(4, a.ndim))
        a = a.transpose(perm)
        return np.ascontiguousarray(a.reshape((2, 128, 64) + sh[3:]))
    out["lamre"] = gp(np.asarray(inp["s5_lambda_re"], f)[0])
    out["lamim"] = gp(np.asarray(inp["s5_lambda_im"], f)[0])
    ls = np.asarray(inp["s5_log_step"], f)[0]
    out["logstep"] = gp(np.broadcast_to(ls[:, :, None], (2, 128, 64)).copy())
    out["bre"] = gp(np.asarray(inp["s5_b_re"], f)[0]).reshape(2, 128, 64 * 16)
    out["bim"] = gp(np.asarray(inp["s5_b_im"], f)[0]).reshape(2, 128, 64 * 16)
    cri = np.zeros((KT, 128, 2, 4, 2, 32), f)
    for ri, key in enumerate(("s5_c_re", "s5_c_im")):
        c = np.asarray(inp[key], f)[0]
        c = c.reshape(2, KT, 4, 2, 16, 64)
        for g2 in range(2):
            cri[:, g2 * 64:(g2 + 1) * 64, :, :, ri, g2 * 16:(g2 + 1) * 16] = c[:, :, :, g2].transpose(1, 4, 0, 2, 3)
    out["CRI"] = np.ascontiguousarray(cri.reshape(KT, 128, 512))
    out["s5d"] = np.ascontiguousarray(np.asarray(inp["s5_d"], f)[0].reshape(KT, 128).T)
    mlD = np.zeros((CT, 128, 8, 128), f)
    cw = np.asarray(inp["ml_conv_w"], f)[0].reshape(5, CT, 128)
    ar = np.arange(128)
    for tau in range(5):
        mlD[:, ar, tau, ar] = cw[tau]
    for j, key in enumerate(("ml_wq", "ml_wk", "ml_wv")):
        w = np.asarray(inp[key], f)[0].reshape(CT, 32, 4, 4)
        for n in range(32):
            mlD[:, 4 * n:4 * n + 4, 5 + j, 4 * n:4 * n + 4] = w[:, n]
    out["mlD"] = np.ascontiguousarray(mlD.reshape(CT, 128, 1024))
    out["ml_wout"] = _tile_rows(np.asarray(inp["ml_w_out"], f)[0], CT)
    wg = np.asarray(inp["ml_w_gates"], f)[0].reshape(3, CT, 128, 64)
    out["wg"] = np.ascontiguousarray(wg.transpose(2, 0, 1, 3).reshape(128, 3 * CT * 64))
    out["bgate"] = np.ascontiguousarray(np.asarray(inp["ml_b_gates"], f)[0].reshape(1, 64))
    vecs = [np.asarray(inp[kk], f)[0].reshape(CT, 128).T for kk in ("ml_conv_b", "ml_norm_g", "ml_skip")]
    out["mlvec"] = np.ascontiguousarray(np.concatenate(vecs, 1))
    return out


_CACHE = {}


def kernel(**inputs):
    f = np.float32
    TILES, SEGT = 16, 4
    NTOK = TILES * NT
    sh = prep_shared(inputs)
    xp = np.asarray(inputs["x_prompt"], f)
    xs = np.asarray(inputs["x_sample"], f)
    in_maps = []
    for c in range(8):
        m = dict(sh)
        if c < 4:
            m["xT"] = np.ascontiguousarray(xp[c].T)
            m["keep"] = np.ones((128, 1), f)
        else:
            xt = np.zeros((D, NTOK), f)
            for j in range(2):
                xt[:, j * 2048:(j + 1) * 2048] = xs[2 * (c - 4) + j].T
            m["xT"] = xt
            m["keep"] = np.zeros((128, 1), f)
        in_maps.append(m)
    if "nc" not in _CACHE:
        _CACHE["nc"] = build(TILES, SEGT)[0]
    res = run_bass_kernel_spmd(_CACHE["nc"], in_maps, core_ids=list(range(8)))
    yp = np.zeros((4, 8192, D), f)
    ys = np.zeros((8, 2048, D), f)
    for c in range(8):
        y = np.asarray(res.results[c]["yT"], f)
        if c < 4:
            yp[c] = y.T
        else:
            for j in range(2):
                ys[2 * (c - 4) + j] = y[:, j * 2048:(j + 1) * 2048].T
    return (yp, ys)
```

```python
import numpy as np
import concourse.bass as bass
import concourse.mybir as mybir

F32 = mybir.dt.float32
BF16 = mybir.dt.bfloat16
I32 = mybir.dt.int32
ALU = mybir.AluOpType
AF = mybir.ActivationFunctionType


class Res:
    __slots__ = ("name", "lw", "rd")

    def __init__(self, name=""):
        self.name = name
        self.lw = None
        self.rd = {}


class Eng:
    def __init__(self, name, h, sem):
        self.name = name
        self.h = h
        self.sem = sem
        self.count = 0
        self.seen = {}
        self.prog = []


class Sched:
    def __init__(self, nc, stack):
        self.nc = nc
        self.stack = stack
        self.sems = {}
        self.engs = {}
        for name, h in (("pe", nc.tensor), ("dve", nc.vector), ("act", nc.scalar),
                        ("pool", nc.gpsimd), ("sp", nc.sync)):
            sem = stack.enter_context(nc.semaphore("sem_" + name))
            self.sems["e:" + name] = sem
            self.engs[name] = Eng(name, h, sem)
        self.dma_sem_val = {}
        self.n_inst = 0
        self.n_wait = 0

    def new_dma_sem(self, key):
        sem = self.stack.enter_context(self.nc.semaphore("dsem_" + key))
        self.sems["d:" + key] = sem
        self.dma_sem_val["d:" + key] = 0
        return "d:" + key

    def _wait(self, eng, deps):
        best = {}
        own = "e:" + eng.name
        for (k, v) in deps:
            if eng.name == "pe" and k == own:
                continue
            if best.get(k, 0) < v:
                best[k] = v
        for k, v in best.items():
            if eng.seen.get(k, 0) >= v:
                continue
            eng.prog.append(("w", self.sems[k], v))
            eng.seen[k] = v
            self.n_wait += 1

    def _deps(self, reads, writes):
        deps = []
        for r in reads:
            if r.lw is not None:
                deps.append(r.lw)
        for r in writes:
            if r.lw is not None:
                deps.append(r.lw)
            for k, v in r.rd.items():
                deps.append((k, v))
        return deps

    def op(self, engname, fn, reads=(), writes=()):
        eng = self.engs[engname]
        self._wait(eng, self._deps(reads, writes))
        eng.count += 1
        eng.prog.append(("i", fn, eng.sem, 1))
        ev = ("e:" + engname, eng.count)
        for r in reads:
            if r.rd.get(ev[0], 0) < ev[1]:
                r.rd[ev[0]] = ev[1]
        for r in writes:
            r.lw = ev
            r.rd = {}
        self.n_inst += 1

    def dma(self, qname, semkey, out, in_, reads=(), writes=()):
        eng = self.engs[qname]
        self._wait(eng, self._deps(reads, writes))
        eng.prog.append(("i", (lambda h, o=out, i=in_: h.dma_start(out=o, in_=i)), self.sems[semkey], 16))
        self.dma_sem_val[semkey] += 16
        v = self.dma_sem_val[semkey]
        ev = (semkey, v)
        for r in reads:
            if r.rd.get(ev[0], 0) < ev[1]:
                r.rd[ev[0]] = ev[1]
        for r in writes:
            r.lw = ev
            r.rd = {}
        self.n_inst += 1

    def wait_all(self, engname, resources):
        eng = self.engs[engname]
        deps = []
        for r in resources:
            if r.lw is not None:
                deps.append(r.lw)
        self._wait(eng, deps)

    def emit_all(self):
        nc = self.nc
        with nc.Block() as block:
            def run(eng):
                def body(h):
                    for it in eng.prog:
                        if it[0] == "w":
                            h.wait_ge(it[1], it[2])
                        else:
                            it[1](h).then_inc(it[2], it[3])
                return body
            block.tensor(run(self.engs["pe"]))
            block.vector(run(self.engs["dve"]))
            block.scalar(run(self.engs["act"]))
            block.gpsimd(run(self.engs["pool"]))
            block.sync(run(self.engs["sp"]))

from contextlib import ExitStack
from concourse.bass_utils import run_bass_kernel_spmd

NT = 512
D = 2048
KT = 16
DFF = 5632
JT = 44
DI = 4096
CT = 32
NH = 16
DH = 256
EPS = 1e-6
MUL = ALU.mult
ADD = ALU.add
SUB = ALU.subtract


class Buf:
    def __init__(self, t, name, nres=0):
        self.t = t
        self.r = Res(name)
        self.rs = [Res("%s_%d" % (name, i)) for i in range(nres)]


class Stream:
    def __init__(self, k, name, width, dt, nslots, queue="sp", hold=1):
        self.hold = hold
        self.k = k
        self.n = nslots
        self.queue = queue
        self.slots = [k.sb("%s_s%d" % (name, i), [128, width], dt) for i in range(nslots)]
        k.nst = getattr(k, "nst", 0) + 1
        self.sems = [k.S.new_dma_sem("%s%d_%d" % (name, k.nst, i)) for i in range(nslots)]
        self.items = []
        self.next_load = 0
        self.next_use = 0

    def push(self, aps):
        self.items.extend(aps)

    def get(self):
        S = self.k.S
        while self.next_load < min(len(self.items), self.next_use + self.n - self.hold + 1):
            i = self.next_load
            s = i % self.n
            ap = self.items[i]
            w = ap.shape[-1]
            S.dma(self.queue, self.sems[s], self.slots[s].t[:, 0:w], ap, writes=[self.slots[s].r])
            self.next_load += 1
        b = self.slots[self.next_use % self.n]
        self.next_use += 1
        return b


class K:
    def __init__(self, nc, st, S):
        self.nc = nc
        self.st = st
        self.S = S
        self.ph = st

    def sb(self, name, shape, dt, nres=0):
        self.nsb = getattr(self, "nsb", 0) + 1
        t = self.ph.enter_context(self.nc.sbuf_tensor("sb%d_%s" % (self.nsb, name), shape, dt))
        return Buf(t, name, nres)

    def MM(self, ps, lhsT, rhs, start, stop, R, W, **kw):
        self.S.op("pe", lambda h: h.matmul(ps, lhsT=lhsT, rhs=rhs, start=start, stop=stop, **kw), R, W)

    def TR(self, ps, in_, ident, R, W):
        self.S.op("pe", lambda h: h.transpose(ps, in_, ident), R, W)

    def ACTF(self, out, in_, func, R, W, bias=None, scale=None):
        kw = {}
        if bias is not None:
            kw["bias"] = bias
        if scale is not None:
            kw["scale"] = scale
        self.S.op("act", lambda h: h.activation(out=out, in_=in_, func=func, **kw), R, W)

    def TT(self, eng, out, in0, in1, op, R, W):
        self.S.op(eng, lambda h: h.tensor_tensor(out=out, in0=in0, in1=in1, op=op), R, W)

    def TS(self, eng, out, in0, s1, s2, op0, op1, R, W):
        if s2 is None:
            self.S.op(eng, lambda h: h.tensor_scalar(out=out, in0=in0, scalar1=s1, scalar2=None, op0=op0), R, W)
        else:
            self.S.op(eng, lambda h: h.tensor_scalar(out=out, in0=in0, scalar1=s1, scalar2=s2, op0=op0, op1=op1), R, W)

    def STT(self, out, in0, scalar, in1, op0, op1, R, W):
        self.S.op("dve", lambda h: h.scalar_tensor_tensor(out=out, in0=in0, scalar=scalar, in1=in1, op0=op0, op1=op1), R, W)

    def CP(self, eng, out, in_, R, W):
        if eng == "act":
            self.S.op("act", lambda h: h.activation(out=out, in_=in_, func=AF.Copy), R, W)
        else:
            self.S.op(eng, lambda h: h.tensor_copy(out=out, in_=in_), R, W)

    def MS(self, eng, ap, val, W):
        self.S.op(eng, lambda h: h.memset(ap, val), [], W)

    def RECIP(self, out, in_, R, W):
        self.S.op("dve", lambda h: h.reciprocal(out=out, in_=in_), R, W)

    def SCAN(self, out, d0, d1, init, op0, op1, R, W):
        self.S.op("dve", lambda h: h.tensor_tensor_scan(out=out, data0=d0, data1=d1, initial=init, op0=op0, op1=op1), R, W)


def barrier(S):
    evs = [("e:" + n, e.count) for n, e in S.engs.items() if e.count > 0]
    evs += [(kk, v) for kk, v in S.dma_sem_val.items() if v > 0]
    for n, e in S.engs.items():
        S._wait(e, [ev for ev in evs if ev[0] != "e:" + n])


def rms_stats(k, x):
    S = k.S
    ps = k.bank[7]
    for kt in range(KT):
        sq = k.sq[kt % 2]
        k.ACTF(sq.t[:], x.t[:, kt, :], AF.Square, [x.rs[kt]], [sq.r])
        k.MM(ps.t[:], k.ones32.t[:], sq.t[:], kt == 0, kt == KT - 1, [sq.r, k.ones32.r], [ps.r])
    k.ACTF(k.rstd.t[:], ps.t[:], AF.Sqrt, [ps.r, k.epsc.r], [k.rstd.r], bias=k.epsc.t[:, 0:1], scale=1.0 / D)
    k.RECIP(k.rstd.t[:], k.rstd.t[:], [k.rstd.r], [k.rstd.r])


def rms_apply(k, x, gi, out):
    for kt in range(KT):
        k.STT(out.t[:, kt, :], x.t[:, kt, :], k.gvec.t[:, gi, kt:kt + 1], k.rstd.t[:], MUL, MUL,
              [x.rs[kt], k.gvec.r, k.rstd.r], [out.rs[kt]])


def ffn(k, x, gi, wi):
    rms_stats(k, x)
    rms_apply(k, x, gi, k.hn)
    W = k.W
    W.push([k.ffn_win_b[wi * JT + j] for j in range(JT)])
    W.push([k.ffn_wout_b[wi * KT + m] for m in range(KT)])
    hn = k.hn
    for j in range(JT):
        wb = W.get()
        pg = k.bank[(j % 2) * 2]
        pu = k.bank[(j % 2) * 2 + 1]
        for g, ps in ((0, pg), (1, pu)):
            for kt in range(KT):
                c0 = (g * KT + kt) * 128
                k.MM(ps.t[:], wb.t[:, c0:c0 + 128], hn.t[:, kt, :], kt == 0, kt == KT - 1, [wb.r, hn.rs[kt]], [ps.r])
        if getattr(k, "bg", None) is not None:
            k.bg()
        sg = k.sg[j % 2]
        k.ACTF(sg.t[:], pg.t[:], AF.Silu, [pg.r], [sg.r])
        k.TT("dve", k.h.t[:, j, :], sg.t[:], pu.t[:], MUL, [sg.r, pu.r], [k.h.rs[j]])
    for m in range(KT):
        wo = W.get()
        ps = k.bank[4 + m % 2]
        for kt in range(JT):
            k.MM(ps.t[:], wo.t[:, kt * 128:(kt + 1) * 128], k.h.t[:, kt, :], kt == 0, kt == JT - 1, [wo.r, k.h.rs[kt]], [ps.r])
        k.STT(x.t[:, m, :], ps.t[:], 0.5, x.t[:, m, :], MUL, ADD, [ps.r, x.rs[m]], [x.rs[m]])


def proj(k, src, wlist, nkt, consume):
    W = k.W
    W.push(wlist)
    for m in range(len(wlist)):
        w = W.get()
        ps = k.bank[4 + m % 2]
        for kt in range(nkt):
            k.MM(ps.t[:], w.t[:, kt * 128:(kt + 1) * 128], src.t[:, kt, :], kt == 0, kt == nkt - 1, [w.r, src.rs[kt]], [ps.r])
        consume(m, ps)


def cast_weights(k, pairs, queue="sp", engs=("dve", "act", "dve")):
    S = k.S
    CW = 4096
    cin = [k.sb("cin%d" % i, [128, CW], F32) for i in range(2)]
    cout = [k.sb("cout%d" % i, [128, CW], BF16) for i in range(3)]
    k.ncast = getattr(k, "ncast", 0) + 1
    sin = [S.new_dma_sem("cin%d_%d" % (k.ncast, i)) for i in range(2)]
    sout = [S.new_dma_sem("cout%d_%d" % (k.ncast, i)) for i in range(3)]
    n = 0
    for (src, dst) in pairs:
        T, _, Fw = src.shape
        for t in range(T):
            for c0 in range(0, Fw, CW):
                w = min(CW, Fw - c0)
                a = cin[n % 2]
                b = cout[n % 3]
                S.dma(queue, sin[n % 2], a.t[:, 0:w], src[t, :, c0:c0 + w], writes=[a.r])
                k.CP(engs[n % len(engs)], b.t[:, 0:w], a.t[:, 0:w], [a.r], [b.r])
                S.dma(queue, sout[n % 3], dst[t, :, c0:c0 + w], b.t[:, 0:w], reads=[b.r], writes=[k.wres])
                n += 1
                yield n


PI = float(np.pi)


class VC:
    def __init__(self, k, name):
        self.k = k
        self.r = Res(name)

    def tt(self, o, a, b, op, eng="dve"):
        self.k.TT(eng, o, a, b, op, [self.r], [self.r])

    def ts(self, o, a, s1, op0, s2=None, op1=None):
        self.k.TS("dve", o, a, s1, s2, op0, op1, [self.r], [self.r])

    def stt(self, o, a, s, b, op0, op1):
        self.k.STT(o, a, s, b, op0, op1, [self.r], [self.r])

    def act(self, o, a, func, scale=None, bias=None):
        self.k.ACTF(o, a, func, [self.r], [self.r], bias=bias, scale=scale)

    def cp(self, o, a):
        self.k.CP("dve", o, a, [self.r], [self.r])

    def ms(self, o, v):
        self.k.MS("dve", o, v, [self.r])

    def recip(self, o, a):
        self.k.RECIP(o, a, [self.r], [self.r])

    def cmul(self, or_, oi, ar, ai, br, bi, t1, t2):
        self.tt(t1, ar, br, MUL)
        self.tt(t2, ai, bi, MUL)
        self.tt(or_, t1, t2, SUB)
        self.tt(t1, ar, bi, MUL)
        self.tt(t2, ai, br, MUL)
        self.tt(oi, t1, t2, ADD)


def s5_setup(k, dr):
    S = k.S
    v = VC(k, "s5setup")
    sem = S.new_dma_sem("s5set")

    def T(name, shape, dt=F32):
        return k.sb(name, shape, dt).t

    lre = T("lre", [128, 64]); lim = T("lim", [128, 64]); lst = T("lst", [128, 64])
    bre = T("bre", [128, 64, 16]); bim = T("bim", [128, 64, 16])
    dl = T("dl", [128, 64]); a = T("a_", [128, 64]); th = T("th", [128, 64])
    mag = T("mag", [128, 64]); imag = T("imag", [128, 64])
    sn = T("sn", [128, 64]); cs = T("cs", [128, 64])
    lr = T("lr", [128, 64]); li = T("li", [128, 64]); ir = T("ir", [128, 64]); ii = T("ii", [128, 64])
    w1 = T("w1", [128, 64]); w2 = T("w2", [128, 64]); w3 = T("w3", [128, 64]); w4 = T("w4", [128, 64])
    wi32 = T("wi32", [128, 64], I32)
    cr = T("cr", [128, 64]); ci = T("ci", [128, 64])
    pwr = T("pwr", [128, 64]); pwi = T("pwi", [128, 64])
    btp = [T("btp%d" % i, [128, 64, 32]) for i in range(2)]
    tabr = T("tabr", [128, 64, 128]); tabi = T("tabi", [128, 64, 128])
    tm1 = T("tm1", [128, 64, 64]); tm2 = T("tm2", [128, 64, 64])
    bt1 = tm1[:, :, 0:16]; bt2 = tm2[:, :, 0:16]
    ppb = T("ppb", [128, 64, 2, 128], BF16)
    ident = T("ident", [128, 128])
    io_i = T("io_i", [128, 128], I32)
    io_f = T("io_f", [128, 128])
    S.op("pool", lambda h: h.iota(io_i[:], [[1, 128]], 0, -1), [], [v.r])
    v.cp(io_f[:], io_i[:])
    v.ts(ident[:], io_f[:], 0.0, ALU.is_equal)

    def sinred(out, arg):
        v.ts(w1[:], arg, 1.0 / (2 * PI), MUL)
        v.cp(wi32[:], w1[:])
        v.cp(w2[:], wi32[:])
        v.stt(w1[:], w2[:], -2 * PI, arg, MUL, ADD)
        v.ts(w2[:], w1[:], PI, ALU.is_gt)
        v.stt(w1[:], w2[:], -2 * PI, w1[:], MUL, ADD)
        v.ts(w2[:], w1[:], -PI, ALU.is_lt)
        v.stt(w1[:], w2[:], 2 * PI, w1[:], MUL, ADD)
        v.act(out, w1[:], AF.Sin)

    for d in range(2):
        for (dst, key) in ((lre, "lamre"), (lim, "lamim"), (lst, "logstep")):
            S.dma("sp", sem, dst[:], dr[key][d], writes=[v.r])
        S.dma("sp", sem, bre[:].rearrange("p a b -> p (a b)"), dr["bre"][d], writes=[v.r])
        S.dma("sp", sem, bim[:].rearrange("p a b -> p (a b)"), dr["bim"][d], writes=[v.r])
        v.ts(lre[:], lre[:], -1e-4, ALU.min)
        v.act(dl[:], lst[:], AF.Exp)
        v.tt(a[:], lre[:], dl[:], MUL)
        v.tt(th[:], lim[:], dl[:], MUL)
        v.act(mag[:], a[:], AF.Exp)
        v.act(imag[:], a[:], AF.Exp, scale=-1.0)
        sinred(sn[:], th[:])
        v.ts(w3[:], th[:], PI / 2, ADD)
        sinred(cs[:], w3[:])
        v.tt(lr[:], mag[:], cs[:], MUL)
        v.tt(li[:], mag[:], sn[:], MUL)
        v.tt(ir[:], imag[:], cs[:], MUL)
        v.tt(ii[:], imag[:], sn[:], MUL)
        v.ts(ii[:], ii[:], -1.0, MUL)
        v.tt(w1[:], lre[:], lre[:], MUL)
        v.tt(w2[:], lim[:], lim[:], MUL)
        v.tt(w1[:], w1[:], w2[:], ADD)
        v.recip(w4[:], w1[:])
        v.ts(w3[:], lr[:], -1.0, ADD)
        v.tt(w1[:], w3[:], lre[:], MUL)
        v.tt(w2[:], li[:], lim[:], MUL)
        v.tt(w1[:], w1[:], w2[:], ADD)
        v.tt(cr[:], w1[:], w4[:], MUL)
        v.tt(w1[:], li[:], lre[:], MUL)
        v.tt(w2[:], w3[:], lim[:], MUL)
        v.tt(w1[:], w1[:], w2[:], SUB)
        v.tt(ci[:], w1[:], w4[:], MUL)
        v.ms(btp[0][:], 0.0)
        v.ms(btp[1][:], 0.0)
        for hf in range(2):
            p0, p1 = hf * 64, hf * 64 + 64
            crb = cr[p0:p1, :].unsqueeze(2).to_broadcast([64, 64, 16])
            cib = ci[p0:p1, :].unsqueeze(2).to_broadcast([64, 64, 16])
            v.tt(bt1[p0:p1], bre[p0:p1], crb, MUL)
            v.tt(bt2[p0:p1], bim[p0:p1], cib, MUL)
            v.tt(btp[0][p0:p1, :, hf * 16:hf * 16 + 16], bt1[p0:p1], bt2[p0:p1], SUB)
            v.tt(bt1[p0:p1], bim[p0:p1], crb, MUL)
            v.tt(bt2[p0:p1], bre[p0:p1], cib, MUL)
            v.tt(btp[1][p0:p1, :, hf * 16:hf * 16 + 16], bt1[p0:p1], bt2[p0:p1], ADD)
        for ri in range(2):
            for ct in range(KT):
                ps = k.bank[ct % 2]
                k.TR(ps.t[:, 0:128], btp[ri][:, 4 * ct:4 * ct + 4, :].rearrange("p a b -> p (a b)"), ident[:], [v.r], [ps.r])
                k.CP("act", k.BL.t[:, d, ri, ct, :], ps.t[:, 0:128], [ps.r], [k.BL.r])
        for tbl, (br_, bi_) in enumerate(((ir, ii), (lr, li))):
            rev = (d == 1)

            def sl(lo, hi):
                return slice(128 - hi, 128 - lo) if rev else slice(lo, hi)
            v.ms(tabr[:, :, sl(0, 1)], 1.0)
            v.ms(tabi[:, :, sl(0, 1)], 0.0)
            v.cp(pwr[:], br_[:])
            v.cp(pwi[:], bi_[:])
            n = 1
            while n < 128:
                pr_b = pwr[:].unsqueeze(2).to_broadcast([128, 64, n])
                pi_b = pwi[:].unsqueeze(2).to_broadcast([128, 64, n])
                src, dst = sl(0, n), sl(n, 2 * n)
                v.tt(tm1[:, :, 0:n], tabr[:, :, src], pr_b, MUL)
                v.tt(tm2[:, :, 0:n], tabi[:, :, src], pi_b, MUL)
                v.tt(tabr[:, :, dst], tm1[:, :, 0:n], tm2[:, :, 0:n], SUB)
                v.tt(tm1[:, :, 0:n], tabr[:, :, src], pi_b, MUL)
                v.tt(tm2[:, :, 0:n], tabi[:, :, src], pr_b, MUL)
                v.tt(tabi[:, :, dst], tm1[:, :, 0:n], tm2[:, :, 0:n], ADD)
                v.tt(w1[:], pwr[:], pwr[:], MUL)
                v.tt(w2[:], pwi[:], pwi[:], MUL)
                v.tt(w3[:], pwr[:], pwi[:], MUL)
                v.tt(pwr[:], w1[:], w2[:], SUB)
                v.ts(pwi[:], w3[:], 2.0, MUL)
                n *= 2
            for ct in range(KT):
                S.dma("sp", sem, dr["TAB"][d, ct][:, :, 2 * tbl, :], tabr[:, 4 * ct:4 * ct + 4, :], reads=[v.r])
                S.dma("sp", sem, dr["TAB"][d, ct][:, :, 2 * tbl + 1, :], tabi[:, 4 * ct:4 * ct + 4, :], reads=[v.r])
            if tbl == 1:
                v.cp(k.L128.t[:, d, 0, :], pwr[:])
                v.cp(k.L128.t[:, d, 1, :], pwi[:])
                v.cmul(k.L127.t[:, d, 0, :], k.L127.t[:, d, 1, :], pwr[:], pwi[:], ir[:], ii[:], w1[:], w2[:])
                lr_b = lr[:].unsqueeze(2).to_broadcast([128, 64, 64])
                li_b = li[:].unsqueeze(2).to_broadcast([128, 64, 64])
                for hj in range(2):
                    js = slice(hj * 64, hj * 64 + 64)
                    v.tt(tm1[:], tabr[:, :, js], lr_b, MUL)
                    v.tt(tm2[:], tabi[:, :, js], li_b, MUL)
                    v.tt(ppb[:, :, 0, js], tm1[:], tm2[:], SUB)
                    v.tt(tm1[:], tabr[:, :, js], li_b, MUL)
                    v.tt(tm2[:], tabi[:, :, js], lr_b, MUL)
                    v.tt(ppb[:, :, 1, js], tm1[:], tm2[:], ADD)
                for ct in range(KT):
                    S.dma("sp", sem, dr["PPL"][d, ct], ppb[:, 4 * ct:4 * ct + 4, :, :], reads=[v.r])


def pass_b(k, dr, TILES):
    S = k.S
    NCH = TILES * 4
    u32 = [k.sb("u32_%d" % i, [128, KT, NT], F32, nres=KT) for i in range(1)]
    ub = k.sb("ub", [128, KT, NT], BF16, nres=KT)
    s5d = k.sb("s5d", [128, KT], F32)
    tabs = Stream(k, "TABS", 2048, F32, 4, hold=2)
    nb = 3
    tq = [[k.sb("tq%d_%d" % (b, i), [128, NT], F32) for i in range(4)] for b in range(nb)]
    gq = [[k.sb("gq%d_%d" % (b, i), [128, NT], F32, nres=4) for i in range(2)] for b in range(nb)]
    xq = [[k.sb("xq%d_%d" % (b, i), [128, NT], BF16) for i in range(4)] for b in range(nb)]
    gend = [k.sb("gend%d" % i, [128, 2, 64, 4, 2], F32) for i in range(2)]
    yst = [k.sb("yst%d" % i, [128, NT], F32) for i in range(2)]
    sem_u = S.new_dma_sem("pbu")
    sem_s = S.new_dma_sem("pbs")
    sem_y = [S.new_dma_sem("pby%d" % i) for i in range(2)]
    sem_g = [S.new_dma_sem("pbg%d" % i) for i in range(2)]
    S.dma("sp", sem_s, s5d.t[:], dr["s5d"][:, :], writes=[s5d.r])
    CL = k.CL

    def v3(t):
        return t[:].rearrange("p (c j) -> p c j", c=4)

    items = []
    for i in range(TILES):
        for ct in range(KT):
            for d in range(2):
                for q in range(4):
                    items.append(dict(i=i, ct=ct, d=d, q=q))
    NI = len(items)

    def tile_start(i):
        t0 = i * NT
        u = u32[0]
        S.dma("pool", sem_u, u.t[:], dr["U"].rearrange("(c p) t -> p c t", p=128)[:, :, t0:t0 + NT], writes=u.rs)
        for kt in range(KT):
            k.CP("act", ub.t[:, kt, :], u.t[:, kt, :], [u.rs[kt]], [ub.rs[kt]])
        tabs.push([dr["TAB"][d, ct].rearrange("p a b c -> p (a b c)") for ct in range(KT) for d in range(2)])

    cur_tb = {}

    def st0(n):
        it = items[n]
        i, ct, d, q = it["i"], it["ct"], it["d"], it["q"]
        if q == 0:
            cur_tb["tb"] = tabs.get()
        it["tb"] = cur_tb["tb"]
        b = n % nb
        par, pai = k.bank[2 * b], k.bank[2 * b + 1]
        rows = slice(32 * q, 32 * q + 32)
        k.MM(par.t[:], k.BL.t[rows, d, 0, ct, :], ub.t[rows, ct, :], True, True, [k.BL.r, ub.rs[ct]], [par.r], tile_position=(32 * q, 0))
        k.MM(pai.t[:], k.BL.t[rows, d, 1, ct, :], ub.t[rows, ct, :], True, True, [k.BL.r, ub.rs[ct]], [pai.r], tile_position=(32 * q, 0))

    def tbb(it, w):
        tb4 = it["tb"].t[:].rearrange("p (q w j) -> p q w j", q=4, w=4)
        return tb4[:, it["q"], w, :].unsqueeze(1).to_broadcast([128, 4, 128])

    def st1(n):
        it = items[n]
        b = n % nb
        tb = it["tb"]
        par, pai = k.bank[2 * b], k.bank[2 * b + 1]
        t1, t2, t3, t4 = tq[b]
        k.TT("dve", v3(t1.t), v3(par.t), tbb(it, 0), MUL, [par.r, tb.r], [t1.r])
        k.TT("dve", v3(t2.t), v3(pai.t), tbb(it, 1), MUL, [pai.r, tb.r], [t2.r])
        k.TT("dve", v3(t3.t), v3(par.t), tbb(it, 1), MUL, [par.r, tb.r], [t3.r])
        k.TT("dve", v3(t4.t), v3(pai.t), tbb(it, 0), MUL, [pai.r, tb.r], [t4.r])

    def st2(n):
        it = items[n]
        d = it["d"]
        b = n % nb
        t1, t2, t3, t4 = tq[b]
        gr, gi = gq[b]
        for c in range(4):
            def cs_(t):
                vv = v3(t.t)[:, c, :]
                return vv[:, ::-1] if d == 1 else vv
            k.SCAN(cs_(gr), cs_(t1), cs_(t2), 0.0, ADD, SUB, [t1.r, t2.r], [gr.rs[c]])
            k.SCAN(cs_(gi), cs_(t3), cs_(t4), 0.0, ADD, ADD, [t3.r, t4.r], [gi.rs[c]])

    def st3(n):
        it = items[n]
        i, ct, d, q = it["i"], it["ct"], it["d"], it["q"]
        b = n % nb
        tb = it["tb"]
        gr, gi = gq[b]
        ge = gend[i % 2]
        pr = 4 * ct + q
        e = 127 if d == 0 else 0
        k.CP("pool", ge.t[:, d, pr, :, 0], v3(gr.t)[:, :, e], gr.rs, [ge.r])
        k.CP("pool", ge.t[:, d, pr, :, 1], v3(gi.t)[:, :, e], gi.rs, [ge.r])
        x1, x2, x3, x4 = xq[b]
        k.TT("pool", v3(x1.t), v3(gr.t), tbb(it, 2), MUL, gr.rs + [tb.r], [x1.r])
        k.TT("pool", v3(x2.t), v3(gi.t), tbb(it, 3), MUL, gi.rs + [tb.r], [x2.r])
        k.TT("pool", v3(x3.t), v3(gr.t), tbb(it, 3), MUL, gr.rs + [tb.r], [x3.r])
        k.TT("pool", v3(x4.t), v3(gi.t), tbb(it, 2), MUL, gi.rs + [tb.r], [x4.r])

    def st4(n):
        it = items[n]
        i, ct, d, q = it["i"], it["ct"], it["d"], it["q"]
        b = n % nb
        pr = 4 * ct + q
        t0 = i * NT
        yb = k.bank[6 + ct % 2]
        rows = slice(32 * q, 32 * q + 32)
        x1, x2, x3, x4 = xq[b]
        yo = yb.t[rows, :]
        first = (d == 0)
        kw = dict(skip_group_check=True, tile_position=(0, 32 * q))
        k.MM(yo, CL.t[:, d, pr, 0, :], x1.t[:], first, False, [CL.r, x1.r], [yb.r], **kw)
        k.MM(yo, CL.t[:, d, pr, 1, :], x2.t[:], False, False, [CL.r, x2.r], [yb.r], **kw)
        k.MM(yo, CL.t[:, d, pr, 2, :], x3.t[:], False, False, [CL.r, x3.r], [yb.r], **kw)
        k.MM(yo, CL.t[:, d, pr, 2, :], x4.t[:], False, d == 1, [CL.r, x4.r], [yb.r], **kw)
        if d == 1 and q == 3:
            u = u32[0]
            ys = yst[ct % 2]
            k.STT(ys.t[:], u.t[:, ct, :], s5d.t[:, ct:ct + 1], yb.t[:], MUL, ADD, [u.rs[ct], s5d.r, yb.r], [ys.r])
            S.dma("sp", sem_y[ct % 2], dr["YL"][ct * 128:(ct + 1) * 128, t0:t0 + NT], ys.t[:], reads=[ys.r])
            if ct == KT - 1:
                ge = gend[i % 2]
                S.dma("sp", sem_g[i % 2], dr["GE"][i], ge.t[:].rearrange("p a b c e -> p (a b c e)"), reads=[ge.r])

    per = KT * 8
    for i in range(TILES):
        tile_start(i)
        lo, hi = i * per, (i + 1) * per
        for n in range(lo - 2, hi + 1):
            if lo <= n + 2 < hi:
                st0(n + 2)
            if lo <= n + 1 < hi:
                st1(n + 1)
            if lo <= n < hi:
                st2(n)
                st3(n)
            if lo <= n - 1 < hi:
                st4(n - 1)


def s5_chain(k, dr, TILES, SEGT):
    S = k.S
    NCH = TILES * 4
    v = VC(k, "chain")
    sem = S.new_dma_sem("chain")
    gE = k.sb("gE", [128, 64, NCH, 2], F32).t
    Sr = k.sb("Sr", [128, 64, NCH], F32).t
    Si = k.sb("Si", [128, 64, NCH], F32).t
    Hin = k.sb("Hin", [128, 64, NCH, 2], F32).t
    t1 = k.sb("ct1", [128, 64, NCH], F32).t
    t2 = k.sb("ct2", [128, 64, NCH], F32).t
    for d in range(2):
        for i in range(TILES):
            S.dma("sp", sem, gE[:, :, 4 * i:4 * i + 4, :],
                  dr["GE"][i].rearrange("p (a b c e) -> p a b c e", a=2, b=64, c=4)[:, d], reads=[], writes=[v.r])
        Lr = k.L127.t[:, d, 0, :].unsqueeze(2).to_broadcast([128, 64, NCH])
        Li = k.L127.t[:, d, 1, :].unsqueeze(2).to_broadcast([128, 64, NCH])
        v.tt(t1[:], gE[:, :, :, 0], Lr, MUL)
        v.tt(t2[:], gE[:, :, :, 1], Li, MUL)
        v.tt(Sr[:], t1[:], t2[:], SUB)
        v.tt(t1[:], gE[:, :, :, 0], Li, MUL)
        v.tt(t2[:], gE[:, :, :, 1], Lr, MUL)
        v.tt(Si[:], t1[:], t2[:], ADD)
        ar = k.L128.t[:, d, 0, :]
        ai = k.L128.t[:, d, 1, :]
        order = list(range(NCH)) if d == 0 else list(range(NCH - 1, -1, -1))
        a1 = t1[:, :, 0]
        a2 = t2[:, :, 0]
        for n, kk in enumerate(order):
            if n == 0:
                v.ms(Hin[:, :, kk, :], 0.0)
            if n == NCH - 1:
                break
            nxt = order[n + 1]
            pr_, pi_ = Hin[:, :, kk, 0], Hin[:, :, kk, 1]
            v.tt(a1, ar, pr_, MUL)
            v.tt(a2, ai, pi_, MUL)
            v.tt(a1, a1, a2, SUB)
            v.tt(Hin[:, :, nxt, 0], a1, Sr[:, :, kk], ADD)
            v.tt(a1, ar, pi_, MUL)
            v.tt(a2, ai, pr_, MUL)
            v.tt(a1, a1, a2, ADD)
            v.tt(Hin[:, :, nxt, 1], a1, Si[:, :, kk], ADD)
            bnd = (nxt % (4 * SEGT) == 0) if d == 0 else ((nxt + 1) % (4 * SEGT) == 0)
            if bnd:
                v.ts(Hin[:, :, nxt, :], Hin[:, :, nxt, :], k.keep.t[:, 0:1], MUL)
        for i in range(TILES):
            S.dma("sp", sem, dr["HIN"][i].rearrange("p (a b c e) -> p a b c e", a=2, b=64, c=4)[:, d],
                  Hin[:, :, 4 * i:4 * i + 4, :], reads=[v.r], writes=[v.r])


def s5_carry_tile(k, dr, i, yact):
    S = k.S
    t0 = i * NT
    c5 = k.c5
    hin = c5["hin"][i % 2]
    S.dma("pool", c5["sem_h"][i % 2], hin.t[:].rearrange("p a b c e -> p (a b c e)"), dr["HIN"][i], writes=[hin.r])
    c5["pp"].push([dr["PPL"][d, ct].rearrange("p a b c -> p (a b c)") for ct in range(KT) for d in range(2)])
    c5["cs"].push([dr["CRI"][ct] for ct in range(KT)])

    def build_ch(ct):
        cs = c5["cs"].get()
        c4v = cs.t[:].rearrange("p (d q w c) -> p d q w c", d=2, q=4, w=2)
        chb = c5["chb"][ct % 2]
        for d in range(2):
            Crb = c4v[:, d, :, 0, :].unsqueeze(2).to_broadcast([128, 4, 4, 32])
            Cib = c4v[:, d, :, 1, :].unsqueeze(2).to_broadcast([128, 4, 4, 32])
            Hr = hin.t[:, d, 4 * ct:4 * ct + 4, :, 0].unsqueeze(3).to_broadcast([128, 4, 4, 32])
            Hi = hin.t[:, d, 4 * ct:4 * ct + 4, :, 1].unsqueeze(3).to_broadcast([128, 4, 4, 32])
            A, B = c5["ta"][d], c5["tb"][d]
            k.TT("dve", A.t[:], Crb, Hr, MUL, [cs.r, hin.r], [A.r])
            k.TT("dve", B.t[:], Cib, Hi, MUL, [cs.r, hin.r], [B.r])
            k.TT("dve", chb.t[:, d, :, :, 0, :], A.t[:], B.t[:], SUB, [A.r, B.r], [chb.r])
            k.TT("dve", A.t[:], Crb, Hi, MUL, [cs.r, hin.r], [A.r])
            k.TT("dve", B.t[:], Cib, Hr, MUL, [cs.r, hin.r], [B.r])
            k.STT(chb.t[:, d, :, :, 1, :], A.t[:], -1.0, B.t[:], MUL, SUB, [A.r, B.r], [chb.r])
        return chb

    chn = build_ch(0)
    for ct in range(KT):
        chb = chn
        yb = k.bank[6 + ct % 2]
        pps = [c5["pp"].get(), c5["pp"].get()]
        yl = c5["yl"][ct % 2]
        S.dma("pool", c5["sem_yl"][ct % 2], yl.t[:], dr["YL"][ct * 128:(ct + 1) * 128, t0:t0 + NT], writes=[yl.r])
        for q in range(4):
            n = 0
            for c4 in range(4):
                for d in range(2):
                    pv = pps[d].t[:].rearrange("p (q r j) -> p q r j", q=4, r=2)
                    for ri in range(2):
                        k.MM(yb.t[32 * q:32 * q + 32, c4 * 128:(c4 + 1) * 128], chb.t[:, d, q, c4, ri, :], pv[:, q, ri, :],
                             n == 0, n == 15, [chb.r, pps[d].r], [yb.r], skip_group_check=True, tile_position=(0, 32 * q))
                        n += 1
        if ct + 1 < KT:
            chn = build_ch(ct + 1)
        k.TT("dve", yl.t[:], yl.t[:], yb.t[:], ADD, [yl.r, yb.r], [yl.r])
        if "YA" in dr:
            S.dma("pool", c5["sem_yl"][ct % 2], dr["YA"][ct * 128:(ct + 1) * 128, t0:t0 + NT], yl.t[:], reads=[yl.r])
        g2 = c5["g2"][ct % 2]
        k.ACTF(g2.t[:], yl.t[:], AF.Square, [yl.r], [g2.r])
        k.TS("dve", g2.t[:], g2.t[:], 0.044715, 1.0, MUL, ADD, [g2.r], [g2.r])
        k.TT("dve", g2.t[:], g2.t[:], yl.t[:], MUL, [g2.r, yl.r], [g2.r])
        k.ACTF(g2.t[:], g2.t[:], AF.Sigmoid, [g2.r], [g2.r], scale=1.5957691216057308)
        k.TT("dve", yact.t[:, ct, :], g2.t[:], yl.t[:], MUL, [g2.r, yl.r], [yact.rs[ct]])


def pass_d(k, dr, TILES, SEGT):
    S = k.S
    NTOK = TILES * NT
    ws = Stream(k, "WD", 1024, BF16, 3)
    xmh = [k.sb("xmh%d" % i, [128, CT, NT + 4], BF16) for i in range(2)]
    sem_x = [S.new_dma_sem("xmh%d" % i) for i in range(2)]
    xcb = [k.sb("xcb%d" % i, [128, NT], BF16) for i in range(2)]
    qst = [k.sb("qst%d" % i, [128, NT], BF16) for i in range(2)]
    kst = [k.sb("kst%d" % i, [128, NT], BF16) for i in range(2)]
    vst = [k.sb("vst%d" % i, [128, NT], BF16) for i in range(2)]
    kts = [k.sb("kts%d" % i, [128, 4, 128], BF16) for i in range(2)]
    vts = [k.sb("vts%d" % i, [128, 4, 128], BF16) for i in range(2)]
    gst = [k.sb("gst%d" % i, [128, 4, 64], F32) for i in range(2)]
    sems = {n: [S.new_dma_sem("pd%s%d" % (n, i)) for i in range(2)] for n in ("xc", "q", "k", "kt", "vt", "g", "w")}
    wgf = k.sb("wgf", [128, 3 * CT * 64], F32)
    wg = k.sb("wg", [128, 3, CT, 64], BF16)
    bgf = k.sb("bgf", [128, 64], F32)
    bgb = k.sb("bgb", [128, 64], BF16)
    onesb = k.sb("onesb", [128, 128], BF16)
    zerob = k.sb("zerob", [128, 256], BF16)
    k.MS("dve", zerob.t[:], 0.0, [zerob.r])
    k.MS("dve", bgf.t[:], 0.0, [bgf.r])
    S.dma("sp", sems["w"][0], wgf.t[:], dr["wg"][:, :], writes=[wgf.r])
    k.CP("dve", wg.t[:].rearrange("p a b c -> p (a b c)"), wgf.t[:], [wgf.r], [wg.r])
    S.dma("sp", sems["w"][1], bgf.t[0:1, :], dr["bgate"][:, :], writes=[bgf.r])
    k.CP("dve", bgb.t[:], bgf.t[:], [bgf.r], [bgb.r])
    k.MS("dve", onesb.t[:], 1.0, [onesb.r])
    XMv = dr["XM"].rearrange("(c p) t -> p c t", p=128)
    n = 0
    for i in range(TILES):
        t0 = i * NT
        xb = xmh[i % 2]
        lo = max(t0 - 2, 0)
        hi = min(t0 + NT + 2, NTOK)
        S.dma("pool", sem_x[i % 2], xb.t[:, :, lo - (t0 - 2):hi - (t0 - 2)], XMv[:, :, lo:hi], writes=[xb.r])
        if i == 0:
            k.MS("pool", xb.t[:, :, 0:2], 0.0, [xb.r])
        elif i % SEGT == 0:
            k.TS("pool", xb.t[:, :, 0:2], xb.t[:, :, 0:2], k.keep.t[:, 0:1], None, MUL, None, [xb.r, k.keep.r], [xb.r])
        if i == TILES - 1:
            k.MS("pool", xb.t[:, :, NT + 2:NT + 4], 0.0, [xb.r])
        elif (i + 1) % SEGT == 0:
            k.TS("pool", xb.t[:, :, NT + 2:NT + 4], xb.t[:, :, NT + 2:NT + 4], k.keep.t[:, 0:1], None, MUL, None, [xb.r, k.keep.r], [xb.r])
        ws.push([dr["mlD_b"][ct] for ct in range(CT)])
        gps = k.bank[7]
        k.MM(gps.t[:, 0:256], zerob.t[:, 0:128], zerob.t[:], True, False, [zerob.r], [gps.r], skip_group_check=True)
        for ct in range(CT):
            w = ws.get()
            b = n % 2
            n += 1
            pc = k.bank[b]
            for tau in range(5):
                k.MM(pc.t[:], w.t[:, tau * 128:(tau + 1) * 128], xb.t[:, ct, tau:tau + NT], tau == 0, tau == 4, [w.r, xb.r], [pc.r])
            xc = xcb[b]
            k.ACTF(xc.t[:], pc.t[:], AF.Silu, [pc.r, k.mlvec.r], [xc.r], bias=k.mlvec.t[:, ct:ct + 1])
            S.dma("sp", sems["xc"][b], dr["XC"][ct * 128:(ct + 1) * 128, t0:t0 + NT], xc.t[:], reads=[xc.r])
            pq, pk, pv = k.bank[2], k.bank[3], k.bank[4]
            k.MM(pq.t[:], w.t[:, 640:768], xc.t[:], True, True, [w.r, xc.r], [pq.r])
            k.MM(pk.t[:], w.t[:, 768:896], xc.t[:], True, True, [w.r, xc.r], [pk.r])
            k.MM(pv.t[:], w.t[:, 896:1024], xb.t[:, ct, 2:NT + 2], True, True, [w.r, xb.r], [pv.r])
            q_, k_, v_ = qst[b], kst[b], vst[b]
            k.CP("dve", q_.t[:], pq.t[:], [pq.r], [q_.r])
            k.CP("act", k_.t[:], pk.t[:], [pk.r], [k_.r])
            k.CP("dve", v_.t[:], pv.t[:], [pv.r], [v_.r])
            S.dma("sp", sems["q"][b], dr["QT"][ct * 128:(ct + 1) * 128, t0:t0 + NT], q_.t[:], reads=[q_.r])
            S.dma("sp", sems["k"][b], dr["KT"][ct * 128:(ct + 1) * 128, t0:t0 + NT], k_.t[:], reads=[k_.r])
            pkt, pvt = k.bank[5], k.bank[6]
            for c4 in range(4):
                k.MM(pkt.t[:, c4 * 128:(c4 + 1) * 128], xc.t[:, c4 * 128:(c4 + 1) * 128], w.t[:, 768:896], True, True, [w.r, xc.r], [pkt.r])
                k.MM(pvt.t[:, c4 * 128:(c4 + 1) * 128], xb.t[:, ct, 2 + c4 * 128:2 + (c4 + 1) * 128], w.t[:, 896:1024], True, True, [w.r, xb.r], [pvt.r])
            kt_, vt_ = kts[b], vts[b]
            k.CP("act", kt_.t[:].rearrange("p a b -> p (a b)"), pkt.t[:], [pkt.r], [kt_.r])
            k.CP("dve", vt_.t[:].rearrange("p a b -> p (a b)"), pvt.t[:], [pvt.r], [vt_.r])
            S.dma("sp", sems["kt"][b], dr["KTOK"][4 * i:4 * i + 4, :, ct * 128:(ct + 1) * 128].rearrange("c t h -> t c h"), kt_.t[:], reads=[kt_.r])
            S.dma("sp", sems["vt"][b], dr["VTOK"][4 * i:4 * i + 4, :, ct * 128:(ct + 1) * 128].rearrange("c t h -> t c h"), vt_.t[:], reads=[vt_.r])
            for c4 in range(4):
                cs_ = slice(c4 * 128, (c4 + 1) * 128)
                for j, src in enumerate((q_, k_, v_)):
                    first = False
                    k.MM(gps.t[:, c4 * 64:(c4 + 1) * 64], src.t[:, cs_], wg.t[:, j, ct, :], first, False, [src.r, wg.r], [gps.r], skip_group_check=True)
        for c4 in range(4):
            k.MM(gps.t[:, c4 * 64:(c4 + 1) * 64], onesb.t[:], bgb.t[:], False, c4 == 3, [onesb.r, bgb.r], [gps.r], skip_group_check=True)
        g_ = gst[i % 2]
        k.CP("dve", g_.t[:].rearrange("p a b -> p (a b)"), gps.t[:, 0:256], [gps.r], [g_.r])
        S.dma("sp", sems["g"][i % 2], dr["G"][4 * i:4 * i + 4].rearrange("c t g -> t c g"), g_.t[:], reads=[g_.r])


def ml_prep(k, dr, TILES, SEGT):
    S = k.S
    NCH = TILES * 4
    v = VC(k, "mlprep")
    sem = S.new_dma_sem("mlprep")
    gall = k.sb("gall", [128, NCH, 64], F32).t
    lf = k.sb("lfall", [128, NCH, 2, 16], F32).t
    bc = k.sb("bcum", [128, NCH, 2, 16], F32).t
    io_i = k.sb("mio_i", [128, 128], I32).t
    io_f = k.sb("mio_f", [128, 128], F32).t
    lnsc = k.sb("lnsc", [128, 1], F32).t
    onec = k.sb("onec", [128, 1], F32).t
    S.dma("sp", sem, gall[:], dr["G"].rearrange("c t g -> t c g"), writes=[v.r])
    S.op("pool", lambda h: h.iota(io_i[:], [[1, 128]], 0, -1), [], [v.r])
    v.cp(io_f[:], io_i[:])
    v.ts(k.TRI[0].t[:], io_f[:], 0.0, ALU.is_ge)
    v.ts(k.TRI[1].t[:], io_f[:], 0.0, ALU.is_le)
    v.ts(k.ident32.t[:], io_f[:], 0.0, ALU.is_equal)
    v.ms(lnsc[:], -0.5 * float(np.log(DH)))
    v.ms(onec[:], 1.0)
    g5 = gall[:].rearrange("p c (d w h) -> p c d w h", d=2, w=2)
    v.act(lf[:], g5[:, :, :, 1, :], AF.Exp, scale=-1.0)
    v.act(lf[:], lf[:], AF.Ln, bias=onec[:, 0:1])
    v.ts(lf[:], lf[:], -1.0, MUL)
    half = NCH // 2 if NCH >= 2 else 1
    for d in range(2):
        for (lhs, dst) in ((k.TRI[d], bc), (k.ones32, k.EG.t)):
            for c0 in range(0, NCH, 32):
                c1 = min(c0 + 32, NCH)
                ps = k.bank[(c0 // 32) % 2]
                k.MM(ps.t[:, 0:(c1 - c0) * 16].rearrange("p (c h) -> p c h", h=16), lhs.t[:], lf[:, c0:c1, d, :], True, True, [v.r], [ps.r])
                k.CP("dve", dst[:, c0:c1, d, :], ps.t[:, 0:(c1 - c0) * 16].rearrange("p (c h) -> p c h", h=16), [ps.r], [v.r])
    v.tt(k.ED.t[:], g5[:, :, :, 0, :], bc[:], SUB)
    v.act(k.ED.t[:], k.ED.t[:], AF.Exp, bias=lnsc[:, 0:1])
    v.act(k.EB.t[:], bc[:], AF.Exp, scale=-1.0)
    v.act(k.EG.t[:], k.EG.t[:], AF.Exp)
    for kk in range(NCH):
        if (kk + 1) % (4 * SEGT) == 0 and kk + 1 < NCH:
            v.ts(k.EG.t[:, kk, 0, :], k.EG.t[:, kk, 0, :], k.keep.t[:, 0:1], MUL)
        if kk % (4 * SEGT) == 0 and kk > 0:
            v.ts(k.EG.t[:, kk, 1, :], k.EG.t[:, kk, 1, :], k.keep.t[:, 0:1], MUL)
    k.mlprep_res = v.r


def pass_e(k, dr, TILES):
    S = k.S
    NCH = TILES * 4
    PR = k.mlprep_res
    C32 = k.sb("C32", [128, NH, 2, 257], F32, nres=NH)
    Cb = k.sb("Cb", [128, NH, 2, 257], BF16, nres=NH)
    qT = [k.sb("qT%d" % i, [128, CT, 128], BF16) for i in range(2)]
    kT = [k.sb("kT%d" % i, [128, CT, 128], BF16) for i in range(2)]
    ktk = [k.sb("ktk%d" % i, [128, DI], BF16) for i in range(2)]
    vau = [k.sb("vau%d" % i, [128, NH, 260], BF16) for i in range(2)]
    sem_l = [[S.new_dma_sem("pe%d_%d" % (j, i)) for i in range(2)] for j in range(4)]
    hx = k.sb("hx", [128, NH, 257], F32, nres=NH)
    hbuf = k.sb("hbuf", [128, DI], F32, nres=NH)
    sem_hb = S.new_dma_sem("hbst")
    sem_hl = S.new_dma_sem("hbld")
    khat = k.sb("khat", [128, NH, DH], BF16, nres=NH)
    smT = k.sb("smT", [128, NH, 128], BF16, nres=NH)
    dn = k.sb("dn", [128, NH], F32)
    dn2 = k.sb("dn2", [128, NH], F32)
    xck = [k.sb("xck%d" % i, [128, CT // 2, 128], BF16) for i in range(1)]
    zk = [k.sb("zk%d" % i, [128, CT // 2, 128], F32) for i in range(1)]
    oab = [k.sb("oab%d" % i, [128, CT // 2, 128], BF16, nres=CT // 2) for i in range(1)]
    skb = [k.sb("skb%d" % i, [128, 128], F32) for i in range(2)]
    o1b = [k.sb("o1b%d" % i, [128, 128], F32) for i in range(2)]
    bst = k.sb("bst", [128, NH, 6], F32, nres=NH)
    mv = k.sb("mv", [128, NH, 2], F32)
    rs = k.sb("rs", [128, NH], F32)
    sem_p = [S.new_dma_sem("pep%d" % i) for i in range(4)]
    pq = [[Res("pq%d_%d" % (b, j)) for j in range(4)] for b in range(2)]
    for b in range(2):
        k.MS("pool", vau[b].t[:, :, 256:260], 1.0, [vau[b].r])
    QTv = dr["QT"].rearrange("(c p) t -> p c t", p=128)
    KTv = dr["KT"].rearrange("(c p) t -> p c t", p=128)
    XCv = dr["XC"].rearrange("(c p) t -> p c t", p=128)
    Zv = dr["Z"].rearrange("(c p) t -> p c t", p=128)
    OAv = dr["OA"].rearrange("(c p) t -> p c t", p=128)
    hx3 = hx.t
    seq = [(d, kk) for d in (1, 0) for kk in (list(range(NCH)) if d == 0 else list(range(NCH - 1, -1, -1)))]

    def emit_loads(n):
        d, kk = seq[n]
        b = n % 2
        ts_ = slice(kk * 128, (kk + 1) * 128)
        q_, k_, kt_, va = qT[b], kT[b], ktk[b], vau[b]
        S.dma("sp", sem_l[0][b], q_.t[:], QTv[:, :, ts_], writes=[q_.r])
        S.dma("sp", sem_l[1][b], k_.t[:], KTv[:, :, ts_], writes=[k_.r])
        S.dma("pool", sem_l[2][b], kt_.t[:], dr["KTOK"][kk], writes=[kt_.r])
        S.dma("pool", sem_l[3][b], va.t[:, :, 0:256], dr["VTOK"][kk].rearrange("t (h e) -> t h e", h=NH), writes=[va.r])
        k.MS("pool", va.t[:, :, 256:260], 1.0, [va.r])

    def stage1(n):
        d, kk = seq[n]
        b = n % 2
        q_, k_, kt_ = qT[b], kT[b], ktk[b]
        for g4 in range(NH // 4):
            pS = k.bank[g4 % 2]
            for h in range(4 * g4, 4 * g4 + 4):
                cols = slice((h % 4) * 128, (h % 4 + 1) * 128)
                for i2 in range(2):
                    k.MM(pS.t[:, cols], k_.t[:, 2 * h + i2, :], q_.t[:, 2 * h + i2, :], i2 == 0, i2 == 1, [k_.r, q_.r], [pS.r], skip_group_check=True)
            for h in range(4 * g4, 4 * g4 + 4):
                cols = slice((h % 4) * 128, (h % 4 + 1) * 128)
                ed = k.ED.t[:, kk, d, h:h + 1]
                k.STT(smT.t[:, h, :], pS.t[:, cols], ed, k.TRI[d].t[:], MUL, MUL, [pS.r, PR], [smT.rs[h]])
                hs = slice(h * DH, (h + 1) * DH)
                k.TS("dve", khat.t[:, h, :], kt_.t[:, hs], ed, None, MUL, None, [kt_.r, PR], [khat.rs[h]])

    emit_loads(0)
    for n in range(len(seq)):
        d, kk = seq[n]
        first_of_dir = (n == 0 or seq[n - 1][0] != d)
        if first_of_dir:
            for h in range(NH):
                k.MS("dve", C32.t[:, h], 0.0, [C32.rs[h]])
                k.MS("pool", Cb.t[:, h], 0.0, [Cb.rs[h]])
        if True:
            b = n % 2
            ts_ = slice(kk * 128, (kk + 1) * 128)
            q_, k_, kt_, va = qT[b], kT[b], ktk[b], vau[b]
            if n + 1 < len(seq):
                emit_loads(n + 1)
            if d == 0:
                S.dma("pool", sem_hl, hbuf.t[:], dr["HB"][kk], reads=[k.hbres], writes=hbuf.rs)
            if n == 0:
                stage1(0)
            for h in range(NH):
                eg = k.EG.t[:, kk, d, h:h + 1]
                k.ACTF(C32.t[:, h], C32.t[:, h], AF.Identity, [C32.rs[h], PR], [C32.rs[h]], scale=eg)
            for h in range(NH):
                pX = k.bank[2 + h % 2]
                for i2 in range(2):
                    k.MM(pX.t[:, 0:257], q_.t[:, 2 * h + i2, :], Cb.t[:, h, i2, :], i2 == 0, False, [q_.r, Cb.rs[h]], [pX.r])
                k.MM(pX.t[:, 0:257], smT.t[:, h, :], va.t[:, h, 0:257], False, True, [smT.rs[h], va.r], [pX.r])
                k.CP("act", hx3[:, h, :], pX.t[:, 0:257], [pX.r], [hx.rs[h]])
            for h in range(NH):
                pC = [k.bank[4 + 2 * (h % 2)], k.bank[5 + 2 * (h % 2)]]
                eg = k.EG.t[:, kk, d, h:h + 1]
                for i2 in range(2):
                    k.MM(pC[i2].t[:, 0:257], khat.t[:, h, i2 * 128:(i2 + 1) * 128], va.t[:, h, 0:257], True, True, [khat.rs[h], va.r], [pC[i2].r])
                    k.STT(C32.t[:, h, i2, :], pC[i2].t[:, 0:257], eg, C32.t[:, h, i2, :], MUL, ADD, [pC[i2].r, PR, C32.rs[h]], [C32.rs[h]])
                k.CP("act", Cb.t[:, h], C32.t[:, h], [C32.rs[h]], [Cb.rs[h]])
            if n + 1 < len(seq):
                stage1(n + 1)
            ycol = hx3[:, :, 256]
            k.TT("dve", dn.t[:], ycol, k.EB.t[:, kk, d, :], ALU.max, hx.rs + [PR], [dn.r])
            k.STT(dn2.t[:], ycol, -1.0, dn.t[:], MUL, ALU.max, hx.rs + [dn.r], [dn2.r])
            k.RECIP(dn2.t[:], dn2.t[:], [dn2.r], [dn2.r])
            hb3 = hbuf.t[:].rearrange("p (h e) -> p h e", h=NH)
            rb = dn2.t[:].unsqueeze(2).to_broadcast([128, NH, DH])
            if d == 1:
                k.TT("dve", hb3, hx3[:, :, 0:256], rb, MUL, hx.rs + [dn2.r], hbuf.rs)
                S.dma("sp", sem_hb, dr["HB"][kk], hbuf.t[:], reads=hbuf.rs, writes=[k.hbres])
                continue
            k.TT("dve", hx3[:, :, 0:256], hx3[:, :, 0:256], rb, MUL, hx.rs + [dn2.r], hx.rs)
            k.TT("pool", hb3, hb3, hx3[:, :, 0:256], ADD, hbuf.rs + hx.rs, hbuf.rs)
            for h in range(NH):
                hs = slice(h * DH, (h + 1) * DH)
                S.op("dve", lambda hh, h=h, hs=hs: hh.bn_stats(out=bst.t[:, h, :], in_=hbuf.t[:, hs]), [hbuf.rs[h]], [bst.rs[h]])
            for h in range(NH):
                S.op("dve", lambda hh, h=h: hh.bn_aggr(out=mv.t[:, h, :], in_=bst.t[:, h, :]), [bst.rs[h]], [mv.r])
            k.ACTF(rs.t[:], mv.t[:, :, 1], AF.Sqrt, [mv.r, k.epsc.r], [rs.r], bias=k.epsc.t[:, 0:1])
            k.RECIP(rs.t[:], rs.t[:], [rs.r], [rs.r])
            for h in range(NH):
                hs = slice(h * DH, (h + 1) * DH)
                k.TS("dve", hbuf.t[:, hs], hbuf.t[:, hs], mv.t[:, h, 0:1], rs.t[:, h:h + 1], SUB, MUL, [hbuf.rs[h], mv.r, rs.r], [hbuf.rs[h]])
            for hf in range(2):
                c0 = hf * (CT // 2)
                xc_, z_, oa_ = xck[0], zk[0], oab[0]
                S.dma("pool", sem_p[hf], xc_.t[:], XCv[:, c0:c0 + CT // 2, ts_], writes=[xc_.r])
                S.dma("pool", sem_p[2], z_.t[:], Zv[:, c0:c0 + CT // 2, ts_], writes=[z_.r])
                k.ACTF(z_.t[:], z_.t[:], AF.Sigmoid, [z_.r], [z_.r])
                for g4 in range(CT // 8):
                    pT = k.bank[g4 % 2]
                    for cl in range(4 * g4, 4 * g4 + 4):
                        ct = c0 + cl
                        cols = slice((cl % 4) * 128, (cl % 4 + 1) * 128)
                        k.TR(pT.t[:, cols], hbuf.t[:, ct * 128:(ct + 1) * 128], k.ident32.t[:], [hbuf.rs[ct // 2], PR], [pT.r])
                    for cl in range(4 * g4, 4 * g4 + 4):
                        ct = c0 + cl
                        cols = slice((cl % 4) * 128, (cl % 4 + 1) * 128)
                        sk = skb[ct % 2]
                        o1 = o1b[ct % 2]
                        k.ACTF(sk.t[:], xc_.t[:, cl, :], AF.Identity, [xc_.r, k.mlvec.r], [sk.r], scale=k.mlvec.t[:, 64 + ct:65 + ct])
                        k.STT(o1.t[:], pT.t[:, cols], k.mlvec.t[:, 32 + ct:33 + ct], sk.t[:], MUL, ADD, [pT.r, sk.r, k.mlvec.r], [o1.r])
                        k.TT("pool", oa_.t[:, cl, :], o1.t[:], z_.t[:, cl, :], MUL, [o1.r, z_.r], [oa_.rs[cl]])
                S.dma("sp", sem_p[3], OAv[:, c0:c0 + CT // 2, ts_], oa_.t[:], reads=oa_.rs)


def build(TILES, SEGT, debug=(), stop_after="F"):
    NTOK = TILES * NT
    NCH = TILES * 4
    nc = bass.Bass("TRN2", target_bir_lowering=False)
    dbg = set(debug)

    def din(name, shape, dt=F32):
        return nc.dram_tensor(name, list(shape), dt, kind="ExternalInput").ap()

    def dscr(name, shape, dt):
        kind = "ExternalOutput" if name in dbg else "Internal"
        return nc.dram_tensor(name, list(shape), dt, kind=kind).ap()

    xT = din("xT", [D, NTOK])
    keep_d = din("keep", [128, 1])
    gvec_d = din("gvec", [128, 7 * KT])
    wsrc = {
        "ffn_win": din("ffn_win", [4 * JT, 128, 2 * KT * 128]),
        "ffn_wout": din("ffn_wout", [4 * KT, 128, JT * 128]),
        "s5_win": din("s5_win", [KT, 128, KT * 128]),
        "wglu": din("wglu", [2 * KT, 128, KT * 128]),
        "ml_win": din("ml_win", [2 * CT, 128, KT * 128]),
        "mlD": din("mlD", [CT, 128, 1024]),
        "ml_wout": din("ml_wout", [KT, 128, CT * 128]),
    }
    dr = {}
    for kk_ in ("lamre", "lamim", "logstep"):
        dr[kk_] = din(kk_, [2, 128, 64])
    dr["bre"] = din("bre", [2, 128, 64 * 16])
    dr["bim"] = din("bim", [2, 128, 64 * 16])
    dr["CRI"] = din("CRI", [KT, 128, 2 * 4 * 2 * 32])
    dr["s5d"] = din("s5d", [128, KT])
    dr["wg"] = din("wg", [128, 3 * CT * 64])
    dr["bgate"] = din("bgate", [1, 64])
    mlvec_d = din("mlvec", [128, 96])
    yT = nc.dram_tensor("yT", [D, NTOK], F32, kind="ExternalOutput").ap()

    wb = {n: dscr(n + "_b", a.shape, BF16) for n, a in wsrc.items()}
    X1 = dscr("X1", [D, NTOK], F32)
    dr["U"] = dscr("U", [D, NTOK], F32)
    dr["YL"] = dscr("YL", [D, NTOK], F32)
    dr["GE"] = dscr("GE", [TILES, 128, 2 * 64 * 4 * 2], F32)
    dr["HIN"] = dscr("HIN", [TILES, 128, 2 * 64 * 4 * 2], F32)
    dr["TAB"] = dscr("TAB", [2, KT, 128, 4, 4, 128], F32)
    dr["PPL"] = dscr("PPL", [2, KT, 128, 4, 2, 128], BF16)
    X2 = dscr("X2", [D, NTOK], F32)
    if "YA" in dbg:
        dr["YA"] = dscr("YA", [D, NTOK], F32)
    X4 = dscr("X4", [D, NTOK], F32)
    XM = dscr("XM", [DI, NTOK], BF16)
    Z = dscr("Z", [DI, NTOK], F32)
    U = dr["U"]
    dr["XM"] = XM
    dr["Z"] = Z
    dr["XC"] = dscr("XC", [DI, NTOK], BF16)
    dr["QT"] = dscr("QT", [DI, NTOK], BF16)
    dr["KT"] = dscr("KT", [DI, NTOK], BF16)
    dr["KTOK"] = dscr("KTOK", [NCH, 128, DI], BF16)
    dr["VTOK"] = dscr("VTOK", [NCH, 128, DI], BF16)
    dr["G"] = dscr("G", [NCH, 128, 64], F32)
    dr["HB"] = dscr("HB", [NCH, 128, DI], F32)
    dr["OA"] = dscr("OA", [DI, NTOK], BF16)
    X5 = dscr("X5", [D, NTOK], F32)

    def fm(ap):
        return ap.rearrange("(c p) t -> p c t", p=128)

    with ExitStack() as st:
        S = Sched(nc, st)
        k = K(nc, st, S)
        k.wres = Res("wres")
        k.bank = []
        for i in range(8):
            t = st.enter_context(nc.psum_tensor("bank%d" % i, [128, 512], F32))
            k.bank.append(Buf(t, "bank%d" % i))
        k.ffn_win_b = wb["ffn_win"]
        k.ffn_wout_b = wb["ffn_wout"]
        io_sem = [S.new_dma_sem("io%d" % i) for i in range(4)]
        st_sem = [S.new_dma_sem("st%d" % i) for i in range(6)]
        k.ones32 = k.sb("ones32", [128, 128], F32)
        k.epsc = k.sb("epsc", [128, 1], F32)
        k.gvec = k.sb("gvec", [128, 7, KT], F32)
        k.keep = k.sb("keepc", [128, 1], F32)
        k.MS("dve", k.ones32.t[:], 1.0, [k.ones32.r])
        k.MS("dve", k.epsc.t[:], EPS, [k.epsc.r])
        S.dma("sp", io_sem[0], k.gvec.t[:].rearrange("p a b -> p (a b)"), gvec_d[:, :], writes=[k.gvec.r])
        S.dma("sp", io_sem[0], k.keep.t[:], keep_d[:, :], writes=[k.keep.r])
        k.mlvec = k.sb("mlvec", [128, 96], F32)
        S.dma("sp", io_sem[0], k.mlvec.t[:], mlvec_d[:, :], writes=[k.mlvec.r])
        k.hbres = Res("hbres")
        dr["mlD_b"] = wb["mlD"]

        def phase():
            ph = ExitStack()
            k.ph = ph
            return ph

        def common_bufs():
            k.W = Stream(k, "W", JT * 128, BF16, 3)
            k.hn = k.sb("hn", [128, KT, NT], BF16, nres=KT)
            k.h = k.sb("h", [128, JT, NT], BF16, nres=JT)
            k.sq = [k.sb("sq%d" % i, [128, NT], F32) for i in range(2)]
            k.sg = [k.sb("sg%d" % i, [128, NT], F32) for i in range(2)]
            k.rstd = k.sb("rstd", [128, NT], F32)

        with phase():
            for _ in cast_weights(k, [(wsrc["ffn_win"][0:JT], wb["ffn_win"][0:JT]), (wsrc["ffn_wout"][0:KT], wb["ffn_wout"][0:KT]),
                                      (wsrc["s5_win"], wb["s5_win"])]):
                pass
            barrier(S)

        with phase():
            common_bufs()
            x = k.sb("xa", [128, KT, NT], F32, nres=KT)
            ust = [k.sb("ust%d" % i, [128, NT], F32) for i in range(2)]
            rest = [(wsrc["ffn_win"][JT:4 * JT], wb["ffn_win"][JT:4 * JT]), (wsrc["ffn_wout"][KT:4 * KT], wb["ffn_wout"][KT:4 * KT])]
            rest += [(wsrc[n], wb[n]) for n in ("wglu", "ml_win", "mlD", "ml_wout")]
            cgen = cast_weights(k, rest, queue="pool")
            nrest = sum(a.shape[0] * ((a.shape[2] + 4095) // 4096) for a, _ in rest)
            per_tile = -(-nrest // TILES)
            state = {"left": 0}

            def bg():
                if state["left"] > 0:
                    state["left"] -= 1
                    next(cgen, None)
            k.bg = bg
            for i in range(TILES):
                t0 = i * NT
                state["left"] = per_tile
                S.dma("pool", io_sem[0], x.t[:], fm(xT)[:, :, t0:t0 + NT], writes=x.rs)
                ffn(k, x, 0, 0)
                while state["left"] > 0:
                    bg()
                S.dma("pool", st_sem[0], fm(X1)[:, :, t0:t0 + NT], x.t[:], reads=x.rs)
                rms_stats(k, x)
                rms_apply(k, x, 1, k.hn)

                def cons(m, ps, t0=t0):
                    b = ust[m % 2]
                    k.CP("act", b.t[:], ps.t[:], [ps.r], [b.r])
                    S.dma("pool", st_sem[1 + m % 2], U[m * 128:(m + 1) * 128, t0:t0 + NT], b.t[:], reads=[b.r])
                proj(k, k.hn, [wb["s5_win"][m] for m in range(KT)], KT, cons)
            k.bg = None
            for _ in cgen:
                pass
            barrier(S)
        if stop_after == "A":
            S.emit_all()
            return nc, S

        s5scope = ExitStack()
        k.ph = s5scope
        k.BL = k.sb("BL", [128, 2, 2, KT, 128], BF16)
        k.CL = k.sb("CL", [128, 2, 64, 3, 32], BF16)
        k.L128 = k.sb("L128", [128, 2, 2, 64], F32)
        k.L127 = k.sb("L127", [128, 2, 2, 64], F32)
        with phase():
            s5_setup(k, dr)
            tmpc = k.sb("tmpc", [128, 2, 4, 2, 32], F32)
            for ct in range(KT):
                S.dma("sp", io_sem[1], tmpc.t[:].rearrange("p a b c e -> p (a b c e)"), dr["CRI"][ct], writes=[tmpc.r])
                k.CP("dve", k.CL.t[:, :, 4 * ct:4 * ct + 4, 0, :], tmpc.t[:, :, :, 0, :], [tmpc.r], [k.CL.r])
                k.TS("dve", k.CL.t[:, :, 4 * ct:4 * ct + 4, 1, :], tmpc.t[:, :, :, 0, :], -1.0, None, MUL, None, [tmpc.r], [k.CL.r])
                k.TS("dve", k.CL.t[:, :, 4 * ct:4 * ct + 4, 2, :], tmpc.t[:, :, :, 1, :], -1.0, None, MUL, None, [tmpc.r], [k.CL.r])
            barrier(S)
        with phase():
            pass_b(k, dr, TILES)
            barrier(S)
        with phase():
            s5_chain(k, dr, TILES, SEGT)
            barrier(S)
        s5scope.close()
        if stop_after == "B":
            S.emit_all()
            return nc, S

        with phase():
            common_bufs()
            x = k.sb("xc_", [128, KT, NT], F32, nres=KT)
            k.c5 = {
                "hin": [k.sb("hin%d" % i, [128, 2, 64, 4, 2], F32) for i in range(2)],
                "sem_h": [S.new_dma_sem("hin%d" % i) for i in range(2)],
                "pp": Stream(k, "PP", 4 * 2 * 128, BF16, 4, hold=2),
                "cs": Stream(k, "CS", 512, F32, 3, hold=2),
                "chb": [k.sb("chb%d" % i, [128, 2, 4, 4, 2, 32], BF16) for i in range(2)],
                "ta": [k.sb("cta%d" % i, [128, 4, 4, 32], F32) for i in range(2)],
                "tb": [k.sb("ctb%d" % i, [128, 4, 4, 32], F32) for i in range(2)],
                "yl": [k.sb("yl%d" % i, [128, NT], F32) for i in range(2)],
                "g2": [k.sb("g2_%d" % i, [128, NT], F32) for i in range(2)],
                "sem_yl": [S.new_dma_sem("yl%d" % i) for i in range(2)],
            }
            gsb = [k.sb("gsb%d" % i, [128, NT], F32) for i in range(2)]
            zst = [k.sb("zst%d" % i, [128, NT], F32) for i in range(2)]
            mst = [k.sb("mst%d" % i, [128, NT], BF16) for i in range(2)]
            for i in range(TILES):
                t0 = i * NT
                S.dma("pool", io_sem[0], x.t[:], fm(X1)[:, :, t0:t0 + NT], writes=x.rs)
                s5_carry_tile(k, dr, i, k.hn)
                held = {}

                def cons_glu(idx, ps):
                    m = idx // 2
                    if idx % 2 == 0:
                        held["v"] = ps
                        return
                    pv = held["v"]
                    g = gsb[m % 2]
                    k.ACTF(g.t[:], ps.t[:], AF.Sigmoid, [ps.r], [g.r])
                    k.TT("dve", g.t[:], g.t[:], pv.t[:], MUL, [g.r, pv.r], [g.r])
                    k.TT("dve", x.t[:, m, :], x.t[:, m, :], g.t[:], ADD, [g.r, x.rs[m]], [x.rs[m]])
                wl = []
                for m in range(KT):
                    wl += [wb["wglu"][m], wb["wglu"][KT + m]]
                proj(k, k.hn, wl, KT, cons_glu)
                if "X2" in dbg:
                    S.dma("pool", st_sem[3], fm(X2)[:, :, t0:t0 + NT], x.t[:], reads=x.rs)
                ffn(k, x, 2, 1)
                ffn(k, x, 3, 2)
                S.dma("pool", st_sem[0], fm(X4)[:, :, t0:t0 + NT], x.t[:], reads=x.rs)
                rms_stats(k, x)
                rms_apply(k, x, 4, k.hn)

                def cons_ml(m, ps, t0=t0):
                    if m < CT:
                        b = mst[m % 2]
                        k.CP("act", b.t[:], ps.t[:], [ps.r], [b.r])
                        S.dma("pool", st_sem[1 + m % 2], XM[m * 128:(m + 1) * 128, t0:t0 + NT], b.t[:], reads=[b.r])
                    else:
                        b = zst[m % 2]
                        k.CP("act", b.t[:], ps.t[:], [ps.r], [b.r])
                        S.dma("pool", st_sem[4 + m % 2], Z[(m - CT) * 128:(m - CT + 1) * 128, t0:t0 + NT], b.t[:], reads=[b.r])
                proj(k, k.hn, [wb["ml_win"][m] for m in range(2 * CT)], KT, cons_ml)
            barrier(S)
        if stop_after == "C":
            S.emit_all()
            return nc, S

        with phase():
            pass_d(k, dr, TILES, SEGT)
            barrier(S)
        if stop_after == "D":
            S.emit_all()
            return nc, S
        mlscope = ExitStack()
        k.ph = mlscope
        k.ED = k.sb("ED", [128, NCH, 2, 16], F32)
        k.EB = k.sb("EB", [128, NCH, 2, 16], F32)
        k.EG = k.sb("EG", [128, NCH, 2, 16], F32)
        k.TRI = [k.sb("TRI%d" % i, [128, 128], F32) for i in range(2)]
        k.ident32 = k.sb("ident32", [128, 128], F32)
        with phase():
            ml_prep(k, dr, TILES, SEGT)
            barrier(S)
        with phase():
            pass_e(k, dr, TILES)
            barrier(S)
        mlscope.close()
        if stop_after == "E":
            S.emit_all()
            return nc, S
        with phase():
            common_bufs()
            x = k.sb("xf_", [128, KT, NT], F32, nres=KT)
            OAv = dr["OA"].rearrange("(c p) t -> p c t", p=128)
            for i in range(TILES):
                t0 = i * NT
                S.dma("pool", io_sem[0], x.t[:], fm(X4)[:, :, t0:t0 + NT], writes=x.rs)
                S.dma("pool", io_sem[1], k.h.t[:, 0:CT, :], OAv[:, :, t0:t0 + NT], writes=k.h.rs)

                def cons_o(m, ps):
                    k.TT("dve", x.t[:, m, :], x.t[:, m, :], ps.t[:], ADD, [ps.r, x.rs[m]], [x.rs[m]])
                proj(k, k.h, [wb["ml_wout"][m] for m in range(KT)], CT, cons_o)
                if "X5" in dbg:
                    S.dma("pool", st_sem[3], fm(X5)[:, :, t0:t0 + NT], x.t[:], reads=x.rs)
                ffn(k, x, 5, 3)
                rms_stats(k, x)
                rms_apply(k, x, 6, x)
                S.dma("pool", st_sem[0], fm(yT)[:, :, t0:t0 + NT], x.t[:], reads=x.rs)
            barrier(S)
        barrier(S)
        S.emit_all()
    return nc, S


def _tile_rows(w, nk):
    K_, M_ = w.shape
    m = M_ // 128
    return np.ascontiguousarray(w.reshape(nk, 128, m, 128).transpose(2, 1, 0, 3).reshape(m, 128, nk * 128))


def prep_shared(inp):
    f = np.float32
    out = {}
    g = np.concatenate([np.asarray(inp["norm_g"], f).reshape(6, D), np.asarray(inp["final_g"], f).reshape(1, D)], 0)
    out["gvec"] = np.ascontiguousarray(g.reshape(7, KT, 128).transpose(2, 0, 1).reshape(128, 7 * KT))
    win = np.asarray(inp["ffn_w_in"], f).reshape(4, D, 2, JT, 128)
    win = win.reshape(4, KT, 128, 2, JT, 128).transpose(0, 4, 2, 3, 1, 5)
    out["ffn_win"] = np.ascontiguousarray(win.reshape(4 * JT, 128, 2 * KT * 128))
    wout = np.asarray(inp["ffn_w_out"], f).reshape(4, DFF, D)
    out["ffn_wout"] = np.concatenate([_tile_rows(wout[i], JT) for i in range(4)], 0)
    out["s5_win"] = _tile_rows(np.asarray(inp["s5_w_in"], f)[0], KT)
    out["wglu"] = _tile_rows(np.asarray(inp["s5_w_glu"], f)[0], KT)
    out["ml_win"] = _tile_rows(np.asarray(inp["ml_w_in"], f)[0], KT)

    def gp(a):
        sh = a.shape
        a = a.reshape((2, 64, 2, 64) + sh[3:])
        perm = (0, 2, 3, 1) + tuple(range(4, a.ndim))
        a = a.transpose(perm)
        return np.ascontiguousarray(a.reshape((2, 128, 64) + sh[3:]))
    out["lamre"] = gp(np.asarray(inp["s5_lambda_re"], f)[0])
    out["lamim"] = gp(np.asarray(inp["s5_lambda_im"], f)[0])
    ls = np.asarray(inp["s5_log_step"], f)[0]
    out["logstep"] = gp(np.broadcast_to(ls[:, :, None], (2, 128, 64)).copy())
    out["bre"] = gp(np.asarray(inp["s5_b_re"], f)[0]).reshape(2, 128, 64 * 16)
    out["bim"] = gp(np.asarray(inp["s5_b_im"], f)[0]).reshape(2, 128, 64 * 16)
    cri = np.zeros((KT, 128, 2, 4, 2, 32), f)
    for ri, key in enumerate(("s5_c_re", "s5_c_im")):
        c = np.asarray(inp[key], f)[0]
        c = c.reshape(2, KT, 4, 2, 16, 64)
        for g2 in range(2):
            cri[:, g2 * 64:(g2 + 1) * 64, :, :, ri, g2 * 16:(g2 + 1) * 16] = c[:, :, :, g2].transpose(1, 4, 0, 2, 3)
    out["CRI"] = np.ascontiguousarray(cri.reshape(KT, 128, 512))
    out["s5d"] = np.ascontiguousarray(np.asarray(inp["s5_d"], f)[0].reshape(KT, 128).T)
    mlD = np.zeros((CT, 128, 8, 128), f)
    cw = np.asarray(inp["ml_conv_w"], f)[0].reshape(5, CT, 128)
    ar = np.arange(128)
    for tau in range(5):
        mlD[:, ar, tau, ar] = cw[tau]
    for j, key in enumerate(("ml_wq", "ml_wk", "ml_wv")):
        w = np.asarray(inp[key], f)[0].reshape(CT, 32, 4, 4)
        for n in range(32):
            mlD[:, 4 * n:4 * n + 4, 5 + j, 4 * n:4 * n + 4] = w[:, n]
    out["mlD"] = np.ascontiguousarray(mlD.reshape(CT, 128, 1024))
    out["ml_wout"] = _tile_rows(np.asarray(inp["ml_w_out"], f)[0], CT)
    wg = np.asarray(inp["ml_w_gates"], f)[0].reshape(3, CT, 128, 64)
    out["wg"] = np.ascontiguousarray(wg.transpose(2, 0, 1, 3).reshape(128, 3 * CT * 64))
    out["bgate"] = np.ascontiguousarray(np.asarray(inp["ml_b_gates"], f)[0].reshape(1, 64))
    vecs = [np.asarray(inp[kk], f)[0].reshape(CT, 128).T for kk in ("ml_conv_b", "ml_norm_g", "ml_skip")]
    out["mlvec"] = np.ascontiguousarray(np.concatenate(vecs, 1))
    return out


_CACHE = {}


def kernel(**inputs):
    f = np.float32
    TILES, SEGT = 16, 4
    NTOK = TILES * NT
    sh = prep_shared(inputs)
    xp = np.asarray(inputs["x_prompt"], f)
    xs = np.asarray(inputs["x_sample"], f)
    in_maps = []
    for c in range(8):
        m = dict(sh)
        if c < 4:
            m["xT"] = np.ascontiguousarray(xp[c].T)
            m["keep"] = np.ones((128, 1), f)
        else:
            xt = np.zeros((D, NTOK), f)
            for j in range(2):
                xt[:, j * 2048:(j + 1) * 2048] = xs[2 * (c - 4) + j].T
            m["xT"] = xt
            m["keep"] = np.zeros((128, 1), f)
        in_maps.append(m)
    if "nc" not in _CACHE:
        _CACHE["nc"] = build(TILES, SEGT)[0]
    res = run_bass_kernel_spmd(_CACHE["nc"], in_maps, core_ids=list(range(8)))
    yp = np.zeros((4, 8192, D), f)
    ys = np.zeros((8, 2048, D), f)
    for c in range(8):
        y = np.asarray(res.results[c]["yT"], f)
        if c < 4:
            yp[c] = y.T
        else:
            for j in range(2):
                ys[2 * (c - 4) + j] = y[:, j * 2048:(j + 1) * 2048].T
    return (yp, ys)
```

```python
import numpy as np
import concourse.bass as bass
import concourse.mybir as mybir

F32 = mybir.dt.float32
BF16 = mybir.dt.bfloat16
I32 = mybir.dt.int32
ALU = mybir.AluOpType
AF = mybir.ActivationFunctionType


class Res:
    __slots__ = ("name", "lw", "rd")

    def __init__(self, name=""):
        self.name = name
        self.lw = None
        self.rd = {}


class Eng:
    def __init__(self, name, h, sem):
        self.name = name
        self.h = h
        self.sem = sem
        self.count = 0
        self.seen = {}
        self.prog = []


class Sched:
    def __init__(self, nc, stack):
        self.nc = nc
        self.stack = stack
        self.sems = {}
        self.engs = {}
        for name, h in (("pe", nc.tensor), ("dve", nc.vector), ("act", nc.scalar),
                        ("pool", nc.gpsimd), ("sp", nc.sync)):
            sem = stack.enter_context(nc.semaphore("sem_" + name))
            self.sems["e:" + name] = sem
            self.engs[name] = Eng(name, h, sem)
        self.dma_sem_val = {}
        self.n_inst = 0
        self.n_wait = 0

    def new_dma_sem(self, key):
        sem = self.stack.enter_context(self.nc.semaphore("dsem_" + key))
        self.sems["d:" + key] = sem
        self.dma_sem_val["d:" + key] = 0
        return "d:" + key

    def _wait(self, eng, deps):
        best = {}
        own = "e:" + eng.name
        for (k, v) in deps:
            if eng.name == "pe" and k == own:
                continue
            if best.get(k, 0) < v:
                best[k] = v
        for k, v in best.items():
            if eng.seen.get(k, 0) >= v:
                continue
            eng.prog.append(("w", self.sems[k], v))
            eng.seen[k] = v
            self.n_wait += 1

    def _deps(self, reads, writes):
        deps = []
        for r in reads:
            if r.lw is not None:
                deps.append(r.lw)
        for r in writes:
            if r.lw is not None:
                deps.append(r.lw)
            for k, v in r.rd.items():
                deps.append((k, v))
        return deps

    def op(self, engname, fn, reads=(), writes=()):
        eng = self.engs[engname]
        self._wait(eng, self._deps(reads, writes))
        eng.count += 1
        eng.prog.append(("i", fn, eng.sem, 1))
        ev = ("e:" + engname, eng.count)
        for r in reads:
            if r.rd.get(ev[0], 0) < ev[1]:
                r.rd[ev[0]] = ev[1]
        for r in writes:
            r.lw = ev
            r.rd = {}
        self.n_inst += 1

    def dma(self, qname, semkey, out, in_, reads=(), writes=()):
        eng = self.engs[qname]
        self._wait(eng, self._deps(reads, writes))
        eng.prog.append(("i", (lambda h, o=out, i=in_: h.dma_start(out=o, in_=i)), self.sems[semkey], 16))
        self.dma_sem_val[semkey] += 16
        v = self.dma_sem_val[semkey]
        ev = (semkey, v)
        for r in reads:
            if r.rd.get(ev[0], 0) < ev[1]:
                r.rd[ev[0]] = ev[1]
        for r in writes:
            r.lw = ev
            r.rd = {}
        self.n_inst += 1

    def wait_all(self, engname, resources):
        eng = self.engs[engname]
        deps = []
        for r in resources:
            if r.lw is not None:
                deps.append(r.lw)
        self._wait(eng, deps)

    def emit_all(self):
        nc = self.nc
        with nc.Block() as block:
            def run(eng):
                def body(h):
                    for it in eng.prog:
                        if it[0] == "w":
                            h.wait_ge(it[1], it[2])
                        else:
                            it[1](h).then_inc(it[2], it[3])
                return body
            block.tensor(run(self.engs["pe"]))
            block.vector(run(self.engs["dve"]))
            block.scalar(run(self.engs["act"]))
            block.gpsimd(run(self.engs["pool"]))
            block.sync(run(self.engs["sp"]))

from contextlib import ExitStack
from concourse.bass_utils import run_bass_kernel_spmd

NT = 512
D = 2048
KT = 16
DFF = 5632
JT = 44
DI = 4096
CT = 32
NH = 16
DH = 256
EPS = 1e-6
MUL = ALU.mult
ADD = ALU.add
SUB = ALU.subtract


class Buf:
    def __init__(self, t, name, nres=0):
        self.t = t
        self.r = Res(name)
        self.rs = [Res("%s_%d" % (name, i)) for i in range(nres)]


class Stream:
    def __init__(self, k, name, width, dt, nslots, queue="sp", hold=1):
        self.hold = hold
        self.k = k
        self.n = nslots
        self.queue = queue
        self.slots = [k.sb("%s_s%d" % (name, i), [128, width], dt) for i in range(nslots)]
        k.nst = getattr(k, "nst", 0) + 1
        self.sems = [k.S.new_dma_sem("%s%d_%d" % (name, k.nst, i)) for i in range(nslots)]
        self.items = []
        self.next_load = 0
        self.next_use = 0

    def push(self, aps):
        self.items.extend(aps)

    def get(self):
        S = self.k.S
        while self.next_load < min(len(self.items), self.next_use + self.n - self.hold + 1):
            i = self.next_load
            s = i % self.n
            ap = self.items[i]
            w = ap.shape[-1]
            S.dma(self.queue, self.sems[s], self.slots[s].t[:, 0:w], ap, writes=[self.slots[s].r])
            self.next_load += 1
        b = self.slots[self.next_use % self.n]
        self.next_use += 1
        return b


class K:
    def __init__(self, nc, st, S):
        self.nc = nc
        self.st = st
        self.S = S
        self.ph = st

    def sb(self, name, shape, dt, nres=0):
        self.nsb = getattr(self, "nsb", 0) + 1
        t = self.ph.enter_context(self.nc.sbuf_tensor("sb%d_%s" % (self.nsb, name), shape, dt))
        return Buf(t, name, nres)

    def MM(self, ps, lhsT, rhs, start, stop, R, W, **kw):
        self.S.op("pe", lambda h: h.matmul(ps, lhsT=lhsT, rhs=rhs, start=start, stop=stop, **kw), R, W)

    def TR(self, ps, in_, ident, R, W):
        self.S.op("pe", lambda h: h.transpose(ps, in_, ident), R, W)

    def ACTF(self, out, in_, func, R, W, bias=None, scale=None):
        kw = {}
        if bias is not None:
            kw["bias"] = bias
        if scale is not None:
            kw["scale"] = scale
        self.S.op("act", lambda h: h.activation(out=out, in_=in_, func=func, **kw), R, W)

    def TT(self, eng, out, in0, in1, op, R, W):
        self.S.op(eng, lambda h: h.tensor_tensor(out=out, in0=in0, in1=in1, op=op), R, W)

    def TS(self, eng, out, in0, s1, s2, op0, op1, R, W):
        if s2 is None:
            self.S.op(eng, lambda h: h.tensor_scalar(out=out, in0=in0, scalar1=s1, scalar2=None, op0=op0), R, W)
        else:
            self.S.op(eng, lambda h: h.tensor_scalar(out=out, in0=in0, scalar1=s1, scalar2=s2, op0=op0, op1=op1), R, W)

    def STT(self, out, in0, scalar, in1, op0, op1, R, W):
        self.S.op("dve", lambda h: h.scalar_tensor_tensor(out=out, in0=in0, scalar=scalar, in1=in1, op0=op0, op1=op1), R, W)

    def CP(self, eng, out, in_, R, W):
        if eng == "act":
            self.S.op("act", lambda h: h.activation(out=out, in_=in_, func=AF.Copy), R, W)
        else:
            self.S.op(eng, lambda h: h.tensor_copy(out=out, in_=in_), R, W)

    def MS(self, eng, ap, val, W):
        self.S.op(eng, lambda h: h.memset(ap, val), [], W)

    def RECIP(self, out, in_, R, W):
        self.S.op("dve", lambda h: h.reciprocal(out=out, in_=in_), R, W)

    def SCAN(self, out, d0, d1, init, op0, op1, R, W):
        self.S.op("dve", lambda h: h.tensor_tensor_scan(out=out, data0=d0, data1=d1, initial=init, op0=op0, op1=op1), R, W)


def barrier(S):
    evs = [("e:" + n, e.count) for n, e in S.engs.items() if e.count > 0]
    evs += [(kk, v) for kk, v in S.dma_sem_val.items() if v > 0]
    for n, e in S.engs.items():
        S._wait(e, [ev for ev in evs if ev[0] != "e:" + n])


def rms_stats(k, x):
    S = k.S
    ps = k.bank[7]
    for kt in range(KT):
        sq = k.sq[kt % 2]
        k.ACTF(sq.t[:], x.t[:, kt, :], AF.Square, [x.rs[kt]], [sq.r])
        k.MM(ps.t[:], k.ones32.t[:], sq.t[:], kt == 0, kt == KT - 1, [sq.r, k.ones32.r], [ps.r])
    k.ACTF(k.rstd.t[:], ps.t[:], AF.Sqrt, [ps.r, k.epsc.r], [k.rstd.r], bias=k.epsc.t[:, 0:1], scale=1.0 / D)
    k.RECIP(k.rstd.t[:], k.rstd.t[:], [k.rstd.r], [k.rstd.r])


def rms_apply(k, x, gi, out):
    for kt in range(KT):
        k.STT(out.t[:, kt, :], x.t[:, kt, :], k.gvec.t[:, gi, kt:kt + 1], k.rstd.t[:], MUL, MUL,
              [x.rs[kt], k.gvec.r, k.rstd.r], [out.rs[kt]])


def ffn(k, x, gi, wi):
    rms_stats(k, x)
    rms_apply(k, x, gi, k.hn)
    W = k.W
    W.push([k.ffn_win_b[wi * JT + j] for j in range(JT)])
    W.push([k.ffn_wout_b[wi * KT + m] for m in range(KT)])
    hn = k.hn
    for j in range(JT):
        wb = W.get()
        pg = k.bank[(j % 2) * 2]
        pu = k.bank[(j % 2) * 2 + 1]
        for g, ps in ((0, pg), (1, pu)):
            for kt in range(KT):
                c0 = (g * KT + kt) * 128
                k.MM(ps.t[:], wb.t[:, c0:c0 + 128], hn.t[:, kt, :], kt == 0, kt == KT - 1, [wb.r, hn.rs[kt]], [ps.r])
        if getattr(k, "bg", None) is not None:
            k.bg()
        sg = k.sg[j % 2]
        k.ACTF(sg.t[:], pg.t[:], AF.Silu, [pg.r], [sg.r])
        k.TT("dve", k.h.t[:, j, :], sg.t[:], pu.t[:], MUL, [sg.r, pu.r], [k.h.rs[j]])
    for m in range(KT):
        wo = W.get()
        ps = k.bank[4 + m % 2]
        for kt in range(JT):
            k.MM(ps.t[:], wo.t[:, kt * 128:(kt + 1) * 128], k.h.t[:, kt, :], kt == 0, kt == JT - 1, [wo.r, k.h.rs[kt]], [ps.r])
        k.STT(x.t[:, m, :], ps.t[:], 0.5, x.t[:, m, :], MUL, ADD, [ps.r, x.rs[m]], [x.rs[m]])


def proj(k, src, wlist, nkt, consume):
    W = k.W
    W.push(wlist)
    for m in range(len(wlist)):
        w = W.get()
        ps = k.bank[4 + m % 2]
        for kt in range(nkt):
            k.MM(ps.t[:], w.t[:, kt * 128:(kt + 1) * 128], src.t[:, kt, :], kt == 0, kt == nkt - 1, [w.r, src.rs[kt]], [ps.r])
        consume(m, ps)


def cast_weights(k, pairs, queue="sp", engs=("dve", "act", "dve")):
    S = k.S
    CW = 4096
    cin = [k.sb("cin%d" % i, [128, CW], F32) for i in range(2)]
    cout = [k.sb("cout%d" % i, [128, CW], BF16) for i in range(3)]
    k.ncast = getattr(k, "ncast", 0) + 1
    sin = [S.new_dma_sem("cin%d_%d" % (k.ncast, i)) for i in range(2)]
    sout = [S.new_dma_sem("cout%d_%d" % (k.ncast, i)) for i in range(3)]
    n = 0
    for (src, dst) in pairs:
        T, _, Fw = src.shape
        for t in range(T):
            for c0 in range(0, Fw, CW):
                w = min(CW, Fw - c0)
                a = cin[n % 2]
                b = cout[n % 3]
                S.dma(queue, sin[n % 2], a.t[:, 0:w], src[t, :, c0:c0 + w], writes=[a.r])
                k.CP(engs[n % len(engs)], b.t[:, 0:w], a.t[:, 0:w], [a.r], [b.r])
                S.dma(queue, sout[n % 3], dst[t, :, c0:c0 + w], b.t[:, 0:w], reads=[b.r], writes=[k.wres])
                n += 1
                yield n


PI = float(np.pi)


class VC:
    def __init__(self, k, name):
        self.k = k
        self.r = Res(name)

    def tt(self, o, a, b, op, eng="dve"):
        self.k.TT(eng, o, a, b, op, [self.r], [self.r])

    def ts(self, o, a, s1, op0, s2=None, op1=None):
        self.k.TS("dve", o, a, s1, s2, op0, op1, [self.r], [self.r])

    def stt(self, o, a, s, b, op0, op1):
        self.k.STT(o, a, s, b, op0, op1, [self.r], [self.r])

    def act(self, o, a, func, scale=None, bias=None):
        self.k.ACTF(o, a, func, [self.r], [self.r], bias=bias, scale=scale)

    def cp(self, o, a):
        self.k.CP("dve", o, a, [self.r], [self.r])

    def ms(self, o, v):
        self.k.MS("dve", o, v, [self.r])

    def recip(self, o, a):
        self.k.RECIP(o, a, [self.r], [self.r])

    def cmul(self, or_, oi, ar, ai, br, bi, t1, t2):
        self.tt(t1, ar, br, MUL)
        self.tt(t2, ai, bi, MUL)
        self.tt(or_, t1, t2, SUB)
        self.tt(t1, ar, bi, MUL)
        self.tt(t2, ai, br, MUL)
        self.tt(oi, t1, t2, ADD)


def s5_setup(k, dr):
    S = k.S
    v = VC(k, "s5setup")
    sem = S.new_dma_sem("s5set")

    def T(name, shape, dt=F32):
        return k.sb(name, shape, dt).t

    lre = T("lre", [128, 64]); lim = T("lim", [128, 64]); lst = T("lst", [128, 64])
    bre = T("bre", [128, 64, 16]); bim = T("bim", [128, 64, 16])
    dl = T("dl", [128, 64]); a = T("a_", [128, 64]); th = T("th", [128, 64])
    mag = T("mag", [128, 64]); imag = T("imag", [128, 64])
    sn = T("sn", [128, 64]); cs = T("cs", [128, 64])
    lr = T("lr", [128, 64]); li = T("li", [128, 64]); ir = T("ir", [128, 64]); ii = T("ii", [128, 64])
    w1 = T("w1", [128, 64]); w2 = T("w2", [128, 64]); w3 = T("w3", [128, 64]); w4 = T("w4", [128, 64])
    wi32 = T("wi32", [128, 64], I32)
    cr = T("cr", [128, 64]); ci = T("ci", [128, 64])
    pwr = T("pwr", [128, 64]); pwi = T("pwi", [128, 64])
    btp = [T("btp%d" % i, [128, 64, 32]) for i in range(2)]
    tabr = T("tabr", [128, 64, 128]); tabi = T("tabi", [128, 64, 128])
    tm1 = T("tm1", [128, 64, 64]); tm2 = T("tm2", [128, 64, 64])
    bt1 = tm1[:, :, 0:16]; bt2 = tm2[:, :, 0:16]
    ppb = T("ppb", [128, 64, 2, 128], BF16)
    ident = T("ident", [128, 128])
    io_i = T("io_i", [128, 128], I32)
    io_f = T("io_f", [128, 128])
    S.op("pool", lambda h: h.iota(io_i[:], [[1, 128]], 0, -1), [], [v.r])
    v.cp(io_f[:], io_i[:])
    v.ts(ident[:], io_f[:], 0.0, ALU.is_equal)

    def sinred(out, arg):
        v.ts(w1[:], arg, 1.0 / (2 * PI), MUL)
        v.cp(wi32[:], w1[:])
        v.cp(w2[:], wi32[:])
        v.stt(w1[:], w2[:], -2 * PI, arg, MUL, ADD)
        v.ts(w2[:], w1[:], PI, ALU.is_gt)
        v.stt(w1[:], w2[:], -2 * PI, w1[:], MUL, ADD)
        v.ts(w2[:], w1[:], -PI, ALU.is_lt)
        v.stt(w1[:], w2[:], 2 * PI, w1[:], MUL, ADD)
        v.act(out, w1[:], AF.Sin)

    for d in range(2):
        for (dst, key) in ((lre, "lamre"), (lim, "lamim"), (lst, "logstep")):
            S.dma("sp", sem, dst[:], dr[key][d], writes=[v.r])
        S.dma("sp", sem, bre[:].rearrange("p a b -> p (a b)"), dr["bre"][d], writes=[v.r])
        S.dma("sp", sem, bim[:].rearrange("p a b -> p (a b)"), dr["bim"][d], writes=[v.r])
        v.ts(lre[:], lre[:], -1e-4, ALU.min)
        v.act(dl[:], lst[:], AF.Exp)
        v.tt(a[:], lre[:], dl[:], MUL)
        v.tt(th[:], lim[:], dl[:], MUL)
        v.act(mag[:], a[:], AF.Exp)
        v.act(imag[:], a[:], AF.Exp, scale=-1.0)
        sinred(sn[:], th[:])
        v.ts(w3[:], th[:], PI / 2, ADD)
        sinred(cs[:], w3[:])
        v.tt(lr[:], mag[:], cs[:], MUL)
        v.tt(li[:], mag[:], sn[:], MUL)
        v.tt(ir[:], imag[:], cs[:], MUL)
        v.tt(ii[:], imag[:], sn[:], MUL)
        v.ts(ii[:], ii[:], -1.0, MUL)
        v.tt(w1[:], lre[:], lre[:], MUL)
        v.tt(w2[:], lim[:], lim[:], MUL)
        v.tt(w1[:], w1[:], w2[:], ADD)
        v.recip(w4[:], w1[:])
        v.ts(w3[:], lr[:], -1.0, ADD)
        v.tt(w1[:], w3[:], lre[:], MUL)
        v.tt(w2[:], li[:], lim[:], MUL)
        v.tt(w1[:], w1[:], w2[:], ADD)
        v.tt(cr[:], w1[:], w4[:], MUL)
        v.tt(w1[:], li[:], lre[:], MUL)
        v.tt(w2[:], w3[:], lim[:], MUL)
        v.tt(w1[:], w1[:], w2[:], SUB)
        v.tt(ci[:], w1[:], w4[:], MUL)
        v.ms(btp[0][:], 0.0)
        v.ms(btp[1][:], 0.0)
        for hf in range(2):
            p0, p1 = hf * 64, hf * 64 + 64
            crb = cr[p0:p1, :].unsqueeze(2).to_broadcast([64, 64, 16])
            cib = ci[p0:p1, :].unsqueeze(2).to_broadcast([64, 64, 16])
            v.tt(bt1[p0:p1], bre[p0:p1], crb, MUL)
            v.tt(bt2[p0:p1], bim[p0:p1], cib, MUL)
            v.tt(btp[0][p0:p1, :, hf * 16:hf * 16 + 16], bt1[p0:p1], bt2[p0:p1], SUB)
            v.tt(bt1[p0:p1], bim[p0:p1], crb, MUL)
            v.tt(bt2[p0:p1], bre[p0:p1], cib, MUL)
            v.tt(btp[1][p0:p1, :, hf * 16:hf * 16 + 16], bt1[p0:p1], bt2[p0:p1], ADD)
        for ri in range(2):
            for ct in range(KT):
                ps = k.bank[ct % 2]
                k.TR(ps.t[:, 0:128], btp[ri][:, 4 * ct:4 * ct + 4, :].rearrange("p a b -> p (a b)"), ident[:], [v.r], [ps.r])
                k.CP("act", k.BL.t[:, d, ri, ct, :], ps.t[:, 0:128], [ps.r], [k.BL.r])
        for tbl, (br_, bi_) in enumerate(((ir, ii), (lr, li))):
            rev = (d == 1)

            def sl(lo, hi):
                return slice(128 - hi, 128 - lo) if rev else slice(lo, hi)
            v.ms(tabr[:, :, sl(0, 1)], 1.0)
            v.ms(tabi[:, :, sl(0, 1)], 0.0)
            v.cp(pwr[:], br_[:])
            v.cp(pwi[:], bi_[:])
            n = 1
            while n < 128:
                pr_b = pwr[:].unsqueeze(2).to_broadcast([128, 64, n])
                pi_b = pwi[:].unsqueeze(2).to_broadcast([128, 64, n])
                src, dst = sl(0, n), sl(n, 2 * n)
                v.tt(tm1[:, :, 0:n], tabr[:, :, src], pr_b, MUL)
                v.tt(tm2[:, :, 0:n], tabi[:, :, src], pi_b, MUL)
                v.tt(tabr[:, :, dst], tm1[:, :, 0:n], tm2[:, :, 0:n], SUB)
                v.tt(tm1[:, :, 0:n], tabr[:, :, src], pi_b, MUL)
                v.tt(tm2[:, :, 0:n], tabi[:, :, src], pr_b, MUL)
                v.tt(tabi[:, :, dst], tm1[:, :, 0:n], tm2[:, :, 0:n], ADD)
                v.tt(w1[:], pwr[:], pwr[:], MUL)
                v.tt(w2[:], pwi[:], pwi[:], MUL)
                v.tt(w3[:], pwr[:], pwi[:], MUL)
                v.tt(pwr[:], w1[:], w2[:], SUB)
                v.ts(pwi[:], w3[:], 2.0, MUL)
                n *= 2
            for ct in range(KT):
                S.dma("sp", sem, dr["TAB"][d, ct][:, :, 2 * tbl, :], tabr[:, 4 * ct:4 * ct + 4, :], reads=[v.r])
                S.dma("sp", sem, dr["TAB"][d, ct][:, :, 2 * tbl + 1, :], tabi[:, 4 * ct:4 * ct + 4, :], reads=[v.r])
            if tbl == 1:
                v.cp(k.L128.t[:, d, 0, :], pwr[:])
                v.cp(k.L128.t[:, d, 1, :], pwi[:])
                v.cmul(k.L127.t[:, d, 0, :], k.L127.t[:, d, 1, :], pwr[:], pwi[:], ir[:], ii[:], w1[:], w2[:])
                lr_b = lr[:].unsqueeze(2).to_broadcast([128, 64, 64])
                li_b = li[:].unsqueeze(2).to_broadcast([128, 64, 64])
                for hj in range(2):
                    js = slice(hj * 64, hj * 64 + 64)
                    v.tt(tm1[:], tabr[:, :, js], lr_b, MUL)
                    v.tt(tm2[:], tabi[:, :, js], li_b, MUL)
                    v.tt(ppb[:, :, 0, js], tm1[:], tm2[:], SUB)
                    v.tt(tm1[:], tabr[:, :, js], li_b, MUL)
                    v.tt(tm2[:], tabi[:, :, js], lr_b, MUL)
                    v.tt(ppb[:, :, 1, js], tm1[:], tm2[:], ADD)
                for ct in range(KT):
                    S.dma("sp", sem, dr["PPL"][d, ct], ppb[:, 4 * ct:4 * ct + 4, :, :], reads=[v.r])


def pass_b(k, dr, TILES):
    S = k.S
    NCH = TILES * 4
    u32 = [k.sb("u32_%d" % i, [128, KT, NT], F32, nres=KT) for i in range(1)]
    ub = k.sb("ub", [128, KT, NT], BF16, nres=KT)
    s5d = k.sb("s5d", [128, KT], F32)
    tabs = Stream(k, "TABS", 2048, F32, 4, hold=2)
    nb = 3
    tq = [[k.sb("tq%d_%d" % (b, i), [128, NT], F32) for i in range(4)] for b in range(nb)]
    gq = [[k.sb("gq%d_%d" % (b, i), [128, NT], F32, nres=4) for i in range(2)] for b in range(nb)]
    xq = [[k.sb("xq%d_%d" % (b, i), [128, NT], BF16) for i in range(4)] for b in range(nb)]
    gend = [k.sb("gend%d" % i, [128, 2, 64, 4, 2], F32) for i in range(2)]
    yst = [k.sb("yst%d" % i, [128, NT], F32) for i in range(2)]
    sem_u = S.new_dma_sem("pbu")
    sem_s = S.new_dma_sem("pbs")
    sem_y = [S.new_dma_sem("pby%d" % i) for i in range(2)]
    sem_g = [S.new_dma_sem("pbg%d" % i) for i in range(2)]
    S.dma("sp", sem_s, s5d.t[:], dr["s5d"][:, :], writes=[s5d.r])
    CL = k.CL

    def v3(t):
        return t[:].rearrange("p (c j) -> p c j", c=4)

    items = []
    for i in range(TILES):
        for ct in range(KT):
            for d in range(2):
                for q in range(4):
                    items.append(dict(i=i, ct=ct, d=d, q=q))
    NI = len(items)

    def tile_start(i):
        t0 = i * NT
        u = u32[0]
        S.dma("pool", sem_u, u.t[:], dr["U"].rearrange("(c p) t -> p c t", p=128)[:, :, t0:t0 + NT], writes=u.rs)
        for kt in range(KT):
            k.CP("act", ub.t[:, kt, :], u.t[:, kt, :], [u.rs[kt]], [ub.rs[kt]])
        tabs.push([dr["TAB"][d, ct].rearrange("p a b c -> p (a b c)") for ct in range(KT) for d in range(2)])

    cur_tb = {}

    def st0(n):
        it = items[n]
        i, ct, d, q = it["i"], it["ct"], it["d"], it["q"]
        if q == 0:
            cur_tb["tb"] = tabs.get()
        it["tb"] = cur_tb["tb"]
        b = n % nb
        par, pai = k.bank[2 * b], k.bank[2 * b + 1]
        rows = slice(32 * q, 32 * q + 32)
        k.MM(par.t[:], k.BL.t[rows, d, 0, ct, :], ub.t[rows, ct, :], True, True, [k.BL.r, ub.rs[ct]], [par.r], tile_position=(32 * q, 0))
        k.MM(pai.t[:], k.BL.t[rows, d, 1, ct, :], ub.t[rows, ct, :], True, True, [k.BL.r, ub.rs[ct]], [pai.r], tile_position=(32 * q, 0))

    def tbb(it, w):
        tb4 = it["tb"].t[:].rearrange("p (q w j) -> p q w j", q=4, w=4)
        return tb4[:, it["q"], w, :].unsqueeze(1).to_broadcast([128, 4, 128])

    def st1(n):
        it = items[n]
        b = n % nb
        tb = it["tb"]
        par, pai = k.bank[2 * b], k.bank[2 * b + 1]
        t1, t2, t3, t4 = tq[b]
        k.TT("dve", v3(t1.t), v3(par.t), tbb(it, 0), MUL, [par.r, tb.r], [t1.r])
        k.TT("dve", v3(t2.t), v3(pai.t), tbb(it, 1), MUL, [pai.r, tb.r], [t2.r])
        k.TT("dve", v3(t3.t), v3(par.t), tbb(it, 1), MUL, [par.r, tb.r], [t3.r])
        k.TT("dve", v3(t4.t), v3(pai.t), tbb(it, 0), MUL, [pai.r, tb.r], [t4.r])

    def st2(n):
        it = items[n]
        d = it["d"]
        b = n % nb
        t1, t2, t3, t4 = tq[b]
        gr, gi = gq[b]
        for c in range(4):
            def cs_(t):
                vv = v3(t.t)[:, c, :]
                return vv[:, ::-1] if d == 1 else vv
            k.SCAN(cs_(gr), cs_(t1), cs_(t2), 0.0, ADD, SUB, [t1.r, t2.r], [gr.rs[c]])
            k.SCAN(cs_(gi), cs_(t3), cs_(t4), 0.0, ADD, ADD, [t3.r, t4.r], [gi.rs[c]])

    def st3(n):
        it = items[n]
        i, ct, d, q = it["i"], it["ct"], it["d"], it["q"]
        b = n % nb
        tb = it["tb"]
        gr, gi = gq[b]
        ge = gend[i % 2]
        pr = 4 * ct + q
        e = 127 if d == 0 else 0
        k.CP("pool", ge.t[:, d, pr, :, 0], v3(gr.t)[:, :, e], gr.rs, [ge.r])
        k.CP("pool", ge.t[:, d, pr, :, 1], v3(gi.t)[:, :, e], gi.rs, [ge.r])
        x1, x2, x3, x4 = xq[b]
        k.TT("pool", v3(x1.t), v3(gr.t), tbb(it, 2), MUL, gr.rs + [tb.r], [x1.r])
        k.TT("pool", v3(x2.t), v3(gi.t), tbb(it, 3), MUL, gi.rs + [tb.r], [x2.r])
        k.TT("pool", v3(x3.t), v3(gr.t), tbb(it, 3), MUL, gr.rs + [tb.r], [x3.r])
        k.TT("pool", v3(x4.t), v3(gi.t), tbb(it, 2), MUL, gi.rs + [tb.r], [x4.r])

    def st4(n):
        it = items[n]
        i, ct, d, q = it["i"], it["ct"], it["d"], it["q"]
        b = n % nb
        pr = 4 * ct + q
        t0 = i * NT
        yb = k.bank[6 + ct % 2]
        rows = slice(32 * q, 32 * q + 32)
        x1, x2, x3, x4 = xq[b]
        yo = yb.t[rows, :]
        first = (d == 0)
        kw = dict(skip_group_check=True, tile_position=(0, 32 * q))
        k.MM(yo, CL.t[:, d, pr, 0, :], x1.t[:], first, False, [CL.r, x1.r], [yb.r], **kw)
        k.MM(yo, CL.t[:, d, pr, 1, :], x2.t[:], False, False, [CL.r, x2.r], [yb.r], **kw)
        k.MM(yo, CL.t[:, d, pr, 2, :], x3.t[:], False, False, [CL.r, x3.r], [yb.r], **kw)
        k.MM(yo, CL.t[:, d, pr, 2, :], x4.t[:], False, d == 1, [CL.r, x4.r], [yb.r], **kw)
        if d == 1 and q == 3:
            u = u32[0]
            ys = yst[ct % 2]
            k.STT(ys.t[:], u.t[:, ct, :], s5d.t[:, ct:ct + 1], yb.t[:], MUL, ADD, [u.rs[ct], s5d.r, yb.r], [ys.r])
            S.dma("sp", sem_y[ct % 2], dr["YL"][ct * 128:(ct + 1) * 128, t0:t0 + NT], ys.t[:], reads=[ys.r])
            if ct == KT - 1:
                ge = gend[i % 2]
                S.dma("sp", sem_g[i % 2], dr["GE"][i], ge.t[:].rearrange("p a b c e -> p (a b c e)"), reads=[ge.r])

    per = KT * 8
    for i in range(TILES):
        tile_start(i)
        lo, hi = i * per, (i + 1) * per
        for n in range(lo - 2, hi + 1):
            if lo <= n + 2 < hi:
                st0(n + 2)
            if lo <= n + 1 < hi:
                st1(n + 1)
            if lo <= n < hi:
                st2(n)
                st3(n)
            if lo <= n - 1 < hi:
                st4(n - 1)


def s5_chain(k, dr, TILES, SEGT):
    S = k.S
    NCH = TILES * 4
    v = VC(k, "chain")
    sem = S.new_dma_sem("chain")
    gE = k.sb("gE", [128, 64, NCH, 2], F32).t
    Sr = k.sb("Sr", [128, 64, NCH], F32).t
    Si = k.sb("Si", [128, 64, NCH], F32).t
    Hin = k.sb("Hin", [128, 64, NCH, 2], F32).t
    t1 = k.sb("ct1", [128, 64, NCH], F32).t
    t2 = k.sb("ct2", [128, 64, NCH], F32).t
    for d in range(2):
        for i in range(TILES):
            S.dma("sp", sem, gE[:, :, 4 * i:4 * i + 4, :],
                  dr["GE"][i].rearrange("p (a b c e) -> p a b c e", a=2, b=64, c=4)[:, d], reads=[], writes=[v.r])
        Lr = k.L127.t[:, d, 0, :].unsqueeze(2).to_broadcast([128, 64, NCH])
        Li = k.L127.t[:, d, 1, :].unsqueeze(2).to_broadcast([128, 64, NCH])
        v.tt(t1[:], gE[:, :, :, 0], Lr, MUL)
        v.tt(t2[:], gE[:, :, :, 1], Li, MUL)
        v.tt(Sr[:], t1[:], t2[:], SUB)
        v.tt(t1[:], gE[:, :, :, 0], Li, MUL)
        v.tt(t2[:], gE[:, :, :, 1], Lr, MUL)
        v.tt(Si[:], t1[:], t2[:], ADD)
        ar = k.L128.t[:, d, 0, :]
        ai = k.L128.t[:, d, 1, :]
        order = list(range(NCH)) if d == 0 else list(range(NCH - 1, -1, -1))
        a1 = t1[:, :, 0]
        a2 = t2[:, :, 0]
        for n, kk in enumerate(order):
            if n == 0:
                v.ms(Hin[:, :, kk, :], 0.0)
            if n == NCH - 1:
                break
            nxt = order[n + 1]
            pr_, pi_ = Hin[:, :, kk, 0], Hin[:, :, kk, 1]
            v.tt(a1, ar, pr_, MUL)
            v.tt(a2, ai, pi_, MUL)
            v.tt(a1, a1, a2, SUB)
            v.tt(Hin[:, :, nxt, 0], a1, Sr[:, :, kk], ADD)
            v.tt(a1, ar, pi_, MUL)
            v.tt(a2, ai, pr_, MUL)
            v.tt(a1, a1, a2, ADD)
            v.tt(Hin[:, :, nxt, 1], a1, Si[:, :, kk], ADD)
            bnd = (nxt % (4 * SEGT) == 0) if d == 0 else ((nxt + 1) % (4 * SEGT) == 0)
            if bnd:
                v.ts(Hin[:, :, nxt, :], Hin[:, :, nxt, :], k.keep.t[:, 0:1], MUL)
        for i in range(TILES):
            S.dma("sp", sem, dr["HIN"][i].rearrange("p (a b c e) -> p a b c e", a=2, b=64, c=4)[:, d],
                  Hin[:, :, 4 * i:4 * i + 4, :], reads=[v.r], writes=[v.r])


def s5_carry_tile(k, dr, i, yact):
    S = k.S
    t0 = i * NT
    c5 = k.c5
    hin = c5["hin"][i % 2]
    S.dma("pool", c5["sem_h"][i % 2], hin.t[:].rearrange("p a b c e -> p (a b c e)"), dr["HIN"][i], writes=[hin.r])
    c5["pp"].push([dr["PPL"][d, ct].rearrange("p a b c -> p (a b c)") for ct in range(KT) for d in range(2)])
    c5["cs"].push([dr["CRI"][ct] for ct in range(KT)])

    def build_ch(ct):
        cs = c5["cs"].get()
        c4v = cs.t[:].rearrange("p (d q w c) -> p d q w c", d=2, q=4, w=2)
        chb = c5["chb"][ct % 2]
        for d in range(2):
            Crb = c4v[:, d, :, 0, :].unsqueeze(2).to_broadcast([128, 4, 4, 32])
            Cib = c4v[:, d, :, 1, :].unsqueeze(2).to_broadcast([128, 4, 4, 32])
            Hr = hin.t[:, d, 4 * ct:4 * ct + 4, :, 0].unsqueeze(3).to_broadcast([128, 4, 4, 32])
            Hi = hin.t[:, d, 4 * ct:4 * ct + 4, :, 1].unsqueeze(3).to_broadcast([128, 4, 4, 32])
            A, B = c5["ta"][d], c5["tb"][d]
            k.TT("dve", A.t[:], Crb, Hr, MUL, [cs.r, hin.r], [A.r])
            k.TT("dve", B.t[:], Cib, Hi, MUL, [cs.r, hin.r], [B.r])
            k.TT("dve", chb.t[:, d, :, :, 0, :], A.t[:], B.t[:], SUB, [A.r, B.r], [chb.r])
            k.TT("dve", A.t[:], Crb, Hi, MUL, [cs.r, hin.r], [A.r])
            k.TT("dve", B.t[:], Cib, Hr, MUL, [cs.r, hin.r], [B.r])
            k.STT(chb.t[:, d, :, :, 1, :], A.t[:], -1.0, B.t[:], MUL, SUB, [A.r, B.r], [chb.r])
        return chb

    chn = build_ch(0)
    for ct in range(KT):
        chb = chn
        yb = k.bank[6 + ct % 2]
        pps = [c5["pp"].get(), c5["pp"].get()]
        yl = c5["yl"][ct % 2]
        S.dma("pool", c5["sem_yl"][ct % 2], yl.t[:], dr["YL"][ct * 128:(ct + 1) * 128, t0:t0 + NT], writes=[yl.r])
        for q in range(4):
            n = 0
            for c4 in range(4):
                for d in range(2):
                    pv = pps[d].t[:].rearrange("p (q r j) -> p q r j", q=4, r=2)
                    for ri in range(2):
                        k.MM(yb.t[32 * q:32 * q + 32, c4 * 128:(c4 + 1) * 128], chb.t[:, d, q, c4, ri, :], pv[:, q, ri, :],
                             n == 0, n == 15, [chb.r, pps[d].r], [yb.r], skip_group_check=True, tile_position=(0, 32 * q))
                        n += 1
        if ct + 1 < KT:
            chn = build_ch(ct + 1)
        k.TT("dve", yl.t[:], yl.t[:], yb.t[:], ADD, [yl.r, yb.r], [yl.r])
        if "YA" in dr:
            S.dma("pool", c5["sem_yl"][ct % 2], dr["YA"][ct * 128:(ct + 1) * 128, t0:t0 + NT], yl.t[:], reads=[yl.r])
        g2 = c5["g2"][ct % 2]
        k.ACTF(g2.t[:], yl.t[:], AF.Square, [yl.r], [g2.r])
        k.TS("dve", g2.t[:], g2.t[:], 0.044715, 1.0, MUL, ADD, [g2.r], [g2.r])
        k.TT("dve", g2.t[:], g2.t[:], yl.t[:], MUL, [g2.r, yl.r], [g2.r])
        k.ACTF(g2.t[:], g2.t[:], AF.Sigmoid, [g2.r], [g2.r], scale=1.5957691216057308)
        k.TT("dve", yact.t[:, ct, :], g2.t[:], yl.t[:], MUL, [g2.r, yl.r], [yact.rs[ct]])


def pass_d(k, dr, TILES, SEGT):
    S = k.S
    NTOK = TILES * NT
    ws = Stream(k, "WD", 1024, BF16, 3)
    xmh = [k.sb("xmh%d" % i, [128, CT, NT + 4], BF16) for i in range(2)]
    sem_x = [S.new_dma_sem("xmh%d" % i) for i in range(2)]
    xcb = [k.sb("xcb%d" % i, [128, NT], BF16) for i in range(2)]
    qst = [k.sb("qst%d" % i, [128, NT], BF16) for i in range(2)]
    kst = [k.sb("kst%d" % i, [128, NT], BF16) for i in range(2)]
    vst = [k.sb("vst%d" % i, [128, NT], BF16) for i in range(2)]
    kts = [k.sb("kts%d" % i, [128, 4, 128], BF16) for i in range(2)]
    vts = [k.sb("vts%d" % i, [128, 4, 128], BF16) for i in range(2)]
    gst = [k.sb("gst%d" % i, [128, 4, 64], F32) for i in range(2)]
    sems = {n: [S.new_dma_sem("pd%s%d" % (n, i)) for i in range(2)] for n in ("xc", "q", "k", "kt", "vt", "g", "w")}
    wgf = k.sb("wgf", [128, 3 * CT * 64], F32)
    wg = k.sb("wg", [128, 3, CT, 64], BF16)
    bgf = k.sb("bgf", [128, 64], F32)
    bgb = k.sb("bgb", [128, 64], BF16)
    onesb = k.sb("onesb", [128, 128], BF16)
    zerob = k.sb("zerob", [128, 256], BF16)
    k.MS("dve", zerob.t[:], 0.0, [zerob.r])
    k.MS("dve", bgf.t[:], 0.0, [bgf.r])
    S.dma("sp", sems["w"][0], wgf.t[:], dr["wg"][:, :], writes=[wgf.r])
    k.CP("dve", wg.t[:].rearrange("p a b c -> p (a b c)"), wgf.t[:], [wgf.r], [wg.r])
    S.dma("sp", sems["w"][1], bgf.t[0:1, :], dr["bgate"][:, :], writes=[bgf.r])
    k.CP("dve", bgb.t[:], bgf.t[:], [bgf.r], [bgb.r])
    k.MS("dve", onesb.t[:], 1.0, [onesb.r])
    XMv = dr["XM"].rearrange("(c p) t -> p c t", p=128)
    n = 0
    for i in range(TILES):
        t0 = i * NT
        xb = xmh[i % 2]
        lo = max(t0 - 2, 0)
        hi = min(t0 + NT + 2, NTOK)
        S.dma("pool", sem_x[i % 2], xb.t[:, :, lo - (t0 - 2):hi - (t0 - 2)], XMv[:, :, lo:hi], writes=[xb.r])
        if i == 0:
            k.MS("pool", xb.t[:, :, 0:2], 0.0, [xb.r])
        elif i % SEGT == 0:
            k.TS("pool", xb.t[:, :, 0:2], xb.t[:, :, 0:2], k.keep.t[:, 0:1], None, MUL, None, [xb.r, k.keep.r], [xb.r])
        if i == TILES - 1:
            k.MS("pool", xb.t[:, :, NT + 2:NT + 4], 0.0, [xb.r])
        elif (i + 1) % SEGT == 0:
            k.TS("pool", xb.t[:, :, NT + 2:NT + 4], xb.t[:, :, NT + 2:NT + 4], k.keep.t[:, 0:1], None, MUL, None, [xb.r, k.keep.r], [xb.r])
        ws.push([dr["mlD_b"][ct] for ct in range(CT)])
        gps = k.bank[7]
        k.MM(gps.t[:, 0:256], zerob.t[:, 0:128], zerob.t[:], True, False, [zerob.r], [gps.r], skip_group_check=True)
        for ct in range(CT):
            w = ws.get()
            b = n % 2
            n += 1
            pc = k.bank[b]
            for tau in range(5):
                k.MM(pc.t[:], w.t[:, tau * 128:(tau + 1) * 128], xb.t[:, ct, tau:tau + NT], tau == 0, tau == 4, [w.r, xb.r], [pc.r])
            xc = xcb[b]
            k.ACTF(xc.t[:], pc.t[:], AF.Silu, [pc.r, k.mlvec.r], [xc.r], bias=k.mlvec.t[:, ct:ct + 1])
            S.dma("sp", sems["xc"][b], dr["XC"][ct * 128:(ct + 1) * 128, t0:t0 + NT], xc.t[:], reads=[xc.r])
            pq, pk, pv = k.bank[2], k.bank[3], k.bank[4]
            k.MM(pq.t[:], w.t[:, 640:768], xc.t[:], True, True, [w.r, xc.r], [pq.r])
            k.MM(pk.t[:], w.t[:, 768:896], xc.t[:], True, True, [w.r, xc.r], [pk.r])
            k.MM(pv.t[:], w.t[:, 896:1024], xb.t[:, ct, 2:NT + 2], True, True, [w.r, xb.r], [pv.r])
            q_, k_, v_ = qst[b], kst[b], vst[b]
            k.CP("dve", q_.t[:], pq.t[:], [pq.r], [q_.r])
            k.CP("act", k_.t[:], pk.t[:], [pk.r], [k_.r])
            k.CP("dve", v_.t[:], pv.t[:], [pv.r], [v_.r])
            S.dma("sp", sems["q"][b], dr["QT"][ct * 128:(ct + 1) * 128, t0:t0 + NT], q_.t[:], reads=[q_.r])
            S.dma("sp", sems["k"][b], dr["KT"][ct * 128:(ct + 1) * 128, t0:t0 + NT], k_.t[:], reads=[k_.r])
            pkt, pvt = k.bank[5], k.bank[6]
            for c4 in range(4):
                k.MM(pkt.t[:, c4 * 128:(c4 + 1) * 128], xc.t[:, c4 * 128:(c4 + 1) * 128], w.t[:, 768:896], True, True, [w.r, xc.r], [pkt.r])
                k.MM(pvt.t[:, c4 * 128:(c4 + 1) * 128], xb.t[:, ct, 2 + c4 * 128:2 + (c4 + 1) * 128], w.t[:, 896:1024], True, True, [w.r, xb.r], [pvt.r])
            kt_, vt_ = kts[b], vts[b]
            k.CP("act", kt_.t[:].rearrange("p a b -> p (a b)"), pkt.t[:], [pkt.r], [kt_.r])
            k.CP("dve", vt_.t[:].rearrange("p a b -> p (a b)"), pvt.t[:], [pvt.r], [vt_.r])
            S.dma("sp", sems["kt"][b], dr["KTOK"][4 * i:4 * i + 4, :, ct * 128:(ct + 1) * 128].rearrange("c t h -> t c h"), kt_.t[:], reads=[kt_.r])
            S.dma("sp", sems["vt"][b], dr["VTOK"][4 * i:4 * i + 4, :, ct * 128:(ct + 1) * 128].rearrange("c t h -> t c h"), vt_.t[:], reads=[vt_.r])
            for c4 in range(4):
                cs_ = slice(c4 * 128, (c4 + 1) * 128)
                for j, src in enumerate((q_, k_, v_)):
                    first = False
                    k.MM(gps.t[:, c4 * 64:(c4 + 1) * 64], src.t[:, cs_], wg.t[:, j, ct, :], first, False, [src.r, wg.r], [gps.r], skip_group_check=True)
        for c4 in range(4):
            k.MM(gps.t[:, c4 * 64:(c4 + 1) * 64], onesb.t[:], bgb.t[:], False, c4 == 3, [onesb.r, bgb.r], [gps.r], skip_group_check=True)
        g_ = gst[i % 2]
        k.CP("dve", g_.t[:].rearrange("p a b -> p (a b)"), gps.t[:, 0:256], [gps.r], [g_.r])
        S.dma("sp", sems["g"][i % 2], dr["G"][4 * i:4 * i + 4].rearrange("c t g -> t c g"), g_.t[:], reads=[g_.r])


def ml_prep(k, dr, TILES, SEGT):
    S = k.S
    NCH = TILES * 4
    v = VC(k, "mlprep")
    sem = S.new_dma_sem("mlprep")
    gall = k.sb("gall", [128, NCH, 64], F32).t
    lf = k.sb("lfall", [128, NCH, 2, 16], F32).t
    bc = k.sb("bcum", [128, NCH, 2, 16], F32).t
    io_i = k.sb("mio_i", [128, 128], I32).t
    io_f = k.sb("mio_f", [128, 128], F32).t
    lnsc = k.sb("lnsc", [128, 1], F32).t
    onec = k.sb("onec", [128, 1], F32).t
    S.dma("sp", sem, gall[:], dr["G"].rearrange("c t g -> t c g"), writes=[v.r])
    S.op("pool", lambda h: h.iota(io_i[:], [[1, 128]], 0, -1), [], [v.r])
    v.cp(io_f[:], io_i[:])
    v.ts(k.TRI[0].t[:], io_f[:], 0.0, ALU.is_ge)
    v.ts(k.TRI[1].t[:], io_f[:], 0.0, ALU.is_le)
    v.ts(k.ident32.t[:], io_f[:], 0.0, ALU.is_equal)
    v.ms(lnsc[:], -0.5 * float(np.log(DH)))
    v.ms(onec[:], 1.0)
    g5 = gall[:].rearrange("p c (d w h) -> p c d w h", d=2, w=2)
    v.act(lf[:], g5[:, :, :, 1, :], AF.Exp, scale=-1.0)
    v.act(lf[:], lf[:], AF.Ln, bias=onec[:, 0:1])
    v.ts(lf[:], lf[:], -1.0, MUL)
    half = NCH // 2 if NCH >= 2 else 1
    for d in range(2):
        for (lhs, dst) in ((k.TRI[d], bc), (k.ones32, k.EG.t)):
            for c0 in range(0, NCH, 32):
                c1 = min(c0 + 32, NCH)
                ps = k.bank[(c0 // 32) % 2]
                k.MM(ps.t[:, 0:(c1 - c0) * 16].rearrange("p (c h) -> p c h", h=16), lhs.t[:], lf[:, c0:c1, d, :], True, True, [v.r], [ps.r])
                k.CP("dve", dst[:, c0:c1, d, :], ps.t[:, 0:(c1 - c0) * 16].rearrange("p (c h) -> p c h", h=16), [ps.r], [v.r])
    v.tt(k.ED.t[:], g5[:, :, :, 0, :], bc[:], SUB)
    v.act(k.ED.t[:], k.ED.t[:], AF.Exp, bias=lnsc[:, 0:1])
    v.act(k.EB.t[:], bc[:], AF.Exp, scale=-1.0)
    v.act(k.EG.t[:], k.EG.t[:], AF.Exp)
    for kk in range(NCH):
        if (kk + 1) % (4 * SEGT) == 0 and kk + 1 < NCH:
            v.ts(k.EG.t[:, kk, 0, :], k.EG.t[:, kk, 0, :], k.keep.t[:, 0:1], MUL)
        if kk % (4 * SEGT) == 0 and kk > 0:
            v.ts(k.EG.t[:, kk, 1, :], k.EG.t[:, kk, 1, :], k.keep.t[:, 0:1], MUL)
    k.mlprep_res = v.r


def pass_e(k, dr, TILES):
    S = k.S
    NCH = TILES * 4
    PR = k.mlprep_res
    C32 = k.sb("C32", [128, NH, 2, 257], F32, nres=NH)
    Cb = k.sb("Cb", [128, NH, 2, 257], BF16, nres=NH)
    qT = [k.sb("qT%d" % i, [128, CT, 128], BF16) for i in range(2)]
    kT = [k.sb("kT%d" % i, [128, CT, 128], BF16) for i in range(2)]
    ktk = [k.sb("ktk%d" % i, [128, DI], BF16) for i in range(2)]
    vau = [k.sb("vau%d" % i, [128, NH, 260], BF16) for i in range(2)]
    sem_l = [[S.new_dma_sem("pe%d_%d" % (j, i)) for i in range(2)] for j in range(4)]
    hx = k.sb("hx", [128, NH, 257], F32, nres=NH)
    hbuf = k.sb("hbuf", [128, DI], F32, nres=NH)
    sem_hb = S.new_dma_sem("hbst")
    sem_hl = S.new_dma_sem("hbld")
    khat = k.sb("khat", [128, NH, DH], BF16, nres=NH)
    smT = k.sb("smT", [128, NH, 128], BF16, nres=NH)
    dn = k.sb("dn", [128, NH], F32)
    dn2 = k.sb("dn2", [128, NH], F32)
    xck = [k.sb("xck%d" % i, [128, CT // 2, 128], BF16) for i in range(1)]
    zk = [k.sb("zk%d" % i, [128, CT // 2, 128], F32) for i in range(1)]
    oab = [k.sb("oab%d" % i, [128, CT // 2, 128], BF16, nres=CT // 2) for i in range(1)]
    skb = [k.sb("skb%d" % i, [128, 128], F32) for i in range(2)]
    o1b = [k.sb("o1b%d" % i, [128, 128], F32) for i in range(2)]
    bst = k.sb("bst", [128, NH, 6], F32, nres=NH)
    mv = k.sb("mv", [128, NH, 2], F32)
    rs = k.sb("rs", [128, NH], F32)
    sem_p = [S.new_dma_sem("pep%d" % i) for i in range(4)]
    pq = [[Res("pq%d_%d" % (b, j)) for j in range(4)] for b in range(2)]
    for b in range(2):
        k.MS("pool", vau[b].t[:, :, 256:260], 1.0, [vau[b].r])
    QTv = dr["QT"].rearrange("(c p) t -> p c t", p=128)
    KTv = dr["KT"].rearrange("(c p) t -> p c t", p=128)
    XCv = dr["XC"].rearrange("(c p) t -> p c t", p=128)
    Zv = dr["Z"].rearrange("(c p) t -> p c t", p=128)
    OAv = dr["OA"].rearrange("(c p) t -> p c t", p=128)
    hx3 = hx.t
    seq = [(d, kk) for d in (1, 0) for kk in (list(range(NCH)) if d == 0 else list(range(NCH - 1, -1, -1)))]

    def emit_loads(n):
        d, kk = seq[n]
        b = n % 2
        ts_ = slice(kk * 128, (kk + 1) * 128)
        q_, k_, kt_, va = qT[b], kT[b], ktk[b], vau[b]
        S.dma("sp", sem_l[0][b], q_.t[:], QTv[:, :, ts_], writes=[q_.r])
        S.dma("sp", sem_l[1][b], k_.t[:], KTv[:, :, ts_], writes=[k_.r])
        S.dma("pool", sem_l[2][b], kt_.t[:], dr["KTOK"][kk], writes=[kt_.r])
        S.dma("pool", sem_l[3][b], va.t[:, :, 0:256], dr["VTOK"][kk].rearrange("t (h e) -> t h e", h=NH), writes=[va.r])
        k.MS("pool", va.t[:, :, 256:260], 1.0, [va.r])

    def stage1(n):
        d, kk = seq[n]
        b = n % 2
        q_, k_, kt_ = qT[b], kT[b], ktk[b]
        for g4 in range(NH // 4):
            pS = k.bank[g4 % 2]
            for h in range(4 * g4, 4 * g4 + 4):
                cols = slice((h % 4) * 128, (h % 4 + 1) * 128)
                for i2 in range(2):
                    k.MM(pS.t[:, cols], k_.t[:, 2 * h + i2, :], q_.t[:, 2 * h + i2, :], i2 == 0, i2 == 1, [k_.r, q_.r], [pS.r], skip_group_check=True)
            for h in range(4 * g4, 4 * g4 + 4):
                cols = slice((h % 4) * 128, (h % 4 + 1) * 128)
                ed = k.ED.t[:, kk, d, h:h + 1]
                k.STT(smT.t[:, h, :], pS.t[:, cols], ed, k.TRI[d].t[:], MUL, MUL, [pS.r, PR], [smT.rs[h]])
                hs = slice(h * DH, (h + 1) * DH)
                k.TS("dve", khat.t[:, h, :], kt_.t[:, hs], ed, None, MUL, None, [kt_.r, PR], [khat.rs[h]])

    emit_loads(0)
    for n in range(len(seq)):
        d, kk = seq[n]
        first_of_dir = (n == 0 or seq[n - 1][0] != d)
        if first_of_dir:
            for h in range(NH):
                k.MS("dve", C32.t[:, h], 0.0, [C32.rs[h]])
                k.MS("pool", Cb.t[:, h], 0.0, [Cb.rs[h]])
        if True:
            b = n % 2
            ts_ = slice(kk * 128, (kk + 1) * 128)
            q_, k_, kt_, va = qT[b], kT[b], ktk[b], vau[b]
            if n + 1 < len(seq):
                emit_loads(n + 1)
            if d == 0:
                S.dma("pool", sem_hl, hbuf.t[:], dr["HB"][kk], reads=[k.hbres], writes=hbuf.rs)
            if n == 0:
                stage1(0)
            for h in range(NH):
                eg = k.EG.t[:, kk, d, h:h + 1]
                k.ACTF(C32.t[:, h], C32.t[:, h], AF.Identity, [C32.rs[h], PR], [C32.rs[h]], scale=eg)
            for h in range(NH):
                pX = k.bank[2 + h % 2]
                for i2 in range(2):
                    k.MM(pX.t[:, 0:257], q_.t[:, 2 * h + i2, :], Cb.t[:, h, i2, :], i2 == 0, False, [q_.r, Cb.rs[h]], [pX.r])
                k.MM(pX.t[:, 0:257], smT.t[:, h, :], va.t[:, h, 0:257], False, True, [smT.rs[h], va.r], [pX.r])
                k.CP("act", hx3[:, h, :], pX.t[:, 0:257], [pX.r], [hx.rs[h]])
            for h in range(NH):
                pC = [k.bank[4 + 2 * (h % 2)], k.bank[5 + 2 * (h % 2)]]
                eg = k.EG.t[:, kk, d, h:h + 1]
                for i2 in range(2):
                    k.MM(pC[i2].t[:, 0:257], khat.t[:, h, i2 * 128:(i2 + 1) * 128], va.t[:, h, 0:257], True, True, [khat.rs[h], va.r], [pC[i2].r])
                    k.STT(C32.t[:, h, i2, :], pC[i2].t[:, 0:257], eg, C32.t[:, h, i2, :], MUL, ADD, [pC[i2].r, PR, C32.rs[h]], [C32.rs[h]])
                k.CP("act", Cb.t[:, h], C32.t[:, h], [C32.rs[h]], [Cb.rs[h]])
            if n + 1 < len(seq):
                stage1(n + 1)
            ycol = hx3[:, :, 256]
            k.TT("dve", dn.t[:], ycol, k.EB.t[:, kk, d, :], ALU.max, hx.rs + [PR], [dn.r])
            k.STT(dn2.t[:], ycol, -1.0, dn.t[:], MUL, ALU.max, hx.rs + [dn.r], [dn2.r])
            k.RECIP(dn2.t[:], dn2.t[:], [dn2.r], [dn2.r])
            hb3 = hbuf.t[:].rearrange("p (h e) -> p h e", h=NH)
            rb = dn2.t[:].unsqueeze(2).to_broadcast([128, NH, DH])
            if d == 1:
                k.TT("dve", hb3, hx3[:, :, 0:256], rb, MUL, hx.rs + [dn2.r], hbuf.rs)
                S.dma("sp", sem_hb, dr["HB"][kk], hbuf.t[:], reads=hbuf.rs, writes=[k.hbres])
                continue
            k.TT("dve", hx3[:, :, 0:256], hx3[:, :, 0:256], rb, MUL, hx.rs + [dn2.r], hx.rs)
            k.TT("pool", hb3, hb3, hx3[:, :, 0:256], ADD, hbuf.rs + hx.rs, hbuf.rs)
            for h in range(NH):
                hs = slice(h * DH, (h + 1) * DH)
                S.op("dve", lambda hh, h=h, hs=hs: hh.bn_stats(out=bst.t[:, h, :], in_=hbuf.t[:, hs]), [hbuf.rs[h]], [bst.rs[h]])
            for h in range(NH):
                S.op("dve", lambda hh, h=h: hh.bn_aggr(out=mv.t[:, h, :], in_=bst.t[:, h, :]), [bst.rs[h]], [mv.r])
            k.ACTF(rs.t[:], mv.t[:, :, 1], AF.Sqrt, [mv.r, k.epsc.r], [rs.r], bias=k.epsc.t[:, 0:1])
            k.RECIP(rs.t[:], rs.t[:], [rs.r], [rs.r])
            for h in range(NH):
                hs = slice(h * DH, (h + 1) * DH)
                k.TS("dve", hbuf.t[:, hs], hbuf.t[:, hs], mv.t[:, h, 0:1], rs.t[:, h:h + 1], SUB, MUL, [hbuf.rs[h], mv.r, rs.r], [hbuf.rs[h]])
            for hf in range(2):
                c0 = hf * (CT // 2)
                xc_, z_, oa_ = xck[0], zk[0], oab[0]
                S.dma("pool", sem_p[hf], xc_.t[:], XCv[:, c0:c0 + CT // 2, ts_], writes=[xc_.r])
                S.dma("pool", sem_p[2], z_.t[:], Zv[:, c0:c0 + CT // 2, ts_], writes=[z_.r])
                k.ACTF(z_.t[:], z_.t[:], AF.Sigmoid, [z_.r], [z_.r])
                for g4 in range(CT // 8):
                    pT = k.bank[g4 % 2]
                    for cl in range(4 * g4, 4 * g4 + 4):
                        ct = c0 + cl
                        cols = slice((cl % 4) * 128, (cl % 4 + 1) * 128)
                        k.TR(pT.t[:, cols], hbuf.t[:, ct * 128:(ct + 1) * 128], k.ident32.t[:], [hbuf.rs[ct // 2], PR], [pT.r])
                    for cl in range(4 * g4, 4 * g4 + 4):
                        ct = c0 + cl
                        cols = slice((cl % 4) * 128, (cl % 4 + 1) * 128)
                        sk = skb[ct % 2]
                        o1 = o1b[ct % 2]
                        k.ACTF(sk.t[:], xc_.t[:, cl, :], AF.Identity, [xc_.r, k.mlvec.r], [sk.r], scale=k.mlvec.t[:, 64 + ct:65 + ct])
                        k.STT(o1.t[:], pT.t[:, cols], k.mlvec.t[:, 32 + ct:33 + ct], sk.t[:], MUL, ADD, [pT.r, sk.r, k.mlvec.r], [o1.r])
                        k.TT("pool", oa_.t[:, cl, :], o1.t[:], z_.t[:, cl, :], MUL, [o1.r, z_.r], [oa_.rs[cl]])
                S.dma("sp", sem_p[3], OAv[:, c0:c0 + CT // 2, ts_], oa_.t[:], reads=oa_.rs)


def build(TILES, SEGT, debug=(), stop_after="F"):
    NTOK = TILES * NT
    NCH = TILES * 4
    nc = bass.Bass("TRN2", target_bir_lowering=False)
    dbg = set(debug)

    def din(name, shape, dt=F32):
        return nc.dram_tensor(name, list(shape), dt, kind="ExternalInput").ap()

    def dscr(name, shape, dt):
        kind = "ExternalOutput" if name in dbg else "Internal"
        return nc.dram_tensor(name, list(shape), dt, kind=kind).ap()

    xT = din("xT", [D, NTOK])
    keep_d = din("keep", [128, 1])
    gvec_d = din("gvec", [128, 7 * KT])
    wsrc = {
        "ffn_win": din("ffn_win", [4 * JT, 128, 2 * KT * 128]),
        "ffn_wout": din("ffn_wout", [4 * KT, 128, JT * 128]),
        "s5_win": din("s5_win", [KT, 128, KT * 128]),
        "wglu": din("wglu", [2 * KT, 128, KT * 128]),
        "ml_win": din("ml_win", [2 * CT, 128, KT * 128]),
        "mlD": din("mlD", [CT, 128, 1024]),
        "ml_wout": din("ml_wout", [KT, 128, CT * 128]),
    }
    dr = {}
    for kk_ in ("lamre", "lamim", "logstep"):
        dr[kk_] = din(kk_, [2, 128, 64])
    dr["bre"] = din("bre", [2, 128, 64 * 16])
    dr["bim"] = din("bim", [2, 128, 64 * 16])
    dr["CRI"] = din("CRI", [KT, 128, 2 * 4 * 2 * 32])
    dr["s5d"] = din("s5d", [128, KT])
    dr["wg"] = din("wg", [128, 3 * CT * 64])
    dr["bgate"] = din("bgate", [1, 64])
    mlvec_d = din("mlvec", [128, 96])
    yT = nc.dram_tensor("yT", [D, NTOK], F32, kind="ExternalOutput").ap()

    wb = {n: dscr(n + "_b", a.shape, BF16) for n, a in wsrc.items()}
    X1 = dscr("X1", [D, NTOK], F32)
    dr["U"] = dscr("U", [D, NTOK], F32)
    dr["YL"] = dscr("YL", [D, NTOK], F32)
    dr["GE"] = dscr("GE", [TILES, 128, 2 * 64 * 4 * 2], F32)
    dr["HIN"] = dscr("HIN", [TILES, 128, 2 * 64 * 4 * 2], F32)
    dr["TAB"] = dscr("TAB", [2, KT, 128, 4, 4, 128], F32)
    dr["PPL"] = dscr("PPL", [2, KT, 128, 4, 2, 128], BF16)
    X2 = dscr("X2", [D, NTOK], F32)
    if "YA" in dbg:
        dr["YA"] = dscr("YA", [D, NTOK], F32)
    X4 = dscr("X4", [D, NTOK], F32)
    XM = dscr("XM", [DI, NTOK], BF16)
    Z = dscr("Z", [DI, NTOK], F32)
    U = dr["U"]
    dr["XM"] = XM
    dr["Z"] = Z
    dr["XC"] = dscr("XC", [DI, NTOK], BF16)
    dr["QT"] = dscr("QT", [DI, NTOK], BF16)
    dr["KT"] = dscr("KT", [DI, NTOK], BF16)
    dr["KTOK"] = dscr("KTOK", [NCH, 128, DI], BF16)
    dr["VTOK"] = dscr("VTOK", [NCH, 128, DI], BF16)
    dr["G"] = dscr("G", [NCH, 128, 64], F32)
    dr["HB"] = dscr("HB", [NCH, 128, DI], F32)
    dr["OA"] = dscr("OA", [DI, NTOK], BF16)
    X5 = dscr("X5", [D, NTOK], F32)

    def fm(ap):
        return ap.rearrange("(c p) t -> p c t", p=128)

    with ExitStack() as st:
        S = Sched(nc, st)
        k = K(nc, st, S)
        k.wres = Res("wres")
        k.bank = []
        for i in range(8):
            t = st.enter_context(nc.psum_tensor("bank%d" % i, [128, 512], F32))
            k.bank.append(Buf(t, "bank%d" % i))
        k.ffn_win_b = wb["ffn_win"]
        k.ffn_wout_b = wb["ffn_wout"]
        io_sem = [S.new_dma_sem("io%d" % i) for i in range(4)]
        st_sem = [S.new_dma_sem("st%d" % i) for i in range(6)]
        k.ones32 = k.sb("ones32", [128, 128], F32)
        k.epsc = k.sb("epsc", [128, 1], F32)
        k.gvec = k.sb("gvec", [128, 7, KT], F32)
        k.keep = k.sb("keepc", [128, 1], F32)
        k.MS("dve", k.ones32.t[:], 1.0, [k.ones32.r])
        k.MS("dve", k.epsc.t[:], EPS, [k.epsc.r])
        S.dma("sp", io_sem[0], k.gvec.t[:].rearrange("p a b -> p (a b)"), gvec_d[:, :], writes=[k.gvec.r])
        S.dma("sp", io_sem[0], k.keep.t[:], keep_d[:, :], writes=[k.keep.r])
        k.mlvec = k.sb("mlvec", [128, 96], F32)
        S.dma("sp", io_sem[0], k.mlvec.t[:], mlvec_d[:, :], writes=[k.mlvec.r])
        k.hbres = Res("hbres")
        dr["mlD_b"] = wb["mlD"]

        def phase():
            ph = ExitStack()
            k.ph = ph
            return ph

        def common_bufs():
            k.W = Stream(k, "W", JT * 128, BF16, 4)
            k.hn = k.sb("hn", [128, KT, NT], BF16, nres=KT)
            k.h = k.sb("h", [128, JT, NT], BF16, nres=JT)
            k.sq = [k.sb("sq%d" % i, [128, NT], F32) for i in range(2)]
            k.sg = [k.sb("sg%d" % i, [128, NT], F32) for i in range(2)]
            k.rstd = k.sb("rstd", [128, NT], F32)

        with phase():
            for _ in cast_weights(k, [(wsrc["ffn_win"][0:JT], wb["ffn_win"][0:JT]), (wsrc["ffn_wout"][0:KT], wb["ffn_wout"][0:KT]),
                                      (wsrc["s5_win"], wb["s5_win"])]):
                pass
            barrier(S)

        with phase():
            common_bufs()
            x = k.sb("xa", [128, KT, NT], F32, nres=KT)
            ust = [k.sb("ust%d" % i, [128, NT], F32) for i in range(2)]
            rest = [(wsrc["ffn_win"][JT:4 * JT], wb["ffn_win"][JT:4 * JT]), (wsrc["ffn_wout"][KT:4 * KT], wb["ffn_wout"][KT:4 * KT])]
            rest += [(wsrc[n], wb[n]) for n in ("wglu", "ml_win", "mlD", "ml_wout")]
            cgen = cast_weights(k, rest, queue="pool")
            nrest = sum(a.shape[0] * ((a.shape[2] + 4095) // 4096) for a, _ in rest)
            per_tile = -(-nrest // TILES)
            state = {"left": 0}

            def bg():
                if state["left"] > 0:
                    state["left"] -= 1
                    next(cgen, None)
            k.bg = bg
            for i in range(TILES):
                t0 = i * NT
                state["left"] = per_tile
                S.dma("pool", io_sem[0], x.t[:], fm(xT)[:, :, t0:t0 + NT], writes=x.rs)
                ffn(k, x, 0, 0)
                while state["left"] > 0:
                    bg()
                S.dma("pool", st_sem[0], fm(X1)[:, :, t0:t0 + NT], x.t[:], reads=x.rs)
                rms_stats(k, x)
                rms_apply(k, x, 1, k.hn)

                def cons(m, ps, t0=t0):
                    b = ust[m % 2]
                    k.CP("act", b.t[:], ps.t[:], [ps.r], [b.r])
                    S.dma("pool", st_sem[1 + m % 2], U[m * 128:(m + 1) * 128, t0:t0 + NT], b.t[:], reads=[b.r])
                proj(k, k.hn, [wb["s5_win"][m] for m in range(KT)], KT, cons)
            k.bg = None
            for _ in cgen:
                pass
            barrier(S)
        if stop_after == "A":
            S.emit_all()
            return nc, S

        s5scope = ExitStack()
        k.ph = s5scope
        k.BL = k.sb("BL", [128, 2, 2, KT, 128], BF16)
        k.CL = k.sb("CL", [128, 2, 64, 3, 32], BF16)
        k.L128 = k.sb("L128", [128, 2, 2, 64], F32)
        k.L127 = k.sb("L127", [128, 2, 2, 64], F32)
        with phase():
            s5_setup(k, dr)
            tmpc = k.sb("tmpc", [128, 2, 4, 2, 32], F32)
            for ct in range(KT):
                S.dma("sp", io_sem[1], tmpc.t[:].rearrange("p a b c e -> p (a b c e)"), dr["CRI"][ct], writes=[tmpc.r])
                k.CP("dve", k.CL.t[:, :, 4 * ct:4 * ct + 4, 0, :], tmpc.t[:, :, :, 0, :], [tmpc.r], [k.CL.r])
                k.TS("dve", k.CL.t[:, :, 4 * ct:4 * ct + 4, 1, :], tmpc.t[:, :, :, 0, :], -1.0, None, MUL, None, [tmpc.r], [k.CL.r])
                k.TS("dve", k.CL.t[:, :, 4 * ct:4 * ct + 4, 2, :], tmpc.t[:, :, :, 1, :], -1.0, None, MUL, None, [tmpc.r], [k.CL.r])
            barrier(S)
        with phase():
            pass_b(k, dr, TILES)
            barrier(S)
        with phase():
            s5_chain(k, dr, TILES, SEGT)
            barrier(S)
        s5scope.close()
        if stop_after == "B":
            S.emit_all()
            return nc, S

        with phase():
            common_bufs()
            x = k.sb("xc_", [128, KT, NT], F32, nres=KT)
            k.c5 = {
                "hin": [k.sb("hin%d" % i, [128, 2, 64, 4, 2], F32) for i in range(2)],
                "sem_h": [S.new_dma_sem("hin%d" % i) for i in range(2)],
                "pp": Stream(k, "PP", 4 * 2 * 128, BF16, 4, hold=2),
                "cs": Stream(k, "CS", 512, F32, 3, hold=2),
                "chb": [k.sb("chb%d" % i, [128, 2, 4, 4, 2, 32], BF16) for i in range(2)],
                "ta": [k.sb("cta%d" % i, [128, 4, 4, 32], F32) for i in range(2)],
                "tb": [k.sb("ctb%d" % i, [128, 4, 4, 32], F32) for i in range(2)],
                "yl": [k.sb("yl%d" % i, [128, NT], F32) for i in range(2)],
                "g2": [k.sb("g2_%d" % i, [128, NT], F32) for i in range(2)],
                "sem_yl": [S.new_dma_sem("yl%d" % i) for i in range(2)],
            }
            gsb = [k.sb("gsb%d" % i, [128, NT], F32) for i in range(2)]
            zst = [k.sb("zst%d" % i, [128, NT], F32) for i in range(2)]
            mst = [k.sb("mst%d" % i, [128, NT], BF16) for i in range(2)]
            for i in range(TILES):
                t0 = i * NT
                S.dma("pool", io_sem[0], x.t[:], fm(X1)[:, :, t0:t0 + NT], writes=x.rs)
                s5_carry_tile(k, dr, i, k.hn)
                held = {}

                def cons_glu(idx, ps):
                    m = idx // 2
                    if idx % 2 == 0:
                        held["v"] = ps
                        return
                    pv = held["v"]
                    g = gsb[m % 2]
                    k.ACTF(g.t[:], ps.t[:], AF.Sigmoid, [ps.r], [g.r])
                    k.TT("dve", g.t[:], g.t[:], pv.t[:], MUL, [g.r, pv.r], [g.r])
                    k.TT("dve", x.t[:, m, :], x.t[:, m, :], g.t[:], ADD, [g.r, x.rs[m]], [x.rs[m]])
                wl = []
                for m in range(KT):
                    wl += [wb["wglu"][m], wb["wglu"][KT + m]]
                proj(k, k.hn, wl, KT, cons_glu)
                if "X2" in dbg:
                    S.dma("pool", st_sem[3], fm(X2)[:, :, t0:t0 + NT], x.t[:], reads=x.rs)
                ffn(k, x, 2, 1)
                ffn(k, x, 3, 2)
                S.dma("pool", st_sem[0], fm(X4)[:, :, t0:t0 + NT], x.t[:], reads=x.rs)
                rms_stats(k, x)
                rms_apply(k, x, 4, k.hn)

                def cons_ml(m, ps, t0=t0):
                    if m < CT:
                        b = mst[m % 2]
                        k.CP("act", b.t[:], ps.t[:], [ps.r], [b.r])
                        S.dma("pool", st_sem[1 + m % 2], XM[m * 128:(m + 1) * 128, t0:t0 + NT], b.t[:], reads=[b.r])
                    else:
                        b = zst[m % 2]
                        k.CP("act", b.t[:], ps.t[:], [ps.r], [b.r])
                        S.dma("pool", st_sem[4 + m % 2], Z[(m - CT) * 128:(m - CT + 1) * 128, t0:t0 + NT], b.t[:], reads=[b.r])
                proj(k, k.hn, [wb["ml_win"][m] for m in range(2 * CT)], KT, cons_ml)
            barrier(S)
        if stop_after == "C":
            S.emit_all()
            return nc, S

        with phase():
            pass_d(k, dr, TILES, SEGT)
            barrier(S)
        if stop_after == "D":
            S.emit_all()
            return nc, S
        mlscope = ExitStack()
        k.ph = mlscope
        k.ED = k.sb("ED", [128, NCH, 2, 16], F32)
        k.EB = k.sb("EB", [128, NCH, 2, 16], F32)
        k.EG = k.sb("EG", [128, NCH, 2, 16], F32)
        k.TRI = [k.sb("TRI%d" % i, [128, 128], F32) for i in range(2)]
        k.ident32 = k.sb("ident32", [128, 128], F32)
        with phase():
            ml_prep(k, dr, TILES, SEGT)
            barrier(S)
        with phase():
            pass_e(k, dr, TILES)
            barrier(S)
        mlscope.close()
        if stop_after == "E":
            S.emit_all()
            return nc, S
        with phase():
            common_bufs()
            x = k.sb("xf_", [128, KT, NT], F32, nres=KT)
            OAv = dr["OA"].rearrange("(c p) t -> p c t", p=128)
            for i in range(TILES):
                t0 = i * NT
                S.dma("pool", io_sem[0], x.t[:], fm(X4)[:, :, t0:t0 + NT], writes=x.rs)
                S.dma("pool", io_sem[1], k.h.t[:, 0:CT, :], OAv[:, :, t0:t0 + NT], writes=k.h.rs)

                def cons_o(m, ps):
                    k.TT("dve", x.t[:, m, :], x.t[:, m, :], ps.t[:], ADD, [ps.r, x.rs[m]], [x.rs[m]])
                proj(k, k.h, [wb["ml_wout"][m] for m in range(KT)], CT, cons_o)
                if "X5" in dbg:
                    S.dma("pool", st_sem[3], fm(X5)[:, :, t0:t0 + NT], x.t[:], reads=x.rs)
                ffn(k, x, 5, 3)
                rms_stats(k, x)
                rms_apply(k, x, 6, x)
                S.dma("pool", st_sem[0], fm(yT)[:, :, t0:t0 + NT], x.t[:], reads=x.rs)
            barrier(S)
        barrier(S)
        S.emit_all()
    return nc, S


def _tile_rows(w, nk):
    K_, M_ = w.shape
    m = M_ // 128
    return np.ascontiguousarray(w.reshape(nk, 128, m, 128).transpose(2, 1, 0, 3).reshape(m, 128, nk * 128))


def prep_shared(inp):
    f = np.float32
    out = {}
    g = np.concatenate([np.asarray(inp["norm_g"], f).reshape(6, D), np.asarray(inp["final_g"], f).reshape(1, D)], 0)
    out["gvec"] = np.ascontiguousarray(g.reshape(7, KT, 128).transpose(2, 0, 1).reshape(128, 7 * KT))
    win = np.asarray(inp["ffn_w_in"], f).reshape(4, D, 2, JT, 128)
    win = win.reshape(4, KT, 128, 2, JT, 128).transpose(0, 4, 2, 3, 1, 5)
    out["ffn_win"] = np.ascontiguousarray(win.reshape(4 * JT, 128, 2 * KT * 128))
    wout = np.asarray(inp["ffn_w_out"], f).reshape(4, DFF, D)
    out["ffn_wout"] = np.concatenate([_tile_rows(wout[i], JT) for i in range(4)], 0)
    out["s5_win"] = _tile_rows(np.asarray(inp["s5_w_in"], f)[0], KT)
    out["wglu"] = _tile_rows(np.asarray(inp["s5_w_glu"], f)[0], KT)
    out["ml_win"] = _tile_rows(np.asarray(inp["ml_w_in"], f)[0], KT)

    def gp(a):
        sh = a.shape
        a = a.reshape((2, 64, 2, 64) + sh[3:])
        perm = (0, 2, 3, 1) + tuple(range(4, a.ndim))
        a = a.transpose(perm)
        return np.ascontiguousarray(a.reshape((2, 128, 64) + sh[3:]))
    out["lamre"] = gp(np.asarray(inp["s5_lambda_re"], f)[0])
    out["lamim"] = gp(np.asarray(inp["s5_lambda_im"], f)[0])
    ls = np.asarray(inp["s5_log_step"], f)[0]
    out["logstep"] = gp(np.broadcast_to(ls[:, :, None], (2, 128, 64)).copy())
    out["bre"] = gp(np.asarray(inp["s5_b_re"], f)[0]).reshape(2, 128, 64 * 16)
    out["bim"] = gp(np.asarray(inp["s5_b_im"], f)[0]).reshape(2, 128, 64 * 16)
    cri = np.zeros((KT, 128, 2, 4, 2, 32), f)
    for ri, key in enumerate(("s5_c_re", "s5_c_im")):
        c = np.asarray(inp[key], f)[0]
        c = c.reshape(2, KT, 4, 2, 16, 64)
        for g2 in range(2):
            cri[:, g2 * 64:(g2 + 1) * 64, :, :, ri, g2 * 16:(g2 + 1) * 16] = c[:, :, :, g2].transpose(1, 4, 0, 2, 3)
    out["CRI"] = np.ascontiguousarray(cri.reshape(KT, 128, 512))
    out["s5d"] = np.ascontiguousarray(np.asarray(inp["s5_d"], f)[0].reshape(KT, 128).T)
    mlD = np.zeros((CT, 128, 8, 128), f)
    cw = np.asarray(inp["ml_conv_w"], f)[0].reshape(5, CT, 128)
    ar = np.arange(128)
    for tau in range(5):
        mlD[:, ar, tau, ar] = cw[tau]
    for j, key in enumerate(("ml_wq", "ml_wk", "ml_wv")):
        w = np.asarray(inp[key], f)[0].reshape(CT, 32, 4, 4)
        for n in range(32):
            mlD[:, 4 * n:4 * n + 4, 5 + j, 4 * n:4 * n + 4] = w[:, n]
    out["mlD"] = np.ascontiguousarray(mlD.reshape(CT, 128, 1024))
    out["ml_wout"] = _tile_rows(np.asarray(inp["ml_w_out"], f)[0], CT)
    wg = np.asarray(inp["ml_w_gates"], f)[0].reshape(3, CT, 128, 64)
    out["wg"] = np.ascontiguousarray(wg.transpose(2, 0, 1, 3).reshape(128, 3 * CT * 64))
    out["bgate"] = np.ascontiguousarray(np.asarray(inp["ml_b_gates"], f)[0].reshape(1, 64))
    vecs = [np.asarray(inp[kk], f)[0].reshape(CT, 128).T for kk in ("ml_conv_b", "ml_norm_g", "ml_skip")]
    out["mlvec"] = np.ascontiguousarray(np.concatenate(vecs, 1))
    return out


_CACHE = {}


def kernel(**inputs):
    f = np.float32
    TILES, SEGT = 16, 4
    NTOK = TILES * NT
    sh = prep_shared(inputs)
    xp = np.asarray(inputs["x_prompt"], f)
    xs = np.asarray(inputs["x_sample"], f)
    in_maps = []
    for c in range(8):
        m = dict(sh)
        if c < 4:
            m["xT"] = np.ascontiguousarray(xp[c].T)
            m["keep"] = np.ones((128, 1), f)
        else:
            xt = np.zeros((D, NTOK), f)
            for j in range(2):
                xt[:, j * 2048:(j + 1) * 2048] = xs[2 * (c - 4) + j].T
            m["xT"] = xt
            m["keep"] = np.zeros((128, 1), f)
        in_maps.append(m)
    if "nc" not in _CACHE:
        _CACHE["nc"] = build(TILES, SEGT)[0]
    res = run_bass_kernel_spmd(_CACHE["nc"], in_maps, core_ids=list(range(8)))
    yp = np.zeros((4, 8192, D), f)
    ys = np.zeros((8, 2048, D), f)
    for c in range(8):
        y = np.asarray(res.results[c]["yT"], f)
        if c < 4:
            yp[c] = y.T
        else:
            for j in range(2):
                ys[2 * (c - 4) + j] = y[:, j * 2048:(j + 1) * 2048].T
    return (yp, ys)
```
